# Optimizing a Trainium2 kernel written in Bass

```python
import math
import jax, jax.numpy as jnp
from jax import lax
import numpy as np

D_MODEL = 2048
BATCH = 4
SEQ = 2048
DEPTH = 1
DEC_BATCH = 128
DEC_SEQ = 8
PAST_LEN = 16384
PAGE_SIZE = 128

MIX_WIDTH = D_MODEL
M_WIDTH = MIX_WIDTH // 2
M_HEADS = 4
M_DH = M_WIDTH // M_HEADS
G_WIDTH = MIX_WIDTH // 2
G_HEADS = 4
G_DK = G_WIDTH // 2 // G_HEADS
G_DV = G_WIDTH // G_HEADS
G_LOWRANK = 16
GLA_NORMALIZER = 16.0
D_FF = 11 * D_MODEL // 4
CONV_W = 3
CHUNK = 128
EPS = 1e-6
NEG_BIG = -1e30

kernel_name = "hybrid_mlstm_gla_convffn_step"


def _in_sizes():
    return (M_WIDTH, M_WIDTH, M_WIDTH, M_WIDTH,
            M_HEADS, M_HEADS,
            G_HEADS * G_DK, G_HEADS * G_DK,
            G_WIDTH, G_WIDTH,
            G_LOWRANK,
            D_MODEL, D_MODEL)


def _f32(a):
    return a.astype(jnp.float32)


def _rmsnorm(x, w):
    xf = _f32(x)
    return xf * lax.rsqrt(jnp.mean(xf * xf, axis=-1, keepdims=True) + EPS) * _f32(w)


def _to_chunks(a, n_chunks, length):
    return jnp.moveaxis(a.reshape((a.shape[0], n_chunks, length) + a.shape[2:]), 1, 0)


def _from_chunks(a):
    a = jnp.moveaxis(a, 0, 1)
    return a.reshape((a.shape[0], a.shape[1] * a.shape[2]) + a.shape[3:])


def _mlstm(q, k, v, i_raw, log_f, C0, n0, m0):
    T = q.shape[1]
    L = math.gcd(T, CHUNK)
    nc = T // L
    causal = jnp.tril(jnp.ones((L, L), dtype=bool))

    def step(carry, xs):
        C, n, m = carry
        qc, kc, vc, ic, fc = xs
        b = jnp.swapaxes(jnp.cumsum(fc, axis=1), 1, 2)
        it = jnp.swapaxes(ic, 1, 2)
        log_d = b[..., :, None] - b[..., None, :] + it[..., None, :]
        log_d = jnp.where(causal, log_d, NEG_BIG)
        inter = b + m[..., None]
        m_t = jnp.maximum(inter, jnp.max(log_d, axis=-1))
        d = jnp.exp(log_d - m_t[..., None])
        s = jnp.einsum('blhd,bshd->bhls', qc, kc) * d
        w_inter = jnp.exp(inter - m_t)
        num = (jnp.einsum('bhls,bshe->blhe', s, vc)
               + jnp.swapaxes(w_inter, 1, 2)[..., None] * jnp.einsum('blhd,bhde->blhe', qc, C))
        den = jnp.sum(s, axis=-1) + w_inter * jnp.einsum('blhd,bhd->bhl', qc, n)
        floor = jnp.maximum(jnp.abs(den), jnp.exp(-m_t))
        h = num / jnp.swapaxes(floor, 1, 2)[..., None]
        m_end = m_t[..., -1]
        w_end = jnp.exp(b[..., -1:] - b + it - m_end[..., None])
        decay = jnp.exp(b[..., -1] + m - m_end)
        C_new = decay[..., None, None] * C + jnp.einsum('bhl,blhd,blhe->bhde', w_end, kc, vc)
        n_new = decay[..., None] * n + jnp.einsum('bhl,blhd->bhd', w_end, kc)
        return (C_new, n_new, m_end), h

    xs = tuple(_to_chunks(a, nc, L) for a in (q, k, v, i_raw, log_f))
    (C, n, m), hs = lax.scan(step, (C0, n0, m0), xs)
    return _from_chunks(hs), C, n, m


def _gla(q, k, v, log_a, S0):
    T = q.shape[1]
    L = math.gcd(T, CHUNK)
    nc = T // L
    causal = jnp.tril(jnp.ones((L, L), dtype=bool))[None, :, :, None, None]

    def step(S, xs):
        qc, kc, vc, ac = xs
        b = jnp.cumsum(ac, axis=1)
        diff = b[:, :, None] - b[:, None, :]
        decay = jnp.where(causal, jnp.exp(jnp.minimum(diff, 0.0)), 0.0)
        a_mat = jnp.einsum('bthd,bshd,btshd->bhts', qc, kc, decay)
        o = (jnp.einsum('bhts,bshe->bthe', a_mat, vc)
             + jnp.einsum('bthd,bhde->bthe', qc * jnp.exp(b), S))
        b_end = b[:, -1]
        S_new = (jnp.exp(b_end)[..., None] * S
                 + jnp.einsum('bshd,bshe->bhde', kc * jnp.exp(b_end[:, None] - b), vc))
        return S_new, o

    xs = tuple(_to_chunks(a, nc, L) for a in (q, k, v, log_a))
    S, os_ = lax.scan(step, S0, xs)
    return _from_chunks(os_), S


def _head_norm(h, w):
    hn = h * lax.rsqrt(jnp.mean(h * h, axis=-1, keepdims=True) + EPS)
    return hn.reshape(h.shape[0], h.shape[1], -1) * _f32(w)


def _layer(x, C0, n0, m0, S0, conv0,
           w_in, mlstm_i_bias, mlstm_f_bias, gla_alpha_up, gla_alpha_bias,
           mlstm_head_norm_w, gla_head_norm_w, w_branch_a, w_branch_b, w_out,
           norm_mix_w, norm_ffn_w, ffn_w_up, ffn_w_gate, ffn_conv_w, ffn_conv_b, ffn_w_down):
    B, T, _ = x.shape
    x = _f32(x)
    h = _rmsnorm(x, norm_mix_w)
    proj = h @ _f32(w_in)
    split_at = np.cumsum(_in_sizes())[:-1].tolist()
    mq, mk, mv, mo, mi, mf, gq, gk, gv, gr, gal, gate_a, gate_b = jnp.split(proj, split_at, axis=-1)

    q_m = mq.reshape(B, T, M_HEADS, M_DH)
    k_m = mk.reshape(B, T, M_HEADS, M_DH) * (M_DH ** -0.5)
    v_m = mv.reshape(B, T, M_HEADS, M_DH)
    i_raw = mi + _f32(mlstm_i_bias)
    log_f = jax.nn.log_sigmoid(mf + _f32(mlstm_f_bias))
    h_m, C1, n1, m1 = _mlstm(q_m, k_m, v_m, i_raw, log_f, _f32(C0), _f32(n0), _f32(m0))
    y_m = _head_norm(h_m, mlstm_head_norm_w) * jax.nn.sigmoid(mo)

    q_g = gq.reshape(B, T, G_HEADS, G_DK) * (G_DK ** -0.5)
    k_g = gk.reshape(B, T, G_HEADS, G_DK)
    v_g = gv.reshape(B, T, G_HEADS, G_DV)
    log_a = jax.nn.log_sigmoid(gal @ _f32(gla_alpha_up) + _f32(gla_alpha_bias)) / GLA_NORMALIZER
    log_a = log_a.reshape(B, T, G_HEADS, G_DK)
    o_g, S1 = _gla(q_g, k_g, v_g, log_a, _f32(S0))
    y_g = _head_norm(o_g, gla_head_norm_w) * jax.nn.silu(gr)

    merged = (jax.nn.sigmoid(gate_a) * (y_m @ _f32(w_branch_a))
              + jax.nn.sigmoid(gate_b) * (y_g @ _f32(w_branch_b)))
    x = x + merged @ _f32(w_out)

    h2 = _rmsnorm(x, norm_ffn_w)
    u = h2 @ _f32(ffn_w_up)
    g = h2 @ _f32(ffn_w_gate)
    u_pad = jnp.concatenate([_f32(conv0), u], axis=1)
    cw = _f32(ffn_conv_w)
    uc = _f32(ffn_conv_b) + sum(cw[j] * u_pad[:, j:j + T] for j in range(CONV_W))
    x = x + (jax.nn.gelu(uc) * g) @ _f32(ffn_w_down)
    conv1 = u_pad[:, T:]
    return x, C1, n1, m1, S1, conv1


def setup_inputs(seed: int = 0) -> dict:
    key = jax.random.key(seed)
    ks = jax.random.split(key, 32)
    n_in = sum(_in_sizes())
    nrm = jax.random.normal
    f32 = jnp.float32
    return {
        "x_prompt": nrm(ks[0], (BATCH, SEQ, D_MODEL), f32),
        "x_sample": nrm(ks[1], (DEC_BATCH, DEC_SEQ, D_MODEL), f32),
        "state_mlstm_C": 0.05 * nrm(ks[2], (DEPTH, DEC_BATCH, M_HEADS, M_DH, M_DH), f32),
        "state_mlstm_n": 0.05 * nrm(ks[3], (DEPTH, DEC_BATCH, M_HEADS, M_DH), f32),
        "state_mlstm_m": nrm(ks[4], (DEPTH, DEC_BATCH, M_HEADS), f32),
        "state_gla_S": 0.05 * nrm(ks[5], (DEPTH, DEC_BATCH, G_HEADS, G_DK, G_DV), f32),
        "state_ffn_conv": nrm(ks[6], (DEPTH, DEC_BATCH, CONV_W - 1, D_FF), f32),
        "w_in": nrm(ks[7], (DEPTH, D_MODEL, n_in), f32) * D_MODEL ** -0.5,
        "mlstm_i_bias": 0.1 * nrm(ks[8], (DEPTH, M_HEADS), f32),
        "mlstm_f_bias": 3.0 + 0.5 * nrm(ks[9], (DEPTH, M_HEADS), f32),
        "gla_alpha_up": nrm(ks[10], (DEPTH, G_LOWRANK, G_HEADS * G_DK), f32) * G_LOWRANK ** -0.5,
        "gla_alpha_bias": 0.01 * nrm(ks[11], (DEPTH, G_HEADS * G_DK), f32),
        "mlstm_head_norm_w": 1.0 + 0.05 * nrm(ks[12], (DEPTH, M_WIDTH), f32),
        "gla_head_norm_w": 1.0 + 0.05 * nrm(ks[13], (DEPTH, G_WIDTH), f32),
        "w_branch_a": nrm(ks[14], (DEPTH, M_WIDTH, D_MODEL), f32) * M_WIDTH ** -0.5,
        "w_branch_b": nrm(ks[15], (DEPTH, G_WIDTH, D_MODEL), f32) * G_WIDTH ** -0.5,
        "w_out": nrm(ks[16], (DEPTH, D_MODEL, D_MODEL), f32) * D_MODEL ** -0.5,
        "norm_mix_w": 1.0 + 0.05 * nrm(ks[17], (DEPTH, D_MODEL), f32),
        "norm_ffn_w": 1.0 + 0.05 * nrm(ks[18], (DEPTH, D_MODEL), f32),
        "ffn_w_up": nrm(ks[19], (DEPTH, D_MODEL, D_FF), f32) * D_MODEL ** -0.5,
        "ffn_w_gate": nrm(ks[20], (DEPTH, D_MODEL, D_FF), f32) * D_MODEL ** -0.5,
        "ffn_conv_w": nrm(ks[21], (DEPTH, CONV_W, D_FF), f32) * CONV_W ** -0.5,
        "ffn_conv_b": 0.01 * nrm(ks[22], (DEPTH, D_FF), f32),
        "ffn_w_down": nrm(ks[23], (DEPTH, D_FF, D_MODEL), f32) * D_FF ** -0.5,
        "final_norm_w": 1.0 + 0.05 * nrm(ks[24], (D_MODEL,), f32),
    }


def reference(x_prompt, x_sample, state_mlstm_C, state_mlstm_n, state_mlstm_m, state_gla_S,
              state_ffn_conv, w_in, mlstm_i_bias, mlstm_f_bias, gla_alpha_up, gla_alpha_bias,
              mlstm_head_norm_w, gla_head_norm_w, w_branch_a, w_branch_b, w_out,
              norm_mix_w, norm_ffn_w, ffn_w_up, ffn_w_gate, ffn_conv_w, ffn_conv_b, ffn_w_down,
              final_norm_w):
    bp = x_prompt.shape[0]
    f32 = jnp.float32
    xp, xs = x_prompt, x_sample
    pC, pn, pm, pS, pconv = [], [], [], [], []
    sC, sn, sm, sS, sconv = [], [], [], [], []
    for l in range(DEPTH):
        params = (w_in[l], mlstm_i_bias[l], mlstm_f_bias[l], gla_alpha_up[l], gla_alpha_bias[l],
                  mlstm_head_norm_w[l], gla_head_norm_w[l], w_branch_a[l], w_branch_b[l], w_out[l],
                  norm_mix_w[l], norm_ffn_w[l], ffn_w_up[l], ffn_w_gate[l], ffn_conv_w[l],
                  ffn_conv_b[l], ffn_w_down[l])
        xp, C1, n1, m1, S1, cv1 = _layer(
            xp, jnp.zeros((bp, M_HEADS, M_DH, M_DH), f32), jnp.zeros((bp, M_HEADS, M_DH), f32),
            jnp.zeros((bp, M_HEADS), f32), jnp.zeros((bp, G_HEADS, G_DK, G_DV), f32),
            jnp.zeros((bp, CONV_W - 1, D_FF), f32), *params)
        pC.append(C1); pn.append(n1); pm.append(m1); pS.append(S1); pconv.append(cv1)
        xs, C2, n2, m2, S2, cv2 = _layer(
            xs, state_mlstm_C[l], state_mlstm_n[l], state_mlstm_m[l], state_gla_S[l],
            state_ffn_conv[l], *params)
        sC.append(C2); sn.append(n2); sm.append(m2); sS.append(S2); sconv.append(cv2)
    y_prompt = _rmsnorm(xp, final_norm_w).astype(x_prompt.dtype)
    y_sample = _rmsnorm(xs, final_norm_w).astype(x_sample.dtype)
    return (y_prompt, y_sample,
            jnp.stack(pC), jnp.stack(pn), jnp.stack(pm), jnp.stack(pS), jnp.stack(pconv),
            jnp.stack(sC), jnp.stack(sn), jnp.stack(sm), jnp.stack(sS), jnp.stack(sconv))
```

```python
import numpy as np
from contextlib import ExitStack
import concourse.bass as bass
import concourse.mybir as mybir
from concourse.bass_utils import run_bass_kernel_spmd

F32 = mybir.dt.float32
BF16 = mybir.dt.bfloat16
AF = mybir.ActivationFunctionType
ALU = mybir.AluOpType
AX = mybir.AxisListType

D = 2048
NIN = 11288
DFF = 5632
KD = D // 128
KF = DFF // 128
PRE = 896
NPT = PRE // 128
TM = 1280
CB_W = 2576
CF_W = 520
SP_W = 22 + 4 * KF
NMT = TM // 128
NCH = 16
EPS = 1e-6

C_MQ, C_MK, C_MV, C_MO = 0, 1024, 2048, 3072
C_MI, C_MF = 4096, 4100
C_GQ, C_GK, C_GV, C_GR = 4104, 4616, 5128, 6152
C_GAL = 7176
C_GA, C_GB = 7192, 9240


def _win_blocks():
    blks = [(C_MI, 8), (C_GAL, 16)]
    for hh in range(4):
        blks += [(C_MQ + hh * 256, 256), (C_MK + hh * 256, 256), (C_MV + hh * 256, 256), (C_MO + hh * 256, 256)]
    for hh in range(4):
        blks += [(C_GQ + hh * 128, 128), (C_GK + hh * 128, 128), (C_GV + hh * 256, 256), (C_GR + hh * 256, 256)]
    for cg in range(8):
        blks += [(C_GA + cg * 256, 256), (C_GB + cg * 256, 256)]
    return blks


def _win_offsets():
    off = {}
    o = 0
    for c0, n in _win_blocks():
        off[c0] = o
        o += KD * n
    return off


def _pack(w, nk, blocks):
    outs = []
    for c0, n in blocks:
        outs.append(np.ascontiguousarray(w[:, c0:c0 + n].reshape(nk, 128, n).transpose(1, 0, 2)).reshape(128, nk * n))
    return np.ascontiguousarray(np.concatenate(outs, axis=1))


class _Probe:
    def __init__(self):
        self.calls = []

    def __getattr__(self, name):
        def f(*a, **k):
            self.calls.append((name, a, k))
            return self
        return f

    def then_inc(self, *a, **k):
        return self


def _free_elems(ap):
    n = 1
    for s in ap.shape[1:]:
        n *= int(s)
    return n


def _est(eng, fn):
    p = _Probe()
    fn(p)
    name, a, k = p.calls[0]
    out = k.get("out", a[0] if a else None)
    if eng == "pe":
        if name == "transpose":
            return 0.09
        rhs = k.get("rhs")
        n = _free_elems(rhs)
        f = 4.0 if rhs.dtype == F32 else 1.0
        return max(n, 64) * f / 2400.0 + 0.012
    n = _free_elems(out) if out is not None else 64
    if eng == "act":
        return (n + 200) / 1400.0 + (0.1 if k.get("accum_out") is not None else 0.0)
    if eng in ("dve", "pool"):
        f = 2.0 if name in ("tensor_tensor_scan",) else 1.0
        return max(n, 60) * f / 960.0 + 0.07
    return 0.1


def _dma_est(fn):
    p = _Probe()
    fn(p)
    name, a, k = p.calls[0]
    out = k.get("out")
    nbytes = 1
    for s in out.shape:
        nbytes *= int(s)
    nbytes *= 4 if out.dtype == F32 else 2
    return 2.0 + nbytes / 180e3, nbytes


class Sched:
    ENGS = ("pe", "act", "dve", "pool", "sp")
    XLAT = 0.12

    def __init__(self, esems, dsems):
        self.esem = esems
        self.dsems = dsems
        self.units = []
        self.bufs = {}
        self.phase = 0
        self.trace_phase = None
        self.open = {e: None for e in esems}

    def _st(self, b):
        st = self.bufs.get(b)
        if st is None:
            st = {"w": None, "r": {}}
            self.bufs[b] = st
        return st

    def _deps(self, uid, reads, writes):
        deps = set()
        for b in reads:
            st = self._st(b)
            if st["w"] is not None:
                deps.add(st["w"])
        for b in writes:
            st = self._st(b)
            if st["w"] is not None:
                deps.add(st["w"])
            deps.update(st["r"].keys())
        deps.discard(uid)
        return deps

    def _mark(self, uid, reads, writes):
        for b in reads:
            self._st(b)["r"][uid] = True
        for b in writes:
            st = self._st(b)
            st["w"] = uid
            st["r"] = {}

    def op(self, eng, fn, reads=(), writes=(), inc=True):
        u = self.open[eng]
        if u is None:
            u = {"eng": eng, "fns": [], "deps": set(), "dur": 0.0, "dma": False, "phase": self.phase,
                 "id": len(self.units), "lab": ",".join(writes)}
            self.units.append(u)
            self.open[eng] = u
        u["deps"] |= self._deps(u["id"], reads, writes)
        self._mark(u["id"], reads, writes)
        u["fns"].append(fn)
        u["dur"] += _est(eng, fn)
        if inc:
            self.open[eng] = None

    def dma(self, q, fn, reads=(), writes=()):
        uid = len(self.units)
        lat, nbytes = _dma_est(fn)
        u = {"eng": q, "fns": [fn], "deps": self._deps(uid, reads, writes), "dur": 0.06 if q == "sp" else 0.35,
             "dma": True, "lat": lat, "phase": self.phase, "id": uid, "lab": "dma:" + ",".join(writes) + "<" + ",".join(reads)}
        self.units.append(u)
        self._mark(uid, reads, writes)

    def barrier(self):
        for e, v in self.open.items():
            assert v is None, e
        self.phase += 1
        self.bufs = {}

    def barrier_all_dma(self, q="sp"):
        pass

    def finish(self):
        import heapq
        order = {e: [] for e in self.ENGS}
        nph = self.phase + 1
        by_phase = [[] for _ in range(nph)]
        for u in self.units:
            by_phase[u["phase"]].append(u)
        for ph in range(nph):
            us = by_phase[ph]
            ids = {u["id"] for u in us}
            ndep = {}
            users = {}
            for u in us:
                d = [x for x in u["deps"] if x in ids]
                u["deps"] = set(d)
                ndep[u["id"]] = len(d)
                for x in d:
                    users.setdefault(x, []).append(u)
            byid = {u["id"]: u for u in us}
            ready = {e: [] for e in self.ENGS}
            avail = {}
            for u in us:
                if ndep[u["id"]] == 0:
                    heapq.heappush(ready[u["eng"]], u["id"])
                    avail[u["id"]] = 0.0
            tfree = {e: 0.0 for e in self.ENGS}
            lastu = {}
            done = 0
            fin = {}
            while done < len(us):
                best = None
                for e in self.ENGS:
                    h = ready[e]
                    if not h:
                        continue
                    cand = None
                    tnow = tfree[e]
                    low = [i for i in h if avail[i] <= tnow]
                    if low:
                        cid = min(low)
                        st = tnow
                    else:
                        cid = min(h, key=lambda i: (avail[i], i))
                        st = avail[cid]
                    if best is None or st < best[0] or (st == best[0] and cid < best[2]):
                        best = (st, e, cid)
                st, e, cid = best
                ready[e].remove(cid)
                heapq.heapify(ready[e])
                u = byid[cid]
                u["st"] = st
                if avail[cid] >= tfree[e] - 1e-9 and u["deps"]:
                    u["why"] = max(u["deps"], key=lambda x: fin[x])
                else:
                    u["why"] = lastu.get(e)
                lastu[e] = cid
                end = st + u["dur"]
                tfree[e] = end
                f = end + (u["lat"] if u["dma"] else 0.0)
                fin[cid] = f
                order[e].append(u)
                done += 1
                for v in users.get(cid, ()):
                    ndep[v["id"]] -= 1
                    if ndep[v["id"]] == 0:
                        t = 0.0
                        for x in v["deps"]:
                            lat = 0.0 if (byid[x]["eng"] == v["eng"] and v["eng"] == "pe") else self.XLAT
                            t = max(t, fin[x] + lat)
                        avail[v["id"]] = t
                        heapq.heappush(ready[v["eng"]], v["id"])
            self.sim_phase_us = getattr(self, "sim_phase_us", []) + [max(tfree.values())]
            if getattr(self, "trace_phase", None) == ph:
                cur = max(us, key=lambda u: fin[u["id"]])["id"]
                chain = []
                while cur is not None and len(chain) < 400:
                    u = byid[cur]
                    chain.append((round(u["st"], 2), u["eng"], round(u["dur"], 2), u["lab"], len(u["fns"])))
                    cur = u.get("why")
                for c in chain[:400]:
                    print("   CP", c)
            busy = {e: 0.0 for e in self.ENGS}
            for u in us:
                busy[u["eng"]] += u["dur"]
            print("phase", ph, "sim %.0f us" % max(tfree.values()), {e: int(v) for e, v in busy.items()}, "units", len(us))
        cnt = {e: 0 for e in self.esem}
        dcnt = {q: [0] * len(v) for q, v in self.dsems.items()}
        dnext = {q: 0 for q in self.dsems}
        for e in self.ENGS:
            for u in order[e]:
                if u["dma"]:
                    i = dnext[e]
                    dnext[e] = (i + 1) % len(self.dsems[e])
                    u["prev"] = (self.dsems[e][i], dcnt[e][i]) if dcnt[e][i] > 0 else None
                    dcnt[e][i] += 16
                    u["ev"] = (self.dsems[e][i], dcnt[e][i])
                else:
                    cnt[e] += 1
                    u["ev"] = (self.esem[e], cnt[e])
        self.order = order
        byid = {u["id"]: u for u in self.units}
        self.prog = {e: [] for e in self.ENGS}
        all_dma = [u for u in self.units if u["dma"]]
        for e in self.ENGS:
            wd = {}
            cur_phase = 0
            for u in order[e]:
                waits = []
                if u["phase"] != cur_phase:
                    for e2 in self.esem:
                        if e2 == e and e == "pe":
                            continue
                        c = max([x["ev"][1] for x in order[e2] if x["phase"] < u["phase"] and not x["dma"]] or [0])
                        sem = self.esem[e2]
                        if c > 0 and wd.get(id(sem), 0) < c:
                            wd[id(sem)] = c
                            waits.append((sem, c))
                    for x in all_dma:
                        if x["phase"] < u["phase"] and wd.get(id(x["ev"][0]), 0) < x["ev"][1]:
                            wd[id(x["ev"][0])] = x["ev"][1]
                            waits.append(x["ev"])
                    cur_phase = u["phase"]
                for d in sorted(u["deps"]):
                    x = byid[d]
                    if x["eng"] == e and e == "pe":
                        continue
                    sem, c = x["ev"]
                    if wd.get(id(sem), 0) >= c:
                        continue
                    wd[id(sem)] = c
                    waits.append((sem, c))
                if u["dma"] and u["prev"] is not None:
                    sem, c = u["prev"]
                    if wd.get(id(sem), 0) < c:
                        wd[id(sem)] = c
                        waits.append((sem, c))
                self.prog[e].append((waits, u["fns"], u["ev"][0], 16 if u["dma"] else 1))
        wd = {}
        waits = []
        for x in all_dma:
            if wd.get(id(x["ev"][0]), 0) < x["ev"][1]:
                wd[id(x["ev"][0])] = x["ev"][1]
        semobj = {}
        for x in all_dma:
            semobj[id(x["ev"][0])] = x["ev"][0]
        self.final_waits = [(semobj[k], v) for k, v in wd.items()]

    def emit(self, eng_name, eng):
        for waits, fns, sem, inc in self.prog[eng_name]:
            for s, v in waits:
                eng.wait_ge(s, v)
            ins = None
            for fn in fns:
                ins = fn(eng)
            ins.then_inc(sem, inc)
        if eng_name == "sp":
            for s, v in self.final_waits:
                eng.wait_ge(s, v)


def build(debug=None, stop_after=None):
    debug = debug or {}
    nc = bass.Bass("TRN2", target_bir_lowering=False)
    es = ExitStack()

    def din(name, shape, dt=F32):
        return nc.dram_tensor(name, list(shape), dt, kind="ExternalInput").ap()

    def dout(name, shape, dt=F32):
        return nc.dram_tensor(name, list(shape), dt, kind="ExternalOutput").ap()

    def dint(name, shape, dt=F32):
        return nc.dram_tensor(name, list(shape), dt, kind="Internal").ap()

    xpre = din("xpre", [PRE, D])
    xmain = din("xmain", [TM, D])
    flag = din("flag", [1, 1])
    w_in = din("w_in", [128, KD * NIN])
    WOFF = _win_offsets()
    nmw = din("nmw", [1, D])
    nfw = din("nfw", [1, D])
    fnw = din("fnw", [1, D])
    cst_bf = din("cst_bf", [128, CB_W], BF16)
    cst_f = din("cst_f", [128, CF_W])
    smallp = din("smallp", [128, SP_W])
    aup = din("aup", [16, 512])
    w_a = din("w_a", [128, 8 * D])
    w_b = din("w_b", [128, 8 * D])
    w_o = din("w_o", [128, KD * D])
    w_up = din("w_up", [128, KD * DFF])
    w_gt = din("w_gt", [128, KD * DFF])
    w_dn = din("w_dn", [128, KF * D])
    sC = din("sC", [16, 4, 256, 256])
    sn = din("sn", [64, 256])
    smcol = din("smcol", [64, 1])
    sS = din("sS", [16, 4, 128, 256])
    sconv = din("sconv", [16, 2, DFF])

    yout = dout("yout", [TM - 128, D])
    o_pC = dout("o_pC", [4, 256, 256])
    o_pn = dout("o_pn", [8, 128])
    o_pm = dout("o_pm", [1, 64])
    o_pS = dout("o_pS", [4, 128, 256])
    o_pconv = dout("o_pconv", [2, DFF])
    o_sC = dout("o_sC", [16, 4, 256, 256])
    o_sn = dout("o_sn", [128, 128])
    o_sm = dout("o_sm", [64, 1])
    o_sS = dout("o_sS", [16, 4, 128, 256])
    o_sconv = dout("o_sconv", [16, 2, DFF])

    gscr = dint("gscr", [8, 2048])
    gscs = dint("gscs", [8, 128])
    yscr = dint("yscr", [TM, D], BF16)
    x1scr = dint("x1scr", [TM, D])
    x2scr_ = dint("x2scr", [TM, D])
    dbg_out = {}

    with es:
        def sb(name, shape, dt=F32):
            return es.enter_context(nc.sbuf_tensor(name, list(shape), dt))

        def ps(name, shape, dt=F32):
            return es.enter_context(nc.psum_tensor(name, list(shape), dt))

        esems = {e: es.enter_context(nc.semaphore("s_" + e)) for e in ("pe", "act", "dve", "pool")}
        dsems = {
            "sp": [es.enter_context(nc.semaphore("d_sp%d" % i)) for i in range(24)],
            "pool": [es.enter_context(nc.semaphore("d_pl%d" % i)) for i in range(12)],
        }
        S = Sched(esems, dsems)

        def act(fn, r, w):
            S.op("act", fn, reads=r, writes=w)

        def dve(fn, r, w):
            S.op("dve", fn, reads=r, writes=w)

        def pool(fn, r, w):
            S.op("pool", fn, reads=r, writes=w)

        def sdma(out, in_, r, w):
            S.dma("sp", lambda e: e.dma_start(out=out, in_=in_), reads=r, writes=w)

        def mm(out_ap, outb, pairs, reads, first=True, last=True):
            n = len(pairs)
            for i, (l, r) in enumerate(pairs):
                S.op("pe", lambda e, l=l, r=r, i=i: e.matmul(out_ap, lhsT=l, rhs=r, start=(first and i == 0),
                                                            stop=(last and i == n - 1)),
                     reads=reads, writes=[outb], inc=(i == n - 1))

        def tr(out_ap, outb, in_ap, inb, ident, inc=True):
            S.op("pe", lambda e: e.transpose(out=out_ap, in_=in_ap, identity=ident), reads=[inb, "cb", "cf"],
                 writes=[outb], inc=inc)

        cb = sb("cb", [128, CB_W], BF16)
        cf = sb("cf", [128, CF_W], F32)
        spm = sb("spm", [128, SP_W], F32)
        sdma(cb[:], cst_bf[:, :], [], ["cb"])
        sdma(cf[:], cst_f[:, :], [], ["cf"])
        sdma(spm[:], smallp[:, :], [], ["spm"])
        identb = cb[:, 0:128]
        maskc = cb[:, 128:256]
        masks = cb[:, 256:384]
        qxmask = cb[:, 512:512 + 2048].rearrange("p (j t) -> p j t", j=16)
        vxmask = cb[:, 2560:2576]
        identf = cf[:, 0:128]
        onesf = cf[:, 128:256]
        resetm = cf[:, 256:384]
        negbig = cf[:, 384:512]
        zerocol = cf[:, 512:513]
        P_IB, P_FB, P_AB, P_HNM, P_HNG, P_CW, P_CB = 0, 1, 2, 6, 14, 22, 22 + 3 * KF
        ib64 = spm[0:64, P_IB:P_IB + 1]
        fb64 = spm[0:64, P_FB:P_FB + 1]
        wbc = sb("wbc", [128, D], F32)
        sdma(wbc[:], nmw.partition_broadcast(128), [], ["wbc"])
        flagt = sb("flagt", [1, 1], F32)
        sdma(flagt[:], flag[:, :], [], ["flagt"])

        hT = sb("hT", [128, KD, TM], BF16)
        hTp = sb("hTp", [128, KD, PRE], BF16)
        wsl = [sb("wsl%d" % i, [128, KD * 256], BF16) for i in range(3)]
        stat = sb("stat", [128, 64], F32)
        ARENA = 99 * 512 + 160
        arena = sb("arena", [128, ARENA], BF16)
        apos = [0]
        aphase = [0]

        def carve(name, shape, dt=BF16):
            n = int(np.prod(shape))
            nb = n * (2 if dt == BF16 else 4)
            nb = (nb + 63) // 64 * 64
            off = apos[0]
            apos[0] += nb // 2
            assert apos[0] <= ARENA, ("arena overflow", name, apos[0])
            v = arena[:, off:off + nb // 2]
            if dt != BF16:
                v = v.bitcast(dt)
            v = v[:, 0:n]
            if len(shape) == 2:
                pat, kw = "p (a b) -> p a b", dict(a=shape[0])
            elif len(shape) == 3:
                pat, kw = "p (a b c) -> p a b c", dict(a=shape[0], b=shape[1])
            else:
                pat, kw = None, None
            if pat:
                v = v.rearrange(pat, **kw)
            return v, "ar%d_%s" % (aphase[0], name)

        def arena_reset():
            S.barrier()
            print("arena phase %d used %d / %d" % (aphase[0], apos[0] * 2, ARENA * 2))
            apos[0] = 0
            aphase[0] += 1

        pa = ps("pa", [128, 512])
        pb = ps("pb", [128, 512])
        ptr = [ps("ptr%d" % i, [128, 1024], BF16) for i in range(2)]
        pst = ps("pst", [128, 512])
        pnum = ps("pnum", [128, 512])
        pstate = ps("pstate", [128, 512])
        pmisc = ps("pmisc", [128, 512])
        pbk = [(pa, "pa"), (pb, "pb")]
        pbk6 = [(pa, "pa"), (pb, "pb"), (pst, "pst"), (pnum, "pnum"), (pstate, "pstate"), (pmisc, "pmisc")]
        pbi = [0]
        wide = [False]

        def nextbank():
            pbi[0] += 1
            if wide[0]:
                return pbk6[pbi[0] % 6]
            return pbk[pbi[0] % 2]

        evi = [0]

        def evac(dst, dstb, src, srcb, func=None, scale=None, eng=None):
            if func is not None or eng == "act":
                f = func if func is not None else AF.Copy
                if scale is None:
                    act(lambda e: e.activation(out=dst, in_=src, func=f), [srcb], [dstb])
                else:
                    act(lambda e: e.activation(out=dst, in_=src, func=f, scale=scale), [srcb], [dstb])
                return
            evi[0] += 1
            if eng is None:
                eng = "act" if evi[0] % 2 == 0 else "dve"
            if eng == "act":
                if scale is None:
                    act(lambda e: e.copy(out=dst, in_=src), [srcb], [dstb])
                else:
                    act(lambda e: e.mul(out=dst, in_=src, mul=scale), [srcb], [dstb])
            else:
                if scale is None:
                    dve(lambda e: e.tensor_copy(out=dst, in_=src), [srcb], [dstb])
                else:
                    dve(lambda e: e.tensor_scalar(out=dst, in0=src, scalar1=scale, scalar2=None, op0=ALU.mult),
                        [srcb], [dstb])

        wi = [0]

        def load_w(packed, off, nk, ncols):
            i = wi[0] % 3
            wi[0] += 1
            name = "wsl%d" % i
            tot = nk * ncols
            flat = wsl[i][:, 0:tot]
            dst = flat.rearrange("p (k c) -> p k c", k=nk)
            hf = tot // 2
            S.dma("pool", lambda e: e.dma_start(out=flat[:, 0:hf], in_=packed[:, off:off + hf]), writes=[name])
            S.dma("pool", lambda e: e.dma_start(out=flat[:, hf:tot], in_=packed[:, off + hf:off + tot]),
                  writes=[name])
            return dst, name

        MG = [(0, 512), (512, 512), (1024, 256)]
        PG = [(0, 448), (448, 448)]

        def proj_fm(wt, wname, c0, M, src, srcname, t0, N):
            bank, bname = nextbank()
            mm(bank[0:M, 0:N], bname, [(wt[:, k, c0:c0 + M], src[:, k, t0:t0 + N]) for k in range(KD)],
               [wname, srcname])
            return bank[0:M, 0:N], bname

        def proj_tm(wt, wname, c0, N, src, srcname, ti):
            bank, bname = nextbank()
            mm(bank[:, 0:N], bname, [(src[:, k, ti * 128:(ti + 1) * 128], wt[:, k, c0:c0 + N]) for k in range(KD)],
               [wname, srcname])
            return bank[:, 0:N], bname

        xt = [carve("xt%d" % i, [D], F32) for i in range(4)]
        xn = [carve("xn%d" % i, [D], BF16) for i in range(4)]
        sq1, sq1b = carve("sq", [D], BF16)
        nstat = [0]

        def rstd_from_ss(ss, rs, b, n):
            dve(lambda e: e.tensor_scalar(out=rs, in0=ss, scalar1=1.0 / n, scalar2=EPS, op0=ALU.mult, op1=ALU.add),
                [b], [b + "r"])
            act(lambda e: e.activation(out=rs, in_=rs, func=AF.Sqrt), [b + "r"], [b + "r"])
            dve(lambda e: e.reciprocal(out=rs, in_=rs), [b + "r"], [b + "r"])

        def statslot():
            c = nstat[0] % 16
            nstat[0] += 1
            return stat[:, c:c + 1], stat[:, 16 + c:17 + c], stat[:, 32 + c:33 + c], "stat%d" % c

        def norm_to_T(x_t, xb, x_n, xnb, dstT, dstname, ti, sqj, sqb):
            ss, rs, _, sbn = statslot()
            act(lambda e: e.activation(out=sqj, in_=x_t, func=AF.Square, accum_out=ss), [xb], [sqb, sbn])
            rstd_from_ss(ss, rs, sbn, D)
            dve(lambda e: e.scalar_tensor_tensor(out=x_n, in0=x_t, scalar=rs, in1=wbc[:], op0=ALU.mult,
                                                 op1=ALU.mult), [xb, sbn + "r", "wbc"], [xnb])
            for half in range(2):
                pt = ptr[half]
                pbn = "ptr%d" % half
                for j in range(8):
                    k = half * 8 + j
                    tr(pt[:, j * 128:(j + 1) * 128], pbn, x_n[:, k * 128:(k + 1) * 128], xnb, identb, inc=(j == 7))
                dst = dstT[:, half * 8:(half + 1) * 8, ti * 128:(ti + 1) * 128]
                src = pt[:].rearrange("p (k t) -> p k t", k=8)
                evac(dst, dstname, src, pbn, eng=("act" if half == 0 else "dve"))

        for ti in range(NPT + NMT):
            par = ti % 4
            (x_t, xb), (x_n, xnb) = xt[par], xn[par]
            if ti < NPT:
                sdma(x_t, xpre[ti * 128:(ti + 1) * 128, :], [], [xb])
                norm_to_T(x_t, xb, x_n, xnb, hTp, "hTp", ti, sq1, sq1b)
            else:
                tj = ti - NPT
                sdma(x_t, xmain[tj * 128:(tj + 1) * 128, :], [], [xb])
                norm_to_T(x_t, xb, x_n, xnb, hT, "hT", tj, sq1, sq1b)

        arena_reset()
        if stop_after == "p1":
            return finish(nc, S, dbg_out)
        Cst = carve("Cst", [2, 2, 257], F32)[0]
        Sst = carve("Sst", [2, 256], F32)[0]
        wtok = carve("wtok", [64], F32)[0]
        ftok = carve("ftok", [64], F32)[0]
        decbc = carve("decbc", [64], F32)[0]
        wtoks = carve("wtoks", [4], F32)[0]
        ftoks = carve("ftoks", [4], F32)[0]
        sdecbc = carve("sdecbc", [64], F32)[0]

        galT, galb = carve("galT", [PRE + TM], F32)
        BT, BTb = carve("BT", [PRE + TM], F32)
        gst, gstb = BT[:, 0:512], BTb
        QT, QTb = carve("QT", [2, TM])
        KT, KTb = carve("KT", [2, TM])
        Vt, Vtb = carve("Vt", [NMT, 257])
        OG, OGb = carve("OG", [NMT, 256])
        KTp, KTpb = carve("KTp", [2, PRE])
        Vp, Vpb = carve("Vp", [NPT, 257])
        Cbf, Cbfb = carve("Cbf", [2, 257])
        QTg_, QTgb = carve("QTg", [TM])
        KTg_, KTgb = carve("KTg", [TM])
        Vtg, Vtgb = carve("Vtg", [NMT, 256])
        RG, RGb = carve("RG", [NMT, 256])
        Ktok2 = [carve("Ktok%d" % i, [256]) for i in range(2)]
        Vw2 = [carve("Vw%d" % i, [257]) for i in range(2)]
        PT2 = [carve("PT%d" % i, [128]) for i in range(2)]
        ytok2 = [carve("ytok%d" % i, [256]) for i in range(2)]
        eqt2 = [carve("eqt%d" % i, [128], F32) for i in range(2)]
        QtT2 = [carve("QtT%d" % i, [128]) for i in range(2)]
        KtT2 = [carve("KtT%d" % i, [128]) for i in range(2)]
        KhT2 = [carve("KhT%d" % i, [128]) for i in range(2)]
        Khtok2 = [carve("Khtok%d" % i, [128]) for i in range(2)]
        Sbf, Sbfb = carve("Sbf", [256])
        nT, nTb = carve("nT", [2, 64], F32)
        nTo, nTob = carve("nTo", [2, 64], F32)
        nrow, nrowb = carve("nrow", [256], F32)
        junk, junkb = nrow.bitcast(BF16)[:, 0:256], nrowb
        SG = 2
        aups, aupb = carve("aups", [512], F32)
        negab = carve("negab", [4], F32)[0]
        dSs, dSsb = carve("dSs", [16], F32)
        gmark = apos[0]
        gat = carve("gat", [8, 128], F32)[0][0:64]
        gas = carve("gas", [8, 8], F32)[0][0:64]
        grow = carve("grow", [8, 64], F32)[0][0:1]
        wg, wgn = load_w(w_in, WOFF[C_MI], KD, 8)
        wl, wln = load_w(w_in, WOFF[C_GAL], KD, 16)
        MGG = [(0, 512), (512, 512), (1024, 128), (1152, 128)]
        for (src, sname, groups, base) in ((hTp, "hTp", PG, 0), (hT, "hT", MGG, PRE)):
            for (t0, N) in groups:
                pv, pn_ = proj_fm(wg, wgn, 0, 8, src, sname, t0, N)
                evac(gst[0:8, 0:N], gstb, pv, pn_, eng="act")
                if base + t0 < 2048:
                    sdma(gscr[:, base + t0:base + t0 + N], gst[0:8, 0:N], [gstb], ["gscr"])
                else:
                    sdma(gscs[:, :], gst[0:8, 0:N], [gstb], ["gscs"])
                pv, pn_ = proj_fm(wl, wln, 0, 16, src, sname, t0, N)
                evac(galT[0:16, base + t0:base + t0 + N], galb, pv, pn_, eng="dve")
        GI, GF, GB_, GA, GW, GFL, GT1, GT2 = range(8)

        def gt(i):
            return gat[:, i, :]

        def gs_(i):
            return gas[:, i, :]
        NTOK = PRE + TM - 128
        sdma(gt(GI), gscr[0:4, :].rearrange("h (c l) -> (h c) l", l=128), ["gscr"], ["g_i"])
        sdma(gt(GF), gscr[4:8, :].rearrange("h (c l) -> (h c) l", l=128), ["gscr"], ["g_f"])
        sdma(gs_(GI), gscs[0:4, :].rearrange("h (j l) -> (h j) l", l=8), ["gscs"], ["s_i"])
        sdma(gs_(GF), gscs[4:8, :].rearrange("h (j l) -> (h j) l", l=8), ["gscs"], ["s_f"])
        negfb = stat[0:64, 48:49]
        dve(lambda e: e.tensor_scalar(out=negfb, in0=fb64, scalar1=-1.0, scalar2=None, op0=ALU.mult),
            ["spm"], ["negfb"])

        def gate_math(T, pfx, L):
            i_, f_, b_, a_ = T(GI), T(GF), T(GB_), T(GA)
            act(lambda e: e.activation(out=f_, in_=f_, func=AF.Exp, bias=negfb, scale=-1.0),
                [pfx + "f", "negfb"], [pfx + "f"])
            act(lambda e: e.activation(out=f_, in_=f_, func=AF.Ln, bias=1.0), [pfx + "f"], [pfx + "f"])
            dve(lambda e: e.tensor_scalar(out=f_, in0=f_, scalar1=-1.0, scalar2=None, op0=ALU.mult),
                [pfx + "f"], [pfx + "f"])
            dve(lambda e: e.tensor_tensor_scan(out=b_, data0=onesf[0:64, 0:L], data1=f_, initial=0.0,
                                               op0=ALU.mult, op1=ALU.add), [pfx + "f", "cf"], [pfx + "b"])
            dve(lambda e: e.scalar_tensor_tensor(out=a_, in0=i_, scalar=ib64, in1=b_, op0=ALU.add,
                                                 op1=ALU.subtract), [pfx + "i", pfx + "b", "spm"], [pfx + "a"])
        gate_math(gt, "g_", 128)
        gate_math(gs_, "s_", 8)
        amax = stat[0:64, 49:50]
        bend = gat[:, GB_, 127:128]
        amaxb = stat[0:64, 50:51]
        dve(lambda e: e.reduce_max(out=amax, in_=gt(GA), axis=AX.X), ["g_a"], ["amax"])
        dve(lambda e: e.tensor_tensor(out=amaxb, in0=amax, in1=bend, op=ALU.add), ["amax", "g_b"], ["amaxb"])
        tr(pmisc[0:1, 0:64], "pmisc", bend, "g_b", identf[0:64, 0:64], inc=False)
        tr(pmisc[0:1, 64:128], "pmisc", amaxb, "amaxb", identf[0:64, 0:64])
        R_BE, R_AB, R_MN, R_MP, R_G, R_DEC = range(6)

        def gr(i):
            return grow[0:1, i, :]
        evac(grow[0:1, 0:2, :], "grow", pmisc[0:1, 0:128].rearrange("p (a b) -> p a b", a=2), "pmisc", eng="dve")
        for h in range(4):
            for seg in range(2):
                lo = h * 16 + seg * 8
                if seg == 0:
                    init = 0.0
                    rd = ["grow"]
                else:
                    mf = stat[0:1, 51 + h:52 + h]
                    dve(lambda e, h=h, mf=mf: e.tensor_tensor(out=mf, in0=grow[0:1, R_MN, h * 16 + 7:h * 16 + 8],
                                                             in1=flagt[0:1, 0:1], op=ALU.mult),
                        ["grow", "flagt"], ["mf%d" % h])
                    init = mf
                    rd = ["grow", "mf%d" % h]
                dve(lambda e, lo=lo, init=init: e.tensor_tensor_scan(
                    out=grow[0:1, R_MN, lo:lo + 8], data0=grow[0:1, R_BE, lo:lo + 8],
                    data1=grow[0:1, R_AB, lo:lo + 8], initial=init, op0=ALU.add, op1=ALU.max), rd, ["grow"])
                if seg == 0:
                    dve(lambda e, lo=lo: e.memset(grow[0:1, R_MP, lo:lo + 1], 0.0), [], ["grow"])
                else:
                    dve(lambda e, lo=lo, mf=mf: e.tensor_copy(out=grow[0:1, R_MP, lo:lo + 1], in_=mf),
                        ["mf%d" % h], ["grow"])
                dve(lambda e, lo=lo: e.tensor_copy(out=grow[0:1, R_MP, lo + 1:lo + 8],
                                                   in_=grow[0:1, R_MN, lo:lo + 7]), ["grow"], ["grow"])
        dve(lambda e: e.tensor_tensor(out=gr(R_G), in0=gr(R_MN), in1=gr(R_BE), op=ALU.subtract), ["grow"], ["grow"])
        dve(lambda e: e.tensor_tensor(out=gr(R_DEC), in0=gr(R_MP), in1=gr(R_G), op=ALU.subtract), ["grow"], ["grow"])
        act(lambda e: e.activation(out=gr(R_DEC), in_=gr(R_DEC), func=AF.Exp), ["grow"], ["grow"])
        dve(lambda e: e.tensor_scalar(out=gr(R_G), in0=gr(R_G), scalar1=-1.0, scalar2=None, op0=ALU.mult),
            ["grow"], ["grow"])
        mm(pmisc[:, 128:192], "pmisc", [(onesf[0:1, 0:128], gr(R_DEC))], ["grow", "cf"])
        evac(decbc[:], "decbc", pmisc[:, 128:192], "pmisc", eng="dve")
        mm(pmisc[0:64, 192:193], "pmisc", [(gr(R_G), onesf[0:1, 0:1])], ["grow", "cf"])
        negG = stat[0:64, 56:57]
        evac(negG, "negG", pmisc[0:64, 192:193], "pmisc", eng="dve")
        sdma(o_pm[:, :], gr(R_MN), ["grow"], [])
        act(lambda e: e.activation(out=gt(GW), in_=gt(GA), func=AF.Exp, bias=negG), ["g_a", "negG"], ["g_w"])
        act(lambda e: e.activation(out=gt(GFL), in_=gt(GB_), func=AF.Exp, bias=negG, scale=-1.0),
            ["g_b", "negG"], ["g_fl"])
        tr(pmisc[:, 256:320], "pmisc", gt(GW), "g_w", identf[0:64, 0:64], inc=False)
        tr(pmisc[:, 320:384], "pmisc", gt(GFL), "g_fl", identf[0:64, 0:64])
        evac(wtok[:], "wtok", pmisc[:, 256:320], "pmisc", eng="dve")
        evac(ftok[:], "ftok", pmisc[:, 320:384], "pmisc", eng="dve")
        smc = stat[0:64, 57:58]
        sdma(smc, smcol[:, :], [], ["smc"])
        samax = stat[0:64, 58:59]
        sG = stat[0:64, 59:60]
        snegG = stat[0:64, 60:61]
        sdec = stat[0:64, 61:62]
        smn = stat[0:64, 62:63]
        dve(lambda e: e.reduce_max(out=samax, in_=gs_(GA), axis=AX.X), ["s_a"], ["samax"])
        dve(lambda e: e.tensor_tensor(out=sG, in0=samax, in1=smc, op=ALU.max), ["samax", "smc"], ["sG"])
        dve(lambda e: e.tensor_tensor(out=smn, in0=sG, in1=gas[:, GB_, 7:8], op=ALU.add), ["sG", "s_b"], ["smn"])
        sdma(o_sm[:, :], smn, ["smn"], [])
        dve(lambda e: e.tensor_tensor(out=sdec, in0=smc, in1=sG, op=ALU.subtract), ["smc", "sG"], ["sdec"])
        act(lambda e: e.activation(out=sdec, in_=sdec, func=AF.Exp), ["sdec"], ["sdec"])
        dve(lambda e: e.tensor_scalar(out=snegG, in0=sG, scalar1=-1.0, scalar2=None, op0=ALU.mult), ["sG"], ["snegG"])
        act(lambda e: e.activation(out=gs_(GW), in_=gs_(GA), func=AF.Exp, bias=snegG), ["s_a", "snegG"], ["s_w"])
        act(lambda e: e.activation(out=gs_(GFL), in_=gs_(GB_), func=AF.Exp, bias=snegG, scale=-1.0),
            ["s_b", "snegG"], ["s_fl"])
        dg = gat[:, GT1, 0:64]
        dve(lambda e: e.tensor_scalar(out=dg, in0=identf[0:64, 0:64], scalar1=sdec, scalar2=None, op0=ALU.mult),
            ["sdec", "cf"], ["dg"])
        mm(pmisc[:, 384:448], "pmisc", [(onesf[0:64, 0:128], dg)], ["dg", "cf"])
        evac(sdecbc[:], "sdecbc", pmisc[:, 384:448], "pmisc", eng="dve")
        sdma(gscs[0:4, :].rearrange("h (j l) -> (h j) l", l=8), gs_(GW), ["s_w"], ["gscs"])
        sdma(gscs[4:8, :].rearrange("h (j l) -> (h j) l", l=8), gs_(GFL), ["s_fl"], ["gscs"])
        wfr = gat[0:8, GT2, :]
        sdma(wfr, gscs[0:8, :], ["gscs"], ["wfr"])
        tr(pmisc[:, 448:456], "pmisc", wfr, "wfr", identf[0:8, 0:8])
        evac(wtoks[:], "wtoks", pmisc[:, 448:452], "pmisc", eng="dve")
        evac(ftoks[:], "ftoks", pmisc[:, 452:456], "pmisc", eng="dve")

        if stop_after == "gates":
            if debug.get("gates"):
                for nm, t_, shp in (("wtok", wtok, [128, 64]), ("ftok", ftok, [128, 64]), ("decbc", decbc, [128, 64]),
                                    ("wtoks", wtoks, [128, 4]), ("sdecbc", sdecbc, [128, 64])):
                    dbg_out[nm] = dout("dbg_" + nm, shp)
                    sdma(dbg_out[nm][:, :], t_[:], [nm], [])
            return finish(nc, S, dbg_out)


        dve(lambda e: e.memset(Vt[:, :, 256:257], 1.0), [], [Vtb])
        dve(lambda e: e.memset(Vp[:, :, 256:257], 1.0), [], [Vpb])
        sdma(nrow[0:64, :], sn[:, :], [], [nrowb])
        for dh in range(2):
            tr(pmisc[:, dh * 64:(dh + 1) * 64], "pmisc", nrow[0:64, dh * 128:(dh + 1) * 128], nrowb,
               identf[0:64, 0:64], inc=(dh == 1))
        evac(nT, nTb, pmisc[:, 0:128].rearrange("p (a b) -> p a b", a=2), "pmisc", eng="dve")

        def head_epilogue(num_ap, numb, scale_pre, gate_ap, gateb, tile_i, dstcol0, hnb, ytok, ytokb):
            ss, rs, sc, sbn = statslot()
            if scale_pre is None:
                act(lambda e: e.activation(out=junk, in_=num_ap, func=AF.Square, accum_out=ss), [numb],
                    [junkb, sbn])
            else:
                act(lambda e: e.activation(out=junk, in_=num_ap, func=AF.Square, scale=scale_pre, accum_out=ss),
                    [numb, hnb], [junkb, sbn])
            rstd_from_ss(ss, rs, sbn, 256)
            if scale_pre is not None:
                dve(lambda e: e.tensor_tensor(out=rs, in0=rs, in1=scale_pre, op=ALU.mult), [sbn + "r", hnb],
                    [sbn + "r"])
            dve(lambda e: e.scalar_tensor_tensor(out=ytok, in0=num_ap, scalar=rs, in1=gate_ap, op0=ALU.mult,
                                                 op1=ALU.mult), [numb, sbn + "r", gateb], [ytokb])
            sdma(yscr[tile_i * 128:(tile_i + 1) * 128, dstcol0:dstcol0 + 256], ytok, [ytokb], ["yscr"])

        def mlstm_proj(h):
            wq, wqn = load_w(w_in, WOFF[C_MQ + h * 256], KD, 256)
            wk, wkn = load_w(w_in, WOFF[C_MK + h * 256], KD, 256)
            wv, wvn = load_w(w_in, WOFF[C_MV + h * 256], KD, 256)
            for dh in range(2):
                for (t0, N) in MG:
                    pv, pn_ = proj_fm(wq, wqn, dh * 128, 128, hT, "hT", t0, N)
                    evac(QT[:, dh, t0:t0 + N], QTb, pv, pn_)
            for dh in range(2):
                for (t0, N) in MG:
                    pv, pn_ = proj_fm(wk, wkn, dh * 128, 128, hT, "hT", t0, N)
                    evac(KT[:, dh, t0:t0 + N], KTb, pv, pn_, scale=0.0625)
                for (t0, N) in PG:
                    pv, pn_ = proj_fm(wk, wkn, dh * 128, 128, hTp, "hTp", t0, N)
                    evac(KTp[:, dh, t0:t0 + N], KTpb, pv, pn_, scale=0.0625)
            wo, won = load_w(w_in, WOFF[C_MO + h * 256], KD, 256)
            for ti in range(NPT):
                pv, pn_ = proj_tm(wv, wvn, 0, 256, hTp, "hTp", ti)
                evac(Vp[:, ti, 0:256], Vpb, pv, pn_)
            for ti in range(NMT):
                pv, pn_ = proj_tm(wv, wvn, 0, 256, hT, "hT", ti)
                evac(Vt[:, ti, 0:256], Vtb, pv, pn_)
            for ti in range(NMT):
                pv, pn_ = proj_tm(wo, won, 0, 256, hT, "hT", ti)
                evac(OG[:, ti, :], OGb, pv, pn_, func=AF.Sigmoid)
        csi = [0]

        def mlstm_chunks(h):
            Ch = Cst[:, h % 2, :, :]
            Chb = "Cst%d" % (h % 2)
            dve(lambda e: e.memset(Ch, 0.0), [], [Chb])
            def mchunk(c):
                full = c >= NPT
                samp = c == NCH
                if c < NPT:
                    ktsrc, ktb, vsrc, vb_, ti = KTp, KTpb, Vp, Vpb, c
                else:
                    ktsrc, ktb, vsrc, vb_, ti = KT, KTb, Vt, Vtb, c - NPT
                tk = slice(ti * 128, (ti + 1) * 128)
                (Ktok, Ktokb), (Vw, Vwb), (PT, PTb), (ytok, ytokb) = Ktok2[c % 2], Vw2[c % 2], PT2[c % 2], ytok2[c % 2]
                if samp:
                    wcol, fcol = wtoks[:, h:h + 1], ftoks[:, h:h + 1]
                    wcb, fcb = "wtoks", "ftoks"
                else:
                    wcol, fcol = wtok[:, h * 16 + c:h * 16 + c + 1], ftok[:, h * 16 + c:h * 16 + c + 1]
                    wcb, fcb = "wtok", "ftok"
                for dh in range(2):
                    tr(ptr[0][:, dh * 128:(dh + 1) * 128], "ptr0", ktsrc[:, dh, tk], ktb, identb, inc=(dh == 1))
                evac(Ktok, Ktokb, ptr[0][:, 0:256], "ptr0")
                dve(lambda e, vsrc=vsrc, ti=ti, wcol=wcol: e.tensor_scalar(out=Vw, in0=vsrc[:, ti, :], scalar1=wcol,
                                                                            scalar2=None, op0=ALU.mult),
                     [vb_, wcb], [Vwb])
                if not samp:
                    dcol = decbc[:, h * 16 + c:h * 16 + c + 1]
                    dve(lambda e, dcol=dcol: e.tensor_scalar(out=Ch, in0=Ch, scalar1=dcol, scalar2=None,
                                                             op0=ALU.mult), [Chb, "decbc"], [Chb])
                if full:
                    mm(pst[:, 0:128], "pst", [(ktsrc[:, dh, tk], QT[:, dh, tk]) for dh in range(2)], [ktb, QTb])
                    msk = masks if samp else maskc
                    dve(lambda e, msk=msk: e.tensor_tensor(out=PT, in0=pst[:, 0:128], in1=msk, op=ALU.mult),
                        ["pst", "cb"], [PTb])
                    if not samp:
                        act(lambda e: e.copy(out=Cbf, in_=Ch), [Chb], [Cbfb])
                        mm(pnum[:, 0:257], "pnum",
                           [(PT, Vw)] + [(QT[:, dh, tk], Cbf[:, dh, :]) for dh in range(2)],
                           [PTb, Vwb, QTb, Cbfb])
                    else:
                        mm(pnum[:, 0:257], "pnum", [(PT, Vw)], [PTb, Vwb], first=True, last=False)
                if not samp:
                    for dh in range(2):
                        mm(pstate[:, 0:257], "pstate", [(Ktok[:, dh * 128:(dh + 1) * 128], Vw)], [Ktokb, Vwb])
                        dve(lambda e, dh=dh: e.tensor_tensor(out=Ch[:, dh, :], in0=Ch[:, dh, :],
                                                             in1=pstate[:, 0:257], op=ALU.add),
                            ["pstate", Chb], [Chb])
                else:
                    def sgrp(g, Cs, Csb, Csbf, Csbfb, QX, QXb, VwX, VwXb):
                        js = slice(g * SG, (g + 1) * SG)
                        for dh in range(2):
                            sdma(Cs[:, :, dh, 0:256],
                                 sC[js, h, dh * 128:(dh + 1) * 128, :].rearrange("j p e -> p j e"), [], [Csb])
                        ncols = nT[:, :, g * SG * 4 + h:(g + 1) * SG * 4:4].rearrange("p dh j -> p j dh")
                        dve(lambda e, ncols=ncols: e.tensor_copy(out=Cs[:, :, :, 256], in_=ncols), [nTb], [Csb])
                        dcols = sdecbc[:, h * 16 + g * SG:h * 16 + (g + 1) * SG]
                        Csv = Cs.rearrange("p j dh e -> p j (dh e)")
                        dve(lambda e, dcols=dcols, Csv=Csv: e.tensor_tensor(
                            out=Csv, in0=Csv, in1=dcols.unsqueeze(2).broadcast_to([128, SG, 514]), op=ALU.mult),
                            [Csb, "sdecbc"], [Csb])
                        act(lambda e: e.copy(out=Csbf, in_=Cs), [Csb], [Csbfb])
                        for dh in range(2):
                            dve(lambda e, dh=dh, js=js, tk=tk: e.tensor_tensor(
                                out=QX[:, dh, :, :], in0=QT[:, dh, tk].unsqueeze(1).broadcast_to([128, SG, 128]),
                                in1=qxmask[:, js, :], op=ALU.mult), [QTb, "cb"], [QXb])
                        pairs = [(QX[:, dh, j, :], Csbf[:, j, dh, :]) for j in range(SG) for dh in range(2)]
                        mm(pnum[:, 0:257], "pnum", pairs, [QXb, Csbfb], first=False, last=(g == 16 // SG - 1))
                        dve(lambda e, js=js: e.tensor_tensor(
                            out=VwX, in0=Vw.unsqueeze(1).broadcast_to([128, SG, 257]),
                            in1=vxmask[:, js].unsqueeze(2).broadcast_to([128, SG, 257]), op=ALU.mult),
                            [Vwb, "cb"], [VwXb])
                        for j in range(SG):
                            for dh in range(2):
                                mm(pstate[:, 0:257], "pstate", [(Ktok[:, dh * 128:(dh + 1) * 128], VwX[:, j, :])],
                                   [Ktokb, VwXb])
                                dve(lambda e, j=j, dh=dh: e.tensor_tensor(out=Cs[:, j, dh, :], in0=Cs[:, j, dh, :],
                                                                          in1=pstate[:, 0:257], op=ALU.add),
                                    ["pstate", Csb], [Csb])
                        for dh in range(2):
                            sdma(o_sC[js, h, dh * 128:(dh + 1) * 128, :].rearrange("j p e -> p j e"),
                                 Cs[:, :, dh, 0:256], [Csb], [])
                        ndst = nTo[:, :, g * SG * 4 + h:(g + 1) * SG * 4:4].rearrange("p dh j -> p j dh")
                        dve(lambda e, ndst=ndst: e.tensor_copy(out=ndst, in_=Cs[:, :, :, 256]), [Csb], [nTob])
                    for g in range(16 // SG):
                        k_ = csi[0]
                        csi[0] += 1
                        sgrp(g, *Cs3[k_ % 3], *Csbf2[k_ % 2], *QX2[k_ % 2], *VwX2[k_ % 2])
                if full:
                    ss, rs, sc, sbn = statslot()
                    dve(lambda e, sc=sc: e.tensor_copy(out=sc, in_=pnum[:, 256:257]), ["pnum"], [sbn + "s"])
                    dve(lambda e, sc=sc: e.scalar_tensor_tensor(out=sc, in0=sc, scalar=-1.0, in1=sc, op0=ALU.mult,
                                                                op1=ALU.max), [sbn + "s"], [sbn + "s"])
                    dve(lambda e, fcol=fcol, sc=sc: e.tensor_tensor(out=sc, in0=sc, in1=fcol, op=ALU.max),
                        [sbn + "s", fcb], [sbn + "s"])
                    dve(lambda e, sc=sc: e.reciprocal(out=sc, in_=sc), [sbn + "s"], [sbn + "s"])
                    head_epilogue(pnum[:, 0:256], "pnum", sc, OG[:, ti, :], OGb, ti, h * 256, sbn + "s", ytok, ytokb)
                if c == NCH - 1:
                    sdma(o_pC[h].rearrange("(dh p) e -> p dh e", p=128), Ch[:, :, 0:256], [Chb], [])
            for c in range(NCH + 1):
                mchunk(c)
            tr(pmisc[0:2, 0:128], "pmisc", Ch[:, :, 256], Chb, identf)
            evac(nrow[0:2, 0:128], nrowb, pmisc[0:2, 0:128], "pmisc", eng="dve")
            sdma(o_pn[h * 2:h * 2 + 2, :], nrow[0:2, 0:128], [nrowb], [])

        aupt = stat
        sdma(aups[0:16, :], aup[:, :], [], [aupb])
        dve(lambda e: e.tensor_scalar(out=negab[:, 0:4], in0=spm[:, P_AB:P_AB + 4], scalar1=-1.0, scalar2=None,
                                      op0=ALU.mult), ["spm"], ["negab"])
        BG = [(0, 512), (512, 512), (1024, 512), (1536, 512), (2048, 128)]

        def gla_head(h):
            wq, wqn = load_w(w_in, WOFF[C_GQ + h * 128], KD, 128)
            wk, wkn = load_w(w_in, WOFF[C_GK + h * 128], KD, 128)
            wv, wvn = load_w(w_in, WOFF[C_GV + h * 256], KD, 256)
            QTg, KTg, KTpg = QTg_, KTg_, KTp[:, 0, :]
            QTb, KTb, Vt, Vtb, OG, OGb = QTgb, KTgb, Vtg, Vtgb, RG, RGb
            for (t0, N) in MG:
                pv, pn_ = proj_fm(wq, wqn, 0, 128, hT, "hT", t0, N)
                evac(QTg[:, t0:t0 + N], QTb, pv, pn_, scale=128.0 ** -0.5)
            for (t0, N) in MG:
                pv, pn_ = proj_fm(wk, wkn, 0, 128, hT, "hT", t0, N)
                evac(KTg[:, t0:t0 + N], KTb, pv, pn_)
            for (t0, N) in PG:
                pv, pn_ = proj_fm(wk, wkn, 0, 128, hTp, "hTp", t0, N)
                evac(KTpg[:, t0:t0 + N], KTpb, pv, pn_)
            wr, wrn = load_w(w_in, WOFF[C_GR + h * 256], KD, 256)
            for ti in range(NPT):
                pv, pn_ = proj_tm(wv, wvn, 0, 256, hTp, "hTp", ti)
                evac(Vp[:, ti, 0:256], Vpb, pv, pn_)
            for ti in range(NMT):
                pv, pn_ = proj_tm(wv, wvn, 0, 256, hT, "hT", ti)
                evac(Vt[:, ti, 0:256], Vtb, pv, pn_)
            for ti in range(NMT):
                pv, pn_ = proj_tm(wr, wrn, 0, 256, hT, "hT", ti)
                evac(OG[:, ti, :], OGb, pv, pn_, func=AF.Silu)
            nab = negab[:, h:h + 1]
            for (t0, N) in BG:
                mm(pmisc[:, 0:N], "pmisc", [(aups[0:16, h * 128:(h + 1) * 128], galT[0:16, t0:t0 + N])],
                   [aupb, galb])
                act(lambda e, t0=t0, N=N: e.activation(out=BT[:, t0:t0 + N], in_=pmisc[:, 0:N], func=AF.Exp,
                                                       bias=nab, scale=-1.0), ["pmisc", "negab"], [BTb])
            act(lambda e: e.activation(out=BT, in_=BT, func=AF.Ln, bias=1.0), [BTb], [BTb])
            dve(lambda e: e.tensor_scalar(out=BT, in0=BT, scalar1=-1.0 / 16.0, scalar2=None, op0=ALU.mult),
                [BTb], [BTb])
            dve(lambda e: e.tensor_tensor_scan(out=BT[:, 0:2048], data0=onesf[:, 0:1].broadcast_to([128, 2048]),
                                               data1=BT[:, 0:2048], initial=0.0, op0=ALU.mult, op1=ALU.add),
                [BTb, "cf"], [BTb])
            dve(lambda e: e.tensor_tensor_scan(out=BT[:, 2048:2176], data0=resetm, data1=BT[:, 2048:2176],
                                               initial=0.0, op0=ALU.mult, op1=ALU.add), [BTb, "cf"], [BTb])
            Sh = Sst[:, h % 2, :]
            Shb = "Sst%d" % (h % 2)
            dve(lambda e: e.memset(Sh, 0.0), [], [Shb])

            def gchunk(c):
                full = c >= NPT
                samp = c == NCH
                if c < NPT:
                    ktsrc, ktb, vsrc, vb_, ti = KTpg, KTpb, Vp, Vpb, c
                else:
                    ktsrc, ktb, vsrc, vb_, ti = KTg, KTb, Vt, Vtb, c - NPT
                tk = slice(ti * 128, (ti + 1) * 128)
                tb = slice(2048, 2176) if samp else slice(c * 128, (c + 1) * 128)
                (eqt, eqtb), (QtT, QtTb), (KtT, KtTb), (KhT, KhTb) = eqt2[c % 2], QtT2[c % 2], KtT2[c % 2], KhT2[c % 2]
                (Ktok, Ktokb), (PT, PTb), (ytok, ytokb) = Khtok2[c % 2], PT2[c % 2], ytok2[c % 2]
                ss, rs, sc, sbn = statslot()
                if samp:
                    bend3 = BT[:, 2048 + 7:2176:8].unsqueeze(2).broadcast_to([128, 16, 8])
                    dve(lambda e: e.tensor_tensor(out=eqt.rearrange("p (j l) -> p j l", l=8), in0=bend3,
                                                  in1=BT[:, tb].rearrange("p (j l) -> p j l", l=8),
                                                  op=ALU.subtract), [BTb], [eqtb])
                    act(lambda e: e.activation(out=eqt, in_=eqt, func=AF.Exp), [eqtb], [eqtb])
                    act(lambda e: e.activation(out=dSs, in_=BT[:, 2048 + 7:2176:8], func=AF.Exp), [BTb], [dSsb])
                else:
                    bendc = BT[:, c * 128 + 127:c * 128 + 128]
                    bstc = zerocol if c == 0 else BT[:, c * 128 - 1:c * 128]
                    act(lambda e: e.activation(out=eqt, in_=BT[:, tb], func=AF.Exp, bias=bendc, scale=-1.0),
                        [BTb], [eqtb])
                    dve(lambda e: e.tensor_tensor(out=ss, in0=bendc, in1=bstc, op=ALU.subtract), [BTb, "cf"],
                        [sbn])
                    act(lambda e: e.activation(out=ss, in_=ss, func=AF.Exp), [sbn], [sbn])
                    dve(lambda e: e.tensor_scalar(out=rs, in0=bstc, scalar1=-1.0, scalar2=None, op0=ALU.mult),
                        [BTb, "cf"], [sbn + "r"])
                dve(lambda e: e.tensor_tensor(out=KhT, in0=ktsrc[:, tk], in1=eqt, op=ALU.mult), [ktb, eqtb], [KhTb])
                tr(ptr[0][:, 0:128], "ptr0", KhT, KhTb, identb)
                evac(Ktok[:, 0:128], Ktokb, ptr[0][:, 0:128], "ptr0")
                if full:
                    if samp:
                        act(lambda e: e.activation(out=eqt, in_=BT[:, tb], func=AF.Exp), [BTb, KhTb], [eqtb])
                    else:
                        act(lambda e: e.activation(out=eqt, in_=BT[:, tb], func=AF.Exp, bias=rs), [BTb, sbn + "r", KhTb],
                            [eqtb])
                    dve(lambda e: e.tensor_tensor(out=QtT, in0=QTg[:, tk], in1=eqt, op=ALU.mult), [QTb, eqtb], [QtTb])
                    if samp:
                        act(lambda e: e.activation(out=eqt, in_=BT[:, tb], func=AF.Exp, scale=-1.0), [BTb, QtTb],
                            [eqtb])
                    else:
                        act(lambda e: e.activation(out=eqt, in_=BT[:, tb], func=AF.Exp, bias=bstc, scale=-1.0),
                            [BTb, QtTb, "cf"], [eqtb])
                    dve(lambda e: e.tensor_tensor(out=KtT, in0=ktsrc[:, tk], in1=eqt, op=ALU.mult), [ktb, eqtb],
                        [KtTb])
                    mm(pst[:, 0:128], "pst", [(KtT, QtT)], [KtTb, QtTb])
                    msk = masks if samp else maskc
                    dve(lambda e: e.tensor_tensor(out=PT, in0=pst[:, 0:128], in1=msk, op=ALU.mult), ["pst", "cb"],
                        [PTb])
                    if not samp:
                        mm(pnum[:, 0:256], "pnum", [(PT, vsrc[:, ti, 0:256]), (QtT, Sbf)], [PTb, vb_, QtTb, Sbfb])
                    else:
                        mm(pnum[:, 0:256], "pnum", [(PT, vsrc[:, ti, 0:256])], [PTb, vb_], first=True, last=False)
                if not samp:
                    mm(pstate[:, 0:256], "pstate", [(Ktok[:, 0:128], vsrc[:, ti, 0:256])], [Ktokb, vb_])
                    dve(lambda e: e.scalar_tensor_tensor(out=Sh, in0=Sh, scalar=ss, in1=pstate[:, 0:256],
                                                         op0=ALU.mult, op1=ALU.add), [Shb, sbn, "pstate"], [Shb])
                    act(lambda e: e.copy(out=Sbf, in_=Sh), [Shb], [Sbfb])
                else:
                    def sgrp(g, Cs, Csb, Csbf, Csbfb, QX, QXb, VwX, VwXb):
                        js = slice(g * SG, (g + 1) * SG)
                        Ss = Cs[:, :, 0, 0:256]
                        Ssb = Csbf[:, :, 0, 0:256]
                        sdma(Ss, sS[js, h].rearrange("j p e -> p j e"), [], [Csb])
                        act(lambda e, Ss=Ss, Ssb=Ssb: e.copy(out=Ssb, in_=Ss), [Csb], [Csbfb])
                        dve(lambda e, js=js: e.tensor_tensor(
                            out=QX[:, 0, :, :], in0=QtT.unsqueeze(1).broadcast_to([128, SG, 128]),
                            in1=qxmask[:, js, :], op=ALU.mult), [QtTb, "cb"], [QXb])
                        mm(pnum[:, 0:256], "pnum", [(QX[:, 0, j, :], Ssb[:, j, :]) for j in range(SG)],
                           [QXb, Csbfb], first=False, last=(g == 16 // SG - 1))
                        dve(lambda e, js=js: e.tensor_tensor(
                            out=VwX[:, :, 0:256], in0=vsrc[:, ti, 0:256].unsqueeze(1).broadcast_to([128, SG, 256]),
                            in1=vxmask[:, js].unsqueeze(2).broadcast_to([128, SG, 256]), op=ALU.mult),
                            [vb_, "cb"], [VwXb])
                        for j in range(SG):
                            mm(pstate[:, 0:256], "pstate", [(Ktok[:, 0:128], VwX[:, j, 0:256])], [Ktokb, VwXb])
                            dcol = dSs[:, g * SG + j:g * SG + j + 1]
                            dve(lambda e, j=j, dcol=dcol, Ss=Ss: e.scalar_tensor_tensor(
                                out=Ss[:, j, :], in0=Ss[:, j, :], scalar=dcol, in1=pstate[:, 0:256], op0=ALU.mult,
                                op1=ALU.add), [Csb, dSsb, "pstate"], [Csb])
                        sdma(o_sS[js, h].rearrange("j p e -> p j e"), Ss, [Csb], [])
                    for g in range(16 // SG):
                        k_ = csi[0]
                        csi[0] += 1
                        sgrp(g, *Cs3[k_ % 3], *Csbf2[k_ % 2], *QX2[k_ % 2], *VwX2[k_ % 2])
                if full:
                    head_epilogue(pnum[:, 0:256], "pnum", None, OG[:, ti, :], OGb, ti, 1024 + h * 256, None, ytok, ytokb)
                if c == NCH - 1:
                    sdma(o_pS[h], Sh, [Shb], [])
            dve(lambda e: e.memset(Sbf, 0.0), [], [Sbfb])
            for c in range(NCH + 1):
                gchunk(c)

        mlstm_proj(0)
        S.barrier()
        print("arena phase 1a used %d / %d (mark %d)" % (apos[0] * 2, ARENA * 2, gmark * 2))
        apos[0] = gmark
        Cs3 = [carve("Cs%d" % i, [SG, 2, 257], F32) for i in range(3)]
        Csbf2 = [carve("Csbf%d" % i, [SG, 2, 257]) for i in range(2)]
        QX2 = [carve("QX%d" % i, [2, SG, 128]) for i in range(2)]
        VwX2 = [carve("VwX%d" % i, [SG, 257]) for i in range(2)]
        for h in range(4):
            if h > 0:
                mlstm_proj(h)
            mlstm_chunks(h)
            gla_head(h)
        tr(pmisc[:, 0:128], "pmisc", nTo.rearrange("p a b -> p (a b)"), nTob, identf)
        evac(nrow[:, 0:128], nrowb, pmisc[:, 0:128], "pmisc", eng="dve")
        sdma(o_sn[:, :], nrow[:, 0:128], [nrowb], [])
        if stop_after == "gla":
            return finish(nc, S, dbg_out)

        arena_reset()
        wide[0] = True
        yTa = hTp[:].rearrange("p k t -> p (k t)")[:, 0:8 * TM].rearrange("p (k t) -> p k t", k=8)
        yTg, yTgb = carve("yTg", [8, TM])
        mT, mTb = carve("mT", [KD, TM])
        ystg, ystgb = carve("ystg", [D])
        sg, sgb = carve("sg", [2, TM])
        tmpm, tmpmb = carve("tmpm", [512])
        xs_ = [carve("xs%d" % i, [256], F32)[0] for i in range(4)]
        for ti in range(NMT):
            sdma(ystg, yscr[ti * 128:(ti + 1) * 128, :], ["yscr"], [ystgb])
            for half in range(2):
                pt = ptr[half]
                pbn = "ptr%d" % half
                for j in range(8):
                    k = half * 8 + j
                    tr(pt[:, j * 128:(j + 1) * 128], pbn, ystg[:, k * 128:(k + 1) * 128], ystgb, identb, inc=(j == 7))
                for j in range(8):
                    k = half * 8 + j
                    dstt, dstb = (yTa, "hTp") if half == 0 else (yTg, yTgb)
                    dst = dstt[:, j, ti * 128:(ti + 1) * 128]
                    hcol = spm[:, P_HNM + k:P_HNM + k + 1]
                    if j % 2 == 0:
                        act(lambda e, dst=dst, j=j, pt=pt, hcol=hcol: e.activation(
                            out=dst, in_=pt[:, j * 128:(j + 1) * 128], func=AF.Copy, scale=hcol),
                            [pbn, "spm"], [dstb])
                    else:
                        dve(lambda e, dst=dst, j=j, pt=pt, hcol=hcol: e.tensor_scalar(
                            out=dst, in0=pt[:, j * 128:(j + 1) * 128], scalar1=hcol, scalar2=None, op0=ALU.mult),
                            [pbn, "spm"], [dstb])

        def branch_group(cg):
            c0 = cg * 256
            for (gcol, wbr, ysrc, ysb, first) in ((C_GA, w_a, yTa, "hTp", True), (C_GB, w_b, yTg, yTgb, False)):
                wgt, wgtn = load_w(w_in, WOFF[gcol + c0], KD, 256)
                for cb_ in range(2):
                    for (t0, N) in MG:
                        pv, pn_ = proj_fm(wgt, wgtn, cb_ * 128, 128, hT, "hT", t0, N)
                        evac(sg[:, cb_, t0:t0 + N], sgb, pv, pn_, func=AF.Sigmoid)
                wbt, wbtn = load_w(wbr, cg * 8 * 256, 8, 256)
                for cb_ in range(2):
                    kk = cg * 2 + cb_
                    for (t0, N) in MG:
                        bank, bname = nextbank()
                        mm(bank[:, 0:N], bname,
                           [(wbt[:, k, cb_ * 128:(cb_ + 1) * 128], ysrc[:, k, t0:t0 + N]) for k in range(8)],
                           [wbtn, ysb])
                        if first:
                            dve(lambda e, bank=bank, N=N, t0=t0, cb_=cb_, kk=kk: e.tensor_tensor(
                                out=mT[:, kk, t0:t0 + N], in0=bank[:, 0:N], in1=sg[:, cb_, t0:t0 + N], op=ALU.mult),
                                [bname, sgb], [mTb])
                        else:
                            dve(lambda e, bank=bank, N=N, t0=t0, cb_=cb_: e.tensor_tensor(
                                out=tmpm[:, 0:N], in0=bank[:, 0:N], in1=sg[:, cb_, t0:t0 + N], op=ALU.mult),
                                [bname, sgb], [tmpmb])
                            dve(lambda e, N=N, t0=t0, kk=kk: e.tensor_tensor(
                                out=mT[:, kk, t0:t0 + N], in0=mT[:, kk, t0:t0 + N], in1=tmpm[:, 0:N], op=ALU.add),
                                [tmpmb, mTb], [mTb])
        for cg in range(8):
            branch_group(cg)

        def wout_group(cg):
            c0 = cg * 256
            wot, wotn = load_w(w_o, cg * KD * 256, KD, 256)
            for ti in range(NMT):
                xs = xs_[ti % 4]
                xsb = "xsb%d" % (ti % 4)
                sdma(xs[:], xmain[ti * 128:(ti + 1) * 128, c0:c0 + 256], [], [xsb])
                bank, bname = nextbank()
                mm(bank[:, 0:256], bname, [(mT[:, k, ti * 128:(ti + 1) * 128], wot[:, k, :]) for k in range(KD)],
                   [wotn, mTb])
                dve(lambda e, xs=xs, bank=bank: e.tensor_tensor(out=xs[:], in0=xs[:], in1=bank[:, 0:256], op=ALU.add),
                    [xsb, bname], [xsb])
                sdma(x1scr[ti * 128:(ti + 1) * 128, c0:c0 + 256], xs[:], [xsb], ["x1scr"])
        for cg in range(8):
            wout_group(cg)

        arena_reset()
        sdma(wbc[:], nfw.partition_broadcast(128), [], ["wbc"])
        xt2 = [carve("xt2_%d" % i, [D], F32) for i in range(4)]
        xn2 = [carve("xn2_%d" % i, [D], BF16) for i in range(4)]
        sq2, sq2b = carve("sq2", [D], BF16)
        for ti in range(NMT):
            (x_t, xb), (x_n, xnb) = xt2[ti % 4], xn2[ti % 4]
            sdma(x_t, x1scr[ti * 128:(ti + 1) * 128, :], ["x1scr"], [xb])
            norm_to_T(x_t, xb, x_n, xnb, hT, "hT", ti, sq2, sq2b)

        arena_reset()
        HT_ = 640
        actT, actTb = carve("actT", [KF, HT_])
        upad2 = [carve("upad%d" % i, [2 + HT_], F32) for i in range(2)]
        tb2 = [carve("tbuf%d" % i, [HT_], F32) for i in range(2)]
        pb2 = [carve("pbuf%d" % i, [HT_], F32) for i in range(2)]
        gb2 = [carve("gbuf%d" % i, [HT_]) for i in range(2)]
        w4f = [wsl[i // 2][:, (i % 2) * KD * 128:((i % 2) + 1) * KD * 128] for i in range(6)]
        w4 = [v.rearrange("p (k c) -> p k c", k=KD) for v in w4f]
        w4i = [0]

        def load_w4(packed, kf):
            i = w4i[0] % 6
            w4i[0] += 1
            name = "w4_%d" % i
            flat = w4f[i]
            off = kf * KD * 128
            S.dma("pool", lambda e: e.dma_start(out=flat, in_=packed[:, off:off + KD * 128]), writes=[name])
            return w4[i], name
        ucar, ucarb = carve("ucar", [KF, 2], F32)
        ucv, ucvb = carve("ucv", [KF, 34], F32)
        scvT, scvTb = carve("scvT", [KF, 32], F32)
        srow, srowb = carve("srow", [512], F32)
        fst, fstb = carve("fst", [5, 128], F32)
        wdsf = [hTp[:].rearrange("p k t -> p (k t)")[:, i * KF * 128:(i + 1) * KF * 128] for i in range(2)]
        wds = [v.rearrange("p (k c) -> p k c", k=KF) for v in wdsf]
        x2scr = x2scr_
        CW = lambda j, k: spm[:, P_CW + j * KF + k:P_CW + j * KF + k + 1]
        CBc = lambda k: spm[:, P_CB + k:P_CB + k + 1]
        sc32 = sconv.rearrange("j r c -> (j r) c")
        for k4 in range(KF // 4):
            sdma(srow[0:32, :], sc32[:, k4 * 512:(k4 + 1) * 512], [], [srowb])
            for q in range(4):
                tr(pmisc[:, q * 32:(q + 1) * 32], "pmisc", srow[0:32, q * 128:(q + 1) * 128], srowb,
                   identf[0:32, 0:32], inc=(q == 3))
            evac(scvT[:, k4 * 4:(k4 + 1) * 4, :], scvTb, pmisc[:, 0:128].rearrange("p (a b) -> p a b", a=4),
                 "pmisc", eng="dve")
        dve(lambda e: e.memset(ucar, 0.0), [], [ucarb])
        wdi = [0]

        def ffn_block(half, kf):
            g0 = half * HT_
            npr = HT_ if half == 0 else 512
            wu, wun = load_w4(w_up, kf)
            wg_, wgn_ = load_w4(w_gt, kf)
            (upad, upadb), (tb_, tbb), (pb_, pbb), (gb_, gbb) = upad2[kf % 2], tb2[kf % 2], pb2[kf % 2], gb2[kf % 2]
            dve(lambda e: e.tensor_copy(out=upad[:, 0:2], in_=ucar[:, kf, :]), [ucarb], [upadb])
            for gi in range(2):
                t0 = g0 + gi * 320
                pv, pn_ = proj_fm(wu, wun, 0, 128, hT, "hT", t0, 320)
                evac(upad[:, 2 + gi * 320:2 + (gi + 1) * 320], upadb, pv, pn_, eng="act")
                act(lambda e, pv=pv, gi=gi: e.activation(out=tb_[:, gi * 320:(gi + 1) * 320], in_=pv, func=AF.Identity,
                                                         bias=CBc(kf), scale=CW(2, kf)), [pn_, "spm"], [tbb])
                pv, pn_ = proj_fm(wg_, wgn_, 0, 128, hT, "hT", t0, 320)
                evac(gb_[:, gi * 320:(gi + 1) * 320], gbb, pv, pn_, eng="act")
            if half == 0:
                dve(lambda e: e.tensor_copy(out=ucar[:, kf, :], in_=upad[:, HT_:HT_ + 2]), [upadb], [ucarb])
            else:
                dve(lambda e: e.tensor_copy(out=ucv[:, kf, 0:2], in_=upad[:, 512:514]), [upadb], [ucvb])
                u3 = upad[:, 2 + 512:2 + 640].rearrange("p (j l) -> p j l", l=8)
                dve(lambda e, u3=u3: e.tensor_copy(out=ucv[:, kf, 2:34].rearrange("p (j r) -> p j r", r=2),
                                                   in_=u3[:, :, 6:8]), [upadb], [ucvb])
            dve(lambda e: e.scalar_tensor_tensor(out=tb_[:, 0:npr], in0=upad[:, 1:1 + npr], scalar=CW(1, kf),
                                                 in1=tb_[:, 0:npr], op0=ALU.mult, op1=ALU.add),
                [upadb, tbb, "spm"], [tbb])
            dve(lambda e: e.scalar_tensor_tensor(out=tb_[:, 0:npr], in0=upad[:, 0:npr], scalar=CW(0, kf),
                                                 in1=tb_[:, 0:npr], op0=ALU.mult, op1=ALU.add),
                [upadb, tbb, "spm"], [tbb])
            if half == 1:
                t3 = tb_[:, 512:640].rearrange("p (j l) -> p j l", l=8)
                u3 = upad[:, 2 + 512:2 + 640].rearrange("p (j l) -> p j l", l=8)
                s3 = scvT[:, kf, :].rearrange("p (j r) -> p j r", r=2)
                dve(lambda e, t3=t3, u3=u3: e.scalar_tensor_tensor(
                    out=t3[:, :, 1:8], in0=u3[:, :, 0:7], scalar=CW(1, kf), in1=t3[:, :, 1:8], op0=ALU.mult,
                    op1=ALU.add), [upadb, tbb, "spm"], [tbb])
                dve(lambda e, t3=t3, s3=s3: e.scalar_tensor_tensor(
                    out=t3[:, :, 0:1], in0=s3[:, :, 1:2], scalar=CW(1, kf), in1=t3[:, :, 0:1], op0=ALU.mult,
                    op1=ALU.add), [scvTb, tbb, "spm"], [tbb])
                dve(lambda e, t3=t3, u3=u3: e.scalar_tensor_tensor(
                    out=t3[:, :, 2:8], in0=u3[:, :, 0:6], scalar=CW(0, kf), in1=t3[:, :, 2:8], op0=ALU.mult,
                    op1=ALU.add), [upadb, tbb, "spm"], [tbb])
                dve(lambda e, t3=t3, s3=s3: e.scalar_tensor_tensor(
                    out=t3[:, :, 0:2], in0=s3[:, :, 0:2], scalar=CW(0, kf), in1=t3[:, :, 0:2], op0=ALU.mult,
                    op1=ALU.add), [scvTb, tbb, "spm"], [tbb])
            act(lambda e: e.activation(out=pb_, in_=tb_, func=AF.Square, scale=0.044715 ** 0.5), [tbb], [pbb])
            dve(lambda e: e.scalar_tensor_tensor(out=pb_, in0=pb_, scalar=1.0, in1=tb_, op0=ALU.add, op1=ALU.mult),
                [pbb, tbb], [pbb])
            act(lambda e: e.activation(out=pb_, in_=pb_, func=AF.Sigmoid, scale=1.5957691216057308), [pbb], [pbb])
            dve(lambda e: e.tensor_tensor(out=pb_, in0=pb_, in1=tb_, op=ALU.mult), [pbb, tbb], [pbb])
            dve(lambda e: e.tensor_tensor(out=actT[:, kf, :], in0=pb_, in1=gb_, op=ALU.mult), [pbb, gbb], [actTb])

        def down_block(half, cbk):
            g0 = half * HT_
            i = wdi[0] % 2
            wdi[0] += 1
            (tb_, tbb) = tb2[i]
            wd = wds[i]
            wdn = "wds%d" % i
            off = cbk * KF * 128
            wdf = wdsf[i]
            S.dma("pool", lambda e: e.dma_start(out=wdf[:, 0:22 * 128], in_=w_dn[:, off:off + 22 * 128]),
                  writes=[wdn])
            S.dma("pool", lambda e: e.dma_start(out=wdf[:, 22 * 128:44 * 128],
                                                in_=w_dn[:, off + 22 * 128:off + 44 * 128]), writes=[wdn])
            for gi in range(2):
                bank, bname = nextbank()
                mm(bank[:, 0:320], bname, [(wd[:, k, :], actT[:, k, gi * 320:(gi + 1) * 320]) for k in range(KF)],
                   [wdn, actTb])
                evac(tb_[:, gi * 320:(gi + 1) * 320], tbb, bank[:, 0:320], bname, eng="act")
            for tt in range(5):
                dstp = pst[:, tt * 128:(tt + 1) * 128] if tt < 4 else pnum[:, 0:128]
                dstn = "pst" if tt < 4 else "pnum"
                tr(dstp, dstn, tb_[:, tt * 128:(tt + 1) * 128], tbb, identf, inc=(tt >= 3))
            evac(fst[:, 0:4, :], fstb, pst[:, 0:512].rearrange("p (a b) -> p a b", a=4), "pst", eng="dve")
            evac(fst[:, 4, :], fstb, pnum[:, 0:128], "pnum", eng="dve")
            for tt in range(5):
                r0 = g0 + tt * 128
                sdma(x2scr[r0:r0 + 128, cbk * 128:(cbk + 1) * 128], fst[:, tt, :], [fstb], ["x2scr"])

        for half in range(2):
            for kf in range(KF):
                ffn_block(half, kf)
            for cbk in range(KD):
                down_block(half, cbk)
        oconv_s = o_sconv.rearrange("j r c -> (j r) c")
        for k4 in range(KF // 4):
            for q in range(4):
                tr(pmisc[0:34, q * 128:(q + 1) * 128], "pmisc", ucv[:, k4 * 4 + q, :], ucvb, identf, inc=(q == 3))
            evac(srow[0:34, :], srowb, pmisc[0:34, 0:512], "pmisc", eng="dve")
            sdma(o_pconv[:, k4 * 512:(k4 + 1) * 512], srow[0:2, :], [srowb], [])
            sdma(oconv_s[:, k4 * 512:(k4 + 1) * 512], srow[2:34, :], [srowb], [])

        arena_reset()
        sdma(wbc[:], fnw.partition_broadcast(128), [], ["wbc"])
        xa = [carve("xa%d" % i, [D], F32) for i in range(4)]
        xf = [carve("xf%d" % i, [D], F32) for i in range(4)]
        sq3, sq3b = carve("sq3", [D], BF16)
        for ti in range(1, NMT):
            (x_a, xab), (x_f, xfb) = xa[ti % 4], xf[ti % 4]
            sdma(x_a, x1scr[ti * 128:(ti + 1) * 128, :], ["x1scr"], [xab])
            sdma(x_f, x2scr[ti * 128:(ti + 1) * 128, :], ["x2scr"], [xfb])
            dve(lambda e, x_a=x_a, x_f=x_f: e.tensor_tensor(out=x_a, in0=x_a, in1=x_f, op=ALU.add), [xab, xfb], [xab])
            ss, rs, _, sbn = statslot()
            act(lambda e, x_a=x_a, ss=ss: e.activation(out=sq3, in_=x_a, func=AF.Square, accum_out=ss), [xab],
                [sq3b, sbn])
            rstd_from_ss(ss, rs, sbn, D)
            dve(lambda e, x_a=x_a, x_f=x_f, rs=rs: e.scalar_tensor_tensor(out=x_f, in0=x_a, scalar=rs, in1=wbc[:],
                                                                          op0=ALU.mult, op1=ALU.mult),
                [xab, sbn + "r", "wbc"], [xfb])
            sdma(yout[(ti - 1) * 128:ti * 128, :], x_f, [xfb], [])
        return finish(nc, S, dbg_out)


def finish(nc, S, dbg_out):
    S.finish()
    print("sim phase us:", [int(x) for x in S.sim_phase_us], "units", len(S.units))
    with nc.Block() as block:
        @block.sync
        def _(eng):
            S.emit("sp", eng)

        @block.tensor
        def _(eng):
            S.emit("pe", eng)

        @block.scalar
        def _(eng):
            S.emit("act", eng)

        @block.vector
        def _(eng):
            S.emit("dve", eng)

        @block.gpsimd
        def _(eng):
            S.emit("pool", eng)
    return nc


def make_consts():
    import ml_dtypes
    cb = np.zeros((128, CB_W), np.float32)
    cb[:, 0:128] = np.eye(128)
    s = np.arange(128)[:, None]
    t = np.arange(128)[None, :]
    cb[:, 128:256] = (s <= t)
    cb[:, 256:384] = (s <= t) & ((s // 8) == (t // 8))
    j = np.arange(16)[:, None]
    cb[:, 512:512 + 2048] = ((np.arange(128)[None, :] // 8) == j).astype(np.float32).reshape(1, 2048)
    cb[:, 2560:2576] = ((np.arange(128)[:, None] // 8) == np.arange(16)[None, :])
    cf = np.zeros((128, CF_W), np.float32)
    cf[:, 0:128] = np.eye(128)
    cf[:, 128:256] = 1.0
    cf[:, 256:384] = (np.arange(128)[None, :] % 8 != 0)
    cf[:, 384:512] = np.where(np.arange(128)[None, :] % 8 == 0, -1e30, 0.0)
    return cb.astype(ml_dtypes.bfloat16), cf


_NC_CACHE = {}


def _prep_inputs(inp):
    f32 = np.float32
    cbc, cfc = make_consts()
    xp = np.asarray(inp["x_prompt"], f32)
    xs = np.asarray(inp["x_sample"], f32)
    sp = np.zeros((128, SP_W), f32)
    ib = np.asarray(inp["mlstm_i_bias"], f32)[0]
    fb = np.asarray(inp["mlstm_f_bias"], f32)[0]
    sp[0:64, 0] = np.repeat(ib, 16)
    sp[0:64, 1] = np.repeat(fb, 16)
    sp[:, 2:6] = np.asarray(inp["gla_alpha_bias"], f32)[0].reshape(4, 128).T
    sp[:, 6:14] = np.asarray(inp["mlstm_head_norm_w"], f32)[0].reshape(8, 128).T
    sp[:, 14:22] = np.asarray(inp["gla_head_norm_w"], f32)[0].reshape(8, 128).T
    cw = np.asarray(inp["ffn_conv_w"], f32)[0]
    for j in range(3):
        sp[:, 22 + j * KF:22 + (j + 1) * KF] = cw[j].reshape(KF, 128).T
    sp[:, 22 + 3 * KF:22 + 4 * KF] = np.asarray(inp["ffn_conv_b"], f32)[0].reshape(KF, 128).T
    shared = {
        "w_in": _pack(np.asarray(inp["w_in"], f32)[0], KD, _win_blocks()),
        "nmw": np.asarray(inp["norm_mix_w"], f32).reshape(1, D),
        "nfw": np.asarray(inp["norm_ffn_w"], f32).reshape(1, D),
        "fnw": np.asarray(inp["final_norm_w"], f32).reshape(1, D),
        "cst_bf": cbc, "cst_f": cfc, "smallp": sp,
        "aup": np.asarray(inp["gla_alpha_up"], f32)[0],
        "w_a": _pack(np.asarray(inp["w_branch_a"], f32)[0], 8, [(c * 256, 256) for c in range(8)]),
        "w_b": _pack(np.asarray(inp["w_branch_b"], f32)[0], 8, [(c * 256, 256) for c in range(8)]),
        "w_o": _pack(np.asarray(inp["w_out"], f32)[0], KD, [(c * 256, 256) for c in range(8)]),
        "w_up": _pack(np.asarray(inp["ffn_w_up"], f32)[0], KD, [(c * 128, 128) for c in range(KF)]),
        "w_gt": _pack(np.asarray(inp["ffn_w_gate"], f32)[0], KD, [(c * 128, 128) for c in range(KF)]),
        "w_dn": _pack(np.asarray(inp["ffn_w_down"], f32)[0], KF, [(c * 128, 128) for c in range(KD)]),
    }
    sC = np.asarray(inp["state_mlstm_C"], f32)[0]
    sn = np.asarray(inp["state_mlstm_n"], f32)[0]
    sm = np.asarray(inp["state_mlstm_m"], f32)[0]
    sS = np.asarray(inp["state_gla_S"], f32)[0]
    scv = np.asarray(inp["state_ffn_conv"], f32)[0]
    maps = []
    for c in range(8):
        s, half = c // 2, c % 2
        xmain = np.zeros((TM, D), f32)
        if half == 1:
            xpre = np.ascontiguousarray(xp[s, 0:PRE])
            xmain[0:1152] = xp[s, PRE:2048]
        else:
            xpre = np.zeros((PRE, D), f32)
            xmain[128:1152] = xp[s, 0:1024]
        xmain[1152:1280] = xs[16 * c:16 * c + 16].reshape(128, D)
        m = dict(shared)
        m.update({
            "xpre": xpre, "xmain": xmain, "flag": np.full((1, 1), float(half), f32),
            "sC": np.ascontiguousarray(sC[16 * c:16 * c + 16]),
            "sn": np.ascontiguousarray(sn[16 * c:16 * c + 16].reshape(64, 256)),
            "smcol": np.ascontiguousarray(sm[16 * c:16 * c + 16].T.reshape(64, 1)),
            "sS": np.ascontiguousarray(sS[16 * c:16 * c + 16]),
            "sconv": np.ascontiguousarray(scv[16 * c:16 * c + 16]),
        })
        maps.append(m)
    return maps


def _assemble(results):
    f32 = np.float32
    y_p = np.zeros((4, 2048, D), f32)
    y_s = np.zeros((128, 8, D), f32)
    pC = np.zeros((1, 4, 4, 256, 256), f32)
    pn = np.zeros((1, 4, 4, 256), f32)
    pm = np.zeros((1, 4, 4), f32)
    pS = np.zeros((1, 4, 4, 128, 256), f32)
    pcv = np.zeros((1, 4, 2, DFF), f32)
    sCo = np.zeros((1, 128, 4, 256, 256), f32)
    sno = np.zeros((1, 128, 4, 256), f32)
    smo = np.zeros((1, 128, 4), f32)
    sSo = np.zeros((1, 128, 4, 128, 256), f32)
    scvo = np.zeros((1, 128, 2, DFF), f32)
    for c in range(8):
        r = results[c]
        s, half = c // 2, c % 2
        yo = np.asarray(r["yout"], f32)
        y_p[s, half * 1024:(half + 1) * 1024] = yo[0:1024]
        y_s[16 * c:16 * c + 16] = yo[1024:1152].reshape(16, 8, D)
        if half == 1:
            pC[0, s] = np.asarray(r["o_pC"], f32)
            pn[0, s] = np.asarray(r["o_pn"], f32).reshape(4, 256)
            pm[0, s] = np.asarray(r["o_pm"], f32).reshape(4, 16)[:, 15]
            pS[0, s] = np.asarray(r["o_pS"], f32)
            pcv[0, s] = np.asarray(r["o_pconv"], f32)
        sl = slice(16 * c, 16 * c + 16)
        sCo[0, sl] = np.asarray(r["o_sC"], f32)
        sno[0, sl] = np.asarray(r["o_sn"], f32).reshape(2, 16, 4, 128).transpose(1, 2, 0, 3).reshape(16, 4, 256)
        smo[0, sl] = np.asarray(r["o_sm"], f32).reshape(4, 16).T
        sSo[0, sl] = np.asarray(r["o_sS"], f32)
        scvo[0, sl] = np.asarray(r["o_sconv"], f32)
    return (y_p, y_s, pC, pn, pm, pS, pcv, sCo, sno, smo, sSo, scvo)


def kernel(**inputs):
    if "nc" not in _NC_CACHE:
        _NC_CACHE["nc"] = build()
    nc = _NC_CACHE["nc"]
    maps = _prep_inputs(inputs)
    res = run_bass_kernel_spmd(nc, maps, core_ids=list(range(8)))
    return _assemble(res.results)
```

```python
import numpy as np
from contextlib import ExitStack
import concourse.bass as bass
import concourse.mybir as mybir
from concourse.bass_utils import run_bass_kernel_spmd

F32 = mybir.dt.float32
BF16 = mybir.dt.bfloat16
AF = mybir.ActivationFunctionType
ALU = mybir.AluOpType
AX = mybir.AxisListType

D = 2048
NIN = 11288
DFF = 5632
KD = D // 128
KF = DFF // 128
PRE = 896
NPT = PRE // 128
TM = 1280
CB_W = 2576
CF_W = 520
SP_W = 22 + 4 * KF
NMT = TM // 128
NCH = 16
EPS = 1e-6

C_MQ, C_MK, C_MV, C_MO = 0, 1024, 2048, 3072
C_MI, C_MF = 4096, 4100
C_GQ, C_GK, C_GV, C_GR = 4104, 4616, 5128, 6152
C_GAL = 7176
C_GA, C_GB = 7192, 9240


def _win_blocks():
    blks = [(C_MI, 8), (C_GAL, 16)]
    for hh in range(4):
        blks += [(C_MQ + hh * 256, 256), (C_MK + hh * 256, 256), (C_MV + hh * 256, 256), (C_MO + hh * 256, 256)]
    for hh in range(4):
        blks += [(C_GQ + hh * 128, 128), (C_GK + hh * 128, 128), (C_GV + hh * 256, 256), (C_GR + hh * 256, 256)]
    for cg in range(8):
        blks += [(C_GA + cg * 256, 256), (C_GB + cg * 256, 256)]
    return blks


def _win_offsets():
    off = {}
    o = 0
    for c0, n in _win_blocks():
        off[c0] = o
        o += KD * n
    return off


def _pack(w, nk, blocks):
    outs = []
    for c0, n in blocks:
        outs.append(np.ascontiguousarray(w[:, c0:c0 + n].reshape(nk, 128, n).transpose(1, 0, 2)).reshape(128, nk * n))
    return np.ascontiguousarray(np.concatenate(outs, axis=1))


class _Probe:
    def __init__(self):
        self.calls = []

    def __getattr__(self, name):
        def f(*a, **k):
            self.calls.append((name, a, k))
            return self
        return f

    def then_inc(self, *a, **k):
        return self


def _free_elems(ap):
    n = 1
    for s in ap.shape[1:]:
        n *= int(s)
    return n


def _est(eng, fn):
    p = _Probe()
    fn(p)
    name, a, k = p.calls[0]
    out = k.get("out", a[0] if a else None)
    if eng == "pe":
        if name == "transpose":
            return 0.09
        rhs = k.get("rhs")
        n = _free_elems(rhs)
        f = 4.0 if rhs.dtype == F32 else 1.0
        return max(n, 64) * f / 2400.0 + 0.012
    n = _free_elems(out) if out is not None else 64
    if eng == "act":
        return (n + 200) / 1400.0 + (0.1 if k.get("accum_out") is not None else 0.0)
    if eng in ("dve", "pool"):
        f = 2.0 if name in ("tensor_tensor_scan",) else 1.0
        return max(n, 60) * f / 960.0 + 0.07
    return 0.1


def _dma_est(fn):
    p = _Probe()
    fn(p)
    name, a, k = p.calls[0]
    out = k.get("out")
    nbytes = 1
    for s in out.shape:
        nbytes *= int(s)
    nbytes *= 4 if out.dtype == F32 else 2
    return 2.0 + nbytes / 180e3, nbytes


class Sched:
    ENGS = ("pe", "act", "dve", "pool", "sp")
    XLAT = 0.12

    def __init__(self, esems, dsems):
        self.esem = esems
        self.dsems = dsems
        self.units = []
        self.bufs = {}
        self.phase = 0
        self.trace_phase = None
        self.open = {e: None for e in esems}

    def _st(self, b):
        st = self.bufs.get(b)
        if st is None:
            st = {"w": None, "r": {}}
            self.bufs[b] = st
        return st

    def _deps(self, uid, reads, writes):
        deps = set()
        for b in reads:
            st = self._st(b)
            if st["w"] is not None:
                deps.add(st["w"])
        for b in writes:
            st = self._st(b)
            if st["w"] is not None:
                deps.add(st["w"])
            deps.update(st["r"].keys())
        deps.discard(uid)
        return deps

    def _mark(self, uid, reads, writes):
        for b in reads:
            self._st(b)["r"][uid] = True
        for b in writes:
            st = self._st(b)
            st["w"] = uid
            st["r"] = {}

    def op(self, eng, fn, reads=(), writes=(), inc=True):
        u = self.open[eng]
        if u is None:
            u = {"eng": eng, "fns": [], "deps": set(), "dur": 0.0, "dma": False, "phase": self.phase,
                 "id": len(self.units), "lab": ",".join(writes)}
            self.units.append(u)
            self.open[eng] = u
        u["deps"] |= self._deps(u["id"], reads, writes)
        self._mark(u["id"], reads, writes)
        u["fns"].append(fn)
        u["dur"] += _est(eng, fn)
        if inc:
            self.open[eng] = None

    def dma(self, q, fn, reads=(), writes=()):
        uid = len(self.units)
        lat, nbytes = _dma_est(fn)
        u = {"eng": q, "fns": [fn], "deps": self._deps(uid, reads, writes), "dur": 0.06 if q == "sp" else 0.35,
             "dma": True, "lat": lat, "phase": self.phase, "id": uid, "lab": "dma:" + ",".join(writes) + "<" + ",".join(reads)}
        self.units.append(u)
        self._mark(uid, reads, writes)

    def barrier(self):
        for e, v in self.open.items():
            assert v is None, e
        self.phase += 1
        self.bufs = {}

    def barrier_all_dma(self, q="sp"):
        pass

    def finish(self):
        import heapq
        order = {e: [] for e in self.ENGS}
        nph = self.phase + 1
        by_phase = [[] for _ in range(nph)]
        for u in self.units:
            by_phase[u["phase"]].append(u)
        for ph in range(nph):
            us = by_phase[ph]
            ids = {u["id"] for u in us}
            ndep = {}
            users = {}
            for u in us:
                d = [x for x in u["deps"] if x in ids]
                u["deps"] = set(d)
                ndep[u["id"]] = len(d)
                for x in d:
                    users.setdefault(x, []).append(u)
            byid = {u["id"]: u for u in us}
            ready = {e: [] for e in self.ENGS}
            avail = {}
            for u in us:
                if ndep[u["id"]] == 0:
                    heapq.heappush(ready[u["eng"]], u["id"])
                    avail[u["id"]] = 0.0
            tfree = {e: 0.0 for e in self.ENGS}
            lastu = {}
            done = 0
            fin = {}
            while done < len(us):
                best = None
                for e in self.ENGS:
                    h = ready[e]
                    if not h:
                        continue
                    cand = None
                    tnow = tfree[e]
                    low = [i for i in h if avail[i] <= tnow]
                    if low:
                        cid = min(low)
                        st = tnow
                    else:
                        cid = min(h, key=lambda i: (avail[i], i))
                        st = avail[cid]
                    if best is None or st < best[0] or (st == best[0] and cid < best[2]):
                        best = (st, e, cid)
                st, e, cid = best
                ready[e].remove(cid)
                heapq.heapify(ready[e])
                u = byid[cid]
                u["st"] = st
                if avail[cid] >= tfree[e] - 1e-9 and u["deps"]:
                    u["why"] = max(u["deps"], key=lambda x: fin[x])
                else:
                    u["why"] = lastu.get(e)
                lastu[e] = cid
                end = st + u["dur"]
                tfree[e] = end
                f = end + (u["lat"] if u["dma"] else 0.0)
                fin[cid] = f
                order[e].append(u)
                done += 1
                for v in users.get(cid, ()):
                    ndep[v["id"]] -= 1
                    if ndep[v["id"]] == 0:
                        t = 0.0
                        for x in v["deps"]:
                            lat = 0.0 if (byid[x]["eng"] == v["eng"] and v["eng"] == "pe") else self.XLAT
                            t = max(t, fin[x] + lat)
                        avail[v["id"]] = t
                        heapq.heappush(ready[v["eng"]], v["id"])
            self.sim_phase_us = getattr(self, "sim_phase_us", []) + [max(tfree.values())]
            if getattr(self, "trace_phase", None) == ph:
                cur = max(us, key=lambda u: fin[u["id"]])["id"]
                chain = []
                while cur is not None and len(chain) < 400:
                    u = byid[cur]
                    chain.append((round(u["st"], 2), u["eng"], round(u["dur"], 2), u["lab"], len(u["fns"])))
                    cur = u.get("why")
                for c in chain[:400]:
                    print("   CP", c)
            busy = {e: 0.0 for e in self.ENGS}
            for u in us:
                busy[u["eng"]] += u["dur"]
            print("phase", ph, "sim %.0f us" % max(tfree.values()), {e: int(v) for e, v in busy.items()}, "units", len(us))
        cnt = {e: 0 for e in self.esem}
        dcnt = {q: [0] * len(v) for q, v in self.dsems.items()}
        dnext = {q: 0 for q in self.dsems}
        for e in self.ENGS:
            for u in order[e]:
                if u["dma"]:
                    i = dnext[e]
                    dnext[e] = (i + 1) % len(self.dsems[e])
                    u["prev"] = (self.dsems[e][i], dcnt[e][i]) if dcnt[e][i] > 0 else None
                    dcnt[e][i] += 16
                    u["ev"] = (self.dsems[e][i], dcnt[e][i])
                else:
                    cnt[e] += 1
                    u["ev"] = (self.esem[e], cnt[e])
        self.order = order
        byid = {u["id"]: u for u in self.units}
        self.prog = {e: [] for e in self.ENGS}
        all_dma = [u for u in self.units if u["dma"]]
        for e in self.ENGS:
            wd = {}
            cur_phase = 0
            for u in order[e]:
                waits = []
                if u["phase"] != cur_phase:
                    for e2 in self.esem:
                        if e2 == e and e == "pe":
                            continue
                        c = max([x["ev"][1] for x in order[e2] if x["phase"] < u["phase"] and not x["dma"]] or [0])
                        sem = self.esem[e2]
                        if c > 0 and wd.get(id(sem), 0) < c:
                            wd[id(sem)] = c
                            waits.append((sem, c))
                    for x in all_dma:
                        if x["phase"] < u["phase"] and wd.get(id(x["ev"][0]), 0) < x["ev"][1]:
                            wd[id(x["ev"][0])] = x["ev"][1]
                            waits.append(x["ev"])
                    cur_phase = u["phase"]
                for d in sorted(u["deps"]):
                    x = byid[d]
                    if x["eng"] == e and e == "pe":
                        continue
                    sem, c = x["ev"]
                    if wd.get(id(sem), 0) >= c:
                        continue
                    wd[id(sem)] = c
                    waits.append((sem, c))
                if u["dma"] and u["prev"] is not None:
                    sem, c = u["prev"]
                    if wd.get(id(sem), 0) < c:
                        wd[id(sem)] = c
                        waits.append((sem, c))
                self.prog[e].append((waits, u["fns"], u["ev"][0], 16 if u["dma"] else 1))
        wd = {}
        waits = []
        for x in all_dma:
            if wd.get(id(x["ev"][0]), 0) < x["ev"][1]:
                wd[id(x["ev"][0])] = x["ev"][1]
        semobj = {}
        for x in all_dma:
            semobj[id(x["ev"][0])] = x["ev"][0]
        self.final_waits = [(semobj[k], v) for k, v in wd.items()]

    def emit(self, eng_name, eng):
        for waits, fns, sem, inc in self.prog[eng_name]:
            for s, v in waits:
                eng.wait_ge(s, v)
            ins = None
            for fn in fns:
                ins = fn(eng)
            ins.then_inc(sem, inc)
        if eng_name == "sp":
            for s, v in self.final_waits:
                eng.wait_ge(s, v)


def build(debug=None, stop_after=None):
    debug = debug or {}
    nc = bass.Bass("TRN2", target_bir_lowering=False)
    es = ExitStack()

    def din(name, shape, dt=F32):
        return nc.dram_tensor(name, list(shape), dt, kind="ExternalInput").ap()

    def dout(name, shape, dt=F32):
        return nc.dram_tensor(name, list(shape), dt, kind="ExternalOutput").ap()

    def dint(name, shape, dt=F32):
        return nc.dram_tensor(name, list(shape), dt, kind="Internal").ap()

    xpre = din("xpre", [PRE, D])
    xmain = din("xmain", [TM, D])
    flag = din("flag", [1, 1])
    w_in = din("w_in", [128, KD * NIN])
    WOFF = _win_offsets()
    nmw = din("nmw", [1, D])
    nfw = din("nfw", [1, D])
    fnw = din("fnw", [1, D])
    cst_bf = din("cst_bf", [128, CB_W], BF16)
    cst_f = din("cst_f", [128, CF_W])
    smallp = din("smallp", [128, SP_W])
    aup = din("aup", [16, 512])
    w_a = din("w_a", [128, 8 * D])
    w_b = din("w_b", [128, 8 * D])
    w_o = din("w_o", [128, KD * D])
    w_up = din("w_up", [128, KD * DFF])
    w_gt = din("w_gt", [128, KD * DFF])
    w_dn = din("w_dn", [128, KF * D])
    sC = din("sC", [16, 4, 256, 256])
    sn = din("sn", [64, 256])
    smcol = din("smcol", [64, 1])
    sS = din("sS", [16, 4, 128, 256])
    sconv = din("sconv", [16, 2, DFF])

    yout = dout("yout", [TM - 128, D])
    o_pC = dout("o_pC", [4, 256, 256])
    o_pn = dout("o_pn", [8, 128])
    o_pm = dout("o_pm", [1, 64])
    o_pS = dout("o_pS", [4, 128, 256])
    o_pconv = dout("o_pconv", [2, DFF])
    o_sC = dout("o_sC", [16, 4, 256, 256])
    o_sn = dout("o_sn", [128, 128])
    o_sm = dout("o_sm", [64, 1])
    o_sS = dout("o_sS", [16, 4, 128, 256])
    o_sconv = dout("o_sconv", [16, 2, DFF])

    gscr = dint("gscr", [8, 2048])
    gscs = dint("gscs", [8, 128])
    yscr = dint("yscr", [TM, D], BF16)
    x1scr = dint("x1scr", [TM, D])
    x2scr_ = dint("x2scr", [TM, D])
    dbg_out = {}

    with es:
        def sb(name, shape, dt=F32):
            return es.enter_context(nc.sbuf_tensor(name, list(shape), dt))

        def ps(name, shape, dt=F32):
            return es.enter_context(nc.psum_tensor(name, list(shape), dt))

        esems = {e: es.enter_context(nc.semaphore("s_" + e)) for e in ("pe", "act", "dve", "pool")}
        dsems = {
            "sp": [es.enter_context(nc.semaphore("d_sp%d" % i)) for i in range(24)],
            "pool": [es.enter_context(nc.semaphore("d_pl%d" % i)) for i in range(12)],
        }
        S = Sched(esems, dsems)

        def act(fn, r, w):
            S.op("act", fn, reads=r, writes=w)

        def dve(fn, r, w):
            S.op("dve", fn, reads=r, writes=w)

        def pool(fn, r, w):
            S.op("pool", fn, reads=r, writes=w)

        def sdma(out, in_, r, w):
            S.dma("sp", lambda e: e.dma_start(out=out, in_=in_), reads=r, writes=w)

        def mm(out_ap, outb, pairs, reads, first=True, last=True):
            n = len(pairs)
            for i, (l, r) in enumerate(pairs):
                S.op("pe", lambda e, l=l, r=r, i=i: e.matmul(out_ap, lhsT=l, rhs=r, start=(first and i == 0),
                                                            stop=(last and i == n - 1)),
                     reads=reads, writes=[outb], inc=(i == n - 1))

        def tr(out_ap, outb, in_ap, inb, ident, inc=True):
            S.op("pe", lambda e: e.transpose(out=out_ap, in_=in_ap, identity=ident), reads=[inb, "cb", "cf"],
                 writes=[outb], inc=inc)

        cb = sb("cb", [128, CB_W], BF16)
        cf = sb("cf", [128, CF_W], F32)
        spm = sb("spm", [128, SP_W], F32)
        sdma(cb[:], cst_bf[:, :], [], ["cb"])
        sdma(cf[:], cst_f[:, :], [], ["cf"])
        sdma(spm[:], smallp[:, :], [], ["spm"])
        identb = cb[:, 0:128]
        maskc = cb[:, 128:256]
        masks = cb[:, 256:384]
        qxmask = cb[:, 512:512 + 2048].rearrange("p (j t) -> p j t", j=16)
        vxmask = cb[:, 2560:2576]
        identf = cf[:, 0:128]
        onesf = cf[:, 128:256]
        resetm = cf[:, 256:384]
        negbig = cf[:, 384:512]
        zerocol = cf[:, 512:513]
        P_IB, P_FB, P_AB, P_HNM, P_HNG, P_CW, P_CB = 0, 1, 2, 6, 14, 22, 22 + 3 * KF
        ib64 = spm[0:64, P_IB:P_IB + 1]
        fb64 = spm[0:64, P_FB:P_FB + 1]
        wbc = sb("wbc", [128, D], F32)
        sdma(wbc[:], nmw.partition_broadcast(128), [], ["wbc"])
        flagt = sb("flagt", [1, 1], F32)
        sdma(flagt[:], flag[:, :], [], ["flagt"])

        hT = sb("hT", [128, KD, TM], BF16)
        hTp = sb("hTp", [128, KD, PRE], BF16)
        wsl = [sb("wsl%d" % i, [128, KD * 256], BF16) for i in range(3)]
        stat = sb("stat", [128, 64], F32)
        ARENA = 99 * 512 + 160
        arena = sb("arena", [128, ARENA], BF16)
        apos = [0]
        aphase = [0]

        def carve(name, shape, dt=BF16):
            n = int(np.prod(shape))
            nb = n * (2 if dt == BF16 else 4)
            nb = (nb + 63) // 64 * 64
            off = apos[0]
            apos[0] += nb // 2
            assert apos[0] <= ARENA, ("arena overflow", name, apos[0])
            v = arena[:, off:off + nb // 2]
            if dt != BF16:
                v = v.bitcast(dt)
            v = v[:, 0:n]
            if len(shape) == 2:
                pat, kw = "p (a b) -> p a b", dict(a=shape[0])
            elif len(shape) == 3:
                pat, kw = "p (a b c) -> p a b c", dict(a=shape[0], b=shape[1])
            else:
                pat, kw = None, None
            if pat:
                v = v.rearrange(pat, **kw)
            return v, "ar%d_%s" % (aphase[0], name)

        def arena_reset():
            S.barrier()
            print("arena phase %d used %d / %d" % (aphase[0], apos[0] * 2, ARENA * 2))
            apos[0] = 0
            aphase[0] += 1

        pa = ps("pa", [128, 512])
        pb = ps("pb", [128, 512])
        ptr = [ps("ptr%d" % i, [128, 1024], BF16) for i in range(2)]
        pst = ps("pst", [128, 512])
        pnum = ps("pnum", [128, 512])
        pstate = ps("pstate", [128, 512])
        pmisc = ps("pmisc", [128, 512])
        pbk = [(pa, "pa"), (pb, "pb")]
        pbk6 = [(pa, "pa"), (pb, "pb"), (pst, "pst"), (pnum, "pnum"), (pstate, "pstate"), (pmisc, "pmisc")]
        pbi = [0]
        wide = [False]

        def nextbank():
            pbi[0] += 1
            if wide[0]:
                return pbk6[pbi[0] % 6]
            return pbk[pbi[0] % 2]

        evi = [0]

        def evac(dst, dstb, src, srcb, func=None, scale=None, eng=None):
            if func is not None or eng == "act":
                f = func if func is not None else AF.Copy
                if scale is None:
                    act(lambda e: e.activation(out=dst, in_=src, func=f), [srcb], [dstb])
                else:
                    act(lambda e: e.activation(out=dst, in_=src, func=f, scale=scale), [srcb], [dstb])
                return
            evi[0] += 1
            if eng is None:
                eng = "act" if evi[0] % 2 == 0 else "dve"
            if eng == "act":
                if scale is None:
                    act(lambda e: e.copy(out=dst, in_=src), [srcb], [dstb])
                else:
                    act(lambda e: e.mul(out=dst, in_=src, mul=scale), [srcb], [dstb])
            else:
                if scale is None:
                    dve(lambda e: e.tensor_copy(out=dst, in_=src), [srcb], [dstb])
                else:
                    dve(lambda e: e.tensor_scalar(out=dst, in0=src, scalar1=scale, scalar2=None, op0=ALU.mult),
                        [srcb], [dstb])

        wi = [0]

        def load_w(packed, off, nk, ncols):
            i = wi[0] % 3
            wi[0] += 1
            name = "wsl%d" % i
            tot = nk * ncols
            flat = wsl[i][:, 0:tot]
            dst = flat.rearrange("p (k c) -> p k c", k=nk)
            hf = tot // 2
            S.dma("pool", lambda e: e.dma_start(out=flat[:, 0:hf], in_=packed[:, off:off + hf]), writes=[name])
            S.dma("pool", lambda e: e.dma_start(out=flat[:, hf:tot], in_=packed[:, off + hf:off + tot]),
                  writes=[name])
            return dst, name

        MG = [(0, 512), (512, 512), (1024, 256)]
        PG = [(0, 448), (448, 448)]

        def proj_fm(wt, wname, c0, M, src, srcname, t0, N):
            bank, bname = nextbank()
            mm(bank[0:M, 0:N], bname, [(wt[:, k, c0:c0 + M], src[:, k, t0:t0 + N]) for k in range(KD)],
               [wname, srcname])
            return bank[0:M, 0:N], bname

        def proj_tm(wt, wname, c0, N, src, srcname, ti):
            bank, bname = nextbank()
            mm(bank[:, 0:N], bname, [(src[:, k, ti * 128:(ti + 1) * 128], wt[:, k, c0:c0 + N]) for k in range(KD)],
               [wname, srcname])
            return bank[:, 0:N], bname

        xt = [carve("xt%d" % i, [D], F32) for i in range(4)]
        xn = [carve("xn%d" % i, [D], BF16) for i in range(4)]
        sq1, sq1b = carve("sq", [D], BF16)
        nstat = [0]

        def rstd_from_ss(ss, rs, b, n):
            dve(lambda e: e.tensor_scalar(out=rs, in0=ss, scalar1=1.0 / n, scalar2=EPS, op0=ALU.mult, op1=ALU.add),
                [b], [b + "r"])
            act(lambda e: e.activation(out=rs, in_=rs, func=AF.Sqrt), [b + "r"], [b + "r"])
            dve(lambda e: e.reciprocal(out=rs, in_=rs), [b + "r"], [b + "r"])

        def statslot():
            c = nstat[0] % 16
            nstat[0] += 1
            return stat[:, c:c + 1], stat[:, 16 + c:17 + c], stat[:, 32 + c:33 + c], "stat%d" % c

        def norm_to_T(x_t, xb, x_n, xnb, dstT, dstname, ti, sqj, sqb):
            ss, rs, _, sbn = statslot()
            act(lambda e: e.activation(out=sqj, in_=x_t, func=AF.Square, accum_out=ss), [xb], [sqb, sbn])
            rstd_from_ss(ss, rs, sbn, D)
            dve(lambda e: e.scalar_tensor_tensor(out=x_n, in0=x_t, scalar=rs, in1=wbc[:], op0=ALU.mult,
                                                 op1=ALU.mult), [xb, sbn + "r", "wbc"], [xnb])
            for half in range(2):
                pt = ptr[half]
                pbn = "ptr%d" % half
                for j in range(8):
                    k = half * 8 + j
                    tr(pt[:, j * 128:(j + 1) * 128], pbn, x_n[:, k * 128:(k + 1) * 128], xnb, identb, inc=(j == 7))
                dst = dstT[:, half * 8:(half + 1) * 8, ti * 128:(ti + 1) * 128]
                src = pt[:].rearrange("p (k t) -> p k t", k=8)
                evac(dst, dstname, src, pbn, eng=("act" if half == 0 else "dve"))

        for ti in range(NPT + NMT):
            par = ti % 4
            (x_t, xb), (x_n, xnb) = xt[par], xn[par]
            if ti < NPT:
                sdma(x_t, xpre[ti * 128:(ti + 1) * 128, :], [], [xb])
                norm_to_T(x_t, xb, x_n, xnb, hTp, "hTp", ti, sq1, sq1b)
            else:
                tj = ti - NPT
                sdma(x_t, xmain[tj * 128:(tj + 1) * 128, :], [], [xb])
                norm_to_T(x_t, xb, x_n, xnb, hT, "hT", tj, sq1, sq1b)

        arena_reset()
        if stop_after == "p1":
            return finish(nc, S, dbg_out)
        Cst = carve("Cst", [2, 2, 257], F32)[0]
        Sst = carve("Sst", [2, 256], F32)[0]
        wtok = carve("wtok", [64], F32)[0]
        ftok = carve("ftok", [64], F32)[0]
        decbc = carve("decbc", [64], F32)[0]
        wtoks = carve("wtoks", [4], F32)[0]
        ftoks = carve("ftoks", [4], F32)[0]
        sdecbc = carve("sdecbc", [64], F32)[0]

        galT, galb = carve("galT", [PRE + TM], F32)
        BT, BTb = carve("BT", [PRE + TM], F32)
        gst, gstb = BT[:, 0:512], BTb
        QT, QTb = carve("QT", [2, TM])
        KT, KTb = carve("KT", [2, TM])
        Vt, Vtb = carve("Vt", [NMT, 257])
        OG, OGb = carve("OG", [NMT, 256])
        KTp, KTpb = carve("KTp", [2, PRE])
        Vp, Vpb = carve("Vp", [NPT, 257])
        Cbf, Cbfb = carve("Cbf", [2, 257])
        QTg_, QTgb = carve("QTg", [TM])
        KTg_, KTgb = carve("KTg", [TM])
        Vtg, Vtgb = carve("Vtg", [NMT, 256])
        RG, RGb = carve("RG", [NMT, 256])
        Ktok2 = [carve("Ktok%d" % i, [256]) for i in range(2)]
        Vw2 = [carve("Vw%d" % i, [257]) for i in range(2)]
        PT2 = [carve("PT%d" % i, [128]) for i in range(2)]
        ytok2 = [carve("ytok%d" % i, [256]) for i in range(2)]
        eqt2 = [carve("eqt%d" % i, [128], F32) for i in range(2)]
        QtT2 = [carve("QtT%d" % i, [128]) for i in range(2)]
        KtT2 = [carve("KtT%d" % i, [128]) for i in range(2)]
        KhT2 = [carve("KhT%d" % i, [128]) for i in range(2)]
        Khtok2 = [carve("Khtok%d" % i, [128]) for i in range(2)]
        Sbf, Sbfb = carve("Sbf", [256])
        nT, nTb = carve("nT", [2, 64], F32)
        nTo, nTob = carve("nTo", [2, 64], F32)
        nrow, nrowb = carve("nrow", [256], F32)
        junk, junkb = nrow.bitcast(BF16)[:, 0:256], nrowb
        SG = 2
        aups, aupb = carve("aups", [512], F32)
        negab = carve("negab", [4], F32)[0]
        dSs, dSsb = carve("dSs", [16], F32)
        gmark = apos[0]
        gat = carve("gat", [8, 128], F32)[0][0:64]
        gas = carve("gas", [8, 8], F32)[0][0:64]
        grow = carve("grow", [8, 64], F32)[0][0:1]
        wg, wgn = load_w(w_in, WOFF[C_MI], KD, 8)
        wl, wln = load_w(w_in, WOFF[C_GAL], KD, 16)
        MGG = [(0, 512), (512, 512), (1024, 128), (1152, 128)]
        for (src, sname, groups, base) in ((hTp, "hTp", PG, 0), (hT, "hT", MGG, PRE)):
            for (t0, N) in groups:
                pv, pn_ = proj_fm(wg, wgn, 0, 8, src, sname, t0, N)
                evac(gst[0:8, 0:N], gstb, pv, pn_, eng="act")
                if base + t0 < 2048:
                    sdma(gscr[:, base + t0:base + t0 + N], gst[0:8, 0:N], [gstb], ["gscr"])
                else:
                    sdma(gscs[:, :], gst[0:8, 0:N], [gstb], ["gscs"])
                pv, pn_ = proj_fm(wl, wln, 0, 16, src, sname, t0, N)
                evac(galT[0:16, base + t0:base + t0 + N], galb, pv, pn_, eng="dve")
        GI, GF, GB_, GA, GW, GFL, GT1, GT2 = range(8)

        def gt(i):
            return gat[:, i, :]

        def gs_(i):
            return gas[:, i, :]
        NTOK = PRE + TM - 128
        sdma(gt(GI), gscr[0:4, :].rearrange("h (c l) -> (h c) l", l=128), ["gscr"], ["g_i"])
        sdma(gt(GF), gscr[4:8, :].rearrange("h (c l) -> (h c) l", l=128), ["gscr"], ["g_f"])
        sdma(gs_(GI), gscs[0:4, :].rearrange("h (j l) -> (h j) l", l=8), ["gscs"], ["s_i"])
        sdma(gs_(GF), gscs[4:8, :].rearrange("h (j l) -> (h j) l", l=8), ["gscs"], ["s_f"])
        negfb = stat[0:64, 48:49]
        dve(lambda e: e.tensor_scalar(out=negfb, in0=fb64, scalar1=-1.0, scalar2=None, op0=ALU.mult),
            ["spm"], ["negfb"])

        def gate_math(T, pfx, L):
            i_, f_, b_, a_ = T(GI), T(GF), T(GB_), T(GA)
            act(lambda e: e.activation(out=f_, in_=f_, func=AF.Exp, bias=negfb, scale=-1.0),
                [pfx + "f", "negfb"], [pfx + "f"])
            act(lambda e: e.activation(out=f_, in_=f_, func=AF.Ln, bias=1.0), [pfx + "f"], [pfx + "f"])
            dve(lambda e: e.tensor_scalar(out=f_, in0=f_, scalar1=-1.0, scalar2=None, op0=ALU.mult),
                [pfx + "f"], [pfx + "f"])
            dve(lambda e: e.tensor_tensor_scan(out=b_, data0=onesf[0:64, 0:L], data1=f_, initial=0.0,
                                               op0=ALU.mult, op1=ALU.add), [pfx + "f", "cf"], [pfx + "b"])
            dve(lambda e: e.scalar_tensor_tensor(out=a_, in0=i_, scalar=ib64, in1=b_, op0=ALU.add,
                                                 op1=ALU.subtract), [pfx + "i", pfx + "b", "spm"], [pfx + "a"])
        gate_math(gt, "g_", 128)
        gate_math(gs_, "s_", 8)
        amax = stat[0:64, 49:50]
        bend = gat[:, GB_, 127:128]
        amaxb = stat[0:64, 50:51]
        dve(lambda e: e.reduce_max(out=amax, in_=gt(GA), axis=AX.X), ["g_a"], ["amax"])
        dve(lambda e: e.tensor_tensor(out=amaxb, in0=amax, in1=bend, op=ALU.add), ["amax", "g_b"], ["amaxb"])
        tr(pmisc[0:1, 0:64], "pmisc", bend, "g_b", identf[0:64, 0:64], inc=False)
        tr(pmisc[0:1, 64:128], "pmisc", amaxb, "amaxb", identf[0:64, 0:64])
        R_BE, R_AB, R_MN, R_MP, R_G, R_DEC = range(6)

        def gr(i):
            return grow[0:1, i, :]
        evac(grow[0:1, 0:2, :], "grow", pmisc[0:1, 0:128].rearrange("p (a b) -> p a b", a=2), "pmisc", eng="dve")
        for h in range(4):
            for seg in range(2):
                lo = h * 16 + seg * 8
                if seg == 0:
                    init = 0.0
                    rd = ["grow"]
                else:
                    mf = stat[0:1, 51 + h:52 + h]
                    dve(lambda e, h=h, mf=mf: e.tensor_tensor(out=mf, in0=grow[0:1, R_MN, h * 16 + 7:h * 16 + 8],
                                                             in1=flagt[0:1, 0:1], op=ALU.mult),
                        ["grow", "flagt"], ["mf%d" % h])
                    init = mf
                    rd = ["grow", "mf%d" % h]
                dve(lambda e, lo=lo, init=init: e.tensor_tensor_scan(
                    out=grow[0:1, R_MN, lo:lo + 8], data0=grow[0:1, R_BE, lo:lo + 8],
                    data1=grow[0:1, R_AB, lo:lo + 8], initial=init, op0=ALU.add, op1=ALU.max), rd, ["grow"])
                if seg == 0:
                    dve(lambda e, lo=lo: e.memset(grow[0:1, R_MP, lo:lo + 1], 0.0), [], ["grow"])
                else:
                    dve(lambda e, lo=lo, mf=mf: e.tensor_copy(out=grow[0:1, R_MP, lo:lo + 1], in_=mf),
                        ["mf%d" % h], ["grow"])
                dve(lambda e, lo=lo: e.tensor_copy(out=grow[0:1, R_MP, lo + 1:lo + 8],
                                                   in_=grow[0:1, R_MN, lo:lo + 7]), ["grow"], ["grow"])
        dve(lambda e: e.tensor_tensor(out=gr(R_G), in0=gr(R_MN), in1=gr(R_BE), op=ALU.subtract), ["grow"], ["grow"])
        dve(lambda e: e.tensor_tensor(out=gr(R_DEC), in0=gr(R_MP), in1=gr(R_G), op=ALU.subtract), ["grow"], ["grow"])
        act(lambda e: e.activation(out=gr(R_DEC), in_=gr(R_DEC), func=AF.Exp), ["grow"], ["grow"])
        dve(lambda e: e.tensor_scalar(out=gr(R_G), in0=gr(R_G), scalar1=-1.0, scalar2=None, op0=ALU.mult),
            ["grow"], ["grow"])
        mm(pmisc[:, 128:192], "pmisc", [(onesf[0:1, 0:128], gr(R_DEC))], ["grow", "cf"])
        evac(decbc[:], "decbc", pmisc[:, 128:192], "pmisc", eng="dve")
        mm(pmisc[0:64, 192:193], "pmisc", [(gr(R_G), onesf[0:1, 0:1])], ["grow", "cf"])
        negG = stat[0:64, 56:57]
        evac(negG, "negG", pmisc[0:64, 192:193], "pmisc", eng="dve")
        sdma(o_pm[:, :], gr(R_MN), ["grow"], [])
        act(lambda e: e.activation(out=gt(GW), in_=gt(GA), func=AF.Exp, bias=negG), ["g_a", "negG"], ["g_w"])
        act(lambda e: e.activation(out=gt(GFL), in_=gt(GB_), func=AF.Exp, bias=negG, scale=-1.0),
            ["g_b", "negG"], ["g_fl"])
        tr(pmisc[:, 256:320], "pmisc", gt(GW), "g_w", identf[0:64, 0:64], inc=False)
        tr(pmisc[:, 320:384], "pmisc", gt(GFL), "g_fl", identf[0:64, 0:64])
        evac(wtok[:], "wtok", pmisc[:, 256:320], "pmisc", eng="dve")
        evac(ftok[:], "ftok", pmisc[:, 320:384], "pmisc", eng="dve")
        smc = stat[0:64, 57:58]
        sdma(smc, smcol[:, :], [], ["smc"])
        samax = stat[0:64, 58:59]
        sG = stat[0:64, 59:60]
        snegG = stat[0:64, 60:61]
        sdec = stat[0:64, 61:62]
        smn = stat[0:64, 62:63]
        dve(lambda e: e.reduce_max(out=samax, in_=gs_(GA), axis=AX.X), ["s_a"], ["samax"])
        dve(lambda e: e.tensor_tensor(out=sG, in0=samax, in1=smc, op=ALU.max), ["samax", "smc"], ["sG"])
        dve(lambda e: e.tensor_tensor(out=smn, in0=sG, in1=gas[:, GB_, 7:8], op=ALU.add), ["sG", "s_b"], ["smn"])
        sdma(o_sm[:, :], smn, ["smn"], [])
        dve(lambda e: e.tensor_tensor(out=sdec, in0=smc, in1=sG, op=ALU.subtract), ["smc", "sG"], ["sdec"])
        act(lambda e: e.activation(out=sdec, in_=sdec, func=AF.Exp), ["sdec"], ["sdec"])
        dve(lambda e: e.tensor_scalar(out=snegG, in0=sG, scalar1=-1.0, scalar2=None, op0=ALU.mult), ["sG"], ["snegG"])
        act(lambda e: e.activation(out=gs_(GW), in_=gs_(GA), func=AF.Exp, bias=snegG), ["s_a", "snegG"], ["s_w"])
        act(lambda e: e.activation(out=gs_(GFL), in_=gs_(GB_), func=AF.Exp, bias=snegG, scale=-1.0),
            ["s_b", "snegG"], ["s_fl"])
        dg = gat[:, GT1, 0:64]
        dve(lambda e: e.tensor_scalar(out=dg, in0=identf[0:64, 0:64], scalar1=sdec, scalar2=None, op0=ALU.mult),
            ["sdec", "cf"], ["dg"])
        mm(pmisc[:, 384:448], "pmisc", [(onesf[0:64, 0:128], dg)], ["dg", "cf"])
        evac(sdecbc[:], "sdecbc", pmisc[:, 384:448], "pmisc", eng="dve")
        sdma(gscs[0:4, :].rearrange("h (j l) -> (h j) l", l=8), gs_(GW), ["s_w"], ["gscs"])
        sdma(gscs[4:8, :].rearrange("h (j l) -> (h j) l", l=8), gs_(GFL), ["s_fl"], ["gscs"])
        wfr = gat[0:8, GT2, :]
        sdma(wfr, gscs[0:8, :], ["gscs"], ["wfr"])
        tr(pmisc[:, 448:456], "pmisc", wfr, "wfr", identf[0:8, 0:8])
        evac(wtoks[:], "wtoks", pmisc[:, 448:452], "pmisc", eng="dve")
        evac(ftoks[:], "ftoks", pmisc[:, 452:456], "pmisc", eng="dve")

        if stop_after == "gates":
            if debug.get("gates"):
                for nm, t_, shp in (("wtok", wtok, [128, 64]), ("ftok", ftok, [128, 64]), ("decbc", decbc, [128, 64]),
                                    ("wtoks", wtoks, [128, 4]), ("sdecbc", sdecbc, [128, 64])):
                    dbg_out[nm] = dout("dbg_" + nm, shp)
                    sdma(dbg_out[nm][:, :], t_[:], [nm], [])
            return finish(nc, S, dbg_out)


        dve(lambda e: e.memset(Vt[:, :, 256:257], 1.0), [], [Vtb])
        dve(lambda e: e.memset(Vp[:, :, 256:257], 1.0), [], [Vpb])
        sdma(nrow[0:64, :], sn[:, :], [], [nrowb])
        for dh in range(2):
            tr(pmisc[:, dh * 64:(dh + 1) * 64], "pmisc", nrow[0:64, dh * 128:(dh + 1) * 128], nrowb,
               identf[0:64, 0:64], inc=(dh == 1))
        evac(nT, nTb, pmisc[:, 0:128].rearrange("p (a b) -> p a b", a=2), "pmisc", eng="dve")

        def head_epilogue(num_ap, numb, scale_pre, gate_ap, gateb, tile_i, dstcol0, hnb, ytok, ytokb):
            ss, rs, sc, sbn = statslot()
            if scale_pre is None:
                act(lambda e: e.activation(out=junk, in_=num_ap, func=AF.Square, accum_out=ss), [numb],
                    [junkb, sbn])
            else:
                act(lambda e: e.activation(out=junk, in_=num_ap, func=AF.Square, scale=scale_pre, accum_out=ss),
                    [numb, hnb], [junkb, sbn])
            rstd_from_ss(ss, rs, sbn, 256)
            if scale_pre is not None:
                dve(lambda e: e.tensor_tensor(out=rs, in0=rs, in1=scale_pre, op=ALU.mult), [sbn + "r", hnb],
                    [sbn + "r"])
            dve(lambda e: e.scalar_tensor_tensor(out=ytok, in0=num_ap, scalar=rs, in1=gate_ap, op0=ALU.mult,
                                                 op1=ALU.mult), [numb, sbn + "r", gateb], [ytokb])
            sdma(yscr[tile_i * 128:(tile_i + 1) * 128, dstcol0:dstcol0 + 256], ytok, [ytokb], ["yscr"])

        def mlstm_proj(h):
            wq, wqn = load_w(w_in, WOFF[C_MQ + h * 256], KD, 256)
            wk, wkn = load_w(w_in, WOFF[C_MK + h * 256], KD, 256)
            wv, wvn = load_w(w_in, WOFF[C_MV + h * 256], KD, 256)
            for dh in range(2):
                for (t0, N) in MG:
                    pv, pn_ = proj_fm(wq, wqn, dh * 128, 128, hT, "hT", t0, N)
                    evac(QT[:, dh, t0:t0 + N], QTb, pv, pn_)
            for dh in range(2):
                for (t0, N) in MG:
                    pv, pn_ = proj_fm(wk, wkn, dh * 128, 128, hT, "hT", t0, N)
                    evac(KT[:, dh, t0:t0 + N], KTb, pv, pn_, scale=0.0625)
                for (t0, N) in PG:
                    pv, pn_ = proj_fm(wk, wkn, dh * 128, 128, hTp, "hTp", t0, N)
                    evac(KTp[:, dh, t0:t0 + N], KTpb, pv, pn_, scale=0.0625)
            wo, won = load_w(w_in, WOFF[C_MO + h * 256], KD, 256)
            for ti in range(NPT):
                pv, pn_ = proj_tm(wv, wvn, 0, 256, hTp, "hTp", ti)
                evac(Vp[:, ti, 0:256], Vpb, pv, pn_)
            for ti in range(NMT):
                pv, pn_ = proj_tm(wv, wvn, 0, 256, hT, "hT", ti)
                evac(Vt[:, ti, 0:256], Vtb, pv, pn_)
            for ti in range(NMT):
                pv, pn_ = proj_tm(wo, won, 0, 256, hT, "hT", ti)
                evac(OG[:, ti, :], OGb, pv, pn_, func=AF.Sigmoid)
        csi = [0]

        def mlstm_chunks(h):
            Ch = Cst[:, h % 2, :, :]
            Chb = "Cst%d" % (h % 2)
            dve(lambda e: e.memset(Ch, 0.0), [], [Chb])
            def mchunk(c):
                full = c >= NPT
                samp = c == NCH
                if c < NPT:
                    ktsrc, ktb, vsrc, vb_, ti = KTp, KTpb, Vp, Vpb, c
                else:
                    ktsrc, ktb, vsrc, vb_, ti = KT, KTb, Vt, Vtb, c - NPT
                tk = slice(ti * 128, (ti + 1) * 128)
                (Ktok, Ktokb), (Vw, Vwb), (PT, PTb), (ytok, ytokb) = Ktok2[c % 2], Vw2[c % 2], PT2[c % 2], ytok2[c % 2]
                if samp:
                    wcol, fcol = wtoks[:, h:h + 1], ftoks[:, h:h + 1]
                    wcb, fcb = "wtoks", "ftoks"
                else:
                    wcol, fcol = wtok[:, h * 16 + c:h * 16 + c + 1], ftok[:, h * 16 + c:h * 16 + c + 1]
                    wcb, fcb = "wtok", "ftok"
                for dh in range(2):
                    tr(ptr[0][:, dh * 128:(dh + 1) * 128], "ptr0", ktsrc[:, dh, tk], ktb, identb, inc=(dh == 1))
                evac(Ktok, Ktokb, ptr[0][:, 0:256], "ptr0")
                dve(lambda e, vsrc=vsrc, ti=ti, wcol=wcol: e.tensor_scalar(out=Vw, in0=vsrc[:, ti, :], scalar1=wcol,
                                                                            scalar2=None, op0=ALU.mult),
                     [vb_, wcb], [Vwb])
                if not samp:
                    dcol = decbc[:, h * 16 + c:h * 16 + c + 1]
                    dve(lambda e, dcol=dcol: e.tensor_scalar(out=Ch, in0=Ch, scalar1=dcol, scalar2=None,
                                                             op0=ALU.mult), [Chb, "decbc"], [Chb])
                if full:
                    mm(pst[:, 0:128], "pst", [(ktsrc[:, dh, tk], QT[:, dh, tk]) for dh in range(2)], [ktb, QTb])
                    msk = masks if samp else maskc
                    dve(lambda e, msk=msk: e.tensor_tensor(out=PT, in0=pst[:, 0:128], in1=msk, op=ALU.mult),
                        ["pst", "cb"], [PTb])
                    if not samp:
                        act(lambda e: e.copy(out=Cbf, in_=Ch), [Chb], [Cbfb])
                        mm(pnum[:, 0:257], "pnum",
                           [(PT, Vw)] + [(QT[:, dh, tk], Cbf[:, dh, :]) for dh in range(2)],
                           [PTb, Vwb, QTb, Cbfb])
                    else:
                        mm(pnum[:, 0:257], "pnum", [(PT, Vw)], [PTb, Vwb], first=True, last=False)
                if not samp:
                    for dh in range(2):
                        mm(pstate[:, 0:257], "pstate", [(Ktok[:, dh * 128:(dh + 1) * 128], Vw)], [Ktokb, Vwb])
                        dve(lambda e, dh=dh: e.tensor_tensor(out=Ch[:, dh, :], in0=Ch[:, dh, :],
                                                             in1=pstate[:, 0:257], op=ALU.add),
                            ["pstate", Chb], [Chb])
                else:
                    def sgrp(g, Cs, Csb, Csbf, Csbfb, QX, QXb, VwX, VwXb):
                        js = slice(g * SG, (g + 1) * SG)
                        for dh in range(2):
                            sdma(Cs[:, :, dh, 0:256],
                                 sC[js, h, dh * 128:(dh + 1) * 128, :].rearrange("j p e -> p j e"), [], [Csb])
                        ncols = nT[:, :, g * SG * 4 + h:(g + 1) * SG * 4:4].rearrange("p dh j -> p j dh")
                        dve(lambda e, ncols=ncols: e.tensor_copy(out=Cs[:, :, :, 256], in_=ncols), [nTb], [Csb])
                        dcols = sdecbc[:, h * 16 + g * SG:h * 16 + (g + 1) * SG]
                        Csv = Cs.rearrange("p j dh e -> p j (dh e)")
                        dve(lambda e, dcols=dcols, Csv=Csv: e.tensor_tensor(
                            out=Csv, in0=Csv, in1=dcols.unsqueeze(2).broadcast_to([128, SG, 514]), op=ALU.mult),
                            [Csb, "sdecbc"], [Csb])
                        act(lambda e: e.copy(out=Csbf, in_=Cs), [Csb], [Csbfb])
                        for dh in range(2):
                            dve(lambda e, dh=dh, js=js, tk=tk: e.tensor_tensor(
                                out=QX[:, dh, :, :], in0=QT[:, dh, tk].unsqueeze(1).broadcast_to([128, SG, 128]),
                                in1=qxmask[:, js, :], op=ALU.mult), [QTb, "cb"], [QXb])
                        pairs = [(QX[:, dh, j, :], Csbf[:, j, dh, :]) for j in range(SG) for dh in range(2)]
                        mm(pnum[:, 0:257], "pnum", pairs, [QXb, Csbfb], first=False, last=(g == 16 // SG - 1))
                        dve(lambda e, js=js: e.tensor_tensor(
                            out=VwX, in0=Vw.unsqueeze(1).broadcast_to([128, SG, 257]),
                            in1=vxmask[:, js].unsqueeze(2).broadcast_to([128, SG, 257]), op=ALU.mult),
                            [Vwb, "cb"], [VwXb])
                        for j in range(SG):
                            for dh in range(2):
                                mm(pstate[:, 0:257], "pstate", [(Ktok[:, dh * 128:(dh + 1) * 128], VwX[:, j, :])],
                                   [Ktokb, VwXb])
                                dve(lambda e, j=j, dh=dh: e.tensor_tensor(out=Cs[:, j, dh, :], in0=Cs[:, j, dh, :],
                                                                          in1=pstate[:, 0:257], op=ALU.add),
                                    ["pstate", Csb], [Csb])
                        for dh in range(2):
                            sdma(o_sC[js, h, dh * 128:(dh + 1) * 128, :].rearrange("j p e -> p j e"),
                                 Cs[:, :, dh, 0:256], [Csb], [])
                        ndst = nTo[:, :, g * SG * 4 + h:(g + 1) * SG * 4:4].rearrange("p dh j -> p j dh")
                        dve(lambda e, ndst=ndst: e.tensor_copy(out=ndst, in_=Cs[:, :, :, 256]), [Csb], [nTob])
                    for g in range(16 // SG):
                        k_ = csi[0]
                        csi[0] += 1
                        sgrp(g, *Cs3[k_ % 3], *Csbf2[k_ % 2], *QX2[k_ % 2], *VwX2[k_ % 2])
                if full:
                    ss, rs, sc, sbn = statslot()
                    dve(lambda e, sc=sc: e.tensor_copy(out=sc, in_=pnum[:, 256:257]), ["pnum"], [sbn + "s"])
                    dve(lambda e, sc=sc: e.scalar_tensor_tensor(out=sc, in0=sc, scalar=-1.0, in1=sc, op0=ALU.mult,
                                                                op1=ALU.max), [sbn + "s"], [sbn + "s"])
                    dve(lambda e, fcol=fcol, sc=sc: e.tensor_tensor(out=sc, in0=sc, in1=fcol, op=ALU.max),
                        [sbn + "s", fcb], [sbn + "s"])
                    dve(lambda e, sc=sc: e.reciprocal(out=sc, in_=sc), [sbn + "s"], [sbn + "s"])
                    head_epilogue(pnum[:, 0:256], "pnum", sc, OG[:, ti, :], OGb, ti, h * 256, sbn + "s", ytok, ytokb)
                if c == NCH - 1:
                    sdma(o_pC[h].rearrange("(dh p) e -> p dh e", p=128), Ch[:, :, 0:256], [Chb], [])
            for c in range(NCH + 1):
                mchunk(c)
            tr(pmisc[0:2, 0:128], "pmisc", Ch[:, :, 256], Chb, identf)
            evac(nrow[0:2, 0:128], nrowb, pmisc[0:2, 0:128], "pmisc", eng="dve")
            sdma(o_pn[h * 2:h * 2 + 2, :], nrow[0:2, 0:128], [nrowb], [])

        aupt = stat
        sdma(aups[0:16, :], aup[:, :], [], [aupb])
        dve(lambda e: e.tensor_scalar(out=negab[:, 0:4], in0=spm[:, P_AB:P_AB + 4], scalar1=-1.0, scalar2=None,
                                      op0=ALU.mult), ["spm"], ["negab"])
        BG = [(0, 512), (512, 512), (1024, 512), (1536, 512), (2048, 128)]

        def gla_head(h):
            wq, wqn = load_w(w_in, WOFF[C_GQ + h * 128], KD, 128)
            wk, wkn = load_w(w_in, WOFF[C_GK + h * 128], KD, 128)
            wv, wvn = load_w(w_in, WOFF[C_GV + h * 256], KD, 256)
            QTg, KTg, KTpg = QTg_, KTg_, KTp[:, 0, :]
            QTb, KTb, Vt, Vtb, OG, OGb = QTgb, KTgb, Vtg, Vtgb, RG, RGb
            for (t0, N) in MG:
                pv, pn_ = proj_fm(wq, wqn, 0, 128, hT, "hT", t0, N)
                evac(QTg[:, t0:t0 + N], QTb, pv, pn_, scale=128.0 ** -0.5)
            for (t0, N) in MG:
                pv, pn_ = proj_fm(wk, wkn, 0, 128, hT, "hT", t0, N)
                evac(KTg[:, t0:t0 + N], KTb, pv, pn_)
            for (t0, N) in PG:
                pv, pn_ = proj_fm(wk, wkn, 0, 128, hTp, "hTp", t0, N)
                evac(KTpg[:, t0:t0 + N], KTpb, pv, pn_)
            wr, wrn = load_w(w_in, WOFF[C_GR + h * 256], KD, 256)
            for ti in range(NPT):
                pv, pn_ = proj_tm(wv, wvn, 0, 256, hTp, "hTp", ti)
                evac(Vp[:, ti, 0:256], Vpb, pv, pn_)
            for ti in range(NMT):
                pv, pn_ = proj_tm(wv, wvn, 0, 256, hT, "hT", ti)
                evac(Vt[:, ti, 0:256], Vtb, pv, pn_)
            for ti in range(NMT):
                pv, pn_ = proj_tm(wr, wrn, 0, 256, hT, "hT", ti)
                evac(OG[:, ti, :], OGb, pv, pn_, func=AF.Silu)
            nab = negab[:, h:h + 1]
            for (t0, N) in BG:
                mm(pmisc[:, 0:N], "pmisc", [(aups[0:16, h * 128:(h + 1) * 128], galT[0:16, t0:t0 + N])],
                   [aupb, galb])
                act(lambda e, t0=t0, N=N: e.activation(out=BT[:, t0:t0 + N], in_=pmisc[:, 0:N], func=AF.Exp,
                                                       bias=nab, scale=-1.0), ["pmisc", "negab"], [BTb])
            act(lambda e: e.activation(out=BT, in_=BT, func=AF.Ln, bias=1.0), [BTb], [BTb])
            dve(lambda e: e.tensor_scalar(out=BT, in0=BT, scalar1=-1.0 / 16.0, scalar2=None, op0=ALU.mult),
                [BTb], [BTb])
            dve(lambda e: e.tensor_tensor_scan(out=BT[:, 0:2048], data0=onesf[:, 0:1].broadcast_to([128, 2048]),
                                               data1=BT[:, 0:2048], initial=0.0, op0=ALU.mult, op1=ALU.add),
                [BTb, "cf"], [BTb])
            dve(lambda e: e.tensor_tensor_scan(out=BT[:, 2048:2176], data0=resetm, data1=BT[:, 2048:2176],
                                               initial=0.0, op0=ALU.mult, op1=ALU.add), [BTb, "cf"], [BTb])
            Sh = Sst[:, h % 2, :]
            Shb = "Sst%d" % (h % 2)
            dve(lambda e: e.memset(Sh, 0.0), [], [Shb])

            def gchunk(c):
                full = c >= NPT
                samp = c == NCH
                if c < NPT:
                    ktsrc, ktb, vsrc, vb_, ti = KTpg, KTpb, Vp, Vpb, c
                else:
                    ktsrc, ktb, vsrc, vb_, ti = KTg, KTb, Vt, Vtb, c - NPT
                tk = slice(ti * 128, (ti + 1) * 128)
                tb = slice(2048, 2176) if samp else slice(c * 128, (c + 1) * 128)
                (eqt, eqtb), (QtT, QtTb), (KtT, KtTb), (KhT, KhTb) = eqt2[c % 2], QtT2[c % 2], KtT2[c % 2], KhT2[c % 2]
                (Ktok, Ktokb), (PT, PTb), (ytok, ytokb) = Khtok2[c % 2], PT2[c % 2], ytok2[c % 2]
                ss, rs, sc, sbn = statslot()
                if samp:
                    bend3 = BT[:, 2048 + 7:2176:8].unsqueeze(2).broadcast_to([128, 16, 8])
                    dve(lambda e: e.tensor_tensor(out=eqt.rearrange("p (j l) -> p j l", l=8), in0=bend3,
                                                  in1=BT[:, tb].rearrange("p (j l) -> p j l", l=8),
                                                  op=ALU.subtract), [BTb], [eqtb])
                    act(lambda e: e.activation(out=eqt, in_=eqt, func=AF.Exp), [eqtb], [eqtb])
                    act(lambda e: e.activation(out=dSs, in_=BT[:, 2048 + 7:2176:8], func=AF.Exp), [BTb], [dSsb])
                else:
                    bendc = BT[:, c * 128 + 127:c * 128 + 128]
                    bstc = zerocol if c == 0 else BT[:, c * 128 - 1:c * 128]
                    act(lambda e: e.activation(out=eqt, in_=BT[:, tb], func=AF.Exp, bias=bendc, scale=-1.0),
                        [BTb], [eqtb])
                    dve(lambda e: e.tensor_tensor(out=ss, in0=bendc, in1=bstc, op=ALU.subtract), [BTb, "cf"],
                        [sbn])
                    act(lambda e: e.activation(out=ss, in_=ss, func=AF.Exp), [sbn], [sbn])
                    dve(lambda e: e.tensor_scalar(out=rs, in0=bstc, scalar1=-1.0, scalar2=None, op0=ALU.mult),
                        [BTb, "cf"], [sbn + "r"])
                dve(lambda e: e.tensor_tensor(out=KhT, in0=ktsrc[:, tk], in1=eqt, op=ALU.mult), [ktb, eqtb], [KhTb])
                tr(ptr[0][:, 0:128], "ptr0", KhT, KhTb, identb)
                evac(Ktok[:, 0:128], Ktokb, ptr[0][:, 0:128], "ptr0")
                if full:
                    if samp:
                        act(lambda e: e.activation(out=eqt, in_=BT[:, tb], func=AF.Exp), [BTb, KhTb], [eqtb])
                    else:
                        act(lambda e: e.activation(out=eqt, in_=BT[:, tb], func=AF.Exp, bias=rs), [BTb, sbn + "r", KhTb],
                            [eqtb])
                    dve(lambda e: e.tensor_tensor(out=QtT, in0=QTg[:, tk], in1=eqt, op=ALU.mult), [QTb, eqtb], [QtTb])
                    if samp:
                        act(lambda e: e.activation(out=eqt, in_=BT[:, tb], func=AF.Exp, scale=-1.0), [BTb, QtTb],
                            [eqtb])
                    else:
                        act(lambda e: e.activation(out=eqt, in_=BT[:, tb], func=AF.Exp, bias=bstc, scale=-1.0),
                            [BTb, QtTb, "cf"], [eqtb])
                    dve(lambda e: e.tensor_tensor(out=KtT, in0=ktsrc[:, tk], in1=eqt, op=ALU.mult), [ktb, eqtb],
                        [KtTb])
                    mm(pst[:, 0:128], "pst", [(KtT, QtT)], [KtTb, QtTb])
                    msk = masks if samp else maskc
                    dve(lambda e: e.tensor_tensor(out=PT, in0=pst[:, 0:128], in1=msk, op=ALU.mult), ["pst", "cb"],
                        [PTb])
                    if not samp:
                        mm(pnum[:, 0:256], "pnum", [(PT, vsrc[:, ti, 0:256]), (QtT, Sbf)], [PTb, vb_, QtTb, Sbfb])
                    else:
                        mm(pnum[:, 0:256], "pnum", [(PT, vsrc[:, ti, 0:256])], [PTb, vb_], first=True, last=False)
                if not samp:
                    mm(pstate[:, 0:256], "pstate", [(Ktok[:, 0:128], vsrc[:, ti, 0:256])], [Ktokb, vb_])
                    dve(lambda e: e.scalar_tensor_tensor(out=Sh, in0=Sh, scalar=ss, in1=pstate[:, 0:256],
                                                         op0=ALU.mult, op1=ALU.add), [Shb, sbn, "pstate"], [Shb])
                    act(lambda e: e.copy(out=Sbf, in_=Sh), [Shb], [Sbfb])
                else:
                    def sgrp(g, Cs, Csb, Csbf, Csbfb, QX, QXb, VwX, VwXb):
                        js = slice(g * SG, (g + 1) * SG)
                        Ss = Cs[:, :, 0, 0:256]
                        Ssb = Csbf[:, :, 0, 0:256]
                        sdma(Ss, sS[js, h].rearrange("j p e -> p j e"), [], [Csb])
                        act(lambda e, Ss=Ss, Ssb=Ssb: e.copy(out=Ssb, in_=Ss), [Csb], [Csbfb])
                        dve(lambda e, js=js: e.tensor_tensor(
                            out=QX[:, 0, :, :], in0=QtT.unsqueeze(1).broadcast_to([128, SG, 128]),
                            in1=qxmask[:, js, :], op=ALU.mult), [QtTb, "cb"], [QXb])
                        mm(pnum[:, 0:256], "pnum", [(QX[:, 0, j, :], Ssb[:, j, :]) for j in range(SG)],
                           [QXb, Csbfb], first=False, last=(g == 16 // SG - 1))
                        dve(lambda e, js=js: e.tensor_tensor(
                            out=VwX[:, :, 0:256], in0=vsrc[:, ti, 0:256].unsqueeze(1).broadcast_to([128, SG, 256]),
                            in1=vxmask[:, js].unsqueeze(2).broadcast_to([128, SG, 256]), op=ALU.mult),
                            [vb_, "cb"], [VwXb])
                        for j in range(SG):
                            mm(pstate[:, 0:256], "pstate", [(Ktok[:, 0:128], VwX[:, j, 0:256])], [Ktokb, VwXb])
                            dcol = dSs[:, g * SG + j:g * SG + j + 1]
                            dve(lambda e, j=j, dcol=dcol, Ss=Ss: e.scalar_tensor_tensor(
                                out=Ss[:, j, :], in0=Ss[:, j, :], scalar=dcol, in1=pstate[:, 0:256], op0=ALU.mult,
                                op1=ALU.add), [Csb, dSsb, "pstate"], [Csb])
                        sdma(o_sS[js, h].rearrange("j p e -> p j e"), Ss, [Csb], [])
                    for g in range(16 // SG):
                        k_ = csi[0]
                        csi[0] += 1
                        sgrp(g, *Cs3[k_ % 3], *Csbf2[k_ % 2], *QX2[k_ % 2], *VwX2[k_ % 2])
                if full:
                    head_epilogue(pnum[:, 0:256], "pnum", None, OG[:, ti, :], OGb, ti, 1024 + h * 256, None, ytok, ytokb)
                if c == NCH - 1:
                    sdma(o_pS[h], Sh, [Shb], [])
            dve(lambda e: e.memset(Sbf, 0.0), [], [Sbfb])
            for c in range(NCH + 1):
                gchunk(c)

        mlstm_proj(0)
        S.barrier()
        print("arena phase 1a used %d / %d (mark %d)" % (apos[0] * 2, ARENA * 2, gmark * 2))
        apos[0] = gmark
        Cs3 = [carve("Cs%d" % i, [SG, 2, 257], F32) for i in range(3)]
        Csbf2 = [carve("Csbf%d" % i, [SG, 2, 257]) for i in range(2)]
        QX2 = [carve("QX%d" % i, [2, SG, 128]) for i in range(2)]
        VwX2 = [carve("VwX%d" % i, [SG, 257]) for i in range(2)]
        for h in range(4):
            if h > 0:
                mlstm_proj(h)
            mlstm_chunks(h)
            gla_head(h)
        tr(pmisc[:, 0:128], "pmisc", nTo.rearrange("p a b -> p (a b)"), nTob, identf)
        evac(nrow[:, 0:128], nrowb, pmisc[:, 0:128], "pmisc", eng="dve")
        sdma(o_sn[:, :], nrow[:, 0:128], [nrowb], [])
        if stop_after == "gla":
            return finish(nc, S, dbg_out)

        arena_reset()
        wide[0] = False
        yTa = hTp[:].rearrange("p k t -> p (k t)")[:, 0:8 * TM].rearrange("p (k t) -> p k t", k=8)
        yTg, yTgb = carve("yTg", [8, TM])
        mT, mTb = carve("mT", [KD, TM])
        ystg, ystgb = carve("ystg", [D])
        sg, sgb = carve("sg", [2, TM])
        tmpm, tmpmb = carve("tmpm", [512])
        xs_ = [carve("xs%d" % i, [256], F32)[0] for i in range(4)]
        for ti in range(NMT):
            sdma(ystg, yscr[ti * 128:(ti + 1) * 128, :], ["yscr"], [ystgb])
            for half in range(2):
                pt = ptr[half]
                pbn = "ptr%d" % half
                for j in range(8):
                    k = half * 8 + j
                    tr(pt[:, j * 128:(j + 1) * 128], pbn, ystg[:, k * 128:(k + 1) * 128], ystgb, identb, inc=(j == 7))
                for j in range(8):
                    k = half * 8 + j
                    dstt, dstb = (yTa, "hTp") if half == 0 else (yTg, yTgb)
                    dst = dstt[:, j, ti * 128:(ti + 1) * 128]
                    hcol = spm[:, P_HNM + k:P_HNM + k + 1]
                    if j % 2 == 0:
                        act(lambda e, dst=dst, j=j, pt=pt, hcol=hcol: e.activation(
                            out=dst, in_=pt[:, j * 128:(j + 1) * 128], func=AF.Copy, scale=hcol),
                            [pbn, "spm"], [dstb])
                    else:
                        dve(lambda e, dst=dst, j=j, pt=pt, hcol=hcol: e.tensor_scalar(
                            out=dst, in0=pt[:, j * 128:(j + 1) * 128], scalar1=hcol, scalar2=None, op0=ALU.mult),
                            [pbn, "spm"], [dstb])

        def branch_group(cg):
            c0 = cg * 256
            for (gcol, wbr, ysrc, ysb, first) in ((C_GA, w_a, yTa, "hTp", True), (C_GB, w_b, yTg, yTgb, False)):
                wgt, wgtn = load_w(w_in, WOFF[gcol + c0], KD, 256)
                for cb_ in range(2):
                    for (t0, N) in MG:
                        pv, pn_ = proj_fm(wgt, wgtn, cb_ * 128, 128, hT, "hT", t0, N)
                        evac(sg[:, cb_, t0:t0 + N], sgb, pv, pn_, func=AF.Sigmoid)
                wbt, wbtn = load_w(wbr, cg * 8 * 256, 8, 256)
                for cb_ in range(2):
                    kk = cg * 2 + cb_
                    for (t0, N) in MG:
                        bank, bname = nextbank()
                        mm(bank[:, 0:N], bname,
                           [(wbt[:, k, cb_ * 128:(cb_ + 1) * 128], ysrc[:, k, t0:t0 + N]) for k in range(8)],
                           [wbtn, ysb])
                        if first:
                            dve(lambda e, bank=bank, N=N, t0=t0, cb_=cb_, kk=kk: e.tensor_tensor(
                                out=mT[:, kk, t0:t0 + N], in0=bank[:, 0:N], in1=sg[:, cb_, t0:t0 + N], op=ALU.mult),
                                [bname, sgb], [mTb])
                        else:
                            dve(lambda e, bank=bank, N=N, t0=t0, cb_=cb_: e.tensor_tensor(
                                out=tmpm[:, 0:N], in0=bank[:, 0:N], in1=sg[:, cb_, t0:t0 + N], op=ALU.mult),
                                [bname, sgb], [tmpmb])
                            dve(lambda e, N=N, t0=t0, kk=kk: e.tensor_tensor(
                                out=mT[:, kk, t0:t0 + N], in0=mT[:, kk, t0:t0 + N], in1=tmpm[:, 0:N], op=ALU.add),
                                [tmpmb, mTb], [mTb])
        for cg in range(8):
            branch_group(cg)

        def wout_group(cg):
            c0 = cg * 256
            wot, wotn = load_w(w_o, cg * KD * 256, KD, 256)
            for ti in range(NMT):
                xs = xs_[ti % 4]
                xsb = "xsb%d" % (ti % 4)
                sdma(xs[:], xmain[ti * 128:(ti + 1) * 128, c0:c0 + 256], [], [xsb])
                bank, bname = nextbank()
                mm(bank[:, 0:256], bname, [(mT[:, k, ti * 128:(ti + 1) * 128], wot[:, k, :]) for k in range(KD)],
                   [wotn, mTb])
                dve(lambda e, xs=xs, bank=bank: e.tensor_tensor(out=xs[:], in0=xs[:], in1=bank[:, 0:256], op=ALU.add),
                    [xsb, bname], [xsb])
                sdma(x1scr[ti * 128:(ti + 1) * 128, c0:c0 + 256], xs[:], [xsb], ["x1scr"])
        for cg in range(8):
            wout_group(cg)

        arena_reset()
        sdma(wbc[:], nfw.partition_broadcast(128), [], ["wbc"])
        xt2 = [carve("xt2_%d" % i, [D], F32) for i in range(4)]
        xn2 = [carve("xn2_%d" % i, [D], BF16) for i in range(4)]
        sq2, sq2b = carve("sq2", [D], BF16)
        for ti in range(NMT):
            (x_t, xb), (x_n, xnb) = xt2[ti % 4], xn2[ti % 4]
            sdma(x_t, x1scr[ti * 128:(ti + 1) * 128, :], ["x1scr"], [xb])
            norm_to_T(x_t, xb, x_n, xnb, hT, "hT", ti, sq2, sq2b)

        arena_reset()
        HT_ = 640
        actT, actTb = carve("actT", [KF, HT_])
        upad2 = [carve("upad%d" % i, [2 + HT_], F32) for i in range(2)]
        tb2 = [carve("tbuf%d" % i, [HT_], F32) for i in range(2)]
        pb2 = [carve("pbuf%d" % i, [HT_], F32) for i in range(2)]
        gb2 = [carve("gbuf%d" % i, [HT_]) for i in range(2)]
        w4f = [wsl[i // 2][:, (i % 2) * KD * 128:((i % 2) + 1) * KD * 128] for i in range(6)]
        w4 = [v.rearrange("p (k c) -> p k c", k=KD) for v in w4f]
        w4i = [0]

        def load_w4(packed, kf):
            i = w4i[0] % 6
            w4i[0] += 1
            name = "w4_%d" % i
            flat = w4f[i]
            off = kf * KD * 128
            S.dma("pool", lambda e: e.dma_start(out=flat, in_=packed[:, off:off + KD * 128]), writes=[name])
            return w4[i], name
        ucar, ucarb = carve("ucar", [KF, 2], F32)
        ucv, ucvb = carve("ucv", [KF, 34], F32)
        scvT, scvTb = carve("scvT", [KF, 32], F32)
        srow, srowb = carve("srow", [512], F32)
        fst, fstb = carve("fst", [5, 128], F32)
        wdsf = [hTp[:].rearrange("p k t -> p (k t)")[:, i * KF * 128:(i + 1) * KF * 128] for i in range(2)]
        wds = [v.rearrange("p (k c) -> p k c", k=KF) for v in wdsf]
        x2scr = x2scr_
        CW = lambda j, k: spm[:, P_CW + j * KF + k:P_CW + j * KF + k + 1]
        CBc = lambda k: spm[:, P_CB + k:P_CB + k + 1]
        sc32 = sconv.rearrange("j r c -> (j r) c")
        for k4 in range(KF // 4):
            sdma(srow[0:32, :], sc32[:, k4 * 512:(k4 + 1) * 512], [], [srowb])
            for q in range(4):
                tr(pmisc[:, q * 32:(q + 1) * 32], "pmisc", srow[0:32, q * 128:(q + 1) * 128], srowb,
                   identf[0:32, 0:32], inc=(q == 3))
            evac(scvT[:, k4 * 4:(k4 + 1) * 4, :], scvTb, pmisc[:, 0:128].rearrange("p (a b) -> p a b", a=4),
                 "pmisc", eng="dve")
        dve(lambda e: e.memset(ucar, 0.0), [], [ucarb])
        wdi = [0]

        def ffn_block(half, kf):
            g0 = half * HT_
            npr = HT_ if half == 0 else 512
            wu, wun = load_w4(w_up, kf)
            wg_, wgn_ = load_w4(w_gt, kf)
            (upad, upadb), (tb_, tbb), (pb_, pbb), (gb_, gbb) = upad2[kf % 2], tb2[kf % 2], pb2[kf % 2], gb2[kf % 2]
            dve(lambda e: e.tensor_copy(out=upad[:, 0:2], in_=ucar[:, kf, :]), [ucarb], [upadb])
            for gi in range(2):
                t0 = g0 + gi * 320
                pv, pn_ = proj_fm(wu, wun, 0, 128, hT, "hT", t0, 320)
                evac(upad[:, 2 + gi * 320:2 + (gi + 1) * 320], upadb, pv, pn_, eng="act")
                act(lambda e, pv=pv, gi=gi: e.activation(out=tb_[:, gi * 320:(gi + 1) * 320], in_=pv, func=AF.Identity,
                                                         bias=CBc(kf), scale=CW(2, kf)), [pn_, "spm"], [tbb])
                pv, pn_ = proj_fm(wg_, wgn_, 0, 128, hT, "hT", t0, 320)
                evac(gb_[:, gi * 320:(gi + 1) * 320], gbb, pv, pn_, eng="act")
            if half == 0:
                dve(lambda e: e.tensor_copy(out=ucar[:, kf, :], in_=upad[:, HT_:HT_ + 2]), [upadb], [ucarb])
            else:
                dve(lambda e: e.tensor_copy(out=ucv[:, kf, 0:2], in_=upad[:, 512:514]), [upadb], [ucvb])
                u3 = upad[:, 2 + 512:2 + 640].rearrange("p (j l) -> p j l", l=8)
                dve(lambda e, u3=u3: e.tensor_copy(out=ucv[:, kf, 2:34].rearrange("p (j r) -> p j r", r=2),
                                                   in_=u3[:, :, 6:8]), [upadb], [ucvb])
            dve(lambda e: e.scalar_tensor_tensor(out=tb_[:, 0:npr], in0=upad[:, 1:1 + npr], scalar=CW(1, kf),
                                                 in1=tb_[:, 0:npr], op0=ALU.mult, op1=ALU.add),
                [upadb, tbb, "spm"], [tbb])
            dve(lambda e: e.scalar_tensor_tensor(out=tb_[:, 0:npr], in0=upad[:, 0:npr], scalar=CW(0, kf),
                                                 in1=tb_[:, 0:npr], op0=ALU.mult, op1=ALU.add),
                [upadb, tbb, "spm"], [tbb])
            if half == 1:
                t3 = tb_[:, 512:640].rearrange("p (j l) -> p j l", l=8)
                u3 = upad[:, 2 + 512:2 + 640].rearrange("p (j l) -> p j l", l=8)
                s3 = scvT[:, kf, :].rearrange("p (j r) -> p j r", r=2)
                dve(lambda e, t3=t3, u3=u3: e.scalar_tensor_tensor(
                    out=t3[:, :, 1:8], in0=u3[:, :, 0:7], scalar=CW(1, kf), in1=t3[:, :, 1:8], op0=ALU.mult,
                    op1=ALU.add), [upadb, tbb, "spm"], [tbb])
                dve(lambda e, t3=t3, s3=s3: e.scalar_tensor_tensor(
                    out=t3[:, :, 0:1], in0=s3[:, :, 1:2], scalar=CW(1, kf), in1=t3[:, :, 0:1], op0=ALU.mult,
                    op1=ALU.add), [scvTb, tbb, "spm"], [tbb])
                dve(lambda e, t3=t3, u3=u3: e.scalar_tensor_tensor(
                    out=t3[:, :, 2:8], in0=u3[:, :, 0:6], scalar=CW(0, kf), in1=t3[:, :, 2:8], op0=ALU.mult,
                    op1=ALU.add), [upadb, tbb, "spm"], [tbb])
                dve(lambda e, t3=t3, s3=s3: e.scalar_tensor_tensor(
                    out=t3[:, :, 0:2], in0=s3[:, :, 0:2], scalar=CW(0, kf), in1=t3[:, :, 0:2], op0=ALU.mult,
                    op1=ALU.add), [scvTb, tbb, "spm"], [tbb])
            act(lambda e: e.activation(out=pb_, in_=tb_, func=AF.Square, scale=0.044715 ** 0.5), [tbb], [pbb])
            dve(lambda e: e.scalar_tensor_tensor(out=pb_, in0=pb_, scalar=1.0, in1=tb_, op0=ALU.add, op1=ALU.mult),
                [pbb, tbb], [pbb])
            act(lambda e: e.activation(out=pb_, in_=pb_, func=AF.Sigmoid, scale=1.5957691216057308), [pbb], [pbb])
            dve(lambda e: e.tensor_tensor(out=pb_, in0=pb_, in1=tb_, op=ALU.mult), [pbb, tbb], [pbb])
            dve(lambda e: e.tensor_tensor(out=actT[:, kf, :], in0=pb_, in1=gb_, op=ALU.mult), [pbb, gbb], [actTb])

        def down_block(half, cbk):
            g0 = half * HT_
            i = wdi[0] % 2
            wdi[0] += 1
            (tb_, tbb) = tb2[i]
            wd = wds[i]
            wdn = "wds%d" % i
            off = cbk * KF * 128
            wdf = wdsf[i]
            S.dma("pool", lambda e: e.dma_start(out=wdf[:, 0:22 * 128], in_=w_dn[:, off:off + 22 * 128]),
                  writes=[wdn])
            S.dma("pool", lambda e: e.dma_start(out=wdf[:, 22 * 128:44 * 128],
                                                in_=w_dn[:, off + 22 * 128:off + 44 * 128]), writes=[wdn])
            for gi in range(2):
                bank, bname = nextbank()
                mm(bank[:, 0:320], bname, [(wd[:, k, :], actT[:, k, gi * 320:(gi + 1) * 320]) for k in range(KF)],
                   [wdn, actTb])
                evac(tb_[:, gi * 320:(gi + 1) * 320], tbb, bank[:, 0:320], bname, eng="act")
            for tt in range(5):
                dstp = pst[:, tt * 128:(tt + 1) * 128] if tt < 4 else pnum[:, 0:128]
                dstn = "pst" if tt < 4 else "pnum"
                tr(dstp, dstn, tb_[:, tt * 128:(tt + 1) * 128], tbb, identf, inc=(tt >= 3))
            evac(fst[:, 0:4, :], fstb, pst[:, 0:512].rearrange("p (a b) -> p a b", a=4), "pst", eng="dve")
            evac(fst[:, 4, :], fstb, pnum[:, 0:128], "pnum", eng="dve")
            for tt in range(5):
                r0 = g0 + tt * 128
                sdma(x2scr[r0:r0 + 128, cbk * 128:(cbk + 1) * 128], fst[:, tt, :], [fstb], ["x2scr"])

        for half in range(2):
            for kf in range(KF):
                ffn_block(half, kf)
            for cbk in range(KD):
                down_block(half, cbk)
        oconv_s = o_sconv.rearrange("j r c -> (j r) c")
        for k4 in range(KF // 4):
            for q in range(4):
                tr(pmisc[0:34, q * 128:(q + 1) * 128], "pmisc", ucv[:, k4 * 4 + q, :], ucvb, identf, inc=(q == 3))
            evac(srow[0:34, :], srowb, pmisc[0:34, 0:512], "pmisc", eng="dve")
            sdma(o_pconv[:, k4 * 512:(k4 + 1) * 512], srow[0:2, :], [srowb], [])
            sdma(oconv_s[:, k4 * 512:(k4 + 1) * 512], srow[2:34, :], [srowb], [])

        arena_reset()
        sdma(wbc[:], fnw.partition_broadcast(128), [], ["wbc"])
        xa = [carve("xa%d" % i, [D], F32) for i in range(4)]
        xf = [carve("xf%d" % i, [D], F32) for i in range(4)]
        sq3, sq3b = carve("sq3", [D], BF16)
        for ti in range(1, NMT):
            (x_a, xab), (x_f, xfb) = xa[ti % 4], xf[ti % 4]
            sdma(x_a, x1scr[ti * 128:(ti + 1) * 128, :], ["x1scr"], [xab])
            sdma(x_f, x2scr[ti * 128:(ti + 1) * 128, :], ["x2scr"], [xfb])
            dve(lambda e, x_a=x_a, x_f=x_f: e.tensor_tensor(out=x_a, in0=x_a, in1=x_f, op=ALU.add), [xab, xfb], [xab])
            ss, rs, _, sbn = statslot()
            act(lambda e, x_a=x_a, ss=ss: e.activation(out=sq3, in_=x_a, func=AF.Square, accum_out=ss), [xab],
                [sq3b, sbn])
            rstd_from_ss(ss, rs, sbn, D)
            dve(lambda e, x_a=x_a, x_f=x_f, rs=rs: e.scalar_tensor_tensor(out=x_f, in0=x_a, scalar=rs, in1=wbc[:],
                                                                          op0=ALU.mult, op1=ALU.mult),
                [xab, sbn + "r", "wbc"], [xfb])
            sdma(yout[(ti - 1) * 128:ti * 128, :], x_f, [xfb], [])
        return finish(nc, S, dbg_out)


def finish(nc, S, dbg_out):
    S.finish()
    print("sim phase us:", [int(x) for x in S.sim_phase_us], "units", len(S.units))
    with nc.Block() as block:
        @block.sync
        def _(eng):
            S.emit("sp", eng)

        @block.tensor
        def _(eng):
            S.emit("pe", eng)

        @block.scalar
        def _(eng):
            S.emit("act", eng)

        @block.vector
        def _(eng):
            S.emit("dve", eng)

        @block.gpsimd
        def _(eng):
            S.emit("pool", eng)
    return nc


def make_consts():
    import ml_dtypes
    cb = np.zeros((128, CB_W), np.float32)
    cb[:, 0:128] = np.eye(128)
    s = np.arange(128)[:, None]
    t = np.arange(128)[None, :]
    cb[:, 128:256] = (s <= t)
    cb[:, 256:384] = (s <= t) & ((s // 8) == (t // 8))
    j = np.arange(16)[:, None]
    cb[:, 512:512 + 2048] = ((np.arange(128)[None, :] // 8) == j).astype(np.float32).reshape(1, 2048)
    cb[:, 2560:2576] = ((np.arange(128)[:, None] // 8) == np.arange(16)[None, :])
    cf = np.zeros((128, CF_W), np.float32)
    cf[:, 0:128] = np.eye(128)
    cf[:, 128:256] = 1.0
    cf[:, 256:384] = (np.arange(128)[None, :] % 8 != 0)
    cf[:, 384:512] = np.where(np.arange(128)[None, :] % 8 == 0, -1e30, 0.0)
    return cb.astype(ml_dtypes.bfloat16), cf


_NC_CACHE = {}


def _prep_inputs(inp):
    f32 = np.float32
    cbc, cfc = make_consts()
    xp = np.asarray(inp["x_prompt"], f32)
    xs = np.asarray(inp["x_sample"], f32)
    sp = np.zeros((128, SP_W), f32)
    ib = np.asarray(inp["mlstm_i_bias"], f32)[0]
    fb = np.asarray(inp["mlstm_f_bias"], f32)[0]
    sp[0:64, 0] = np.repeat(ib, 16)
    sp[0:64, 1] = np.repeat(fb, 16)
    sp[:, 2:6] = np.asarray(inp["gla_alpha_bias"], f32)[0].reshape(4, 128).T
    sp[:, 6:14] = np.asarray(inp["mlstm_head_norm_w"], f32)[0].reshape(8, 128).T
    sp[:, 14:22] = np.asarray(inp["gla_head_norm_w"], f32)[0].reshape(8, 128).T
    cw = np.asarray(inp["ffn_conv_w"], f32)[0]
    for j in range(3):
        sp[:, 22 + j * KF:22 + (j + 1) * KF] = cw[j].reshape(KF, 128).T
    sp[:, 22 + 3 * KF:22 + 4 * KF] = np.asarray(inp["ffn_conv_b"], f32)[0].reshape(KF, 128).T
    shared = {
        "w_in": _pack(np.asarray(inp["w_in"], f32)[0], KD, _win_blocks()),
        "nmw": np.asarray(inp["norm_mix_w"], f32).reshape(1, D),
        "nfw": np.asarray(inp["norm_ffn_w"], f32).reshape(1, D),
        "fnw": np.asarray(inp["final_norm_w"], f32).reshape(1, D),
        "cst_bf": cbc, "cst_f": cfc, "smallp": sp,
        "aup": np.asarray(inp["gla_alpha_up"], f32)[0],
        "w_a": _pack(np.asarray(inp["w_branch_a"], f32)[0], 8, [(c * 256, 256) for c in range(8)]),
        "w_b": _pack(np.asarray(inp["w_branch_b"], f32)[0], 8, [(c * 256, 256) for c in range(8)]),
        "w_o": _pack(np.asarray(inp["w_out"], f32)[0], KD, [(c * 256, 256) for c in range(8)]),
        "w_up": _pack(np.asarray(inp["ffn_w_up"], f32)[0], KD, [(c * 128, 128) for c in range(KF)]),
        "w_gt": _pack(np.asarray(inp["ffn_w_gate"], f32)[0], KD, [(c * 128, 128) for c in range(KF)]),
        "w_dn": _pack(np.asarray(inp["ffn_w_down"], f32)[0], KF, [(c * 128, 128) for c in range(KD)]),
    }
    sC = np.asarray(inp["state_mlstm_C"], f32)[0]
    sn = np.asarray(inp["state_mlstm_n"], f32)[0]
    sm = np.asarray(inp["state_mlstm_m"], f32)[0]
    sS = np.asarray(inp["state_gla_S"], f32)[0]
    scv = np.asarray(inp["state_ffn_conv"], f32)[0]
    maps = []
    for c in range(8):
        s, half = c // 2, c % 2
        xmain = np.zeros((TM, D), f32)
        if half == 1:
            xpre = np.ascontiguousarray(xp[s, 0:PRE])
            xmain[0:1152] = xp[s, PRE:2048]
        else:
            xpre = np.zeros((PRE, D), f32)
            xmain[128:1152] = xp[s, 0:1024]
        xmain[1152:1280] = xs[16 * c:16 * c + 16].reshape(128, D)
        m = dict(shared)
        m.update({
            "xpre": xpre, "xmain": xmain, "flag": np.full((1, 1), float(half), f32),
            "sC": np.ascontiguousarray(sC[16 * c:16 * c + 16]),
            "sn": np.ascontiguousarray(sn[16 * c:16 * c + 16].reshape(64, 256)),
            "smcol": np.ascontiguousarray(sm[16 * c:16 * c + 16].T.reshape(64, 1)),
            "sS": np.ascontiguousarray(sS[16 * c:16 * c + 16]),
            "sconv": np.ascontiguousarray(scv[16 * c:16 * c + 16]),
        })
        maps.append(m)
    return maps


def _assemble(results):
    f32 = np.float32
    y_p = np.zeros((4, 2048, D), f32)
    y_s = np.zeros((128, 8, D), f32)
    pC = np.zeros((1, 4, 4, 256, 256), f32)
    pn = np.zeros((1, 4, 4, 256), f32)
    pm = np.zeros((1, 4, 4), f32)
    pS = np.zeros((1, 4, 4, 128, 256), f32)
    pcv = np.zeros((1, 4, 2, DFF), f32)
    sCo = np.zeros((1, 128, 4, 256, 256), f32)
    sno = np.zeros((1, 128, 4, 256), f32)
    smo = np.zeros((1, 128, 4), f32)
    sSo = np.zeros((1, 128, 4, 128, 256), f32)
    scvo = np.zeros((1, 128, 2, DFF), f32)
    for c in range(8):
        r = results[c]
        s, half = c // 2, c % 2
        yo = np.asarray(r["yout"], f32)
        y_p[s, half * 1024:(half + 1) * 1024] = yo[0:1024]
        y_s[16 * c:16 * c + 16] = yo[1024:1152].reshape(16, 8, D)
        if half == 1:
            pC[0, s] = np.asarray(r["o_pC"], f32)
            pn[0, s] = np.asarray(r["o_pn"], f32).reshape(4, 256)
            pm[0, s] = np.asarray(r["o_pm"], f32).reshape(4, 16)[:, 15]
            pS[0, s] = np.asarray(r["o_pS"], f32)
            pcv[0, s] = np.asarray(r["o_pconv"], f32)
        sl = slice(16 * c, 16 * c + 16)
        sCo[0, sl] = np.asarray(r["o_sC"], f32)
        sno[0, sl] = np.asarray(r["o_sn"], f32).reshape(2, 16, 4, 128).transpose(1, 2, 0, 3).reshape(16, 4, 256)
        smo[0, sl] = np.asarray(r["o_sm"], f32).reshape(4, 16).T
        sSo[0, sl] = np.asarray(r["o_sS"], f32)
        scvo[0, sl] = np.asarray(r["o_sconv"], f32)
    return (y_p, y_s, pC, pn, pm, pS, pcv, sCo, sno, smo, sSo, scvo)


def kernel(**inputs):
    if "nc" not in _NC_CACHE:
        _NC_CACHE["nc"] = build()
    nc = _NC_CACHE["nc"]
    maps = _prep_inputs(inputs)
    res = run_bass_kernel_spmd(nc, maps, core_ids=list(range(8)))
    return _assemble(res.results)
```

```python
import numpy as np
from contextlib import ExitStack
import concourse.bass as bass
import concourse.mybir as mybir
from concourse.bass_utils import run_bass_kernel_spmd

F32 = mybir.dt.float32
BF16 = mybir.dt.bfloat16
AF = mybir.ActivationFunctionType
ALU = mybir.AluOpType
AX = mybir.AxisListType

D = 2048
NIN = 11288
DFF = 5632
KD = D // 128
KF = DFF // 128
PRE = 896
NPT = PRE // 128
TM = 1280
CB_W = 2576
CF_W = 520
SP_W = 22 + 4 * KF
NMT = TM // 128
NCH = 16
EPS = 1e-6

C_MQ, C_MK, C_MV, C_MO = 0, 1024, 2048, 3072
C_MI, C_MF = 4096, 4100
C_GQ, C_GK, C_GV, C_GR = 4104, 4616, 5128, 6152
C_GAL = 7176
C_GA, C_GB = 7192, 9240


def _win_blocks():
    blks = [(C_MI, 8), (C_GAL, 16)]
    for hh in range(4):
        blks += [(C_MQ + hh * 256, 256), (C_MK + hh * 256, 256), (C_MV + hh * 256, 256), (C_MO + hh * 256, 256)]
    for hh in range(4):
        blks += [(C_GQ + hh * 128, 128), (C_GK + hh * 128, 128), (C_GV + hh * 256, 256), (C_GR + hh * 256, 256)]
    for cg in range(8):
        blks += [(C_GA + cg * 256, 256), (C_GB + cg * 256, 256)]
    return blks


def _win_offsets():
    off = {}
    o = 0
    for c0, n in _win_blocks():
        off[c0] = o
        o += KD * n
    return off


def _pack(w, nk, blocks):
    outs = []
    for c0, n in blocks:
        outs.append(np.ascontiguousarray(w[:, c0:c0 + n].reshape(nk, 128, n).transpose(1, 0, 2)).reshape(128, nk * n))
    return np.ascontiguousarray(np.concatenate(outs, axis=1))


class _Probe:
    def __init__(self):
        self.calls = []

    def __getattr__(self, name):
        def f(*a, **k):
            self.calls.append((name, a, k))
            return self
        return f

    def then_inc(self, *a, **k):
        return self


def _free_elems(ap):
    n = 1
    for s in ap.shape[1:]:
        n *= int(s)
    return n


def _est(eng, fn):
    p = _Probe()
    fn(p)
    name, a, k = p.calls[0]
    out = k.get("out", a[0] if a else None)
    if eng == "pe":
        if name == "transpose":
            return 0.09
        rhs = k.get("rhs")
        n = _free_elems(rhs)
        f = 4.0 if rhs.dtype == F32 else 1.0
        return max(n, 64) * f / 2400.0 + 0.012
    n = _free_elems(out) if out is not None else 64
    if eng == "act":
        return (n + 200) / 1400.0 + (0.1 if k.get("accum_out") is not None else 0.0)
    if eng in ("dve", "pool"):
        f = 2.0 if name in ("tensor_tensor_scan",) else 1.0
        return max(n, 60) * f / 960.0 + 0.07
    return 0.1


def _dma_est(fn):
    p = _Probe()
    fn(p)
    name, a, k = p.calls[0]
    out = k.get("out")
    nbytes = 1
    for s in out.shape:
        nbytes *= int(s)
    nbytes *= 4 if out.dtype == F32 else 2
    return 2.0 + nbytes / 180e3, nbytes


class Sched:
    ENGS = ("pe", "act", "dve", "pool", "sp")
    XLAT = 0.12

    def __init__(self, esems, dsems):
        self.esem = esems
        self.dsems = dsems
        self.units = []
        self.bufs = {}
        self.phase = 0
        self.trace_phase = None
        self.open = {e: None for e in esems}

    def _st(self, b):
        st = self.bufs.get(b)
        if st is None:
            st = {"w": None, "r": {}}
            self.bufs[b] = st
        return st

    def _deps(self, uid, reads, writes):
        deps = set()
        for b in reads:
            st = self._st(b)
            if st["w"] is not None:
                deps.add(st["w"])
        for b in writes:
            st = self._st(b)
            if st["w"] is not None:
                deps.add(st["w"])
            deps.update(st["r"].keys())
        deps.discard(uid)
        return deps

    def _mark(self, uid, reads, writes):
        for b in reads:
            self._st(b)["r"][uid] = True
        for b in writes:
            st = self._st(b)
            st["w"] = uid
            st["r"] = {}

    def op(self, eng, fn, reads=(), writes=(), inc=True):
        u = self.open[eng]
        if u is None:
            u = {"eng": eng, "fns": [], "deps": set(), "dur": 0.0, "dma": False, "phase": self.phase,
                 "id": len(self.units), "lab": ",".join(writes)}
            self.units.append(u)
            self.open[eng] = u
        u["deps"] |= self._deps(u["id"], reads, writes)
        self._mark(u["id"], reads, writes)
        u["fns"].append(fn)
        u["dur"] += _est(eng, fn)
        if inc:
            self.open[eng] = None

    def dma(self, q, fn, reads=(), writes=()):
        uid = len(self.units)
        lat, nbytes = _dma_est(fn)
        u = {"eng": q, "fns": [fn], "deps": self._deps(uid, reads, writes), "dur": 0.06 if q == "sp" else 0.35,
             "dma": True, "lat": lat, "phase": self.phase, "id": uid, "lab": "dma:" + ",".join(writes) + "<" + ",".join(reads)}
        self.units.append(u)
        self._mark(uid, reads, writes)

    def barrier(self):
        for e, v in self.open.items():
            assert v is None, e
        self.phase += 1
        self.bufs = {}

    def barrier_all_dma(self, q="sp"):
        pass

    def finish(self):
        import heapq
        order = {e: [] for e in self.ENGS}
        nph = self.phase + 1
        by_phase = [[] for _ in range(nph)]
        for u in self.units:
            by_phase[u["phase"]].append(u)
        for ph in range(nph):
            us = by_phase[ph]
            ids = {u["id"] for u in us}
            ndep = {}
            users = {}
            for u in us:
                d = [x for x in u["deps"] if x in ids]
                u["deps"] = set(d)
                ndep[u["id"]] = len(d)
                for x in d:
                    users.setdefault(x, []).append(u)
            byid = {u["id"]: u for u in us}
            ready = {e: [] for e in self.ENGS}
            avail = {}
            for u in us:
                if ndep[u["id"]] == 0:
                    heapq.heappush(ready[u["eng"]], u["id"])
                    avail[u["id"]] = 0.0
            tfree = {e: 0.0 for e in self.ENGS}
            lastu = {}
            done = 0
            fin = {}
            while done < len(us):
                best = None
                for e in self.ENGS:
                    h = ready[e]
                    if not h:
                        continue
                    cand = None
                    tnow = tfree[e]
                    low = [i for i in h if avail[i] <= tnow]
                    if low:
                        cid = min(low)
                        st = tnow
                    else:
                        cid = min(h, key=lambda i: (avail[i], i))
                        st = avail[cid]
                    if best is None or st < best[0] or (st == best[0] and cid < best[2]):
                        best = (st, e, cid)
                st, e, cid = best
                ready[e].remove(cid)
                heapq.heapify(ready[e])
                u = byid[cid]
                u["st"] = st
                if avail[cid] >= tfree[e] - 1e-9 and u["deps"]:
                    u["why"] = max(u["deps"], key=lambda x: fin[x])
                else:
                    u["why"] = lastu.get(e)
                lastu[e] = cid
                end = st + u["dur"]
                tfree[e] = end
                f = end + (u["lat"] if u["dma"] else 0.0)
                fin[cid] = f
                order[e].append(u)
                done += 1
                for v in users.get(cid, ()):
                    ndep[v["id"]] -= 1
                    if ndep[v["id"]] == 0:
                        t = 0.0
                        for x in v["deps"]:
                            lat = 0.0 if (byid[x]["eng"] == v["eng"] and v["eng"] == "pe") else self.XLAT
                            t = max(t, fin[x] + lat)
                        avail[v["id"]] = t
                        heapq.heappush(ready[v["eng"]], v["id"])
            self.sim_phase_us = getattr(self, "sim_phase_us", []) + [max(tfree.values())]
            if getattr(self, "trace_phase", None) == ph:
                cur = max(us, key=lambda u: fin[u["id"]])["id"]
                chain = []
                while cur is not None and len(chain) < 400:
                    u = byid[cur]
                    chain.append((round(u["st"], 2), u["eng"], round(u["dur"], 2), u["lab"], len(u["fns"])))
                    cur = u.get("why")
                for c in chain[:400]:
                    print("   CP", c)
            busy = {e: 0.0 for e in self.ENGS}
            for u in us:
                busy[u["eng"]] += u["dur"]
            print("phase", ph, "sim %.0f us" % max(tfree.values()), {e: int(v) for e, v in busy.items()}, "units", len(us))
        cnt = {e: 0 for e in self.esem}
        dcnt = {q: [0] * len(v) for q, v in self.dsems.items()}
        dnext = {q: 0 for q in self.dsems}
        for e in self.ENGS:
            for u in order[e]:
                if u["dma"]:
                    i = dnext[e]
                    dnext[e] = (i + 1) % len(self.dsems[e])
                    u["prev"] = (self.dsems[e][i], dcnt[e][i]) if dcnt[e][i] > 0 else None
                    dcnt[e][i] += 16
                    u["ev"] = (self.dsems[e][i], dcnt[e][i])
                else:
                    cnt[e] += 1
                    u["ev"] = (self.esem[e], cnt[e])
        self.order = order
        byid = {u["id"]: u for u in self.units}
        self.prog = {e: [] for e in self.ENGS}
        all_dma = [u for u in self.units if u["dma"]]
        for e in self.ENGS:
            wd = {}
            cur_phase = 0
            for u in order[e]:
                waits = []
                if u["phase"] != cur_phase:
                    for e2 in self.esem:
                        if e2 == e and e == "pe":
                            continue
                        c = max([x["ev"][1] for x in order[e2] if x["phase"] < u["phase"] and not x["dma"]] or [0])
                        sem = self.esem[e2]
                        if c > 0 and wd.get(id(sem), 0) < c:
                            wd[id(sem)] = c
                            waits.append((sem, c))
                    for x in all_dma:
                        if x["phase"] < u["phase"] and wd.get(id(x["ev"][0]), 0) < x["ev"][1]:
                            wd[id(x["ev"][0])] = x["ev"][1]
                            waits.append(x["ev"])
                    cur_phase = u["phase"]
                for d in sorted(u["deps"]):
                    x = byid[d]
                    if x["eng"] == e and e == "pe":
                        continue
                    sem, c = x["ev"]
                    if wd.get(id(sem), 0) >= c:
                        continue
                    wd[id(sem)] = c
                    waits.append((sem, c))
                if u["dma"] and u["prev"] is not None:
                    sem, c = u["prev"]
                    if wd.get(id(sem), 0) < c:
                        wd[id(sem)] = c
                        waits.append((sem, c))
                self.prog[e].append((waits, u["fns"], u["ev"][0], 16 if u["dma"] else 1))
        wd = {}
        waits = []
        for x in all_dma:
            if wd.get(id(x["ev"][0]), 0) < x["ev"][1]:
                wd[id(x["ev"][0])] = x["ev"][1]
        semobj = {}
        for x in all_dma:
            semobj[id(x["ev"][0])] = x["ev"][0]
        self.final_waits = [(semobj[k], v) for k, v in wd.items()]

    def emit(self, eng_name, eng):
        for waits, fns, sem, inc in self.prog[eng_name]:
            for s, v in waits:
                eng.wait_ge(s, v)
            ins = None
            for fn in fns:
                ins = fn(eng)
            ins.then_inc(sem, inc)
        if eng_name == "sp":
            for s, v in self.final_waits:
                eng.wait_ge(s, v)


def build(debug=None, stop_after=None):
    debug = debug or {}
    nc = bass.Bass("TRN2", target_bir_lowering=False)
    es = ExitStack()

    def din(name, shape, dt=F32):
        return nc.dram_tensor(name, list(shape), dt, kind="ExternalInput").ap()

    def dout(name, shape, dt=F32):
        return nc.dram_tensor(name, list(shape), dt, kind="ExternalOutput").ap()

    def dint(name, shape, dt=F32):
        return nc.dram_tensor(name, list(shape), dt, kind="Internal").ap()

    xpre = din("xpre", [PRE, D])
    xmain = din("xmain", [TM, D])
    flag = din("flag", [1, 1])
    w_in = din("w_in", [128, KD * NIN])
    WOFF = _win_offsets()
    nmw = din("nmw", [1, D])
    nfw = din("nfw", [1, D])
    fnw = din("fnw", [1, D])
    cst_bf = din("cst_bf", [128, CB_W], BF16)
    cst_f = din("cst_f", [128, CF_W])
    smallp = din("smallp", [128, SP_W])
    aup = din("aup", [16, 512])
    w_a = din("w_a", [128, 8 * D])
    w_b = din("w_b", [128, 8 * D])
    w_o = din("w_o", [128, KD * D])
    w_up = din("w_up", [128, KD * DFF])
    w_gt = din("w_gt", [128, KD * DFF])
    w_dn = din("w_dn", [128, KF * D])
    sC = din("sC", [16, 4, 256, 256])
    sn = din("sn", [64, 256])
    smcol = din("smcol", [64, 1])
    sS = din("sS", [16, 4, 128, 256])
    sconv = din("sconv", [16, 2, DFF])

    yout = dout("yout", [TM - 128, D])
    o_pC = dout("o_pC", [4, 256, 256])
    o_pn = dout("o_pn", [8, 128])
    o_pm = dout("o_pm", [1, 64])
    o_pS = dout("o_pS", [4, 128, 256])
    o_pconv = dout("o_pconv", [2, DFF])
    o_sC = dout("o_sC", [16, 4, 256, 256])
    o_sn = dout("o_sn", [128, 128])
    o_sm = dout("o_sm", [64, 1])
    o_sS = dout("o_sS", [16, 4, 128, 256])
    o_sconv = dout("o_sconv", [16, 2, DFF])

    gscr = dint("gscr", [8, 2048])
    gscs = dint("gscs", [8, 128])
    yscr = dint("yscr", [TM, D], BF16)
    x1scr = dint("x1scr", [TM, D])
    x2scr_ = dint("x2scr", [TM, D])
    dbg_out = {}

    with es:
        def sb(name, shape, dt=F32):
            return es.enter_context(nc.sbuf_tensor(name, list(shape), dt))

        def ps(name, shape, dt=F32):
            return es.enter_context(nc.psum_tensor(name, list(shape), dt))

        esems = {e: es.enter_context(nc.semaphore("s_" + e)) for e in ("pe", "act", "dve", "pool")}
        dsems = {
            "sp": [es.enter_context(nc.semaphore("d_sp%d" % i)) for i in range(24)],
            "pool": [es.enter_context(nc.semaphore("d_pl%d" % i)) for i in range(12)],
        }
        S = Sched(esems, dsems)

        def act(fn, r, w):
            S.op("act", fn, reads=r, writes=w)

        def dve(fn, r, w):
            S.op("dve", fn, reads=r, writes=w)

        def pool(fn, r, w):
            S.op("pool", fn, reads=r, writes=w)

        def sdma(out, in_, r, w):
            S.dma("sp", lambda e: e.dma_start(out=out, in_=in_), reads=r, writes=w)

        def mm(out_ap, outb, pairs, reads, first=True, last=True):
            n = len(pairs)
            for i, (l, r) in enumerate(pairs):
                S.op("pe", lambda e, l=l, r=r, i=i: e.matmul(out_ap, lhsT=l, rhs=r, start=(first and i == 0),
                                                            stop=(last and i == n - 1)),
                     reads=reads, writes=[outb], inc=(i == n - 1))

        def tr(out_ap, outb, in_ap, inb, ident, inc=True):
            S.op("pe", lambda e: e.transpose(out=out_ap, in_=in_ap, identity=ident), reads=[inb, "cb", "cf"],
                 writes=[outb], inc=inc)

        cb = sb("cb", [128, CB_W], BF16)
        cf = sb("cf", [128, CF_W], F32)
        spm = sb("spm", [128, SP_W], F32)
        sdma(cb[:], cst_bf[:, :], [], ["cb"])
        sdma(cf[:], cst_f[:, :], [], ["cf"])
        sdma(spm[:], smallp[:, :], [], ["spm"])
        identb = cb[:, 0:128]
        maskc = cb[:, 128:256]
        masks = cb[:, 256:384]
        qxmask = cb[:, 512:512 + 2048].rearrange("p (j t) -> p j t", j=16)
        vxmask = cb[:, 2560:2576]
        identf = cf[:, 0:128]
        onesf = cf[:, 128:256]
        resetm = cf[:, 256:384]
        negbig = cf[:, 384:512]
        zerocol = cf[:, 512:513]
        P_IB, P_FB, P_AB, P_HNM, P_HNG, P_CW, P_CB = 0, 1, 2, 6, 14, 22, 22 + 3 * KF
        ib64 = spm[0:64, P_IB:P_IB + 1]
        fb64 = spm[0:64, P_FB:P_FB + 1]
        wbc = sb("wbc", [128, D], F32)
        sdma(wbc[:], nmw.partition_broadcast(128), [], ["wbc"])
        flagt = sb("flagt", [1, 1], F32)
        sdma(flagt[:], flag[:, :], [], ["flagt"])

        hT = sb("hT", [128, KD, TM], BF16)
        hTp = sb("hTp", [128, KD, PRE], BF16)
        wsl = [sb("wsl%d" % i, [128, KD * 256], BF16) for i in range(3)]
        stat = sb("stat", [128, 64], F32)
        ARENA = 99 * 512 + 160
        arena = sb("arena", [128, ARENA], BF16)
        apos = [0]
        aphase = [0]

        def carve(name, shape, dt=BF16):
            n = int(np.prod(shape))
            nb = n * (2 if dt == BF16 else 4)
            nb = (nb + 63) // 64 * 64
            off = apos[0]
            apos[0] += nb // 2
            assert apos[0] <= ARENA, ("arena overflow", name, apos[0])
            v = arena[:, off:off + nb // 2]
            if dt != BF16:
                v = v.bitcast(dt)
            v = v[:, 0:n]
            if len(shape) == 2:
                pat, kw = "p (a b) -> p a b", dict(a=shape[0])
            elif len(shape) == 3:
                pat, kw = "p (a b c) -> p a b c", dict(a=shape[0], b=shape[1])
            else:
                pat, kw = None, None
            if pat:
                v = v.rearrange(pat, **kw)
            return v, "ar%d_%s" % (aphase[0], name)

        def arena_reset():
            S.barrier()
            print("arena phase %d used %d / %d" % (aphase[0], apos[0] * 2, ARENA * 2))
            apos[0] = 0
            aphase[0] += 1

        pa = ps("pa", [128, 512])
        pb = ps("pb", [128, 512])
        ptr = [ps("ptr%d" % i, [128, 1024], BF16) for i in range(2)]
        pst = ps("pst", [128, 512])
        pnum = ps("pnum", [128, 512])
        pstate = ps("pstate", [128, 512])
        pmisc = ps("pmisc", [128, 512])
        pbk = [(pa, "pa"), (pb, "pb")]
        pbk6 = [(pa, "pa"), (pb, "pb"), (pst, "pst"), (pnum, "pnum"), (pstate, "pstate"), (pmisc, "pmisc")]
        pbi = [0]
        wide = [False]

        def nextbank():
            pbi[0] += 1
            if wide[0]:
                return pbk6[pbi[0] % 6]
            return pbk[pbi[0] % 2]

        evi = [0]

        def evac(dst, dstb, src, srcb, func=None, scale=None, eng=None):
            if func is not None or eng == "act":
                f = func if func is not None else AF.Copy
                if scale is None:
                    act(lambda e: e.activation(out=dst, in_=src, func=f), [srcb], [dstb])
                else:
                    act(lambda e: e.activation(out=dst, in_=src, func=f, scale=scale), [srcb], [dstb])
                return
            evi[0] += 1
            if eng is None:
                eng = "act" if evi[0] % 2 == 0 else "dve"
            if eng == "act":
                if scale is None:
                    act(lambda e: e.copy(out=dst, in_=src), [srcb], [dstb])
                else:
                    act(lambda e: e.mul(out=dst, in_=src, mul=scale), [srcb], [dstb])
            else:
                if scale is None:
                    dve(lambda e: e.tensor_copy(out=dst, in_=src), [srcb], [dstb])
                else:
                    dve(lambda e: e.tensor_scalar(out=dst, in0=src, scalar1=scale, scalar2=None, op0=ALU.mult),
                        [srcb], [dstb])

        wi = [0]

        def load_w(packed, off, nk, ncols):
            i = wi[0] % 3
            wi[0] += 1
            name = "wsl%d" % i
            tot = nk * ncols
            flat = wsl[i][:, 0:tot]
            dst = flat.rearrange("p (k c) -> p k c", k=nk)
            hf = tot // 2
            S.dma("pool", lambda e: e.dma_start(out=flat[:, 0:hf], in_=packed[:, off:off + hf]), writes=[name])
            S.dma("pool", lambda e: e.dma_start(out=flat[:, hf:tot], in_=packed[:, off + hf:off + tot]),
                  writes=[name])
            return dst, name

        MG = [(0, 512), (512, 512), (1024, 256)]
        PG = [(0, 448), (448, 448)]

        def proj_fm(wt, wname, c0, M, src, srcname, t0, N):
            bank, bname = nextbank()
            mm(bank[0:M, 0:N], bname, [(wt[:, k, c0:c0 + M], src[:, k, t0:t0 + N]) for k in range(KD)],
               [wname, srcname])
            return bank[0:M, 0:N], bname

        def proj_tm(wt, wname, c0, N, src, srcname, ti):
            bank, bname = nextbank()
            mm(bank[:, 0:N], bname, [(src[:, k, ti * 128:(ti + 1) * 128], wt[:, k, c0:c0 + N]) for k in range(KD)],
               [wname, srcname])
            return bank[:, 0:N], bname

        xt = [carve("xt%d" % i, [D], F32) for i in range(4)]
        xn = [carve("xn%d" % i, [D], BF16) for i in range(4)]
        sq1, sq1b = carve("sq", [D], BF16)
        nstat = [0]

        def rstd_from_ss(ss, rs, b, n):
            dve(lambda e: e.tensor_scalar(out=rs, in0=ss, scalar1=1.0 / n, scalar2=EPS, op0=ALU.mult, op1=ALU.add),
                [b], [b + "r"])
            act(lambda e: e.activation(out=rs, in_=rs, func=AF.Sqrt), [b + "r"], [b + "r"])
            dve(lambda e: e.reciprocal(out=rs, in_=rs), [b + "r"], [b + "r"])

        def statslot():
            c = nstat[0] % 16
            nstat[0] += 1
            return stat[:, c:c + 1], stat[:, 16 + c:17 + c], stat[:, 32 + c:33 + c], "stat%d" % c

        def norm_to_T(x_t, xb, x_n, xnb, dstT, dstname, ti, sqj, sqb):
            ss, rs, _, sbn = statslot()
            act(lambda e: e.activation(out=sqj, in_=x_t, func=AF.Square, accum_out=ss), [xb], [sqb, sbn])
            rstd_from_ss(ss, rs, sbn, D)
            dve(lambda e: e.scalar_tensor_tensor(out=x_n, in0=x_t, scalar=rs, in1=wbc[:], op0=ALU.mult,
                                                 op1=ALU.mult), [xb, sbn + "r", "wbc"], [xnb])
            for half in range(2):
                pt = ptr[half]
                pbn = "ptr%d" % half
                for j in range(8):
                    k = half * 8 + j
                    tr(pt[:, j * 128:(j + 1) * 128], pbn, x_n[:, k * 128:(k + 1) * 128], xnb, identb, inc=(j == 7))
                dst = dstT[:, half * 8:(half + 1) * 8, ti * 128:(ti + 1) * 128]
                src = pt[:].rearrange("p (k t) -> p k t", k=8)
                evac(dst, dstname, src, pbn, eng=("act" if half == 0 else "dve"))

        for ti in range(NPT + NMT):
            par = ti % 4
            (x_t, xb), (x_n, xnb) = xt[par], xn[par]
            if ti < NPT:
                sdma(x_t, xpre[ti * 128:(ti + 1) * 128, :], [], [xb])
                norm_to_T(x_t, xb, x_n, xnb, hTp, "hTp", ti, sq1, sq1b)
            else:
                tj = ti - NPT
                sdma(x_t, xmain[tj * 128:(tj + 1) * 128, :], [], [xb])
                norm_to_T(x_t, xb, x_n, xnb, hT, "hT", tj, sq1, sq1b)

        arena_reset()
        if stop_after == "p1":
            return finish(nc, S, dbg_out)
        Cst = carve("Cst", [2, 2, 257], F32)[0]
        Sst = carve("Sst", [2, 256], F32)[0]
        wtok = carve("wtok", [64], F32)[0]
        ftok = carve("ftok", [64], F32)[0]
        decbc = carve("decbc", [64], F32)[0]
        wtoks = carve("wtoks", [4], F32)[0]
        ftoks = carve("ftoks", [4], F32)[0]
        sdecbc = carve("sdecbc", [64], F32)[0]

        galT, galb = carve("galT", [PRE + TM], F32)
        BT, BTb = carve("BT", [PRE + TM], F32)
        gst, gstb = BT[:, 0:512], BTb
        QT, QTb = carve("QT", [2, TM])
        KT, KTb = carve("KT", [2, TM])
        Vt, Vtb = carve("Vt", [NMT, 257])
        OG, OGb = carve("OG", [NMT, 256])
        KTp, KTpb = carve("KTp", [2, PRE])
        Vp, Vpb = carve("Vp", [NPT, 257])
        Cbf, Cbfb = carve("Cbf", [2, 257])
        QTg_, QTgb = carve("QTg", [TM])
        KTg_, KTgb = carve("KTg", [TM])
        Vtg, Vtgb = carve("Vtg", [NMT, 256])
        RG, RGb = carve("RG", [NMT, 256])
        Ktok2 = [carve("Ktok%d" % i, [256]) for i in range(2)]
        Vw2 = [carve("Vw%d" % i, [257]) for i in range(2)]
        PT2 = [carve("PT%d" % i, [128]) for i in range(2)]
        ytok2 = [carve("ytok%d" % i, [256]) for i in range(2)]
        eqt2 = [carve("eqt%d" % i, [128], F32) for i in range(2)]
        QtT2 = [carve("QtT%d" % i, [128]) for i in range(2)]
        KtT2 = [carve("KtT%d" % i, [128]) for i in range(2)]
        KhT2 = [carve("KhT%d" % i, [128]) for i in range(2)]
        Khtok2 = [carve("Khtok%d" % i, [128]) for i in range(2)]
        Sbf, Sbfb = carve("Sbf", [256])
        nT, nTb = carve("nT", [2, 64], F32)
        nTo, nTob = carve("nTo", [2, 64], F32)
        nrow, nrowb = carve("nrow", [256], F32)
        junk, junkb = nrow.bitcast(BF16)[:, 0:256], nrowb
        SG = 2
        aups, aupb = carve("aups", [512], F32)
        negab = carve("negab", [4], F32)[0]
        dSs, dSsb = carve("dSs", [16], F32)
        gmark = apos[0]
        gat = carve("gat", [8, 128], F32)[0][0:64]
        gas = carve("gas", [8, 8], F32)[0][0:64]
        grow = carve("grow", [8, 64], F32)[0][0:1]
        wg, wgn = load_w(w_in, WOFF[C_MI], KD, 8)
        wl, wln = load_w(w_in, WOFF[C_GAL], KD, 16)
        MGG = [(0, 512), (512, 512), (1024, 128), (1152, 128)]
        for (src, sname, groups, base) in ((hTp, "hTp", PG, 0), (hT, "hT", MGG, PRE)):
            for (t0, N) in groups:
                pv, pn_ = proj_fm(wg, wgn, 0, 8, src, sname, t0, N)
                evac(gst[0:8, 0:N], gstb, pv, pn_, eng="act")
                if base + t0 < 2048:
                    sdma(gscr[:, base + t0:base + t0 + N], gst[0:8, 0:N], [gstb], ["gscr"])
                else:
                    sdma(gscs[:, :], gst[0:8, 0:N], [gstb], ["gscs"])
                pv, pn_ = proj_fm(wl, wln, 0, 16, src, sname, t0, N)
                evac(galT[0:16, base + t0:base + t0 + N], galb, pv, pn_, eng="dve")
        GI, GF, GB_, GA, GW, GFL, GT1, GT2 = range(8)

        def gt(i):
            return gat[:, i, :]

        def gs_(i):
            return gas[:, i, :]
        NTOK = PRE + TM - 128
        sdma(gt(GI), gscr[0:4, :].rearrange("h (c l) -> (h c) l", l=128), ["gscr"], ["g_i"])
        sdma(gt(GF), gscr[4:8, :].rearrange("h (c l) -> (h c) l", l=128), ["gscr"], ["g_f"])
        sdma(gs_(GI), gscs[0:4, :].rearrange("h (j l) -> (h j) l", l=8), ["gscs"], ["s_i"])
        sdma(gs_(GF), gscs[4:8, :].rearrange("h (j l) -> (h j) l", l=8), ["gscs"], ["s_f"])
        negfb = stat[0:64, 48:49]
        dve(lambda e: e.tensor_scalar(out=negfb, in0=fb64, scalar1=-1.0, scalar2=None, op0=ALU.mult),
            ["spm"], ["negfb"])

        def gate_math(T, pfx, L):
            i_, f_, b_, a_ = T(GI), T(GF), T(GB_), T(GA)
            act(lambda e: e.activation(out=f_, in_=f_, func=AF.Exp, bias=negfb, scale=-1.0),
                [pfx + "f", "negfb"], [pfx + "f"])
            act(lambda e: e.activation(out=f_, in_=f_, func=AF.Ln, bias=1.0), [pfx + "f"], [pfx + "f"])
            dve(lambda e: e.tensor_scalar(out=f_, in0=f_, scalar1=-1.0, scalar2=None, op0=ALU.mult),
                [pfx + "f"], [pfx + "f"])
            dve(lambda e: e.tensor_tensor_scan(out=b_, data0=onesf[0:64, 0:L], data1=f_, initial=0.0,
                                               op0=ALU.mult, op1=ALU.add), [pfx + "f", "cf"], [pfx + "b"])
            dve(lambda e: e.scalar_tensor_tensor(out=a_, in0=i_, scalar=ib64, in1=b_, op0=ALU.add,
                                                 op1=ALU.subtract), [pfx + "i", pfx + "b", "spm"], [pfx + "a"])
        gate_math(gt, "g_", 128)
        gate_math(gs_, "s_", 8)
        amax = stat[0:64, 49:50]
        bend = gat[:, GB_, 127:128]
        amaxb = stat[0:64, 50:51]
        dve(lambda e: e.reduce_max(out=amax, in_=gt(GA), axis=AX.X), ["g_a"], ["amax"])
        dve(lambda e: e.tensor_tensor(out=amaxb, in0=amax, in1=bend, op=ALU.add), ["amax", "g_b"], ["amaxb"])
        tr(pmisc[0:1, 0:64], "pmisc", bend, "g_b", identf[0:64, 0:64], inc=False)
        tr(pmisc[0:1, 64:128], "pmisc", amaxb, "amaxb", identf[0:64, 0:64])
        R_BE, R_AB, R_MN, R_MP, R_G, R_DEC = range(6)

        def gr(i):
            return grow[0:1, i, :]
        evac(grow[0:1, 0:2, :], "grow", pmisc[0:1, 0:128].rearrange("p (a b) -> p a b", a=2), "pmisc", eng="dve")
        for h in range(4):
            for seg in range(2):
                lo = h * 16 + seg * 8
                if seg == 0:
                    init = 0.0
                    rd = ["grow"]
                else:
                    mf = stat[0:1, 51 + h:52 + h]
                    dve(lambda e, h=h, mf=mf: e.tensor_tensor(out=mf, in0=grow[0:1, R_MN, h * 16 + 7:h * 16 + 8],
                                                             in1=flagt[0:1, 0:1], op=ALU.mult),
                        ["grow", "flagt"], ["mf%d" % h])
                    init = mf
                    rd = ["grow", "mf%d" % h]
                dve(lambda e, lo=lo, init=init: e.tensor_tensor_scan(
                    out=grow[0:1, R_MN, lo:lo + 8], data0=grow[0:1, R_BE, lo:lo + 8],
                    data1=grow[0:1, R_AB, lo:lo + 8], initial=init, op0=ALU.add, op1=ALU.max), rd, ["grow"])
                if seg == 0:
                    dve(lambda e, lo=lo: e.memset(grow[0:1, R_MP, lo:lo + 1], 0.0), [], ["grow"])
                else:
                    dve(lambda e, lo=lo, mf=mf: e.tensor_copy(out=grow[0:1, R_MP, lo:lo + 1], in_=mf),
                        ["mf%d" % h], ["grow"])
                dve(lambda e, lo=lo: e.tensor_copy(out=grow[0:1, R_MP, lo + 1:lo + 8],
                                                   in_=grow[0:1, R_MN, lo:lo + 7]), ["grow"], ["grow"])
        dve(lambda e: e.tensor_tensor(out=gr(R_G), in0=gr(R_MN), in1=gr(R_BE), op=ALU.subtract), ["grow"], ["grow"])
        dve(lambda e: e.tensor_tensor(out=gr(R_DEC), in0=gr(R_MP), in1=gr(R_G), op=ALU.subtract), ["grow"], ["grow"])
        act(lambda e: e.activation(out=gr(R_DEC), in_=gr(R_DEC), func=AF.Exp), ["grow"], ["grow"])
        dve(lambda e: e.tensor_scalar(out=gr(R_G), in0=gr(R_G), scalar1=-1.0, scalar2=None, op0=ALU.mult),
            ["grow"], ["grow"])
        mm(pmisc[:, 128:192], "pmisc", [(onesf[0:1, 0:128], gr(R_DEC))], ["grow", "cf"])
        evac(decbc[:], "decbc", pmisc[:, 128:192], "pmisc", eng="dve")
        mm(pmisc[0:64, 192:193], "pmisc", [(gr(R_G), onesf[0:1, 0:1])], ["grow", "cf"])
        negG = stat[0:64, 56:57]
        evac(negG, "negG", pmisc[0:64, 192:193], "pmisc", eng="dve")
        sdma(o_pm[:, :], gr(R_MN), ["grow"], [])
        act(lambda e: e.activation(out=gt(GW), in_=gt(GA), func=AF.Exp, bias=negG), ["g_a", "negG"], ["g_w"])
        act(lambda e: e.activation(out=gt(GFL), in_=gt(GB_), func=AF.Exp, bias=negG, scale=-1.0),
            ["g_b", "negG"], ["g_fl"])
        tr(pmisc[:, 256:320], "pmisc", gt(GW), "g_w", identf[0:64, 0:64], inc=False)
        tr(pmisc[:, 320:384], "pmisc", gt(GFL), "g_fl", identf[0:64, 0:64])
        evac(wtok[:], "wtok", pmisc[:, 256:320], "pmisc", eng="dve")
        evac(ftok[:], "ftok", pmisc[:, 320:384], "pmisc", eng="dve")
        smc = stat[0:64, 57:58]
        sdma(smc, smcol[:, :], [], ["smc"])
        samax = stat[0:64, 58:59]
        sG = stat[0:64, 59:60]
        snegG = stat[0:64, 60:61]
        sdec = stat[0:64, 61:62]
        smn = stat[0:64, 62:63]
        dve(lambda e: e.reduce_max(out=samax, in_=gs_(GA), axis=AX.X), ["s_a"], ["samax"])
        dve(lambda e: e.tensor_tensor(out=sG, in0=samax, in1=smc, op=ALU.max), ["samax", "smc"], ["sG"])
        dve(lambda e: e.tensor_tensor(out=smn, in0=sG, in1=gas[:, GB_, 7:8], op=ALU.add), ["sG", "s_b"], ["smn"])
        sdma(o_sm[:, :], smn, ["smn"], [])
        dve(lambda e: e.tensor_tensor(out=sdec, in0=smc, in1=sG, op=ALU.subtract), ["smc", "sG"], ["sdec"])
        act(lambda e: e.activation(out=sdec, in_=sdec, func=AF.Exp), ["sdec"], ["sdec"])
        dve(lambda e: e.tensor_scalar(out=snegG, in0=sG, scalar1=-1.0, scalar2=None, op0=ALU.mult), ["sG"], ["snegG"])
        act(lambda e: e.activation(out=gs_(GW), in_=gs_(GA), func=AF.Exp, bias=snegG), ["s_a", "snegG"], ["s_w"])
        act(lambda e: e.activation(out=gs_(GFL), in_=gs_(GB_), func=AF.Exp, bias=snegG, scale=-1.0),
            ["s_b", "snegG"], ["s_fl"])
        dg = gat[:, GT1, 0:64]
        dve(lambda e: e.tensor_scalar(out=dg, in0=identf[0:64, 0:64], scalar1=sdec, scalar2=None, op0=ALU.mult),
            ["sdec", "cf"], ["dg"])
        mm(pmisc[:, 384:448], "pmisc", [(onesf[0:64, 0:128], dg)], ["dg", "cf"])
        evac(sdecbc[:], "sdecbc", pmisc[:, 384:448], "pmisc", eng="dve")
        sdma(gscs[0:4, :].rearrange("h (j l) -> (h j) l", l=8), gs_(GW), ["s_w"], ["gscs"])
        sdma(gscs[4:8, :].rearrange("h (j l) -> (h j) l", l=8), gs_(GFL), ["s_fl"], ["gscs"])
        wfr = gat[0:8, GT2, :]
        sdma(wfr, gscs[0:8, :], ["gscs"], ["wfr"])
        tr(pmisc[:, 448:456], "pmisc", wfr, "wfr", identf[0:8, 0:8])
        evac(wtoks[:], "wtoks", pmisc[:, 448:452], "pmisc", eng="dve")
        evac(ftoks[:], "ftoks", pmisc[:, 452:456], "pmisc", eng="dve")

        if stop_after == "gates":
            if debug.get("gates"):
                for nm, t_, shp in (("wtok", wtok, [128, 64]), ("ftok", ftok, [128, 64]), ("decbc", decbc, [128, 64]),
                                    ("wtoks", wtoks, [128, 4]), ("sdecbc", sdecbc, [128, 64])):
                    dbg_out[nm] = dout("dbg_" + nm, shp)
                    sdma(dbg_out[nm][:, :], t_[:], [nm], [])
            return finish(nc, S, dbg_out)


        dve(lambda e: e.memset(Vt[:, :, 256:257], 1.0), [], [Vtb])
        dve(lambda e: e.memset(Vp[:, :, 256:257], 1.0), [], [Vpb])
        sdma(nrow[0:64, :], sn[:, :], [], [nrowb])
        for dh in range(2):
            tr(pmisc[:, dh * 64:(dh + 1) * 64], "pmisc", nrow[0:64, dh * 128:(dh + 1) * 128], nrowb,
               identf[0:64, 0:64], inc=(dh == 1))
        evac(nT, nTb, pmisc[:, 0:128].rearrange("p (a b) -> p a b", a=2), "pmisc", eng="dve")

        def head_epilogue(num_ap, numb, scale_pre, gate_ap, gateb, tile_i, dstcol0, hnb, ytok, ytokb):
            ss, rs, sc, sbn = statslot()
            if scale_pre is None:
                act(lambda e: e.activation(out=junk, in_=num_ap, func=AF.Square, accum_out=ss), [numb],
                    [junkb, sbn])
            else:
                act(lambda e: e.activation(out=junk, in_=num_ap, func=AF.Square, scale=scale_pre, accum_out=ss),
                    [numb, hnb], [junkb, sbn])
            rstd_from_ss(ss, rs, sbn, 256)
            if scale_pre is not None:
                dve(lambda e: e.tensor_tensor(out=rs, in0=rs, in1=scale_pre, op=ALU.mult), [sbn + "r", hnb],
                    [sbn + "r"])
            dve(lambda e: e.scalar_tensor_tensor(out=ytok, in0=num_ap, scalar=rs, in1=gate_ap, op0=ALU.mult,
                                                 op1=ALU.mult), [numb, sbn + "r", gateb], [ytokb])
            sdma(yscr[tile_i * 128:(tile_i + 1) * 128, dstcol0:dstcol0 + 256], ytok, [ytokb], ["yscr"])

        def mlstm_proj(h):
            wq, wqn = load_w(w_in, WOFF[C_MQ + h * 256], KD, 256)
            wk, wkn = load_w(w_in, WOFF[C_MK + h * 256], KD, 256)
            wv, wvn = load_w(w_in, WOFF[C_MV + h * 256], KD, 256)
            for dh in range(2):
                for (t0, N) in MG:
                    pv, pn_ = proj_fm(wq, wqn, dh * 128, 128, hT, "hT", t0, N)
                    evac(QT[:, dh, t0:t0 + N], QTb, pv, pn_)
            for dh in range(2):
                for (t0, N) in MG:
                    pv, pn_ = proj_fm(wk, wkn, dh * 128, 128, hT, "hT", t0, N)
                    evac(KT[:, dh, t0:t0 + N], KTb, pv, pn_, scale=0.0625)
                for (t0, N) in PG:
                    pv, pn_ = proj_fm(wk, wkn, dh * 128, 128, hTp, "hTp", t0, N)
                    evac(KTp[:, dh, t0:t0 + N], KTpb, pv, pn_, scale=0.0625)
            wo, won = load_w(w_in, WOFF[C_MO + h * 256], KD, 256)
            for ti in range(NPT):
                pv, pn_ = proj_tm(wv, wvn, 0, 256, hTp, "hTp", ti)
                evac(Vp[:, ti, 0:256], Vpb, pv, pn_)
            for ti in range(NMT):
                pv, pn_ = proj_tm(wv, wvn, 0, 256, hT, "hT", ti)
                evac(Vt[:, ti, 0:256], Vtb, pv, pn_)
            for ti in range(NMT):
                pv, pn_ = proj_tm(wo, won, 0, 256, hT, "hT", ti)
                evac(OG[:, ti, :], OGb, pv, pn_, func=AF.Sigmoid)
        csi = [0]

        def mlstm_chunks(h):
            Ch = Cst[:, h % 2, :, :]
            Chb = "Cst%d" % (h % 2)
            dve(lambda e: e.memset(Ch, 0.0), [], [Chb])
            def mchunk(c):
                full = c >= NPT
                samp = c == NCH
                if c < NPT:
                    ktsrc, ktb, vsrc, vb_, ti = KTp, KTpb, Vp, Vpb, c
                else:
                    ktsrc, ktb, vsrc, vb_, ti = KT, KTb, Vt, Vtb, c - NPT
                tk = slice(ti * 128, (ti + 1) * 128)
                (Ktok, Ktokb), (Vw, Vwb), (PT, PTb), (ytok, ytokb) = Ktok2[c % 2], Vw2[c % 2], PT2[c % 2], ytok2[c % 2]
                if samp:
                    wcol, fcol = wtoks[:, h:h + 1], ftoks[:, h:h + 1]
                    wcb, fcb = "wtoks", "ftoks"
                else:
                    wcol, fcol = wtok[:, h * 16 + c:h * 16 + c + 1], ftok[:, h * 16 + c:h * 16 + c + 1]
                    wcb, fcb = "wtok", "ftok"
                for dh in range(2):
                    tr(ptr[0][:, dh * 128:(dh + 1) * 128], "ptr0", ktsrc[:, dh, tk], ktb, identb, inc=(dh == 1))
                evac(Ktok, Ktokb, ptr[0][:, 0:256], "ptr0")
                dve(lambda e, vsrc=vsrc, ti=ti, wcol=wcol: e.tensor_scalar(out=Vw, in0=vsrc[:, ti, :], scalar1=wcol,
                                                                            scalar2=None, op0=ALU.mult),
                     [vb_, wcb], [Vwb])
                if not samp:
                    dcol = decbc[:, h * 16 + c:h * 16 + c + 1]
                    dve(lambda e, dcol=dcol: e.tensor_scalar(out=Ch, in0=Ch, scalar1=dcol, scalar2=None,
                                                             op0=ALU.mult), [Chb, "decbc"], [Chb])
                if full:
                    mm(pst[:, 0:128], "pst", [(ktsrc[:, dh, tk], QT[:, dh, tk]) for dh in range(2)], [ktb, QTb])
                    msk = masks if samp else maskc
                    dve(lambda e, msk=msk: e.tensor_tensor(out=PT, in0=pst[:, 0:128], in1=msk, op=ALU.mult),
                        ["pst", "cb"], [PTb])
                    if not samp:
                        act(lambda e: e.copy(out=Cbf, in_=Ch), [Chb], [Cbfb])
                        mm(pnum[:, 0:257], "pnum",
                           [(PT, Vw)] + [(QT[:, dh, tk], Cbf[:, dh, :]) for dh in range(2)],
                           [PTb, Vwb, QTb, Cbfb])
                    else:
                        mm(pnum[:, 0:257], "pnum", [(PT, Vw)], [PTb, Vwb], first=True, last=False)
                if not samp:
                    for dh in range(2):
                        mm(pstate[:, 0:257], "pstate", [(Ktok[:, dh * 128:(dh + 1) * 128], Vw)], [Ktokb, Vwb])
                        dve(lambda e, dh=dh: e.tensor_tensor(out=Ch[:, dh, :], in0=Ch[:, dh, :],
                                                             in1=pstate[:, 0:257], op=ALU.add),
                            ["pstate", Chb], [Chb])
                else:
                    def sgrp(g, Cs, Csb, Csbf, Csbfb, QX, QXb, VwX, VwXb):
                        js = slice(g * SG, (g + 1) * SG)
                        for dh in range(2):
                            sdma(Cs[:, :, dh, 0:256],
                                 sC[js, h, dh * 128:(dh + 1) * 128, :].rearrange("j p e -> p j e"), [], [Csb])
                        ncols = nT[:, :, g * SG * 4 + h:(g + 1) * SG * 4:4].rearrange("p dh j -> p j dh")
                        dve(lambda e, ncols=ncols: e.tensor_copy(out=Cs[:, :, :, 256], in_=ncols), [nTb], [Csb])
                        dcols = sdecbc[:, h * 16 + g * SG:h * 16 + (g + 1) * SG]
                        Csv = Cs.rearrange("p j dh e -> p j (dh e)")
                        dve(lambda e, dcols=dcols, Csv=Csv: e.tensor_tensor(
                            out=Csv, in0=Csv, in1=dcols.unsqueeze(2).broadcast_to([128, SG, 514]), op=ALU.mult),
                            [Csb, "sdecbc"], [Csb])
                        act(lambda e: e.copy(out=Csbf, in_=Cs), [Csb], [Csbfb])
                        for dh in range(2):
                            dve(lambda e, dh=dh, js=js, tk=tk: e.tensor_tensor(
                                out=QX[:, dh, :, :], in0=QT[:, dh, tk].unsqueeze(1).broadcast_to([128, SG, 128]),
                                in1=qxmask[:, js, :], op=ALU.mult), [QTb, "cb"], [QXb])
                        pairs = [(QX[:, dh, j, :], Csbf[:, j, dh, :]) for j in range(SG) for dh in range(2)]
                        mm(pnum[:, 0:257], "pnum", pairs, [QXb, Csbfb], first=False, last=(g == 16 // SG - 1))
                        dve(lambda e, js=js: e.tensor_tensor(
                            out=VwX, in0=Vw.unsqueeze(1).broadcast_to([128, SG, 257]),
                            in1=vxmask[:, js].unsqueeze(2).broadcast_to([128, SG, 257]), op=ALU.mult),
                            [Vwb, "cb"], [VwXb])
                        for j in range(SG):
                            for dh in range(2):
                                mm(pstate[:, 0:257], "pstate", [(Ktok[:, dh * 128:(dh + 1) * 128], VwX[:, j, :])],
                                   [Ktokb, VwXb])
                                dve(lambda e, j=j, dh=dh: e.tensor_tensor(out=Cs[:, j, dh, :], in0=Cs[:, j, dh, :],
                                                                          in1=pstate[:, 0:257], op=ALU.add),
                                    ["pstate", Csb], [Csb])
                        for dh in range(2):
                            sdma(o_sC[js, h, dh * 128:(dh + 1) * 128, :].rearrange("j p e -> p j e"),
                                 Cs[:, :, dh, 0:256], [Csb], [])
                        ndst = nTo[:, :, g * SG * 4 + h:(g + 1) * SG * 4:4].rearrange("p dh j -> p j dh")
                        dve(lambda e, ndst=ndst: e.tensor_copy(out=ndst, in_=Cs[:, :, :, 256]), [Csb], [nTob])
                    for g in range(16 // SG):
                        k_ = csi[0]
                        csi[0] += 1
                        sgrp(g, *Cs3[k_ % 3], *Csbf2[k_ % 2], *QX2[k_ % 2], *VwX2[k_ % 2])
                if full:
                    ss, rs, sc, sbn = statslot()
                    dve(lambda e, sc=sc: e.tensor_copy(out=sc, in_=pnum[:, 256:257]), ["pnum"], [sbn + "s"])
                    dve(lambda e, sc=sc: e.scalar_tensor_tensor(out=sc, in0=sc, scalar=-1.0, in1=sc, op0=ALU.mult,
                                                                op1=ALU.max), [sbn + "s"], [sbn + "s"])
                    dve(lambda e, fcol=fcol, sc=sc: e.tensor_tensor(out=sc, in0=sc, in1=fcol, op=ALU.max),
                        [sbn + "s", fcb], [sbn + "s"])
                    dve(lambda e, sc=sc: e.reciprocal(out=sc, in_=sc), [sbn + "s"], [sbn + "s"])
                    head_epilogue(pnum[:, 0:256], "pnum", sc, OG[:, ti, :], OGb, ti, h * 256, sbn + "s", ytok, ytokb)
                if c == NCH - 1:
                    sdma(o_pC[h].rearrange("(dh p) e -> p dh e", p=128), Ch[:, :, 0:256], [Chb], [])
            for c in range(NCH + 1):
                mchunk(c)
            tr(pmisc[0:2, 0:128], "pmisc", Ch[:, :, 256], Chb, identf)
            evac(nrow[0:2, 0:128], nrowb, pmisc[0:2, 0:128], "pmisc", eng="dve")
            sdma(o_pn[h * 2:h * 2 + 2, :], nrow[0:2, 0:128], [nrowb], [])

        aupt = stat
        sdma(aups[0:16, :], aup[:, :], [], [aupb])
        dve(lambda e: e.tensor_scalar(out=negab[:, 0:4], in0=spm[:, P_AB:P_AB + 4], scalar1=-1.0, scalar2=None,
                                      op0=ALU.mult), ["spm"], ["negab"])
        BG = [(0, 512), (512, 512), (1024, 512), (1536, 512), (2048, 128)]

        def gla_head(h):
            wq, wqn = load_w(w_in, WOFF[C_GQ + h * 128], KD, 128)
            wk, wkn = load_w(w_in, WOFF[C_GK + h * 128], KD, 128)
            wv, wvn = load_w(w_in, WOFF[C_GV + h * 256], KD, 256)
            QTg, KTg, KTpg = QTg_, KTg_, KTp[:, 0, :]
            QTb, KTb, Vt, Vtb, OG, OGb = QTgb, KTgb, Vtg, Vtgb, RG, RGb
            for (t0, N) in MG:
                pv, pn_ = proj_fm(wq, wqn, 0, 128, hT, "hT", t0, N)
                evac(QTg[:, t0:t0 + N], QTb, pv, pn_, scale=128.0 ** -0.5)
            for (t0, N) in MG:
                pv, pn_ = proj_fm(wk, wkn, 0, 128, hT, "hT", t0, N)
                evac(KTg[:, t0:t0 + N], KTb, pv, pn_)
            for (t0, N) in PG:
                pv, pn_ = proj_fm(wk, wkn, 0, 128, hTp, "hTp", t0, N)
                evac(KTpg[:, t0:t0 + N], KTpb, pv, pn_)
            wr, wrn = load_w(w_in, WOFF[C_GR + h * 256], KD, 256)
            for ti in range(NPT):
                pv, pn_ = proj_tm(wv, wvn, 0, 256, hTp, "hTp", ti)
                evac(Vp[:, ti, 0:256], Vpb, pv, pn_)
            for ti in range(NMT):
                pv, pn_ = proj_tm(wv, wvn, 0, 256, hT, "hT", ti)
                evac(Vt[:, ti, 0:256], Vtb, pv, pn_)
            for ti in range(NMT):
                pv, pn_ = proj_tm(wr, wrn, 0, 256, hT, "hT", ti)
                evac(OG[:, ti, :], OGb, pv, pn_, func=AF.Silu)
            nab = negab[:, h:h + 1]
            for (t0, N) in BG:
                mm(pmisc[:, 0:N], "pmisc", [(aups[0:16, h * 128:(h + 1) * 128], galT[0:16, t0:t0 + N])],
                   [aupb, galb])
                act(lambda e, t0=t0, N=N: e.activation(out=BT[:, t0:t0 + N], in_=pmisc[:, 0:N], func=AF.Exp,
                                                       bias=nab, scale=-1.0), ["pmisc", "negab"], [BTb])
            act(lambda e: e.activation(out=BT, in_=BT, func=AF.Ln, bias=1.0), [BTb], [BTb])
            dve(lambda e: e.tensor_scalar(out=BT, in0=BT, scalar1=-1.0 / 16.0, scalar2=None, op0=ALU.mult),
                [BTb], [BTb])
            dve(lambda e: e.tensor_tensor_scan(out=BT[:, 0:2048], data0=onesf[:, 0:1].broadcast_to([128, 2048]),
                                               data1=BT[:, 0:2048], initial=0.0, op0=ALU.mult, op1=ALU.add),
                [BTb, "cf"], [BTb])
            dve(lambda e: e.tensor_tensor_scan(out=BT[:, 2048:2176], data0=resetm, data1=BT[:, 2048:2176],
                                               initial=0.0, op0=ALU.mult, op1=ALU.add), [BTb, "cf"], [BTb])
            Sh = Sst[:, h % 2, :]
            Shb = "Sst%d" % (h % 2)
            dve(lambda e: e.memset(Sh, 0.0), [], [Shb])

            def gchunk(c):
                full = c >= NPT
                samp = c == NCH
                if c < NPT:
                    ktsrc, ktb, vsrc, vb_, ti = KTpg, KTpb, Vp, Vpb, c
                else:
                    ktsrc, ktb, vsrc, vb_, ti = KTg, KTb, Vt, Vtb, c - NPT
                tk = slice(ti * 128, (ti + 1) * 128)
                tb = slice(2048, 2176) if samp else slice(c * 128, (c + 1) * 128)
                (eqt, eqtb), (QtT, QtTb), (KtT, KtTb), (KhT, KhTb) = eqt2[c % 2], QtT2[c % 2], KtT2[c % 2], KhT2[c % 2]
                (Ktok, Ktokb), (PT, PTb), (ytok, ytokb) = Khtok2[c % 2], PT2[c % 2], ytok2[c % 2]
                ss, rs, sc, sbn = statslot()
                if samp:
                    bend3 = BT[:, 2048 + 7:2176:8].unsqueeze(2).broadcast_to([128, 16, 8])
                    dve(lambda e: e.tensor_tensor(out=eqt.rearrange("p (j l) -> p j l", l=8), in0=bend3,
                                                  in1=BT[:, tb].rearrange("p (j l) -> p j l", l=8),
                                                  op=ALU.subtract), [BTb], [eqtb])
                    act(lambda e: e.activation(out=eqt, in_=eqt, func=AF.Exp), [eqtb], [eqtb])
                    act(lambda e: e.activation(out=dSs, in_=BT[:, 2048 + 7:2176:8], func=AF.Exp), [BTb], [dSsb])
                else:
                    bendc = BT[:, c * 128 + 127:c * 128 + 128]
                    bstc = zerocol if c == 0 else BT[:, c * 128 - 1:c * 128]
                    act(lambda e: e.activation(out=eqt, in_=BT[:, tb], func=AF.Exp, bias=bendc, scale=-1.0),
                        [BTb], [eqtb])
                    dve(lambda e: e.tensor_tensor(out=ss, in0=bendc, in1=bstc, op=ALU.subtract), [BTb, "cf"],
                        [sbn])
                    act(lambda e: e.activation(out=ss, in_=ss, func=AF.Exp), [sbn], [sbn])
                    dve(lambda e: e.tensor_scalar(out=rs, in0=bstc, scalar1=-1.0, scalar2=None, op0=ALU.mult),
                        [BTb, "cf"], [sbn + "r"])
                dve(lambda e: e.tensor_tensor(out=KhT, in0=ktsrc[:, tk], in1=eqt, op=ALU.mult), [ktb, eqtb], [KhTb])
                tr(ptr[0][:, 0:128], "ptr0", KhT, KhTb, identb)
                evac(Ktok[:, 0:128], Ktokb, ptr[0][:, 0:128], "ptr0")
                if full:
                    if samp:
                        act(lambda e: e.activation(out=eqt, in_=BT[:, tb], func=AF.Exp), [BTb, KhTb], [eqtb])
                    else:
                        act(lambda e: e.activation(out=eqt, in_=BT[:, tb], func=AF.Exp, bias=rs), [BTb, sbn + "r", KhTb],
                            [eqtb])
                    dve(lambda e: e.tensor_tensor(out=QtT, in0=QTg[:, tk], in1=eqt, op=ALU.mult), [QTb, eqtb], [QtTb])
                    if samp:
                        act(lambda e: e.activation(out=eqt, in_=BT[:, tb], func=AF.Exp, scale=-1.0), [BTb, QtTb],
                            [eqtb])
                    else:
                        act(lambda e: e.activation(out=eqt, in_=BT[:, tb], func=AF.Exp, bias=bstc, scale=-1.0),
                            [BTb, QtTb, "cf"], [eqtb])
                    dve(lambda e: e.tensor_tensor(out=KtT, in0=ktsrc[:, tk], in1=eqt, op=ALU.mult), [ktb, eqtb],
                        [KtTb])
                    mm(pst[:, 0:128], "pst", [(KtT, QtT)], [KtTb, QtTb])
                    msk = masks if samp else maskc
                    dve(lambda e: e.tensor_tensor(out=PT, in0=pst[:, 0:128], in1=msk, op=ALU.mult), ["pst", "cb"],
                        [PTb])
                    if not samp:
                        mm(pnum[:, 0:256], "pnum", [(PT, vsrc[:, ti, 0:256]), (QtT, Sbf)], [PTb, vb_, QtTb, Sbfb])
                    else:
                        mm(pnum[:, 0:256], "pnum", [(PT, vsrc[:, ti, 0:256])], [PTb, vb_], first=True, last=False)
                if not samp:
                    mm(pstate[:, 0:256], "pstate", [(Ktok[:, 0:128], vsrc[:, ti, 0:256])], [Ktokb, vb_])
                    dve(lambda e: e.scalar_tensor_tensor(out=Sh, in0=Sh, scalar=ss, in1=pstate[:, 0:256],
                                                         op0=ALU.mult, op1=ALU.add), [Shb, sbn, "pstate"], [Shb])
                    act(lambda e: e.copy(out=Sbf, in_=Sh), [Shb], [Sbfb])
                else:
                    def sgrp(g, Cs, Csb, Csbf, Csbfb, QX, QXb, VwX, VwXb):
                        js = slice(g * SG, (g + 1) * SG)
                        Ss = Cs[:, :, 0, 0:256]
                        Ssb = Csbf[:, :, 0, 0:256]
                        sdma(Ss, sS[js, h].rearrange("j p e -> p j e"), [], [Csb])
                        act(lambda e, Ss=Ss, Ssb=Ssb: e.copy(out=Ssb, in_=Ss), [Csb], [Csbfb])
                        dve(lambda e, js=js: e.tensor_tensor(
                            out=QX[:, 0, :, :], in0=QtT.unsqueeze(1).broadcast_to([128, SG, 128]),
                            in1=qxmask[:, js, :], op=ALU.mult), [QtTb, "cb"], [QXb])
                        mm(pnum[:, 0:256], "pnum", [(QX[:, 0, j, :], Ssb[:, j, :]) for j in range(SG)],
                           [QXb, Csbfb], first=False, last=(g == 16 // SG - 1))
                        dve(lambda e, js=js: e.tensor_tensor(
                            out=VwX[:, :, 0:256], in0=vsrc[:, ti, 0:256].unsqueeze(1).broadcast_to([128, SG, 256]),
                            in1=vxmask[:, js].unsqueeze(2).broadcast_to([128, SG, 256]), op=ALU.mult),
                            [vb_, "cb"], [VwXb])
                        for j in range(SG):
                            mm(pstate[:, 0:256], "pstate", [(Ktok[:, 0:128], VwX[:, j, 0:256])], [Ktokb, VwXb])
                            dcol = dSs[:, g * SG + j:g * SG + j + 1]
                            dve(lambda e, j=j, dcol=dcol, Ss=Ss: e.scalar_tensor_tensor(
                                out=Ss[:, j, :], in0=Ss[:, j, :], scalar=dcol, in1=pstate[:, 0:256], op0=ALU.mult,
                                op1=ALU.add), [Csb, dSsb, "pstate"], [Csb])
                        sdma(o_sS[js, h].rearrange("j p e -> p j e"), Ss, [Csb], [])
                    for g in range(16 // SG):
                        k_ = csi[0]
                        csi[0] += 1
                        sgrp(g, *Cs3[k_ % 3], *Csbf2[k_ % 2], *QX2[k_ % 2], *VwX2[k_ % 2])
                if full:
                    head_epilogue(pnum[:, 0:256], "pnum", None, OG[:, ti, :], OGb, ti, 1024 + h * 256, None, ytok, ytokb)
                if c == NCH - 1:
                    sdma(o_pS[h], Sh, [Shb], [])
            dve(lambda e: e.memset(Sbf, 0.0), [], [Sbfb])
            for c in range(NCH + 1):
                gchunk(c)

        mlstm_proj(0)
        S.barrier()
        print("arena phase 1a used %d / %d (mark %d)" % (apos[0] * 2, ARENA * 2, gmark * 2))
        apos[0] = gmark
        Cs3 = [carve("Cs%d" % i, [SG, 2, 257], F32) for i in range(3)]
        Csbf2 = [carve("Csbf%d" % i, [SG, 2, 257]) for i in range(2)]
        QX2 = [carve("QX%d" % i, [2, SG, 128]) for i in range(2)]
        VwX2 = [carve("VwX%d" % i, [SG, 257]) for i in range(2)]
        for h in range(4):
            if h > 0:
                mlstm_proj(h)
            mlstm_chunks(h)
            gla_head(h)
        tr(pmisc[:, 0:128], "pmisc", nTo.rearrange("p a b -> p (a b)"), nTob, identf)
        evac(nrow[:, 0:128], nrowb, pmisc[:, 0:128], "pmisc", eng="dve")
        sdma(o_sn[:, :], nrow[:, 0:128], [nrowb], [])
        if stop_after == "gla":
            return finish(nc, S, dbg_out)

        arena_reset()
        wide[0] = False
        yTa = hTp[:].rearrange("p k t -> p (k t)")[:, 0:8 * TM].rearrange("p (k t) -> p k t", k=8)
        yTg, yTgb = carve("yTg", [8, TM])
        mT, mTb = carve("mT", [KD, TM])
        ystg, ystgb = carve("ystg", [D])
        sg, sgb = carve("sg", [2, TM])
        tmpm, tmpmb = carve("tmpm", [512])
        xs_ = [carve("xs%d" % i, [256], F32)[0] for i in range(4)]
        for ti in range(NMT):
            sdma(ystg, yscr[ti * 128:(ti + 1) * 128, :], ["yscr"], [ystgb])
            for half in range(2):
                pt = ptr[half]
                pbn = "ptr%d" % half
                for j in range(8):
                    k = half * 8 + j
                    tr(pt[:, j * 128:(j + 1) * 128], pbn, ystg[:, k * 128:(k + 1) * 128], ystgb, identb, inc=(j == 7))
                for j in range(8):
                    k = half * 8 + j
                    dstt, dstb = (yTa, "hTp") if half == 0 else (yTg, yTgb)
                    dst = dstt[:, j, ti * 128:(ti + 1) * 128]
                    hcol = spm[:, P_HNM + k:P_HNM + k + 1]
                    if j % 2 == 0:
                        act(lambda e, dst=dst, j=j, pt=pt, hcol=hcol: e.activation(
                            out=dst, in_=pt[:, j * 128:(j + 1) * 128], func=AF.Copy, scale=hcol),
                            [pbn, "spm"], [dstb])
                    else:
                        dve(lambda e, dst=dst, j=j, pt=pt, hcol=hcol: e.tensor_scalar(
                            out=dst, in0=pt[:, j * 128:(j + 1) * 128], scalar1=hcol, scalar2=None, op0=ALU.mult),
                            [pbn, "spm"], [dstb])

        MGB = [(120, 392), (512, 512), (1024, 256)]

        def branch_group(cg):
            MG = MGB
            c0 = cg * 256
            for (gcol, wbr, ysrc, ysb, first) in ((C_GA, w_a, yTa, "hTp", True), (C_GB, w_b, yTg, yTgb, False)):
                wgt, wgtn = load_w(w_in, WOFF[gcol + c0], KD, 256)
                for cb_ in range(2):
                    for (t0, N) in MG:
                        pv, pn_ = proj_fm(wgt, wgtn, cb_ * 128, 128, hT, "hT", t0, N)
                        evac(sg[:, cb_, t0:t0 + N], sgb, pv, pn_, func=AF.Sigmoid)
                wbt, wbtn = load_w(wbr, cg * 8 * 256, 8, 256)
                for cb_ in range(2):
                    kk = cg * 2 + cb_
                    for (t0, N) in MG:
                        bank, bname = nextbank()
                        mm(bank[:, 0:N], bname,
                           [(wbt[:, k, cb_ * 128:(cb_ + 1) * 128], ysrc[:, k, t0:t0 + N]) for k in range(8)],
                           [wbtn, ysb])
                        if first:
                            dve(lambda e, bank=bank, N=N, t0=t0, cb_=cb_, kk=kk: e.tensor_tensor(
                                out=mT[:, kk, t0:t0 + N], in0=bank[:, 0:N], in1=sg[:, cb_, t0:t0 + N], op=ALU.mult),
                                [bname, sgb], [mTb])
                        else:
                            dve(lambda e, bank=bank, N=N, t0=t0, cb_=cb_: e.tensor_tensor(
                                out=tmpm[:, 0:N], in0=bank[:, 0:N], in1=sg[:, cb_, t0:t0 + N], op=ALU.mult),
                                [bname, sgb], [tmpmb])
                            dve(lambda e, N=N, t0=t0, kk=kk: e.tensor_tensor(
                                out=mT[:, kk, t0:t0 + N], in0=mT[:, kk, t0:t0 + N], in1=tmpm[:, 0:N], op=ALU.add),
                                [tmpmb, mTb], [mTb])
        for cg in range(8):
            branch_group(cg)

        def wout_group(cg):
            c0 = cg * 256
            wot, wotn = load_w(w_o, cg * KD * 256, KD, 256)
            for ti in range(NMT):
                xs = xs_[ti % 4]
                xsb = "xsb%d" % (ti % 4)
                sdma(xs[:], xmain[ti * 128:(ti + 1) * 128, c0:c0 + 256], [], [xsb])
                bank, bname = nextbank()
                mm(bank[:, 0:256], bname, [(mT[:, k, ti * 128:(ti + 1) * 128], wot[:, k, :]) for k in range(KD)],
                   [wotn, mTb])
                dve(lambda e, xs=xs, bank=bank: e.tensor_tensor(out=xs[:], in0=xs[:], in1=bank[:, 0:256], op=ALU.add),
                    [xsb, bname], [xsb])
                sdma(x1scr[ti * 128:(ti + 1) * 128, c0:c0 + 256], xs[:], [xsb], ["x1scr"])
        for cg in range(8):
            wout_group(cg)

        arena_reset()
        sdma(wbc[:], nfw.partition_broadcast(128), [], ["wbc"])
        xt2 = [carve("xt2_%d" % i, [D], F32) for i in range(4)]
        xn2 = [carve("xn2_%d" % i, [D], BF16) for i in range(4)]
        sq2, sq2b = carve("sq2", [D], BF16)
        for ti in range(NMT):
            (x_t, xb), (x_n, xnb) = xt2[ti % 4], xn2[ti % 4]
            sdma(x_t, x1scr[ti * 128:(ti + 1) * 128, :], ["x1scr"], [xb])
            norm_to_T(x_t, xb, x_n, xnb, hT, "hT", ti, sq2, sq2b)

        arena_reset()
        HT_ = 640
        actT, actTb = carve("actT", [KF, HT_])
        upad2 = [carve("upad%d" % i, [2 + HT_], F32) for i in range(2)]
        tb2 = [carve("tbuf%d" % i, [HT_], F32) for i in range(2)]
        pb2 = [carve("pbuf%d" % i, [HT_], F32) for i in range(2)]
        gb2 = [carve("gbuf%d" % i, [HT_]) for i in range(2)]
        w4f = [wsl[i // 2][:, (i % 2) * KD * 128:((i % 2) + 1) * KD * 128] for i in range(6)]
        w4 = [v.rearrange("p (k c) -> p k c", k=KD) for v in w4f]
        w4i = [0]

        def load_w4(packed, kf):
            i = w4i[0] % 6
            w4i[0] += 1
            name = "w4_%d" % i
            flat = w4f[i]
            off = kf * KD * 128
            S.dma("pool", lambda e: e.dma_start(out=flat, in_=packed[:, off:off + KD * 128]), writes=[name])
            return w4[i], name
        ucar, ucarb = carve("ucar", [KF, 2], F32)
        ucv, ucvb = carve("ucv", [KF, 34], F32)
        scvT, scvTb = carve("scvT", [KF, 32], F32)
        srow, srowb = carve("srow", [512], F32)
        fst, fstb = carve("fst", [5, 128], F32)
        wdsf = [hTp[:].rearrange("p k t -> p (k t)")[:, i * KF * 128:(i + 1) * KF * 128] for i in range(2)]
        wds = [v.rearrange("p (k c) -> p k c", k=KF) for v in wdsf]
        x2scr = x2scr_
        CW = lambda j, k: spm[:, P_CW + j * KF + k:P_CW + j * KF + k + 1]
        CBc = lambda k: spm[:, P_CB + k:P_CB + k + 1]
        sc32 = sconv.rearrange("j r c -> (j r) c")
        for k4 in range(KF // 4):
            sdma(srow[0:32, :], sc32[:, k4 * 512:(k4 + 1) * 512], [], [srowb])
            for q in range(4):
                tr(pmisc[:, q * 32:(q + 1) * 32], "pmisc", srow[0:32, q * 128:(q + 1) * 128], srowb,
                   identf[0:32, 0:32], inc=(q == 3))
            evac(scvT[:, k4 * 4:(k4 + 1) * 4, :], scvTb, pmisc[:, 0:128].rearrange("p (a b) -> p a b", a=4),
                 "pmisc", eng="dve")
        dve(lambda e: e.memset(ucar, 0.0), [], [ucarb])
        wdi = [0]

        FLO = [120, 0]
        FGR = [[(120, 200), (320, 320)], [(0, 320), (320, 320)]]

        def ffn_block(half, kf):
            g0 = half * HT_
            npr = HT_ if half == 0 else 512
            wu, wun = load_w4(w_up, kf)
            wg_, wgn_ = load_w4(w_gt, kf)
            (upad, upadb), (tb_, tbb), (pb_, pbb), (gb_, gbb) = upad2[kf % 2], tb2[kf % 2], pb2[kf % 2], gb2[kf % 2]
            dve(lambda e: e.tensor_copy(out=upad[:, 0:2], in_=ucar[:, kf, :]), [ucarb], [upadb])
            lo = FLO[half]
            for (l0, N) in FGR[half]:
                t0 = g0 + l0
                pv, pn_ = proj_fm(wu, wun, 0, 128, hT, "hT", t0, N)
                evac(upad[:, 2 + l0:2 + l0 + N], upadb, pv, pn_, eng="act")
                act(lambda e, pv=pv, l0=l0, N=N: e.activation(out=tb_[:, l0:l0 + N], in_=pv, func=AF.Identity,
                                                              bias=CBc(kf), scale=CW(2, kf)), [pn_, "spm"], [tbb])
                pv, pn_ = proj_fm(wg_, wgn_, 0, 128, hT, "hT", t0, N)
                evac(gb_[:, l0:l0 + N], gbb, pv, pn_, eng="act")
            if half == 0:
                dve(lambda e: e.tensor_copy(out=ucar[:, kf, :], in_=upad[:, HT_:HT_ + 2]), [upadb], [ucarb])
            else:
                dve(lambda e: e.tensor_copy(out=ucv[:, kf, 0:2], in_=upad[:, 512:514]), [upadb], [ucvb])
                u3 = upad[:, 2 + 512:2 + 640].rearrange("p (j l) -> p j l", l=8)
                dve(lambda e, u3=u3: e.tensor_copy(out=ucv[:, kf, 2:34].rearrange("p (j r) -> p j r", r=2),
                                                   in_=u3[:, :, 6:8]), [upadb], [ucvb])
            dve(lambda e: e.scalar_tensor_tensor(out=tb_[:, lo:npr], in0=upad[:, 1 + lo:1 + npr], scalar=CW(1, kf),
                                                 in1=tb_[:, lo:npr], op0=ALU.mult, op1=ALU.add),
                [upadb, tbb, "spm"], [tbb])
            dve(lambda e: e.scalar_tensor_tensor(out=tb_[:, lo:npr], in0=upad[:, lo:npr], scalar=CW(0, kf),
                                                 in1=tb_[:, lo:npr], op0=ALU.mult, op1=ALU.add),
                [upadb, tbb, "spm"], [tbb])
            if half == 1:
                t3 = tb_[:, 512:640].rearrange("p (j l) -> p j l", l=8)
                u3 = upad[:, 2 + 512:2 + 640].rearrange("p (j l) -> p j l", l=8)
                s3 = scvT[:, kf, :].rearrange("p (j r) -> p j r", r=2)
                dve(lambda e, t3=t3, u3=u3: e.scalar_tensor_tensor(
                    out=t3[:, :, 1:8], in0=u3[:, :, 0:7], scalar=CW(1, kf), in1=t3[:, :, 1:8], op0=ALU.mult,
                    op1=ALU.add), [upadb, tbb, "spm"], [tbb])
                dve(lambda e, t3=t3, s3=s3: e.scalar_tensor_tensor(
                    out=t3[:, :, 0:1], in0=s3[:, :, 1:2], scalar=CW(1, kf), in1=t3[:, :, 0:1], op0=ALU.mult,
                    op1=ALU.add), [scvTb, tbb, "spm"], [tbb])
                dve(lambda e, t3=t3, u3=u3: e.scalar_tensor_tensor(
                    out=t3[:, :, 2:8], in0=u3[:, :, 0:6], scalar=CW(0, kf), in1=t3[:, :, 2:8], op0=ALU.mult,
                    op1=ALU.add), [upadb, tbb, "spm"], [tbb])
                dve(lambda e, t3=t3, s3=s3: e.scalar_tensor_tensor(
                    out=t3[:, :, 0:2], in0=s3[:, :, 0:2], scalar=CW(0, kf), in1=t3[:, :, 0:2], op0=ALU.mult,
                    op1=ALU.add), [scvTb, tbb, "spm"], [tbb])
            fs = slice(lo, HT_)
            act(lambda e: e.activation(out=pb_[:, fs], in_=tb_[:, fs], func=AF.Square, scale=0.044715 ** 0.5), [tbb],
                [pbb])
            dve(lambda e: e.scalar_tensor_tensor(out=pb_[:, fs], in0=pb_[:, fs], scalar=1.0, in1=tb_[:, fs],
                                                 op0=ALU.add, op1=ALU.mult), [pbb, tbb], [pbb])
            act(lambda e: e.activation(out=pb_[:, fs], in_=pb_[:, fs], func=AF.Sigmoid, scale=1.5957691216057308),
                [pbb], [pbb])
            dve(lambda e: e.tensor_tensor(out=pb_[:, fs], in0=pb_[:, fs], in1=tb_[:, fs], op=ALU.mult), [pbb, tbb],
                [pbb])
            dve(lambda e: e.tensor_tensor(out=actT[:, kf, fs], in0=pb_[:, fs], in1=gb_[:, fs], op=ALU.mult),
                [pbb, gbb], [actTb])

        def down_block(half, cbk):
            g0 = half * HT_
            i = wdi[0] % 2
            wdi[0] += 1
            (tb_, tbb) = tb2[i]
            wd = wds[i]
            wdn = "wds%d" % i
            off = cbk * KF * 128
            wdf = wdsf[i]
            S.dma("pool", lambda e: e.dma_start(out=wdf[:, 0:22 * 128], in_=w_dn[:, off:off + 22 * 128]),
                  writes=[wdn])
            S.dma("pool", lambda e: e.dma_start(out=wdf[:, 22 * 128:44 * 128],
                                                in_=w_dn[:, off + 22 * 128:off + 44 * 128]), writes=[wdn])
            for (l0, N) in FGR[half]:
                bank, bname = nextbank()
                mm(bank[:, 0:N], bname, [(wd[:, k, :], actT[:, k, l0:l0 + N]) for k in range(KF)],
                   [wdn, actTb])
                evac(tb_[:, l0:l0 + N], tbb, bank[:, 0:N], bname, eng="act")
            tts = range(1, 5) if half == 0 else range(5)
            for tt in tts:
                dstp = pst[:, tt * 128:(tt + 1) * 128] if tt < 4 else pnum[:, 0:128]
                dstn = "pst" if tt < 4 else "pnum"
                tr(dstp, dstn, tb_[:, tt * 128:(tt + 1) * 128], tbb, identf, inc=(tt >= 3))
            evac(fst[:, 0:4, :], fstb, pst[:, 0:512].rearrange("p (a b) -> p a b", a=4), "pst", eng="dve")
            evac(fst[:, 4, :], fstb, pnum[:, 0:128], "pnum", eng="dve")
            for tt in tts:
                r0 = g0 + tt * 128
                sdma(x2scr[r0:r0 + 128, cbk * 128:(cbk + 1) * 128], fst[:, tt, :], [fstb], ["x2scr"])

        for half in range(2):
            for kf in range(KF):
                ffn_block(half, kf)
            for cbk in range(KD):
                down_block(half, cbk)
        oconv_s = o_sconv.rearrange("j r c -> (j r) c")
        for k4 in range(KF // 4):
            for q in range(4):
                tr(pmisc[0:34, q * 128:(q + 1) * 128], "pmisc", ucv[:, k4 * 4 + q, :], ucvb, identf, inc=(q == 3))
            evac(srow[0:34, :], srowb, pmisc[0:34, 0:512], "pmisc", eng="dve")
            sdma(o_pconv[:, k4 * 512:(k4 + 1) * 512], srow[0:2, :], [srowb], [])
            sdma(oconv_s[:, k4 * 512:(k4 + 1) * 512], srow[2:34, :], [srowb], [])

        arena_reset()
        sdma(wbc[:], fnw.partition_broadcast(128), [], ["wbc"])
        xa = [carve("xa%d" % i, [D], F32) for i in range(4)]
        xf = [carve("xf%d" % i, [D], F32) for i in range(4)]
        sq3, sq3b = carve("sq3", [D], BF16)
        for ti in range(1, NMT):
            (x_a, xab), (x_f, xfb) = xa[ti % 4], xf[ti % 4]
            sdma(x_a, x1scr[ti * 128:(ti + 1) * 128, :], ["x1scr"], [xab])
            sdma(x_f, x2scr[ti * 128:(ti + 1) * 128, :], ["x2scr"], [xfb])
            dve(lambda e, x_a=x_a, x_f=x_f: e.tensor_tensor(out=x_a, in0=x_a, in1=x_f, op=ALU.add), [xab, xfb], [xab])
            ss, rs, _, sbn = statslot()
            act(lambda e, x_a=x_a, ss=ss: e.activation(out=sq3, in_=x_a, func=AF.Square, accum_out=ss), [xab],
                [sq3b, sbn])
            rstd_from_ss(ss, rs, sbn, D)
            dve(lambda e, x_a=x_a, x_f=x_f, rs=rs: e.scalar_tensor_tensor(out=x_f, in0=x_a, scalar=rs, in1=wbc[:],
                                                                          op0=ALU.mult, op1=ALU.mult),
                [xab, sbn + "r", "wbc"], [xfb])
            sdma(yout[(ti - 1) * 128:ti * 128, :], x_f, [xfb], [])
        return finish(nc, S, dbg_out)


def finish(nc, S, dbg_out):
    S.finish()
    print("sim phase us:", [int(x) for x in S.sim_phase_us], "units", len(S.units))
    with nc.Block() as block:
        @block.sync
        def _(eng):
            S.emit("sp", eng)

        @block.tensor
        def _(eng):
            S.emit("pe", eng)

        @block.scalar
        def _(eng):
            S.emit("act", eng)

        @block.vector
        def _(eng):
            S.emit("dve", eng)

        @block.gpsimd
        def _(eng):
            S.emit("pool", eng)
    return nc


def make_consts():
    import ml_dtypes
    cb = np.zeros((128, CB_W), np.float32)
    cb[:, 0:128] = np.eye(128)
    s = np.arange(128)[:, None]
    t = np.arange(128)[None, :]
    cb[:, 128:256] = (s <= t)
    cb[:, 256:384] = (s <= t) & ((s // 8) == (t // 8))
    j = np.arange(16)[:, None]
    cb[:, 512:512 + 2048] = ((np.arange(128)[None, :] // 8) == j).astype(np.float32).reshape(1, 2048)
    cb[:, 2560:2576] = ((np.arange(128)[:, None] // 8) == np.arange(16)[None, :])
    cf = np.zeros((128, CF_W), np.float32)
    cf[:, 0:128] = np.eye(128)
    cf[:, 128:256] = 1.0
    cf[:, 256:384] = (np.arange(128)[None, :] % 8 != 0)
    cf[:, 384:512] = np.where(np.arange(128)[None, :] % 8 == 0, -1e30, 0.0)
    return cb.astype(ml_dtypes.bfloat16), cf


_NC_CACHE = {}


def _prep_inputs(inp):
    f32 = np.float32
    cbc, cfc = make_consts()
    xp = np.asarray(inp["x_prompt"], f32)
    xs = np.asarray(inp["x_sample"], f32)
    sp = np.zeros((128, SP_W), f32)
    ib = np.asarray(inp["mlstm_i_bias"], f32)[0]
    fb = np.asarray(inp["mlstm_f_bias"], f32)[0]
    sp[0:64, 0] = np.repeat(ib, 16)
    sp[0:64, 1] = np.repeat(fb, 16)
    sp[:, 2:6] = np.asarray(inp["gla_alpha_bias"], f32)[0].reshape(4, 128).T
    sp[:, 6:14] = np.asarray(inp["mlstm_head_norm_w"], f32)[0].reshape(8, 128).T
    sp[:, 14:22] = np.asarray(inp["gla_head_norm_w"], f32)[0].reshape(8, 128).T
    cw = np.asarray(inp["ffn_conv_w"], f32)[0]
    for j in range(3):
        sp[:, 22 + j * KF:22 + (j + 1) * KF] = cw[j].reshape(KF, 128).T
    sp[:, 22 + 3 * KF:22 + 4 * KF] = np.asarray(inp["ffn_conv_b"], f32)[0].reshape(KF, 128).T
    shared = {
        "w_in": _pack(np.asarray(inp["w_in"], f32)[0], KD, _win_blocks()),
        "nmw": np.asarray(inp["norm_mix_w"], f32).reshape(1, D),
        "nfw": np.asarray(inp["norm_ffn_w"], f32).reshape(1, D),
        "fnw": np.asarray(inp["final_norm_w"], f32).reshape(1, D),
        "cst_bf": cbc, "cst_f": cfc, "smallp": sp,
        "aup": np.asarray(inp["gla_alpha_up"], f32)[0],
        "w_a": _pack(np.asarray(inp["w_branch_a"], f32)[0], 8, [(c * 256, 256) for c in range(8)]),
        "w_b": _pack(np.asarray(inp["w_branch_b"], f32)[0], 8, [(c * 256, 256) for c in range(8)]),
        "w_o": _pack(np.asarray(inp["w_out"], f32)[0], KD, [(c * 256, 256) for c in range(8)]),
        "w_up": _pack(np.asarray(inp["ffn_w_up"], f32)[0], KD, [(c * 128, 128) for c in range(KF)]),
        "w_gt": _pack(np.asarray(inp["ffn_w_gate"], f32)[0], KD, [(c * 128, 128) for c in range(KF)]),
        "w_dn": _pack(np.asarray(inp["ffn_w_down"], f32)[0], KF, [(c * 128, 128) for c in range(KD)]),
    }
    sC = np.asarray(inp["state_mlstm_C"], f32)[0]
    sn = np.asarray(inp["state_mlstm_n"], f32)[0]
    sm = np.asarray(inp["state_mlstm_m"], f32)[0]
    sS = np.asarray(inp["state_gla_S"], f32)[0]
    scv = np.asarray(inp["state_ffn_conv"], f32)[0]
    maps = []
    for c in range(8):
        s, half = c // 2, c % 2
        xmain = np.zeros((TM, D), f32)
        if half == 1:
            xpre = np.ascontiguousarray(xp[s, 0:PRE])
            xmain[0:1152] = xp[s, PRE:2048]
        else:
            xpre = np.zeros((PRE, D), f32)
            xmain[128:1152] = xp[s, 0:1024]
        xmain[1152:1280] = xs[16 * c:16 * c + 16].reshape(128, D)
        m = dict(shared)
        m.update({
            "xpre": xpre, "xmain": xmain, "flag": np.full((1, 1), float(half), f32),
            "sC": np.ascontiguousarray(sC[16 * c:16 * c + 16]),
            "sn": np.ascontiguousarray(sn[16 * c:16 * c + 16].reshape(64, 256)),
            "smcol": np.ascontiguousarray(sm[16 * c:16 * c + 16].T.reshape(64, 1)),
            "sS": np.ascontiguousarray(sS[16 * c:16 * c + 16]),
            "sconv": np.ascontiguousarray(scv[16 * c:16 * c + 16]),
        })
        maps.append(m)
    return maps


def _assemble(results):
    f32 = np.float32
    y_p = np.zeros((4, 2048, D), f32)
    y_s = np.zeros((128, 8, D), f32)
    pC = np.zeros((1, 4, 4, 256, 256), f32)
    pn = np.zeros((1, 4, 4, 256), f32)
    pm = np.zeros((1, 4, 4), f32)
    pS = np.zeros((1, 4, 4, 128, 256), f32)
    pcv = np.zeros((1, 4, 2, DFF), f32)
    sCo = np.zeros((1, 128, 4, 256, 256), f32)
    sno = np.zeros((1, 128, 4, 256), f32)
    smo = np.zeros((1, 128, 4), f32)
    sSo = np.zeros((1, 128, 4, 128, 256), f32)
    scvo = np.zeros((1, 128, 2, DFF), f32)
    for c in range(8):
        r = results[c]
        s, half = c // 2, c % 2
        yo = np.asarray(r["yout"], f32)
        y_p[s, half * 1024:(half + 1) * 1024] = yo[0:1024]
        y_s[16 * c:16 * c + 16] = yo[1024:1152].reshape(16, 8, D)
        if half == 1:
            pC[0, s] = np.asarray(r["o_pC"], f32)
            pn[0, s] = np.asarray(r["o_pn"], f32).reshape(4, 256)
            pm[0, s] = np.asarray(r["o_pm"], f32).reshape(4, 16)[:, 15]
            pS[0, s] = np.asarray(r["o_pS"], f32)
            pcv[0, s] = np.asarray(r["o_pconv"], f32)
        sl = slice(16 * c, 16 * c + 16)
        sCo[0, sl] = np.asarray(r["o_sC"], f32)
        sno[0, sl] = np.asarray(r["o_sn"], f32).reshape(2, 16, 4, 128).transpose(1, 2, 0, 3).reshape(16, 4, 256)
        smo[0, sl] = np.asarray(r["o_sm"], f32).reshape(4, 16).T
        sSo[0, sl] = np.asarray(r["o_sS"], f32)
        scvo[0, sl] = np.asarray(r["o_sconv"], f32)
    return (y_p, y_s, pC, pn, pm, pS, pcv, sCo, sno, smo, sSo, scvo)


def kernel(**inputs):
    if "nc" not in _NC_CACHE:
        _NC_CACHE["nc"] = build()
    nc = _NC_CACHE["nc"]
    maps = _prep_inputs(inputs)
    res = run_bass_kernel_spmd(nc, maps, core_ids=list(range(8)))
    return _assemble(res.results)
```

```python
import numpy as np
from contextlib import ExitStack
import concourse.bass as bass
import concourse.mybir as mybir
from concourse.bass_utils import run_bass_kernel_spmd

F32 = mybir.dt.float32
BF16 = mybir.dt.bfloat16
AF = mybir.ActivationFunctionType
ALU = mybir.AluOpType
AX = mybir.AxisListType

D = 2048
NIN = 11288
DFF = 5632
KD = D // 128
KF = DFF // 128
PRE = 896
NPT = PRE // 128
TM = 1280
CB_W = 2576
CF_W = 520
SP_W = 22 + 4 * KF
NMT = TM // 128
NCH = 16
EPS = 1e-6

C_MQ, C_MK, C_MV, C_MO = 0, 1024, 2048, 3072
C_MI, C_MF = 4096, 4100
C_GQ, C_GK, C_GV, C_GR = 4104, 4616, 5128, 6152
C_GAL = 7176
C_GA, C_GB = 7192, 9240


def _win_blocks():
    blks = [(C_MI, 8), (C_GAL, 16)]
    for hh in range(4):
        blks += [(C_MQ + hh * 256, 256), (C_MK + hh * 256, 256), (C_MV + hh * 256, 256), (C_MO + hh * 256, 256)]
    for hh in range(4):
        blks += [(C_GQ + hh * 128, 128), (C_GK + hh * 128, 128), (C_GV + hh * 256, 256), (C_GR + hh * 256, 256)]
    for cg in range(8):
        blks += [(C_GA + cg * 256, 256), (C_GB + cg * 256, 256)]
    return blks


def _win_offsets():
    off = {}
    o = 0
    for c0, n in _win_blocks():
        off[c0] = o
        o += KD * n
    return off


def _pack(w, nk, blocks):
    outs = []
    for c0, n in blocks:
        outs.append(np.ascontiguousarray(w[:, c0:c0 + n].reshape(nk, 128, n).transpose(1, 0, 2)).reshape(128, nk * n))
    return np.ascontiguousarray(np.concatenate(outs, axis=1))


class _Probe:
    def __init__(self):
        self.calls = []

    def __getattr__(self, name):
        def f(*a, **k):
            self.calls.append((name, a, k))
            return self
        return f

    def then_inc(self, *a, **k):
        return self


def _free_elems(ap):
    n = 1
    for s in ap.shape[1:]:
        n *= int(s)
    return n


def _est(eng, fn):
    p = _Probe()
    fn(p)
    name, a, k = p.calls[0]
    out = k.get("out", a[0] if a else None)
    if eng == "pe":
        if name == "transpose":
            return 0.09
        rhs = k.get("rhs")
        n = _free_elems(rhs)
        f = 4.0 if rhs.dtype == F32 else 1.0
        return max(n, 64) * f / 2400.0 + 0.012
    n = _free_elems(out) if out is not None else 64
    if eng == "act":
        return (n + 200) / 1400.0 + (0.1 if k.get("accum_out") is not None else 0.0)
    if eng in ("dve", "pool"):
        f = 2.0 if name in ("tensor_tensor_scan",) else 1.0
        return max(n, 60) * f / 960.0 + 0.07
    return 0.1


def _dma_est(fn):
    p = _Probe()
    fn(p)
    name, a, k = p.calls[0]
    out = k.get("out")
    nbytes = 1
    for s in out.shape:
        nbytes *= int(s)
    nbytes *= 4 if out.dtype == F32 else 2
    return 2.0 + nbytes / 180e3, nbytes


class Sched:
    ENGS = ("pe", "act", "dve", "pool", "sp")
    XLAT = 0.12

    def __init__(self, esems, dsems):
        self.esem = esems
        self.dsems = dsems
        self.units = []
        self.bufs = {}
        self.phase = 0
        self.trace_phase = None
        self.open = {e: None for e in esems}

    def _st(self, b):
        st = self.bufs.get(b)
        if st is None:
            st = {"w": None, "r": {}}
            self.bufs[b] = st
        return st

    def _deps(self, uid, reads, writes):
        deps = set()
        for b in reads:
            st = self._st(b)
            if st["w"] is not None:
                deps.add(st["w"])
        for b in writes:
            st = self._st(b)
            if st["w"] is not None:
                deps.add(st["w"])
            deps.update(st["r"].keys())
        deps.discard(uid)
        return deps

    def _mark(self, uid, reads, writes):
        for b in reads:
            self._st(b)["r"][uid] = True
        for b in writes:
            st = self._st(b)
            st["w"] = uid
            st["r"] = {}

    def op(self, eng, fn, reads=(), writes=(), inc=True):
        u = self.open[eng]
        if u is None:
            u = {"eng": eng, "fns": [], "deps": set(), "dur": 0.0, "dma": False, "phase": self.phase,
                 "id": len(self.units), "lab": ",".join(writes)}
            self.units.append(u)
            self.open[eng] = u
        u["deps"] |= self._deps(u["id"], reads, writes)
        self._mark(u["id"], reads, writes)
        u["fns"].append(fn)
        u["dur"] += _est(eng, fn)
        if inc:
            self.open[eng] = None

    def dma(self, q, fn, reads=(), writes=()):
        uid = len(self.units)
        lat, nbytes = _dma_est(fn)
        u = {"eng": q, "fns": [fn], "deps": self._deps(uid, reads, writes), "dur": 0.06 if q == "sp" else 0.35,
             "dma": True, "lat": lat, "phase": self.phase, "id": uid, "lab": "dma:" + ",".join(writes) + "<" + ",".join(reads)}
        self.units.append(u)
        self._mark(uid, reads, writes)

    def barrier(self):
        for e, v in self.open.items():
            assert v is None, e
        self.phase += 1
        self.bufs = {}

    def barrier_all_dma(self, q="sp"):
        pass

    def finish(self):
        import heapq
        order = {e: [] for e in self.ENGS}
        nph = self.phase + 1
        by_phase = [[] for _ in range(nph)]
        for u in self.units:
            by_phase[u["phase"]].append(u)
        for ph in range(nph):
            us = by_phase[ph]
            ids = {u["id"] for u in us}
            ndep = {}
            users = {}
            for u in us:
                d = [x for x in u["deps"] if x in ids]
                u["deps"] = set(d)
                ndep[u["id"]] = len(d)
                for x in d:
                    users.setdefault(x, []).append(u)
            byid = {u["id"]: u for u in us}
            ready = {e: [] for e in self.ENGS}
            avail = {}
            for u in us:
                if ndep[u["id"]] == 0:
                    heapq.heappush(ready[u["eng"]], u["id"])
                    avail[u["id"]] = 0.0
            tfree = {e: 0.0 for e in self.ENGS}
            lastu = {}
            done = 0
            fin = {}
            while done < len(us):
                best = None
                for e in self.ENGS:
                    h = ready[e]
                    if not h:
                        continue
                    cand = None
                    tnow = tfree[e]
                    low = [i for i in h if avail[i] <= tnow]
                    if low:
                        cid = min(low)
                        st = tnow
                    else:
                        cid = min(h, key=lambda i: (avail[i], i))
                        st = avail[cid]
                    if best is None or st < best[0] or (st == best[0] and cid < best[2]):
                        best = (st, e, cid)
                st, e, cid = best
                ready[e].remove(cid)
                heapq.heapify(ready[e])
                u = byid[cid]
                u["st"] = st
                if avail[cid] >= tfree[e] - 1e-9 and u["deps"]:
                    u["why"] = max(u["deps"], key=lambda x: fin[x])
                else:
                    u["why"] = lastu.get(e)
                lastu[e] = cid
                end = st + u["dur"]
                tfree[e] = end
                f = end + (u["lat"] if u["dma"] else 0.0)
                fin[cid] = f
                order[e].append(u)
                done += 1
                for v in users.get(cid, ()):
                    ndep[v["id"]] -= 1
                    if ndep[v["id"]] == 0:
                        t = 0.0
                        for x in v["deps"]:
                            lat = 0.0 if (byid[x]["eng"] == v["eng"] and v["eng"] == "pe") else self.XLAT
                            t = max(t, fin[x] + lat)
                        avail[v["id"]] = t
                        heapq.heappush(ready[v["eng"]], v["id"])
            self.sim_phase_us = getattr(self, "sim_phase_us", []) + [max(tfree.values())]
            if getattr(self, "trace_phase", None) == ph:
                cur = max(us, key=lambda u: fin[u["id"]])["id"]
                chain = []
                while cur is not None and len(chain) < 400:
                    u = byid[cur]
                    chain.append((round(u["st"], 2), u["eng"], round(u["dur"], 2), u["lab"], len(u["fns"])))
                    cur = u.get("why")
                for c in chain[:400]:
                    print("   CP", c)
            busy = {e: 0.0 for e in self.ENGS}
            for u in us:
                busy[u["eng"]] += u["dur"]
            print("phase", ph, "sim %.0f us" % max(tfree.values()), {e: int(v) for e, v in busy.items()}, "units", len(us))
        cnt = {e: 0 for e in self.esem}
        dcnt = {q: [0] * len(v) for q, v in self.dsems.items()}
        dnext = {q: 0 for q in self.dsems}
        for e in self.ENGS:
            for u in order[e]:
                if u["dma"]:
                    i = dnext[e]
                    dnext[e] = (i + 1) % len(self.dsems[e])
                    u["prev"] = (self.dsems[e][i], dcnt[e][i]) if dcnt[e][i] > 0 else None
                    dcnt[e][i] += 16
                    u["ev"] = (self.dsems[e][i], dcnt[e][i])
                else:
                    cnt[e] += 1
                    u["ev"] = (self.esem[e], cnt[e])
        self.order = order
        byid = {u["id"]: u for u in self.units}
        self.prog = {e: [] for e in self.ENGS}
        all_dma = [u for u in self.units if u["dma"]]
        for e in self.ENGS:
            wd = {}
            cur_phase = 0
            for u in order[e]:
                waits = []
                if u["phase"] != cur_phase:
                    for e2 in self.esem:
                        if e2 == e and e == "pe":
                            continue
                        c = max([x["ev"][1] for x in order[e2] if x["phase"] < u["phase"] and not x["dma"]] or [0])
                        sem = self.esem[e2]
                        if c > 0 and wd.get(id(sem), 0) < c:
                            wd[id(sem)] = c
                            waits.append((sem, c))
                    for x in all_dma:
                        if x["phase"] < u["phase"] and wd.get(id(x["ev"][0]), 0) < x["ev"][1]:
                            wd[id(x["ev"][0])] = x["ev"][1]
                            waits.append(x["ev"])
                    cur_phase = u["phase"]
                for d in sorted(u["deps"]):
                    x = byid[d]
                    if x["eng"] == e and e == "pe":
                        continue
                    sem, c = x["ev"]
                    if wd.get(id(sem), 0) >= c:
                        continue
                    wd[id(sem)] = c
                    waits.append((sem, c))
                if u["dma"] and u["prev"] is not None:
                    sem, c = u["prev"]
                    if wd.get(id(sem), 0) < c:
                        wd[id(sem)] = c
                        waits.append((sem, c))
                self.prog[e].append((waits, u["fns"], u["ev"][0], 16 if u["dma"] else 1))
        wd = {}
        waits = []
        for x in all_dma:
            if wd.get(id(x["ev"][0]), 0) < x["ev"][1]:
                wd[id(x["ev"][0])] = x["ev"][1]
        semobj = {}
        for x in all_dma:
            semobj[id(x["ev"][0])] = x["ev"][0]
        self.final_waits = [(semobj[k], v) for k, v in wd.items()]

    def emit(self, eng_name, eng):
        for waits, fns, sem, inc in self.prog[eng_name]:
            for s, v in waits:
                eng.wait_ge(s, v)
            ins = None
            for fn in fns:
                ins = fn(eng)
            ins.then_inc(sem, inc)
        if eng_name == "sp":
            for s, v in self.final_waits:
                eng.wait_ge(s, v)


def build(debug=None, stop_after=None):
    debug = debug or {}
    nc = bass.Bass("TRN2", target_bir_lowering=False)
    es = ExitStack()

    def din(name, shape, dt=F32):
        return nc.dram_tensor(name, list(shape), dt, kind="ExternalInput").ap()

    def dout(name, shape, dt=F32):
        return nc.dram_tensor(name, list(shape), dt, kind="ExternalOutput").ap()

    def dint(name, shape, dt=F32):
        return nc.dram_tensor(name, list(shape), dt, kind="Internal").ap()

    xpre = din("xpre", [PRE, D])
    xmain = din("xmain", [TM, D])
    flag = din("flag", [1, 1])
    w_in = din("w_in", [128, KD * NIN])
    WOFF = _win_offsets()
    nmw = din("nmw", [1, D])
    nfw = din("nfw", [1, D])
    fnw = din("fnw", [1, D])
    cst_bf = din("cst_bf", [128, CB_W], BF16)
    cst_f = din("cst_f", [128, CF_W])
    smallp = din("smallp", [128, SP_W])
    aup = din("aup", [16, 512])
    w_a = din("w_a", [128, 8 * D])
    w_b = din("w_b", [128, 8 * D])
    w_o = din("w_o", [128, KD * D])
    w_up = din("w_up", [128, KD * DFF])
    w_gt = din("w_gt", [128, KD * DFF])
    w_dn = din("w_dn", [128, KF * D])
    sC = din("sC", [16, 4, 256, 256])
    sn = din("sn", [64, 256])
    smcol = din("smcol", [64, 1])
    sS = din("sS", [16, 4, 128, 256])
    sconv = din("sconv", [16, 2, DFF])

    yout = dout("yout", [TM - 128, D])
    o_pC = dout("o_pC", [4, 256, 256])
    o_pn = dout("o_pn", [8, 128])
    o_pm = dout("o_pm", [1, 64])
    o_pS = dout("o_pS", [4, 128, 256])
    o_pconv = dout("o_pconv", [2, DFF])
    o_sC = dout("o_sC", [16, 4, 256, 256])
    o_sn = dout("o_sn", [128, 128])
    o_sm = dout("o_sm", [64, 1])
    o_sS = dout("o_sS", [16, 4, 128, 256])
    o_sconv = dout("o_sconv", [16, 2, DFF])

    gscr = dint("gscr", [8, 2048])
    gscs = dint("gscs", [8, 128])
    yscr = dint("yscr", [TM, D], BF16)
    x1scr = dint("x1scr", [TM, D])
    x2scr_ = dint("x2scr", [TM, D])
    dbg_out = {}

    with es:
        def sb(name, shape, dt=F32):
            return es.enter_context(nc.sbuf_tensor(name, list(shape), dt))

        def ps(name, shape, dt=F32):
            return es.enter_context(nc.psum_tensor(name, list(shape), dt))

        esems = {e: es.enter_context(nc.semaphore("s_" + e)) for e in ("pe", "act", "dve", "pool")}
        dsems = {
            "sp": [es.enter_context(nc.semaphore("d_sp%d" % i)) for i in range(24)],
            "pool": [es.enter_context(nc.semaphore("d_pl%d" % i)) for i in range(12)],
        }
        S = Sched(esems, dsems)

        def act(fn, r, w):
            S.op("act", fn, reads=r, writes=w)

        def dve(fn, r, w):
            S.op("dve", fn, reads=r, writes=w)

        def pool(fn, r, w):
            S.op("pool", fn, reads=r, writes=w)

        def sdma(out, in_, r, w):
            S.dma("sp", lambda e: e.dma_start(out=out, in_=in_), reads=r, writes=w)

        def mm(out_ap, outb, pairs, reads, first=True, last=True):
            n = len(pairs)
            for i, (l, r) in enumerate(pairs):
                S.op("pe", lambda e, l=l, r=r, i=i: e.matmul(out_ap, lhsT=l, rhs=r, start=(first and i == 0),
                                                            stop=(last and i == n - 1)),
                     reads=reads, writes=[outb], inc=(i == n - 1))

        def tr(out_ap, outb, in_ap, inb, ident, inc=True):
            S.op("pe", lambda e: e.transpose(out=out_ap, in_=in_ap, identity=ident), reads=[inb, "cb", "cf"],
                 writes=[outb], inc=inc)

        cb = sb("cb", [128, CB_W], BF16)
        cf = sb("cf", [128, CF_W], F32)
        spm = sb("spm", [128, SP_W], F32)
        sdma(cb[:], cst_bf[:, :], [], ["cb"])
        sdma(cf[:], cst_f[:, :], [], ["cf"])
        sdma(spm[:], smallp[:, :], [], ["spm"])
        identb = cb[:, 0:128]
        maskc = cb[:, 128:256]
        masks = cb[:, 256:384]
        qxmask = cb[:, 512:512 + 2048].rearrange("p (j t) -> p j t", j=16)
        vxmask = cb[:, 2560:2576]
        identf = cf[:, 0:128]
        onesf = cf[:, 128:256]
        resetm = cf[:, 256:384]
        negbig = cf[:, 384:512]
        zerocol = cf[:, 512:513]
        P_IB, P_FB, P_AB, P_HNM, P_HNG, P_CW, P_CB = 0, 1, 2, 6, 14, 22, 22 + 3 * KF
        ib64 = spm[0:64, P_IB:P_IB + 1]
        fb64 = spm[0:64, P_FB:P_FB + 1]
        wbc = sb("wbc", [128, D], F32)
        sdma(wbc[:], nmw.partition_broadcast(128), [], ["wbc"])
        flagt = sb("flagt", [1, 1], F32)
        sdma(flagt[:], flag[:, :], [], ["flagt"])

        hT = sb("hT", [128, KD, TM], BF16)
        hTp = sb("hTp", [128, KD, PRE], BF16)
        wsl = [sb("wsl%d" % i, [128, KD * 256], BF16) for i in range(3)]
        stat = sb("stat", [128, 64], F32)
        ARENA = 99 * 512 + 160
        arena = sb("arena", [128, ARENA], BF16)
        apos = [0]
        aphase = [0]

        def carve(name, shape, dt=BF16):
            n = int(np.prod(shape))
            nb = n * (2 if dt == BF16 else 4)
            nb = (nb + 63) // 64 * 64
            off = apos[0]
            apos[0] += nb // 2
            assert apos[0] <= ARENA, ("arena overflow", name, apos[0])
            v = arena[:, off:off + nb // 2]
            if dt != BF16:
                v = v.bitcast(dt)
            v = v[:, 0:n]
            if len(shape) == 2:
                pat, kw = "p (a b) -> p a b", dict(a=shape[0])
            elif len(shape) == 3:
                pat, kw = "p (a b c) -> p a b c", dict(a=shape[0], b=shape[1])
            else:
                pat, kw = None, None
            if pat:
                v = v.rearrange(pat, **kw)
            return v, "ar%d_%s" % (aphase[0], name)

        def arena_reset():
            S.barrier()
            print("arena phase %d used %d / %d" % (aphase[0], apos[0] * 2, ARENA * 2))
            apos[0] = 0
            aphase[0] += 1

        pa = ps("pa", [128, 512])
        pb = ps("pb", [128, 512])
        ptr = [ps("ptr%d" % i, [128, 1024], BF16) for i in range(2)]
        pst = ps("pst", [128, 512])
        pnum = ps("pnum", [128, 512])
        pstate = ps("pstate", [128, 512])
        pmisc = ps("pmisc", [128, 512])
        pbk = [(pa, "pa"), (pb, "pb")]
        pbk6 = [(pa, "pa"), (pb, "pb"), (pst, "pst"), (pnum, "pnum"), (pstate, "pstate"), (pmisc, "pmisc")]
        pbi = [0]
        wide = [False]

        def nextbank():
            pbi[0] += 1
            if wide[0]:
                return pbk6[pbi[0] % 6]
            return pbk[pbi[0] % 2]

        evi = [0]

        def evac(dst, dstb, src, srcb, func=None, scale=None, eng=None):
            if func is not None or eng == "act":
                f = func if func is not None else AF.Copy
                if scale is None:
                    act(lambda e: e.activation(out=dst, in_=src, func=f), [srcb], [dstb])
                else:
                    act(lambda e: e.activation(out=dst, in_=src, func=f, scale=scale), [srcb], [dstb])
                return
            evi[0] += 1
            if eng is None:
                eng = "act" if evi[0] % 2 == 0 else "dve"
            if eng == "act":
                if scale is None:
                    act(lambda e: e.copy(out=dst, in_=src), [srcb], [dstb])
                else:
                    act(lambda e: e.mul(out=dst, in_=src, mul=scale), [srcb], [dstb])
            else:
                if scale is None:
                    dve(lambda e: e.tensor_copy(out=dst, in_=src), [srcb], [dstb])
                else:
                    dve(lambda e: e.tensor_scalar(out=dst, in0=src, scalar1=scale, scalar2=None, op0=ALU.mult),
                        [srcb], [dstb])

        wi = [0]

        def load_w(packed, off, nk, ncols):
            i = wi[0] % 3
            wi[0] += 1
            name = "wsl%d" % i
            tot = nk * ncols
            flat = wsl[i][:, 0:tot]
            dst = flat.rearrange("p (k c) -> p k c", k=nk)
            hf = tot // 2
            S.dma("pool", lambda e: e.dma_start(out=flat[:, 0:hf], in_=packed[:, off:off + hf]), writes=[name])
            S.dma("pool", lambda e: e.dma_start(out=flat[:, hf:tot], in_=packed[:, off + hf:off + tot]),
                  writes=[name])
            return dst, name

        MG = [(0, 512), (512, 512), (1024, 256)]
        PG = [(0, 448), (448, 448)]

        def proj_fm(wt, wname, c0, M, src, srcname, t0, N):
            bank, bname = nextbank()
            mm(bank[0:M, 0:N], bname, [(wt[:, k, c0:c0 + M], src[:, k, t0:t0 + N]) for k in range(KD)],
               [wname, srcname])
            return bank[0:M, 0:N], bname

        def proj_tm(wt, wname, c0, N, src, srcname, ti):
            bank, bname = nextbank()
            mm(bank[:, 0:N], bname, [(src[:, k, ti * 128:(ti + 1) * 128], wt[:, k, c0:c0 + N]) for k in range(KD)],
               [wname, srcname])
            return bank[:, 0:N], bname

        xt = [carve("xt%d" % i, [D], F32) for i in range(4)]
        xn = [carve("xn%d" % i, [D], BF16) for i in range(4)]
        sq1, sq1b = carve("sq", [D], BF16)
        nstat = [0]

        def rstd_from_ss(ss, rs, b, n):
            dve(lambda e: e.tensor_scalar(out=rs, in0=ss, scalar1=1.0 / n, scalar2=EPS, op0=ALU.mult, op1=ALU.add),
                [b], [b + "r"])
            act(lambda e: e.activation(out=rs, in_=rs, func=AF.Sqrt), [b + "r"], [b + "r"])
            dve(lambda e: e.reciprocal(out=rs, in_=rs), [b + "r"], [b + "r"])

        def statslot():
            c = nstat[0] % 16
            nstat[0] += 1
            return stat[:, c:c + 1], stat[:, 16 + c:17 + c], stat[:, 32 + c:33 + c], "stat%d" % c

        def norm_to_T(x_t, xb, x_n, xnb, dstT, dstname, ti, sqj, sqb):
            ss, rs, _, sbn = statslot()
            act(lambda e: e.activation(out=sqj, in_=x_t, func=AF.Square, accum_out=ss), [xb], [sqb, sbn])
            rstd_from_ss(ss, rs, sbn, D)
            dve(lambda e: e.scalar_tensor_tensor(out=x_n, in0=x_t, scalar=rs, in1=wbc[:], op0=ALU.mult,
                                                 op1=ALU.mult), [xb, sbn + "r", "wbc"], [xnb])
            for half in range(2):
                pt = ptr[half]
                pbn = "ptr%d" % half
                for j in range(8):
                    k = half * 8 + j
                    tr(pt[:, j * 128:(j + 1) * 128], pbn, x_n[:, k * 128:(k + 1) * 128], xnb, identb, inc=(j == 7))
                dst = dstT[:, half * 8:(half + 1) * 8, ti * 128:(ti + 1) * 128]
                src = pt[:].rearrange("p (k t) -> p k t", k=8)
                evac(dst, dstname, src, pbn, eng=("act" if half == 0 else "dve"))

        for ti in range(NPT + NMT):
            par = ti % 4
            (x_t, xb), (x_n, xnb) = xt[par], xn[par]
            if ti < NPT:
                sdma(x_t, xpre[ti * 128:(ti + 1) * 128, :], [], [xb])
                norm_to_T(x_t, xb, x_n, xnb, hTp, "hTp", ti, sq1, sq1b)
            else:
                tj = ti - NPT
                sdma(x_t, xmain[tj * 128:(tj + 1) * 128, :], [], [xb])
                norm_to_T(x_t, xb, x_n, xnb, hT, "hT", tj, sq1, sq1b)

        arena_reset()
        if stop_after == "p1":
            return finish(nc, S, dbg_out)
        Cst = carve("Cst", [2, 2, 257], F32)[0]
        Sst = carve("Sst", [2, 256], F32)[0]
        wtok = carve("wtok", [64], F32)[0]
        ftok = carve("ftok", [64], F32)[0]
        decbc = carve("decbc", [64], F32)[0]
        wtoks = carve("wtoks", [4], F32)[0]
        ftoks = carve("ftoks", [4], F32)[0]
        sdecbc = carve("sdecbc", [64], F32)[0]

        galT, galb = carve("galT", [PRE + TM], F32)
        BT, BTb = carve("BT", [PRE + TM], F32)
        gst, gstb = BT[:, 0:512], BTb
        QT, QTb = carve("QT", [2, TM])
        KT, KTb = carve("KT", [2, TM])
        Vt, Vtb = carve("Vt", [NMT, 257])
        OG, OGb = carve("OG", [NMT, 256])
        KTp, KTpb = carve("KTp", [2, PRE])
        Vp, Vpb = carve("Vp", [NPT, 257])
        Cbf, Cbfb = carve("Cbf", [2, 257])
        QTg_, QTgb = carve("QTg", [TM])
        KTg_, KTgb = carve("KTg", [TM])
        Vtg, Vtgb = carve("Vtg", [NMT, 256])
        RG, RGb = carve("RG", [NMT, 256])
        Ktok2 = [carve("Ktok%d" % i, [256]) for i in range(2)]
        Vw2 = [carve("Vw%d" % i, [257]) for i in range(2)]
        PT2 = [carve("PT%d" % i, [128]) for i in range(2)]
        ytok2 = [carve("ytok%d" % i, [256]) for i in range(2)]
        eqt2 = [carve("eqt%d" % i, [128], F32) for i in range(2)]
        QtT2 = [carve("QtT%d" % i, [128]) for i in range(2)]
        KtT2 = [carve("KtT%d" % i, [128]) for i in range(2)]
        KhT2 = [carve("KhT%d" % i, [128]) for i in range(2)]
        Khtok2 = [carve("Khtok%d" % i, [128]) for i in range(2)]
        Sbf, Sbfb = carve("Sbf", [256])
        nT, nTb = carve("nT", [2, 64], F32)
        nTo, nTob = carve("nTo", [2, 64], F32)
        nrow, nrowb = carve("nrow", [256], F32)
        junk, junkb = nrow.bitcast(BF16)[:, 0:256], nrowb
        SG = 2
        aups, aupb = carve("aups", [512], F32)
        negab = carve("negab", [4], F32)[0]
        dSs, dSsb = carve("dSs", [16], F32)
        gmark = apos[0]
        gat = carve("gat", [8, 128], F32)[0][0:64]
        gas = carve("gas", [8, 8], F32)[0][0:64]
        grow = carve("grow", [8, 64], F32)[0][0:1]
        wg, wgn = load_w(w_in, WOFF[C_MI], KD, 8)
        wl, wln = load_w(w_in, WOFF[C_GAL], KD, 16)
        MGG = [(0, 512), (512, 512), (1024, 128), (1152, 128)]
        for (src, sname, groups, base) in ((hTp, "hTp", PG, 0), (hT, "hT", MGG, PRE)):
            for (t0, N) in groups:
                pv, pn_ = proj_fm(wg, wgn, 0, 8, src, sname, t0, N)
                evac(gst[0:8, 0:N], gstb, pv, pn_, eng="act")
                if base + t0 < 2048:
                    sdma(gscr[:, base + t0:base + t0 + N], gst[0:8, 0:N], [gstb], ["gscr"])
                else:
                    sdma(gscs[:, :], gst[0:8, 0:N], [gstb], ["gscs"])
                pv, pn_ = proj_fm(wl, wln, 0, 16, src, sname, t0, N)
                evac(galT[0:16, base + t0:base + t0 + N], galb, pv, pn_, eng="dve")
        GI, GF, GB_, GA, GW, GFL, GT1, GT2 = range(8)

        def gt(i):
            return gat[:, i, :]

        def gs_(i):
            return gas[:, i, :]
        NTOK = PRE + TM - 128
        sdma(gt(GI), gscr[0:4, :].rearrange("h (c l) -> (h c) l", l=128), ["gscr"], ["g_i"])
        sdma(gt(GF), gscr[4:8, :].rearrange("h (c l) -> (h c) l", l=128), ["gscr"], ["g_f"])
        sdma(gs_(GI), gscs[0:4, :].rearrange("h (j l) -> (h j) l", l=8), ["gscs"], ["s_i"])
        sdma(gs_(GF), gscs[4:8, :].rearrange("h (j l) -> (h j) l", l=8), ["gscs"], ["s_f"])
        negfb = stat[0:64, 48:49]
        dve(lambda e: e.tensor_scalar(out=negfb, in0=fb64, scalar1=-1.0, scalar2=None, op0=ALU.mult),
            ["spm"], ["negfb"])

        def gate_math(T, pfx, L):
            i_, f_, b_, a_ = T(GI), T(GF), T(GB_), T(GA)
            act(lambda e: e.activation(out=f_, in_=f_, func=AF.Exp, bias=negfb, scale=-1.0),
                [pfx + "f", "negfb"], [pfx + "f"])
            act(lambda e: e.activation(out=f_, in_=f_, func=AF.Ln, bias=1.0), [pfx + "f"], [pfx + "f"])
            dve(lambda e: e.tensor_scalar(out=f_, in0=f_, scalar1=-1.0, scalar2=None, op0=ALU.mult),
                [pfx + "f"], [pfx + "f"])
            dve(lambda e: e.tensor_tensor_scan(out=b_, data0=onesf[0:64, 0:L], data1=f_, initial=0.0,
                                               op0=ALU.mult, op1=ALU.add), [pfx + "f", "cf"], [pfx + "b"])
            dve(lambda e: e.scalar_tensor_tensor(out=a_, in0=i_, scalar=ib64, in1=b_, op0=ALU.add,
                                                 op1=ALU.subtract), [pfx + "i", pfx + "b", "spm"], [pfx + "a"])
        gate_math(gt, "g_", 128)
        gate_math(gs_, "s_", 8)
        amax = stat[0:64, 49:50]
        bend = gat[:, GB_, 127:128]
        amaxb = stat[0:64, 50:51]
        dve(lambda e: e.reduce_max(out=amax, in_=gt(GA), axis=AX.X), ["g_a"], ["amax"])
        dve(lambda e: e.tensor_tensor(out=amaxb, in0=amax, in1=bend, op=ALU.add), ["amax", "g_b"], ["amaxb"])
        tr(pmisc[0:1, 0:64], "pmisc", bend, "g_b", identf[0:64, 0:64], inc=False)
        tr(pmisc[0:1, 64:128], "pmisc", amaxb, "amaxb", identf[0:64, 0:64])
        R_BE, R_AB, R_MN, R_MP, R_G, R_DEC = range(6)

        def gr(i):
            return grow[0:1, i, :]
        evac(grow[0:1, 0:2, :], "grow", pmisc[0:1, 0:128].rearrange("p (a b) -> p a b", a=2), "pmisc", eng="dve")
        for h in range(4):
            for seg in range(2):
                lo = h * 16 + seg * 8
                if seg == 0:
                    init = 0.0
                    rd = ["grow"]
                else:
                    mf = stat[0:1, 51 + h:52 + h]
                    dve(lambda e, h=h, mf=mf: e.tensor_tensor(out=mf, in0=grow[0:1, R_MN, h * 16 + 7:h * 16 + 8],
                                                             in1=flagt[0:1, 0:1], op=ALU.mult),
                        ["grow", "flagt"], ["mf%d" % h])
                    init = mf
                    rd = ["grow", "mf%d" % h]
                dve(lambda e, lo=lo, init=init: e.tensor_tensor_scan(
                    out=grow[0:1, R_MN, lo:lo + 8], data0=grow[0:1, R_BE, lo:lo + 8],
                    data1=grow[0:1, R_AB, lo:lo + 8], initial=init, op0=ALU.add, op1=ALU.max), rd, ["grow"])
                if seg == 0:
                    dve(lambda e, lo=lo: e.memset(grow[0:1, R_MP, lo:lo + 1], 0.0), [], ["grow"])
                else:
                    dve(lambda e, lo=lo, mf=mf: e.tensor_copy(out=grow[0:1, R_MP, lo:lo + 1], in_=mf),
                        ["mf%d" % h], ["grow"])
                dve(lambda e, lo=lo: e.tensor_copy(out=grow[0:1, R_MP, lo + 1:lo + 8],
                                                   in_=grow[0:1, R_MN, lo:lo + 7]), ["grow"], ["grow"])
        dve(lambda e: e.tensor_tensor(out=gr(R_G), in0=gr(R_MN), in1=gr(R_BE), op=ALU.subtract), ["grow"], ["grow"])
        dve(lambda e: e.tensor_tensor(out=gr(R_DEC), in0=gr(R_MP), in1=gr(R_G), op=ALU.subtract), ["grow"], ["grow"])
        act(lambda e: e.activation(out=gr(R_DEC), in_=gr(R_DEC), func=AF.Exp), ["grow"], ["grow"])
        dve(lambda e: e.tensor_scalar(out=gr(R_G), in0=gr(R_G), scalar1=-1.0, scalar2=None, op0=ALU.mult),
            ["grow"], ["grow"])
        mm(pmisc[:, 128:192], "pmisc", [(onesf[0:1, 0:128], gr(R_DEC))], ["grow", "cf"])
        evac(decbc[:], "decbc", pmisc[:, 128:192], "pmisc", eng="dve")
        mm(pmisc[0:64, 192:193], "pmisc", [(gr(R_G), onesf[0:1, 0:1])], ["grow", "cf"])
        negG = stat[0:64, 56:57]
        evac(negG, "negG", pmisc[0:64, 192:193], "pmisc", eng="dve")
        sdma(o_pm[:, :], gr(R_MN), ["grow"], [])
        act(lambda e: e.activation(out=gt(GW), in_=gt(GA), func=AF.Exp, bias=negG), ["g_a", "negG"], ["g_w"])
        act(lambda e: e.activation(out=gt(GFL), in_=gt(GB_), func=AF.Exp, bias=negG, scale=-1.0),
            ["g_b", "negG"], ["g_fl"])
        tr(pmisc[:, 256:320], "pmisc", gt(GW), "g_w", identf[0:64, 0:64], inc=False)
        tr(pmisc[:, 320:384], "pmisc", gt(GFL), "g_fl", identf[0:64, 0:64])
        evac(wtok[:], "wtok", pmisc[:, 256:320], "pmisc", eng="dve")
        evac(ftok[:], "ftok", pmisc[:, 320:384], "pmisc", eng="dve")
        smc = stat[0:64, 57:58]
        sdma(smc, smcol[:, :], [], ["smc"])
        samax = stat[0:64, 58:59]
        sG = stat[0:64, 59:60]
        snegG = stat[0:64, 60:61]
        sdec = stat[0:64, 61:62]
        smn = stat[0:64, 62:63]
        dve(lambda e: e.reduce_max(out=samax, in_=gs_(GA), axis=AX.X), ["s_a"], ["samax"])
        dve(lambda e: e.tensor_tensor(out=sG, in0=samax, in1=smc, op=ALU.max), ["samax", "smc"], ["sG"])
        dve(lambda e: e.tensor_tensor(out=smn, in0=sG, in1=gas[:, GB_, 7:8], op=ALU.add), ["sG", "s_b"], ["smn"])
        sdma(o_sm[:, :], smn, ["smn"], [])
        dve(lambda e: e.tensor_tensor(out=sdec, in0=smc, in1=sG, op=ALU.subtract), ["smc", "sG"], ["sdec"])
        act(lambda e: e.activation(out=sdec, in_=sdec, func=AF.Exp), ["sdec"], ["sdec"])
        dve(lambda e: e.tensor_scalar(out=snegG, in0=sG, scalar1=-1.0, scalar2=None, op0=ALU.mult), ["sG"], ["snegG"])
        act(lambda e: e.activation(out=gs_(GW), in_=gs_(GA), func=AF.Exp, bias=snegG), ["s_a", "snegG"], ["s_w"])
        act(lambda e: e.activation(out=gs_(GFL), in_=gs_(GB_), func=AF.Exp, bias=snegG, scale=-1.0),
            ["s_b", "snegG"], ["s_fl"])
        dg = gat[:, GT1, 0:64]
        dve(lambda e: e.tensor_scalar(out=dg, in0=identf[0:64, 0:64], scalar1=sdec, scalar2=None, op0=ALU.mult),
            ["sdec", "cf"], ["dg"])
        mm(pmisc[:, 384:448], "pmisc", [(onesf[0:64, 0:128], dg)], ["dg", "cf"])
        evac(sdecbc[:], "sdecbc", pmisc[:, 384:448], "pmisc", eng="dve")
        sdma(gscs[0:4, :].rearrange("h (j l) -> (h j) l", l=8), gs_(GW), ["s_w"], ["gscs"])
        sdma(gscs[4:8, :].rearrange("h (j l) -> (h j) l", l=8), gs_(GFL), ["s_fl"], ["gscs"])
        wfr = gat[0:8, GT2, :]
        sdma(wfr, gscs[0:8, :], ["gscs"], ["wfr"])
        tr(pmisc[:, 448:456], "pmisc", wfr, "wfr", identf[0:8, 0:8])
        evac(wtoks[:], "wtoks", pmisc[:, 448:452], "pmisc", eng="dve")
        evac(ftoks[:], "ftoks", pmisc[:, 452:456], "pmisc", eng="dve")

        if stop_after == "gates":
            if debug.get("gates"):
                for nm, t_, shp in (("wtok", wtok, [128, 64]), ("ftok", ftok, [128, 64]), ("decbc", decbc, [128, 64]),
                                    ("wtoks", wtoks, [128, 4]), ("sdecbc", sdecbc, [128, 64])):
                    dbg_out[nm] = dout("dbg_" + nm, shp)
                    sdma(dbg_out[nm][:, :], t_[:], [nm], [])
            return finish(nc, S, dbg_out)


        dve(lambda e: e.memset(Vt[:, :, 256:257], 1.0), [], [Vtb])
        dve(lambda e: e.memset(Vp[:, :, 256:257], 1.0), [], [Vpb])
        sdma(nrow[0:64, :], sn[:, :], [], [nrowb])
        for dh in range(2):
            tr(pmisc[:, dh * 64:(dh + 1) * 64], "pmisc", nrow[0:64, dh * 128:(dh + 1) * 128], nrowb,
               identf[0:64, 0:64], inc=(dh == 1))
        evac(nT, nTb, pmisc[:, 0:128].rearrange("p (a b) -> p a b", a=2), "pmisc", eng="dve")

        def head_epilogue(num_ap, numb, scale_pre, gate_ap, gateb, tile_i, dstcol0, hnb, ytok, ytokb):
            ss, rs, sc, sbn = statslot()
            if scale_pre is None:
                act(lambda e: e.activation(out=junk, in_=num_ap, func=AF.Square, accum_out=ss), [numb],
                    [junkb, sbn])
            else:
                act(lambda e: e.activation(out=junk, in_=num_ap, func=AF.Square, scale=scale_pre, accum_out=ss),
                    [numb, hnb], [junkb, sbn])
            rstd_from_ss(ss, rs, sbn, 256)
            if scale_pre is not None:
                dve(lambda e: e.tensor_tensor(out=rs, in0=rs, in1=scale_pre, op=ALU.mult), [sbn + "r", hnb],
                    [sbn + "r"])
            dve(lambda e: e.scalar_tensor_tensor(out=ytok, in0=num_ap, scalar=rs, in1=gate_ap, op0=ALU.mult,
                                                 op1=ALU.mult), [numb, sbn + "r", gateb], [ytokb])
            sdma(yscr[tile_i * 128:(tile_i + 1) * 128, dstcol0:dstcol0 + 256], ytok, [ytokb], ["yscr"])

        def mlstm_proj(h):
            wq, wqn = load_w(w_in, WOFF[C_MQ + h * 256], KD, 256)
            wk, wkn = load_w(w_in, WOFF[C_MK + h * 256], KD, 256)
            wv, wvn = load_w(w_in, WOFF[C_MV + h * 256], KD, 256)
            for dh in range(2):
                for (t0, N) in MG:
                    pv, pn_ = proj_fm(wq, wqn, dh * 128, 128, hT, "hT", t0, N)
                    evac(QT[:, dh, t0:t0 + N], QTb, pv, pn_)
            for dh in range(2):
                for (t0, N) in MG:
                    pv, pn_ = proj_fm(wk, wkn, dh * 128, 128, hT, "hT", t0, N)
                    evac(KT[:, dh, t0:t0 + N], KTb, pv, pn_, scale=0.0625)
                for (t0, N) in PG:
                    pv, pn_ = proj_fm(wk, wkn, dh * 128, 128, hTp, "hTp", t0, N)
                    evac(KTp[:, dh, t0:t0 + N], KTpb, pv, pn_, scale=0.0625)
            wo, won = load_w(w_in, WOFF[C_MO + h * 256], KD, 256)
            for ti in range(NPT):
                pv, pn_ = proj_tm(wv, wvn, 0, 256, hTp, "hTp", ti)
                evac(Vp[:, ti, 0:256], Vpb, pv, pn_)
            for ti in range(NMT):
                pv, pn_ = proj_tm(wv, wvn, 0, 256, hT, "hT", ti)
                evac(Vt[:, ti, 0:256], Vtb, pv, pn_)
            for ti in range(NMT):
                pv, pn_ = proj_tm(wo, won, 0, 256, hT, "hT", ti)
                evac(OG[:, ti, :], OGb, pv, pn_, func=AF.Sigmoid)
        csi = [0]

        def mlstm_chunks(h):
            Ch = Cst[:, h % 2, :, :]
            Chb = "Cst%d" % (h % 2)
            dve(lambda e: e.memset(Ch, 0.0), [], [Chb])
            def mchunk(c):
                full = c >= NPT
                samp = c == NCH
                if c < NPT:
                    ktsrc, ktb, vsrc, vb_, ti = KTp, KTpb, Vp, Vpb, c
                else:
                    ktsrc, ktb, vsrc, vb_, ti = KT, KTb, Vt, Vtb, c - NPT
                tk = slice(ti * 128, (ti + 1) * 128)
                (Ktok, Ktokb), (Vw, Vwb), (PT, PTb), (ytok, ytokb) = Ktok2[c % 2], Vw2[c % 2], PT2[c % 2], ytok2[c % 2]
                if samp:
                    wcol, fcol = wtoks[:, h:h + 1], ftoks[:, h:h + 1]
                    wcb, fcb = "wtoks", "ftoks"
                else:
                    wcol, fcol = wtok[:, h * 16 + c:h * 16 + c + 1], ftok[:, h * 16 + c:h * 16 + c + 1]
                    wcb, fcb = "wtok", "ftok"
                for dh in range(2):
                    tr(ptr[0][:, dh * 128:(dh + 1) * 128], "ptr0", ktsrc[:, dh, tk], ktb, identb, inc=(dh == 1))
                evac(Ktok, Ktokb, ptr[0][:, 0:256], "ptr0")
                dve(lambda e, vsrc=vsrc, ti=ti, wcol=wcol: e.tensor_scalar(out=Vw, in0=vsrc[:, ti, :], scalar1=wcol,
                                                                            scalar2=None, op0=ALU.mult),
                     [vb_, wcb], [Vwb])
                if not samp:
                    dcol = decbc[:, h * 16 + c:h * 16 + c + 1]
                    dve(lambda e, dcol=dcol: e.tensor_scalar(out=Ch, in0=Ch, scalar1=dcol, scalar2=None,
                                                             op0=ALU.mult), [Chb, "decbc"], [Chb])
                if full:
                    mm(pst[:, 0:128], "pst", [(ktsrc[:, dh, tk], QT[:, dh, tk]) for dh in range(2)], [ktb, QTb])
                    msk = masks if samp else maskc
                    dve(lambda e, msk=msk: e.tensor_tensor(out=PT, in0=pst[:, 0:128], in1=msk, op=ALU.mult),
                        ["pst", "cb"], [PTb])
                    if not samp:
                        act(lambda e: e.copy(out=Cbf, in_=Ch), [Chb], [Cbfb])
                        mm(pnum[:, 0:257], "pnum",
                           [(PT, Vw)] + [(QT[:, dh, tk], Cbf[:, dh, :]) for dh in range(2)],
                           [PTb, Vwb, QTb, Cbfb])
                    else:
                        mm(pnum[:, 0:257], "pnum", [(PT, Vw)], [PTb, Vwb], first=True, last=False)
                if not samp:
                    for dh in range(2):
                        mm(pstate[:, 0:257], "pstate", [(Ktok[:, dh * 128:(dh + 1) * 128], Vw)], [Ktokb, Vwb])
                        dve(lambda e, dh=dh: e.tensor_tensor(out=Ch[:, dh, :], in0=Ch[:, dh, :],
                                                             in1=pstate[:, 0:257], op=ALU.add),
                            ["pstate", Chb], [Chb])
                else:
                    def sgrp(g, Cs, Csb, Csbf, Csbfb, QX, QXb, VwX, VwXb):
                        js = slice(g * SG, (g + 1) * SG)
                        for dh in range(2):
                            sdma(Cs[:, :, dh, 0:256],
                                 sC[js, h, dh * 128:(dh + 1) * 128, :].rearrange("j p e -> p j e"), [], [Csb])
                        ncols = nT[:, :, g * SG * 4 + h:(g + 1) * SG * 4:4].rearrange("p dh j -> p j dh")
                        dve(lambda e, ncols=ncols: e.tensor_copy(out=Cs[:, :, :, 256], in_=ncols), [nTb], [Csb])
                        dcols = sdecbc[:, h * 16 + g * SG:h * 16 + (g + 1) * SG]
                        Csv = Cs.rearrange("p j dh e -> p j (dh e)")
                        dve(lambda e, dcols=dcols, Csv=Csv: e.tensor_tensor(
                            out=Csv, in0=Csv, in1=dcols.unsqueeze(2).broadcast_to([128, SG, 514]), op=ALU.mult),
                            [Csb, "sdecbc"], [Csb])
                        act(lambda e: e.copy(out=Csbf, in_=Cs), [Csb], [Csbfb])
                        for dh in range(2):
                            dve(lambda e, dh=dh, js=js, tk=tk: e.tensor_tensor(
                                out=QX[:, dh, :, :], in0=QT[:, dh, tk].unsqueeze(1).broadcast_to([128, SG, 128]),
                                in1=qxmask[:, js, :], op=ALU.mult), [QTb, "cb"], [QXb])
                        pairs = [(QX[:, dh, j, :], Csbf[:, j, dh, :]) for j in range(SG) for dh in range(2)]
                        mm(pnum[:, 0:257], "pnum", pairs, [QXb, Csbfb], first=False, last=(g == 16 // SG - 1))
                        dve(lambda e, js=js: e.tensor_tensor(
                            out=VwX, in0=Vw.unsqueeze(1).broadcast_to([128, SG, 257]),
                            in1=vxmask[:, js].unsqueeze(2).broadcast_to([128, SG, 257]), op=ALU.mult),
                            [Vwb, "cb"], [VwXb])
                        for j in range(SG):
                            for dh in range(2):
                                mm(pstate[:, 0:257], "pstate", [(Ktok[:, dh * 128:(dh + 1) * 128], VwX[:, j, :])],
                                   [Ktokb, VwXb])
                                dve(lambda e, j=j, dh=dh: e.tensor_tensor(out=Cs[:, j, dh, :], in0=Cs[:, j, dh, :],
                                                                          in1=pstate[:, 0:257], op=ALU.add),
                                    ["pstate", Csb], [Csb])
                        for dh in range(2):
                            sdma(o_sC[js, h, dh * 128:(dh + 1) * 128, :].rearrange("j p e -> p j e"),
                                 Cs[:, :, dh, 0:256], [Csb], [])
                        ndst = nTo[:, :, g * SG * 4 + h:(g + 1) * SG * 4:4].rearrange("p dh j -> p j dh")
                        dve(lambda e, ndst=ndst: e.tensor_copy(out=ndst, in_=Cs[:, :, :, 256]), [Csb], [nTob])
                    for g in range(16 // SG):
                        k_ = csi[0]
                        csi[0] += 1
                        sgrp(g, *Cs3[k_ % 3], *Csbf2[k_ % 2], *QX2[k_ % 2], *VwX2[k_ % 2])
                if full:
                    ss, rs, sc, sbn = statslot()
                    dve(lambda e, sc=sc: e.tensor_copy(out=sc, in_=pnum[:, 256:257]), ["pnum"], [sbn + "s"])
                    dve(lambda e, sc=sc: e.scalar_tensor_tensor(out=sc, in0=sc, scalar=-1.0, in1=sc, op0=ALU.mult,
                                                                op1=ALU.max), [sbn + "s"], [sbn + "s"])
                    dve(lambda e, fcol=fcol, sc=sc: e.tensor_tensor(out=sc, in0=sc, in1=fcol, op=ALU.max),
                        [sbn + "s", fcb], [sbn + "s"])
                    dve(lambda e, sc=sc: e.reciprocal(out=sc, in_=sc), [sbn + "s"], [sbn + "s"])
                    head_epilogue(pnum[:, 0:256], "pnum", sc, OG[:, ti, :], OGb, ti, h * 256, sbn + "s", ytok, ytokb)
                if c == NCH - 1:
                    sdma(o_pC[h].rearrange("(dh p) e -> p dh e", p=128), Ch[:, :, 0:256], [Chb], [])
            for c in range(NCH + 1):
                mchunk(c)
            tr(pmisc[0:2, 0:128], "pmisc", Ch[:, :, 256], Chb, identf)
            evac(nrow[0:2, 0:128], nrowb, pmisc[0:2, 0:128], "pmisc", eng="dve")
            sdma(o_pn[h * 2:h * 2 + 2, :], nrow[0:2, 0:128], [nrowb], [])

        aupt = stat
        sdma(aups[0:16, :], aup[:, :], [], [aupb])
        dve(lambda e: e.tensor_scalar(out=negab[:, 0:4], in0=spm[:, P_AB:P_AB + 4], scalar1=-1.0, scalar2=None,
                                      op0=ALU.mult), ["spm"], ["negab"])
        BG = [(0, 512), (512, 512), (1024, 512), (1536, 512), (2048, 128)]

        def gla_head(h):
            wq, wqn = load_w(w_in, WOFF[C_GQ + h * 128], KD, 128)
            wk, wkn = load_w(w_in, WOFF[C_GK + h * 128], KD, 128)
            wv, wvn = load_w(w_in, WOFF[C_GV + h * 256], KD, 256)
            QTg, KTg, KTpg = QTg_, KTg_, KTp[:, 0, :]
            QTb, KTb, Vt, Vtb, OG, OGb = QTgb, KTgb, Vtg, Vtgb, RG, RGb
            for (t0, N) in MG:
                pv, pn_ = proj_fm(wq, wqn, 0, 128, hT, "hT", t0, N)
                evac(QTg[:, t0:t0 + N], QTb, pv, pn_, scale=128.0 ** -0.5)
            for (t0, N) in MG:
                pv, pn_ = proj_fm(wk, wkn, 0, 128, hT, "hT", t0, N)
                evac(KTg[:, t0:t0 + N], KTb, pv, pn_)
            for (t0, N) in PG:
                pv, pn_ = proj_fm(wk, wkn, 0, 128, hTp, "hTp", t0, N)
                evac(KTpg[:, t0:t0 + N], KTpb, pv, pn_)
            wr, wrn = load_w(w_in, WOFF[C_GR + h * 256], KD, 256)
            for ti in range(NPT):
                pv, pn_ = proj_tm(wv, wvn, 0, 256, hTp, "hTp", ti)
                evac(Vp[:, ti, 0:256], Vpb, pv, pn_)
            for ti in range(NMT):
                pv, pn_ = proj_tm(wv, wvn, 0, 256, hT, "hT", ti)
                evac(Vt[:, ti, 0:256], Vtb, pv, pn_)
            for ti in range(NMT):
                pv, pn_ = proj_tm(wr, wrn, 0, 256, hT, "hT", ti)
                evac(OG[:, ti, :], OGb, pv, pn_, func=AF.Silu)
            nab = negab[:, h:h + 1]
            for (t0, N) in BG:
                mm(pmisc[:, 0:N], "pmisc", [(aups[0:16, h * 128:(h + 1) * 128], galT[0:16, t0:t0 + N])],
                   [aupb, galb])
                act(lambda e, t0=t0, N=N: e.activation(out=BT[:, t0:t0 + N], in_=pmisc[:, 0:N], func=AF.Exp,
                                                       bias=nab, scale=-1.0), ["pmisc", "negab"], [BTb])
            act(lambda e: e.activation(out=BT, in_=BT, func=AF.Ln, bias=1.0), [BTb], [BTb])
            dve(lambda e: e.tensor_scalar(out=BT, in0=BT, scalar1=-1.0 / 16.0, scalar2=None, op0=ALU.mult),
                [BTb], [BTb])
            dve(lambda e: e.tensor_tensor_scan(out=BT[:, 0:2048], data0=onesf[:, 0:1].broadcast_to([128, 2048]),
                                               data1=BT[:, 0:2048], initial=0.0, op0=ALU.mult, op1=ALU.add),
                [BTb, "cf"], [BTb])
            dve(lambda e: e.tensor_tensor_scan(out=BT[:, 2048:2176], data0=resetm, data1=BT[:, 2048:2176],
                                               initial=0.0, op0=ALU.mult, op1=ALU.add), [BTb, "cf"], [BTb])
            Sh = Sst[:, h % 2, :]
            Shb = "Sst%d" % (h % 2)
            dve(lambda e: e.memset(Sh, 0.0), [], [Shb])

            def gchunk(c):
                full = c >= NPT
                samp = c == NCH
                if c < NPT:
                    ktsrc, ktb, vsrc, vb_, ti = KTpg, KTpb, Vp, Vpb, c
                else:
                    ktsrc, ktb, vsrc, vb_, ti = KTg, KTb, Vt, Vtb, c - NPT
                tk = slice(ti * 128, (ti + 1) * 128)
                tb = slice(2048, 2176) if samp else slice(c * 128, (c + 1) * 128)
                (eqt, eqtb), (QtT, QtTb), (KtT, KtTb), (KhT, KhTb) = eqt2[c % 2], QtT2[c % 2], KtT2[c % 2], KhT2[c % 2]
                (Ktok, Ktokb), (PT, PTb), (ytok, ytokb) = Khtok2[c % 2], PT2[c % 2], ytok2[c % 2]
                ss, rs, sc, sbn = statslot()
                if samp:
                    bend3 = BT[:, 2048 + 7:2176:8].unsqueeze(2).broadcast_to([128, 16, 8])
                    dve(lambda e: e.tensor_tensor(out=eqt.rearrange("p (j l) -> p j l", l=8), in0=bend3,
                                                  in1=BT[:, tb].rearrange("p (j l) -> p j l", l=8),
                                                  op=ALU.subtract), [BTb], [eqtb])
                    act(lambda e: e.activation(out=eqt, in_=eqt, func=AF.Exp), [eqtb], [eqtb])
                    act(lambda e: e.activation(out=dSs, in_=BT[:, 2048 + 7:2176:8], func=AF.Exp), [BTb], [dSsb])
                else:
                    bendc = BT[:, c * 128 + 127:c * 128 + 128]
                    bstc = zerocol if c == 0 else BT[:, c * 128 - 1:c * 128]
                    act(lambda e: e.activation(out=eqt, in_=BT[:, tb], func=AF.Exp, bias=bendc, scale=-1.0),
                        [BTb], [eqtb])
                    dve(lambda e: e.tensor_tensor(out=ss, in0=bendc, in1=bstc, op=ALU.subtract), [BTb, "cf"],
                        [sbn])
                    act(lambda e: e.activation(out=ss, in_=ss, func=AF.Exp), [sbn], [sbn])
                    dve(lambda e: e.tensor_scalar(out=rs, in0=bstc, scalar1=-1.0, scalar2=None, op0=ALU.mult),
                        [BTb, "cf"], [sbn + "r"])
                dve(lambda e: e.tensor_tensor(out=KhT, in0=ktsrc[:, tk], in1=eqt, op=ALU.mult), [ktb, eqtb], [KhTb])
                tr(ptr[0][:, 0:128], "ptr0", KhT, KhTb, identb)
                evac(Ktok[:, 0:128], Ktokb, ptr[0][:, 0:128], "ptr0")
                if full:
                    if samp:
                        act(lambda e: e.activation(out=eqt, in_=BT[:, tb], func=AF.Exp), [BTb, KhTb], [eqtb])
                    else:
                        act(lambda e: e.activation(out=eqt, in_=BT[:, tb], func=AF.Exp, bias=rs), [BTb, sbn + "r", KhTb],
                            [eqtb])
                    dve(lambda e: e.tensor_tensor(out=QtT, in0=QTg[:, tk], in1=eqt, op=ALU.mult), [QTb, eqtb], [QtTb])
                    if samp:
                        act(lambda e: e.activation(out=eqt, in_=BT[:, tb], func=AF.Exp, scale=-1.0), [BTb, QtTb],
                            [eqtb])
                    else:
                        act(lambda e: e.activation(out=eqt, in_=BT[:, tb], func=AF.Exp, bias=bstc, scale=-1.0),
                            [BTb, QtTb, "cf"], [eqtb])
                    dve(lambda e: e.tensor_tensor(out=KtT, in0=ktsrc[:, tk], in1=eqt, op=ALU.mult), [ktb, eqtb],
                        [KtTb])
                    mm(pst[:, 0:128], "pst", [(KtT, QtT)], [KtTb, QtTb])
                    msk = masks if samp else maskc
                    dve(lambda e: e.tensor_tensor(out=PT, in0=pst[:, 0:128], in1=msk, op=ALU.mult), ["pst", "cb"],
                        [PTb])
                    if not samp:
                        mm(pnum[:, 0:256], "pnum", [(PT, vsrc[:, ti, 0:256]), (QtT, Sbf)], [PTb, vb_, QtTb, Sbfb])
                    else:
                        mm(pnum[:, 0:256], "pnum", [(PT, vsrc[:, ti, 0:256])], [PTb, vb_], first=True, last=False)
                if not samp:
                    mm(pstate[:, 0:256], "pstate", [(Ktok[:, 0:128], vsrc[:, ti, 0:256])], [Ktokb, vb_])
                    dve(lambda e: e.scalar_tensor_tensor(out=Sh, in0=Sh, scalar=ss, in1=pstate[:, 0:256],
                                                         op0=ALU.mult, op1=ALU.add), [Shb, sbn, "pstate"], [Shb])
                    act(lambda e: e.copy(out=Sbf, in_=Sh), [Shb], [Sbfb])
                else:
                    def sgrp(g, Cs, Csb, Csbf, Csbfb, QX, QXb, VwX, VwXb):
                        js = slice(g * SG, (g + 1) * SG)
                        Ss = Cs[:, :, 0, 0:256]
                        Ssb = Csbf[:, :, 0, 0:256]
                        sdma(Ss, sS[js, h].rearrange("j p e -> p j e"), [], [Csb])
                        act(lambda e, Ss=Ss, Ssb=Ssb: e.copy(out=Ssb, in_=Ss), [Csb], [Csbfb])
                        dve(lambda e, js=js: e.tensor_tensor(
                            out=QX[:, 0, :, :], in0=QtT.unsqueeze(1).broadcast_to([128, SG, 128]),
                            in1=qxmask[:, js, :], op=ALU.mult), [QtTb, "cb"], [QXb])
                        mm(pnum[:, 0:256], "pnum", [(QX[:, 0, j, :], Ssb[:, j, :]) for j in range(SG)],
                           [QXb, Csbfb], first=False, last=(g == 16 // SG - 1))
                        dve(lambda e, js=js: e.tensor_tensor(
                            out=VwX[:, :, 0:256], in0=vsrc[:, ti, 0:256].unsqueeze(1).broadcast_to([128, SG, 256]),
                            in1=vxmask[:, js].unsqueeze(2).broadcast_to([128, SG, 256]), op=ALU.mult),
                            [vb_, "cb"], [VwXb])
                        for j in range(SG):
                            mm(pstate[:, 0:256], "pstate", [(Ktok[:, 0:128], VwX[:, j, 0:256])], [Ktokb, VwXb])
                            dcol = dSs[:, g * SG + j:g * SG + j + 1]
                            dve(lambda e, j=j, dcol=dcol, Ss=Ss: e.scalar_tensor_tensor(
                                out=Ss[:, j, :], in0=Ss[:, j, :], scalar=dcol, in1=pstate[:, 0:256], op0=ALU.mult,
                                op1=ALU.add), [Csb, dSsb, "pstate"], [Csb])
                        sdma(o_sS[js, h].rearrange("j p e -> p j e"), Ss, [Csb], [])
                    for g in range(16 // SG):
                        k_ = csi[0]
                        csi[0] += 1
                        sgrp(g, *Cs3[k_ % 3], *Csbf2[k_ % 2], *QX2[k_ % 2], *VwX2[k_ % 2])
                if full:
                    head_epilogue(pnum[:, 0:256], "pnum", None, OG[:, ti, :], OGb, ti, 1024 + h * 256, None, ytok, ytokb)
                if c == NCH - 1:
                    sdma(o_pS[h], Sh, [Shb], [])
            dve(lambda e: e.memset(Sbf, 0.0), [], [Sbfb])
            for c in range(NCH + 1):
                gchunk(c)

        mlstm_proj(0)
        S.barrier()
        print("arena phase 1a used %d / %d (mark %d)" % (apos[0] * 2, ARENA * 2, gmark * 2))
        apos[0] = gmark
        Cs3 = [carve("Cs%d" % i, [SG, 2, 257], F32) for i in range(3)]
        Csbf2 = [carve("Csbf%d" % i, [SG, 2, 257]) for i in range(2)]
        QX2 = [carve("QX%d" % i, [2, SG, 128]) for i in range(2)]
        VwX2 = [carve("VwX%d" % i, [SG, 257]) for i in range(2)]
        for h in range(4):
            if h > 0:
                mlstm_proj(h)
            mlstm_chunks(h)
            gla_head(h)
        tr(pmisc[:, 0:128], "pmisc", nTo.rearrange("p a b -> p (a b)"), nTob, identf)
        evac(nrow[:, 0:128], nrowb, pmisc[:, 0:128], "pmisc", eng="dve")
        sdma(o_sn[:, :], nrow[:, 0:128], [nrowb], [])
        if stop_after == "gla":
            return finish(nc, S, dbg_out)

        arena_reset()
        wide[0] = False
        yTa = hTp[:].rearrange("p k t -> p (k t)")[:, 0:8 * TM].rearrange("p (k t) -> p k t", k=8)
        yTg, yTgb = carve("yTg", [8, TM])
        mT, mTb = carve("mT", [KD, TM])
        ystg, ystgb = carve("ystg", [D])
        sg, sgb = carve("sg", [2, TM])
        tmpm, tmpmb = carve("tmpm", [512])
        xs_ = [carve("xs%d" % i, [256], F32)[0] for i in range(8)]
        for ti in range(NMT):
            sdma(ystg, yscr[ti * 128:(ti + 1) * 128, :], ["yscr"], [ystgb])
            for half in range(2):
                pt = ptr[half]
                pbn = "ptr%d" % half
                for j in range(8):
                    k = half * 8 + j
                    tr(pt[:, j * 128:(j + 1) * 128], pbn, ystg[:, k * 128:(k + 1) * 128], ystgb, identb, inc=(j == 7))
                for j in range(8):
                    k = half * 8 + j
                    dstt, dstb = (yTa, "hTp") if half == 0 else (yTg, yTgb)
                    dst = dstt[:, j, ti * 128:(ti + 1) * 128]
                    hcol = spm[:, P_HNM + k:P_HNM + k + 1]
                    if j % 2 == 0:
                        act(lambda e, dst=dst, j=j, pt=pt, hcol=hcol: e.activation(
                            out=dst, in_=pt[:, j * 128:(j + 1) * 128], func=AF.Copy, scale=hcol),
                            [pbn, "spm"], [dstb])
                    else:
                        dve(lambda e, dst=dst, j=j, pt=pt, hcol=hcol: e.tensor_scalar(
                            out=dst, in0=pt[:, j * 128:(j + 1) * 128], scalar1=hcol, scalar2=None, op0=ALU.mult),
                            [pbn, "spm"], [dstb])

        MGB = [(120, 392), (512, 512), (1024, 256)]

        def branch_group(cg):
            MG = MGB
            c0 = cg * 256
            for (gcol, wbr, ysrc, ysb, first) in ((C_GA, w_a, yTa, "hTp", True), (C_GB, w_b, yTg, yTgb, False)):
                wgt, wgtn = load_w(w_in, WOFF[gcol + c0], KD, 256)
                for cb_ in range(2):
                    for (t0, N) in MG:
                        pv, pn_ = proj_fm(wgt, wgtn, cb_ * 128, 128, hT, "hT", t0, N)
                        evac(sg[:, cb_, t0:t0 + N], sgb, pv, pn_, func=AF.Sigmoid)
                wbt, wbtn = load_w(wbr, cg * 8 * 256, 8, 256)
                for cb_ in range(2):
                    kk = cg * 2 + cb_
                    for (t0, N) in MG:
                        bank, bname = nextbank()
                        mm(bank[:, 0:N], bname,
                           [(wbt[:, k, cb_ * 128:(cb_ + 1) * 128], ysrc[:, k, t0:t0 + N]) for k in range(8)],
                           [wbtn, ysb])
                        if first:
                            dve(lambda e, bank=bank, N=N, t0=t0, cb_=cb_, kk=kk: e.tensor_tensor(
                                out=mT[:, kk, t0:t0 + N], in0=bank[:, 0:N], in1=sg[:, cb_, t0:t0 + N], op=ALU.mult),
                                [bname, sgb], [mTb])
                        else:
                            dve(lambda e, bank=bank, N=N, t0=t0, cb_=cb_: e.tensor_tensor(
                                out=tmpm[:, 0:N], in0=bank[:, 0:N], in1=sg[:, cb_, t0:t0 + N], op=ALU.mult),
                                [bname, sgb], [tmpmb])
                            dve(lambda e, N=N, t0=t0, kk=kk: e.tensor_tensor(
                                out=mT[:, kk, t0:t0 + N], in0=mT[:, kk, t0:t0 + N], in1=tmpm[:, 0:N], op=ALU.add),
                                [tmpmb, mTb], [mTb])
        for cg in range(8):
            branch_group(cg)

        def wout_group(cg):
            c0 = cg * 256
            wot, wotn = load_w(w_o, cg * KD * 256, KD, 256)
            for ti in range(NMT):
                xs = xs_[(cg * NMT + ti) % 8]
                xsb = "xsb%d" % ((cg * NMT + ti) % 8)
                S.dma("pool", lambda e, xs=xs, ti=ti: e.dma_start(out=xs[:], in_=xmain[ti * 128:(ti + 1) * 128, c0:c0 + 256]),
                      writes=[xsb])
                bank, bname = nextbank()
                mm(bank[:, 0:256], bname, [(mT[:, k, ti * 128:(ti + 1) * 128], wot[:, k, :]) for k in range(KD)],
                   [wotn, mTb])
                dve(lambda e, xs=xs, bank=bank: e.tensor_tensor(out=xs[:], in0=xs[:], in1=bank[:, 0:256], op=ALU.add),
                    [xsb, bname], [xsb])
                sdma(x1scr[ti * 128:(ti + 1) * 128, c0:c0 + 256], xs[:], [xsb], ["x1scr"])
        for cg in range(8):
            wout_group(cg)

        arena_reset()
        sdma(wbc[:], nfw.partition_broadcast(128), [], ["wbc"])
        xt2 = [carve("xt2_%d" % i, [D], F32) for i in range(4)]
        xn2 = [carve("xn2_%d" % i, [D], BF16) for i in range(4)]
        sq2, sq2b = carve("sq2", [D], BF16)
        for ti in range(NMT):
            (x_t, xb), (x_n, xnb) = xt2[ti % 4], xn2[ti % 4]
            sdma(x_t, x1scr[ti * 128:(ti + 1) * 128, :], ["x1scr"], [xb])
            norm_to_T(x_t, xb, x_n, xnb, hT, "hT", ti, sq2, sq2b)

        arena_reset()
        HT_ = 640
        actT, actTb = carve("actT", [KF, HT_])
        upad2 = [carve("upad%d" % i, [2 + HT_], F32) for i in range(2)]
        tb2 = [carve("tbuf%d" % i, [HT_], F32) for i in range(2)]
        pb2 = [carve("pbuf%d" % i, [HT_], F32) for i in range(2)]
        gb2 = [carve("gbuf%d" % i, [HT_]) for i in range(2)]
        w4f = [wsl[i // 2][:, (i % 2) * KD * 128:((i % 2) + 1) * KD * 128] for i in range(6)]
        w4 = [v.rearrange("p (k c) -> p k c", k=KD) for v in w4f]
        w4i = [0]

        def load_w4(packed, kf):
            i = w4i[0] % 6
            w4i[0] += 1
            name = "w4_%d" % i
            flat = w4f[i]
            off = kf * KD * 128
            S.dma("pool", lambda e: e.dma_start(out=flat, in_=packed[:, off:off + KD * 128]), writes=[name])
            return w4[i], name
        ucar, ucarb = carve("ucar", [KF, 2], F32)
        ucv, ucvb = carve("ucv", [KF, 34], F32)
        scvT, scvTb = carve("scvT", [KF, 32], F32)
        srow, srowb = carve("srow", [512], F32)
        fst, fstb = carve("fst", [5, 128], F32)
        wdsf = [hTp[:].rearrange("p k t -> p (k t)")[:, i * KF * 128:(i + 1) * KF * 128] for i in range(2)]
        wds = [v.rearrange("p (k c) -> p k c", k=KF) for v in wdsf]
        x2scr = x2scr_
        CW = lambda j, k: spm[:, P_CW + j * KF + k:P_CW + j * KF + k + 1]
        CBc = lambda k: spm[:, P_CB + k:P_CB + k + 1]
        sc32 = sconv.rearrange("j r c -> (j r) c")
        for k4 in range(KF // 4):
            sdma(srow[0:32, :], sc32[:, k4 * 512:(k4 + 1) * 512], [], [srowb])
            for q in range(4):
                tr(pmisc[:, q * 32:(q + 1) * 32], "pmisc", srow[0:32, q * 128:(q + 1) * 128], srowb,
                   identf[0:32, 0:32], inc=(q == 3))
            evac(scvT[:, k4 * 4:(k4 + 1) * 4, :], scvTb, pmisc[:, 0:128].rearrange("p (a b) -> p a b", a=4),
                 "pmisc", eng="dve")
        dve(lambda e: e.memset(ucar, 0.0), [], [ucarb])
        wdi = [0]

        FLO = [120, 0]
        FGR = [[(120, 200), (320, 320)], [(0, 320), (320, 320)]]

        def ffn_block(half, kf):
            g0 = half * HT_
            npr = HT_ if half == 0 else 512
            wu, wun = load_w4(w_up, kf)
            wg_, wgn_ = load_w4(w_gt, kf)
            (upad, upadb), (tb_, tbb), (pb_, pbb), (gb_, gbb) = upad2[kf % 2], tb2[kf % 2], pb2[kf % 2], gb2[kf % 2]
            dve(lambda e: e.tensor_copy(out=upad[:, 0:2], in_=ucar[:, kf, :]), [ucarb], [upadb])
            lo = FLO[half]
            for (l0, N) in FGR[half]:
                t0 = g0 + l0
                pv, pn_ = proj_fm(wu, wun, 0, 128, hT, "hT", t0, N)
                evac(upad[:, 2 + l0:2 + l0 + N], upadb, pv, pn_, eng="act")
                act(lambda e, pv=pv, l0=l0, N=N: e.activation(out=tb_[:, l0:l0 + N], in_=pv, func=AF.Identity,
                                                              bias=CBc(kf), scale=CW(2, kf)), [pn_, "spm"], [tbb])
                pv, pn_ = proj_fm(wg_, wgn_, 0, 128, hT, "hT", t0, N)
                evac(gb_[:, l0:l0 + N], gbb, pv, pn_, eng="act")
            if half == 0:
                dve(lambda e: e.tensor_copy(out=ucar[:, kf, :], in_=upad[:, HT_:HT_ + 2]), [upadb], [ucarb])
            else:
                dve(lambda e: e.tensor_copy(out=ucv[:, kf, 0:2], in_=upad[:, 512:514]), [upadb], [ucvb])
                u3 = upad[:, 2 + 512:2 + 640].rearrange("p (j l) -> p j l", l=8)
                dve(lambda e, u3=u3: e.tensor_copy(out=ucv[:, kf, 2:34].rearrange("p (j r) -> p j r", r=2),
                                                   in_=u3[:, :, 6:8]), [upadb], [ucvb])
            dve(lambda e: e.scalar_tensor_tensor(out=tb_[:, lo:npr], in0=upad[:, 1 + lo:1 + npr], scalar=CW(1, kf),
                                                 in1=tb_[:, lo:npr], op0=ALU.mult, op1=ALU.add),
                [upadb, tbb, "spm"], [tbb])
            dve(lambda e: e.scalar_tensor_tensor(out=tb_[:, lo:npr], in0=upad[:, lo:npr], scalar=CW(0, kf),
                                                 in1=tb_[:, lo:npr], op0=ALU.mult, op1=ALU.add),
                [upadb, tbb, "spm"], [tbb])
            if half == 1:
                t3 = tb_[:, 512:640].rearrange("p (j l) -> p j l", l=8)
                u3 = upad[:, 2 + 512:2 + 640].rearrange("p (j l) -> p j l", l=8)
                s3 = scvT[:, kf, :].rearrange("p (j r) -> p j r", r=2)
                dve(lambda e, t3=t3, u3=u3: e.scalar_tensor_tensor(
                    out=t3[:, :, 1:8], in0=u3[:, :, 0:7], scalar=CW(1, kf), in1=t3[:, :, 1:8], op0=ALU.mult,
                    op1=ALU.add), [upadb, tbb, "spm"], [tbb])
                dve(lambda e, t3=t3, s3=s3: e.scalar_tensor_tensor(
                    out=t3[:, :, 0:1], in0=s3[:, :, 1:2], scalar=CW(1, kf), in1=t3[:, :, 0:1], op0=ALU.mult,
                    op1=ALU.add), [scvTb, tbb, "spm"], [tbb])
                dve(lambda e, t3=t3, u3=u3: e.scalar_tensor_tensor(
                    out=t3[:, :, 2:8], in0=u3[:, :, 0:6], scalar=CW(0, kf), in1=t3[:, :, 2:8], op0=ALU.mult,
                    op1=ALU.add), [upadb, tbb, "spm"], [tbb])
                dve(lambda e, t3=t3, s3=s3: e.scalar_tensor_tensor(
                    out=t3[:, :, 0:2], in0=s3[:, :, 0:2], scalar=CW(0, kf), in1=t3[:, :, 0:2], op0=ALU.mult,
                    op1=ALU.add), [scvTb, tbb, "spm"], [tbb])
            fs = slice(lo, HT_)
            act(lambda e: e.activation(out=pb_[:, fs], in_=tb_[:, fs], func=AF.Square, scale=0.044715 ** 0.5), [tbb],
                [pbb])
            dve(lambda e: e.scalar_tensor_tensor(out=pb_[:, fs], in0=pb_[:, fs], scalar=1.0, in1=tb_[:, fs],
                                                 op0=ALU.add, op1=ALU.mult), [pbb, tbb], [pbb])
            act(lambda e: e.activation(out=pb_[:, fs], in_=pb_[:, fs], func=AF.Sigmoid, scale=1.5957691216057308),
                [pbb], [pbb])
            dve(lambda e: e.tensor_tensor(out=pb_[:, fs], in0=pb_[:, fs], in1=tb_[:, fs], op=ALU.mult), [pbb, tbb],
                [pbb])
            dve(lambda e: e.tensor_tensor(out=actT[:, kf, fs], in0=pb_[:, fs], in1=gb_[:, fs], op=ALU.mult),
                [pbb, gbb], [actTb])

        def down_block(half, cbk):
            g0 = half * HT_
            i = wdi[0] % 2
            wdi[0] += 1
            (tb_, tbb) = tb2[i]
            wd = wds[i]
            wdn = "wds%d" % i
            off = cbk * KF * 128
            wdf = wdsf[i]
            S.dma("pool", lambda e: e.dma_start(out=wdf[:, 0:22 * 128], in_=w_dn[:, off:off + 22 * 128]),
                  writes=[wdn])
            S.dma("pool", lambda e: e.dma_start(out=wdf[:, 22 * 128:44 * 128],
                                                in_=w_dn[:, off + 22 * 128:off + 44 * 128]), writes=[wdn])
            for (l0, N) in FGR[half]:
                bank, bname = nextbank()
                mm(bank[:, 0:N], bname, [(wd[:, k, :], actT[:, k, l0:l0 + N]) for k in range(KF)],
                   [wdn, actTb])
                evac(tb_[:, l0:l0 + N], tbb, bank[:, 0:N], bname, eng="act")
            tts = range(1, 5) if half == 0 else range(5)
            for tt in tts:
                dstp = pst[:, tt * 128:(tt + 1) * 128] if tt < 4 else pnum[:, 0:128]
                dstn = "pst" if tt < 4 else "pnum"
                tr(dstp, dstn, tb_[:, tt * 128:(tt + 1) * 128], tbb, identf, inc=(tt >= 3))
            evac(fst[:, 0:4, :], fstb, pst[:, 0:512].rearrange("p (a b) -> p a b", a=4), "pst", eng="dve")
            evac(fst[:, 4, :], fstb, pnum[:, 0:128], "pnum", eng="dve")
            for tt in tts:
                r0 = g0 + tt * 128
                sdma(x2scr[r0:r0 + 128, cbk * 128:(cbk + 1) * 128], fst[:, tt, :], [fstb], ["x2scr"])

        for half in range(2):
            for kf in range(KF):
                ffn_block(half, kf)
            for cbk in range(KD):
                down_block(half, cbk)
        oconv_s = o_sconv.rearrange("j r c -> (j r) c")
        for k4 in range(KF // 4):
            for q in range(4):
                tr(pmisc[0:34, q * 128:(q + 1) * 128], "pmisc", ucv[:, k4 * 4 + q, :], ucvb, identf, inc=(q == 3))
            evac(srow[0:34, :], srowb, pmisc[0:34, 0:512], "pmisc", eng="dve")
            sdma(o_pconv[:, k4 * 512:(k4 + 1) * 512], srow[0:2, :], [srowb], [])
            sdma(oconv_s[:, k4 * 512:(k4 + 1) * 512], srow[2:34, :], [srowb], [])

        arena_reset()
        sdma(wbc[:], fnw.partition_broadcast(128), [], ["wbc"])
        xa = [carve("xa%d" % i, [D], F32) for i in range(4)]
        xf = [carve("xf%d" % i, [D], F32) for i in range(4)]
        sq3, sq3b = carve("sq3", [D], BF16)
        for ti in range(1, NMT):
            (x_a, xab), (x_f, xfb) = xa[ti % 4], xf[ti % 4]
            sdma(x_a, x1scr[ti * 128:(ti + 1) * 128, :], ["x1scr"], [xab])
            S.dma("pool", lambda e, x_f=x_f, ti=ti: e.dma_start(out=x_f, in_=x2scr[ti * 128:(ti + 1) * 128, :]),
                  reads=["x2scr"], writes=[xfb])
            dve(lambda e, x_a=x_a, x_f=x_f: e.tensor_tensor(out=x_a, in0=x_a, in1=x_f, op=ALU.add), [xab, xfb], [xab])
            ss, rs, _, sbn = statslot()
            act(lambda e, x_a=x_a, ss=ss: e.activation(out=sq3, in_=x_a, func=AF.Square, accum_out=ss), [xab],
                [sq3b, sbn])
            rstd_from_ss(ss, rs, sbn, D)
            dve(lambda e, x_a=x_a, x_f=x_f, rs=rs: e.scalar_tensor_tensor(out=x_f, in0=x_a, scalar=rs, in1=wbc[:],
                                                                          op0=ALU.mult, op1=ALU.mult),
                [xab, sbn + "r", "wbc"], [xfb])
            sdma(yout[(ti - 1) * 128:ti * 128, :], x_f, [xfb], [])
        return finish(nc, S, dbg_out)


def finish(nc, S, dbg_out):
    S.finish()
    print("sim phase us:", [int(x) for x in S.sim_phase_us], "units", len(S.units))
    with nc.Block() as block:
        @block.sync
        def _(eng):
            S.emit("sp", eng)

        @block.tensor
        def _(eng):
            S.emit("pe", eng)

        @block.scalar
        def _(eng):
            S.emit("act", eng)

        @block.vector
        def _(eng):
            S.emit("dve", eng)

        @block.gpsimd
        def _(eng):
            S.emit("pool", eng)
    return nc


def make_consts():
    import ml_dtypes
    cb = np.zeros((128, CB_W), np.float32)
    cb[:, 0:128] = np.eye(128)
    s = np.arange(128)[:, None]
    t = np.arange(128)[None, :]
    cb[:, 128:256] = (s <= t)
    cb[:, 256:384] = (s <= t) & ((s // 8) == (t // 8))
    j = np.arange(16)[:, None]
    cb[:, 512:512 + 2048] = ((np.arange(128)[None, :] // 8) == j).astype(np.float32).reshape(1, 2048)
    cb[:, 2560:2576] = ((np.arange(128)[:, None] // 8) == np.arange(16)[None, :])
    cf = np.zeros((128, CF_W), np.float32)
    cf[:, 0:128] = np.eye(128)
    cf[:, 128:256] = 1.0
    cf[:, 256:384] = (np.arange(128)[None, :] % 8 != 0)
    cf[:, 384:512] = np.where(np.arange(128)[None, :] % 8 == 0, -1e30, 0.0)
    return cb.astype(ml_dtypes.bfloat16), cf


_NC_CACHE = {}


def _prep_inputs(inp):
    f32 = np.float32
    cbc, cfc = make_consts()
    xp = np.asarray(inp["x_prompt"], f32)
    xs = np.asarray(inp["x_sample"], f32)
    sp = np.zeros((128, SP_W), f32)
    ib = np.asarray(inp["mlstm_i_bias"], f32)[0]
    fb = np.asarray(inp["mlstm_f_bias"], f32)[0]
    sp[0:64, 0] = np.repeat(ib, 16)
    sp[0:64, 1] = np.repeat(fb, 16)
    sp[:, 2:6] = np.asarray(inp["gla_alpha_bias"], f32)[0].reshape(4, 128).T
    sp[:, 6:14] = np.asarray(inp["mlstm_head_norm_w"], f32)[0].reshape(8, 128).T
    sp[:, 14:22] = np.asarray(inp["gla_head_norm_w"], f32)[0].reshape(8, 128).T
    cw = np.asarray(inp["ffn_conv_w"], f32)[0]
    for j in range(3):
        sp[:, 22 + j * KF:22 + (j + 1) * KF] = cw[j].reshape(KF, 128).T
    sp[:, 22 + 3 * KF:22 + 4 * KF] = np.asarray(inp["ffn_conv_b"], f32)[0].reshape(KF, 128).T
    shared = {
        "w_in": _pack(np.asarray(inp["w_in"], f32)[0], KD, _win_blocks()),
        "nmw": np.asarray(inp["norm_mix_w"], f32).reshape(1, D),
        "nfw": np.asarray(inp["norm_ffn_w"], f32).reshape(1, D),
        "fnw": np.asarray(inp["final_norm_w"], f32).reshape(1, D),
        "cst_bf": cbc, "cst_f": cfc, "smallp": sp,
        "aup": np.asarray(inp["gla_alpha_up"], f32)[0],
        "w_a": _pack(np.asarray(inp["w_branch_a"], f32)[0], 8, [(c * 256, 256) for c in range(8)]),
        "w_b": _pack(np.asarray(inp["w_branch_b"], f32)[0], 8, [(c * 256, 256) for c in range(8)]),
        "w_o": _pack(np.asarray(inp["w_out"], f32)[0], KD, [(c * 256, 256) for c in range(8)]),
        "w_up": _pack(np.asarray(inp["ffn_w_up"], f32)[0], KD, [(c * 128, 128) for c in range(KF)]),
        "w_gt": _pack(np.asarray(inp["ffn_w_gate"], f32)[0], KD, [(c * 128, 128) for c in range(KF)]),
        "w_dn": _pack(np.asarray(inp["ffn_w_down"], f32)[0], KF, [(c * 128, 128) for c in range(KD)]),
    }
    sC = np.asarray(inp["state_mlstm_C"], f32)[0]
    sn = np.asarray(inp["state_mlstm_n"], f32)[0]
    sm = np.asarray(inp["state_mlstm_m"], f32)[0]
    sS = np.asarray(inp["state_gla_S"], f32)[0]
    scv = np.asarray(inp["state_ffn_conv"], f32)[0]
    maps = []
    for c in range(8):
        s, half = c // 2, c % 2
        xmain = np.zeros((TM, D), f32)
        if half == 1:
            xpre = np.ascontiguousarray(xp[s, 0:PRE])
            xmain[0:1152] = xp[s, PRE:2048]
        else:
            xpre = np.zeros((PRE, D), f32)
            xmain[128:1152] = xp[s, 0:1024]
        xmain[1152:1280] = xs[16 * c:16 * c + 16].reshape(128, D)
        m = dict(shared)
        m.update({
            "xpre": xpre, "xmain": xmain, "flag": np.full((1, 1), float(half), f32),
            "sC": np.ascontiguousarray(sC[16 * c:16 * c + 16]),
            "sn": np.ascontiguousarray(sn[16 * c:16 * c + 16].reshape(64, 256)),
            "smcol": np.ascontiguousarray(sm[16 * c:16 * c + 16].T.reshape(64, 1)),
            "sS": np.ascontiguousarray(sS[16 * c:16 * c + 16]),
            "sconv": np.ascontiguousarray(scv[16 * c:16 * c + 16]),
        })
        maps.append(m)
    return maps


def _assemble(results):
    f32 = np.float32
    y_p = np.zeros((4, 2048, D), f32)
    y_s = np.zeros((128, 8, D), f32)
    pC = np.zeros((1, 4, 4, 256, 256), f32)
    pn = np.zeros((1, 4, 4, 256), f32)
    pm = np.zeros((1, 4, 4), f32)
    pS = np.zeros((1, 4, 4, 128, 256), f32)
    pcv = np.zeros((1, 4, 2, DFF), f32)
    sCo = np.zeros((1, 128, 4, 256, 256), f32)
    sno = np.zeros((1, 128, 4, 256), f32)
    smo = np.zeros((1, 128, 4), f32)
    sSo = np.zeros((1, 128, 4, 128, 256), f32)
    scvo = np.zeros((1, 128, 2, DFF), f32)
    for c in range(8):
        r = results[c]
        s, half = c // 2, c % 2
        yo = np.asarray(r["yout"], f32)
        y_p[s, half * 1024:(half + 1) * 1024] = yo[0:1024]
        y_s[16 * c:16 * c + 16] = yo[1024:1152].reshape(16, 8, D)
        if half == 1:
            pC[0, s] = np.asarray(r["o_pC"], f32)
            pn[0, s] = np.asarray(r["o_pn"], f32).reshape(4, 256)
            pm[0, s] = np.asarray(r["o_pm"], f32).reshape(4, 16)[:, 15]
            pS[0, s] = np.asarray(r["o_pS"], f32)
            pcv[0, s] = np.asarray(r["o_pconv"], f32)
        sl = slice(16 * c, 16 * c + 16)
        sCo[0, sl] = np.asarray(r["o_sC"], f32)
        sno[0, sl] = np.asarray(r["o_sn"], f32).reshape(2, 16, 4, 128).transpose(1, 2, 0, 3).reshape(16, 4, 256)
        smo[0, sl] = np.asarray(r["o_sm"], f32).reshape(4, 16).T
        sSo[0, sl] = np.asarray(r["o_sS"], f32)
        scvo[0, sl] = np.asarray(r["o_sconv"], f32)
    return (y_p, y_s, pC, pn, pm, pS, pcv, sCo, sno, smo, sSo, scvo)


def kernel(**inputs):
    if "nc" not in _NC_CACHE:
        _NC_CACHE["nc"] = build()
    nc = _NC_CACHE["nc"]
    maps = _prep_inputs(inputs)
    res = run_bass_kernel_spmd(nc, maps, core_ids=list(range(8)))
    return _assemble(res.results)
```

```python
import numpy as np
from contextlib import ExitStack
import concourse.bass as bass
import concourse.mybir as mybir
from concourse.bass_utils import run_bass_kernel_spmd

F32 = mybir.dt.float32
BF16 = mybir.dt.bfloat16
AF = mybir.ActivationFunctionType
ALU = mybir.AluOpType
AX = mybir.AxisListType

D = 2048
NIN = 11288
DFF = 5632
KD = D // 128
KF = DFF // 128
PRE = 896
NPT = PRE // 128
TM = 1280
CB_W = 2576
CF_W = 520
SP_W = 22 + 4 * KF
NMT = TM // 128
NCH = 16
EPS = 1e-6

C_MQ, C_MK, C_MV, C_MO = 0, 1024, 2048, 3072
C_MI, C_MF = 4096, 4100
C_GQ, C_GK, C_GV, C_GR = 4104, 4616, 5128, 6152
C_GAL = 7176
C_GA, C_GB = 7192, 9240


def _win_blocks():
    blks = [(C_MI, 8), (C_GAL, 16)]
    for hh in range(4):
        blks += [(C_MQ + hh * 256, 256), (C_MK + hh * 256, 256), (C_MV + hh * 256, 256), (C_MO + hh * 256, 256)]
    for hh in range(4):
        blks += [(C_GQ + hh * 128, 128), (C_GK + hh * 128, 128), (C_GV + hh * 256, 256), (C_GR + hh * 256, 256)]
    for cg in range(8):
        blks += [(C_GA + cg * 256, 256), (C_GB + cg * 256, 256)]
    return blks


def _win_offsets():
    off = {}
    o = 0
    for c0, n in _win_blocks():
        off[c0] = o
        o += KD * n
    return off


def _pack(w, nk, blocks):
    outs = []
    for c0, n in blocks:
        outs.append(np.ascontiguousarray(w[:, c0:c0 + n].reshape(nk, 128, n).transpose(1, 0, 2)).reshape(128, nk * n))
    return np.ascontiguousarray(np.concatenate(outs, axis=1))


class _Probe:
    def __init__(self):
        self.calls = []

    def __getattr__(self, name):
        def f(*a, **k):
            self.calls.append((name, a, k))
            return self
        return f

    def then_inc(self, *a, **k):
        return self


def _free_elems(ap):
    n = 1
    for s in ap.shape[1:]:
        n *= int(s)
    return n


def _est(eng, fn):
    p = _Probe()
    fn(p)
    name, a, k = p.calls[0]
    out = k.get("out", a[0] if a else None)
    if eng == "pe":
        if name == "transpose":
            return 0.09
        rhs = k.get("rhs")
        n = _free_elems(rhs)
        f = 4.0 if rhs.dtype == F32 else 1.0
        return max(n, 64) * f / 2400.0 + 0.012
    n = _free_elems(out) if out is not None else 64
    if eng == "act":
        return (n + 400) / 1400.0 + (0.1 if k.get("accum_out") is not None else 0.0)
    if eng in ("dve", "pool"):
        f = 2.0 if name in ("tensor_tensor_scan",) else 1.0
        return max(n, 60) * f / 960.0 + 0.2
    return 0.1


def _dma_est(fn):
    p = _Probe()
    fn(p)
    name, a, k = p.calls[0]
    out = k.get("out")
    nbytes = 1
    for s in out.shape:
        nbytes *= int(s)
    nbytes *= 4 if out.dtype == F32 else 2
    return 2.0 + nbytes / 180e3, nbytes


class Sched:
    ENGS = ("pe", "act", "dve", "pool", "sp")
    XLAT = 0.4

    def __init__(self, esems, dsems):
        self.esem = esems
        self.dsems = dsems
        self.units = []
        self.bufs = {}
        self.phase = 0
        self.trace_phase = None
        self.open = {e: None for e in esems}

    def _st(self, b):
        st = self.bufs.get(b)
        if st is None:
            st = {"w": None, "r": {}}
            self.bufs[b] = st
        return st

    def _deps(self, uid, reads, writes):
        deps = set()
        for b in reads:
            st = self._st(b)
            if st["w"] is not None:
                deps.add(st["w"])
        for b in writes:
            st = self._st(b)
            if st["w"] is not None:
                deps.add(st["w"])
            deps.update(st["r"].keys())
        deps.discard(uid)
        return deps

    def _mark(self, uid, reads, writes):
        for b in reads:
            self._st(b)["r"][uid] = True
        for b in writes:
            st = self._st(b)
            st["w"] = uid
            st["r"] = {}

    def op(self, eng, fn, reads=(), writes=(), inc=True):
        u = self.open[eng]
        if u is None:
            u = {"eng": eng, "fns": [], "deps": set(), "dur": 0.0, "dma": False, "phase": self.phase,
                 "id": len(self.units), "lab": ",".join(writes)}
            self.units.append(u)
            self.open[eng] = u
        u["deps"] |= self._deps(u["id"], reads, writes)
        self._mark(u["id"], reads, writes)
        u["fns"].append(fn)
        u["dur"] += _est(eng, fn)
        if inc:
            self.open[eng] = None

    def dma(self, q, fn, reads=(), writes=()):
        uid = len(self.units)
        lat, nbytes = _dma_est(fn)
        u = {"eng": q, "fns": [fn], "deps": self._deps(uid, reads, writes), "dur": 0.06 if q == "sp" else 0.35,
             "dma": True, "lat": lat, "phase": self.phase, "id": uid, "lab": "dma:" + ",".join(writes) + "<" + ",".join(reads)}
        self.units.append(u)
        self._mark(uid, reads, writes)

    def barrier(self):
        for e, v in self.open.items():
            assert v is None, e
        self.phase += 1
        self.bufs = {}

    def barrier_all_dma(self, q="sp"):
        pass

    def finish(self):
        import heapq
        order = {e: [] for e in self.ENGS}
        nph = self.phase + 1
        by_phase = [[] for _ in range(nph)]
        for u in self.units:
            by_phase[u["phase"]].append(u)
        for ph in range(nph):
            us = by_phase[ph]
            ids = {u["id"] for u in us}
            ndep = {}
            users = {}
            for u in us:
                d = [x for x in u["deps"] if x in ids]
                u["deps"] = set(d)
                ndep[u["id"]] = len(d)
                for x in d:
                    users.setdefault(x, []).append(u)
            byid = {u["id"]: u for u in us}
            ready = {e: [] for e in self.ENGS}
            avail = {}
            for u in us:
                if ndep[u["id"]] == 0:
                    heapq.heappush(ready[u["eng"]], u["id"])
                    avail[u["id"]] = 0.0
            tfree = {e: 0.0 for e in self.ENGS}
            lastu = {}
            done = 0
            fin = {}
            while done < len(us):
                best = None
                for e in self.ENGS:
                    h = ready[e]
                    if not h:
                        continue
                    cand = None
                    tnow = tfree[e]
                    low = [i for i in h if avail[i] <= tnow]
                    if low:
                        cid = min(low)
                        st = tnow
                    else:
                        cid = min(h, key=lambda i: (avail[i], i))
                        st = avail[cid]
                    if best is None or st < best[0] or (st == best[0] and cid < best[2]):
                        best = (st, e, cid)
                st, e, cid = best
                ready[e].remove(cid)
                heapq.heapify(ready[e])
                u = byid[cid]
                u["st"] = st
                if avail[cid] >= tfree[e] - 1e-9 and u["deps"]:
                    u["why"] = max(u["deps"], key=lambda x: fin[x])
                else:
                    u["why"] = lastu.get(e)
                lastu[e] = cid
                end = st + u["dur"]
                tfree[e] = end
                f = end + (u["lat"] if u["dma"] else 0.0)
                fin[cid] = f
                order[e].append(u)
                done += 1
                for v in users.get(cid, ()):
                    ndep[v["id"]] -= 1
                    if ndep[v["id"]] == 0:
                        t = 0.0
                        for x in v["deps"]:
                            lat = 0.0 if (byid[x]["eng"] == v["eng"] and v["eng"] == "pe") else self.XLAT
                            t = max(t, fin[x] + lat)
                        avail[v["id"]] = t
                        heapq.heappush(ready[v["eng"]], v["id"])
            self.sim_phase_us = getattr(self, "sim_phase_us", []) + [max(tfree.values())]
            if getattr(self, "trace_phase", None) == ph:
                cur = max(us, key=lambda u: fin[u["id"]])["id"]
                chain = []
                while cur is not None and len(chain) < 400:
                    u = byid[cur]
                    chain.append((round(u["st"], 2), u["eng"], round(u["dur"], 2), u["lab"], len(u["fns"])))
                    cur = u.get("why")
                for c in chain[:400]:
                    print("   CP", c)
            busy = {e: 0.0 for e in self.ENGS}
            for u in us:
                busy[u["eng"]] += u["dur"]
            print("phase", ph, "sim %.0f us" % max(tfree.values()), {e: int(v) for e, v in busy.items()}, "units", len(us))
        cnt = {e: 0 for e in self.esem}
        dcnt = {q: [0] * len(v) for q, v in self.dsems.items()}
        dnext = {q: 0 for q in self.dsems}
        for e in self.ENGS:
            for u in order[e]:
                if u["dma"]:
                    i = dnext[e]
                    dnext[e] = (i + 1) % len(self.dsems[e])
                    u["prev"] = (self.dsems[e][i], dcnt[e][i]) if dcnt[e][i] > 0 else None
                    dcnt[e][i] += 16
                    u["ev"] = (self.dsems[e][i], dcnt[e][i])
                else:
                    cnt[e] += 1
                    u["ev"] = (self.esem[e], cnt[e])
        self.order = order
        byid = {u["id"]: u for u in self.units}
        self.prog = {e: [] for e in self.ENGS}
        all_dma = [u for u in self.units if u["dma"]]
        for e in self.ENGS:
            wd = {}
            cur_phase = 0
            for u in order[e]:
                waits = []
                if u["phase"] != cur_phase:
                    for e2 in self.esem:
                        if e2 == e and e == "pe":
                            continue
                        c = max([x["ev"][1] for x in order[e2] if x["phase"] < u["phase"] and not x["dma"]] or [0])
                        sem = self.esem[e2]
                        if c > 0 and wd.get(id(sem), 0) < c:
                            wd[id(sem)] = c
                            waits.append((sem, c))
                    for x in all_dma:
                        if x["phase"] < u["phase"] and wd.get(id(x["ev"][0]), 0) < x["ev"][1]:
                            wd[id(x["ev"][0])] = x["ev"][1]
                            waits.append(x["ev"])
                    cur_phase = u["phase"]
                for d in sorted(u["deps"]):
                    x = byid[d]
                    if x["eng"] == e and e == "pe":
                        continue
                    sem, c = x["ev"]
                    if wd.get(id(sem), 0) >= c:
                        continue
                    wd[id(sem)] = c
                    waits.append((sem, c))
                if u["dma"] and u["prev"] is not None:
                    sem, c = u["prev"]
                    if wd.get(id(sem), 0) < c:
                        wd[id(sem)] = c
                        waits.append((sem, c))
                self.prog[e].append((waits, u["fns"], u["ev"][0], 16 if u["dma"] else 1))
        wd = {}
        waits = []
        for x in all_dma:
            if wd.get(id(x["ev"][0]), 0) < x["ev"][1]:
                wd[id(x["ev"][0])] = x["ev"][1]
        semobj = {}
        for x in all_dma:
            semobj[id(x["ev"][0])] = x["ev"][0]
        self.final_waits = [(semobj[k], v) for k, v in wd.items()]

    def emit(self, eng_name, eng):
        for waits, fns, sem, inc in self.prog[eng_name]:
            for s, v in waits:
                eng.wait_ge(s, v)
            ins = None
            for fn in fns:
                ins = fn(eng)
            ins.then_inc(sem, inc)
        if eng_name == "sp":
            for s, v in self.final_waits:
                eng.wait_ge(s, v)


def build(debug=None, stop_after=None):
    debug = debug or {}
    nc = bass.Bass("TRN2", target_bir_lowering=False)
    es = ExitStack()

    def din(name, shape, dt=F32):
        return nc.dram_tensor(name, list(shape), dt, kind="ExternalInput").ap()

    def dout(name, shape, dt=F32):
        return nc.dram_tensor(name, list(shape), dt, kind="ExternalOutput").ap()

    def dint(name, shape, dt=F32):
        return nc.dram_tensor(name, list(shape), dt, kind="Internal").ap()

    xpre = din("xpre", [PRE, D])
    xmain = din("xmain", [TM, D])
    flag = din("flag", [1, 1])
    w_in = din("w_in", [128, KD * NIN])
    WOFF = _win_offsets()
    nmw = din("nmw", [1, D])
    nfw = din("nfw", [1, D])
    fnw = din("fnw", [1, D])
    cst_bf = din("cst_bf", [128, CB_W], BF16)
    cst_f = din("cst_f", [128, CF_W])
    smallp = din("smallp", [128, SP_W])
    aup = din("aup", [16, 512])
    w_a = din("w_a", [128, 8 * D])
    w_b = din("w_b", [128, 8 * D])
    w_o = din("w_o", [128, KD * D])
    w_up = din("w_up", [128, KD * DFF])
    w_gt = din("w_gt", [128, KD * DFF])
    w_dn = din("w_dn", [128, KF * D])
    sC = din("sC", [16, 4, 256, 256])
    sn = din("sn", [64, 256])
    smcol = din("smcol", [64, 1])
    sS = din("sS", [16, 4, 128, 256])
    sconv = din("sconv", [16, 2, DFF])

    yout = dout("yout", [TM - 128, D])
    o_pC = dout("o_pC", [4, 256, 256])
    o_pn = dout("o_pn", [8, 128])
    o_pm = dout("o_pm", [1, 64])
    o_pS = dout("o_pS", [4, 128, 256])
    o_pconv = dout("o_pconv", [2, DFF])
    o_sC = dout("o_sC", [16, 4, 256, 256])
    o_sn = dout("o_sn", [128, 128])
    o_sm = dout("o_sm", [64, 1])
    o_sS = dout("o_sS", [16, 4, 128, 256])
    o_sconv = dout("o_sconv", [16, 2, DFF])

    gscr = dint("gscr", [8, 2048])
    gscs = dint("gscs", [8, 128])
    yscr = dint("yscr", [TM, D], BF16)
    x1scr = dint("x1scr", [TM, D])
    x2scr_ = dint("x2scr", [TM, D])
    dbg_out = {}

    with es:
        def sb(name, shape, dt=F32):
            return es.enter_context(nc.sbuf_tensor(name, list(shape), dt))

        def ps(name, shape, dt=F32):
            return es.enter_context(nc.psum_tensor(name, list(shape), dt))

        esems = {e: es.enter_context(nc.semaphore("s_" + e)) for e in ("pe", "act", "dve", "pool")}
        dsems = {
            "sp": [es.enter_context(nc.semaphore("d_sp%d" % i)) for i in range(24)],
            "pool": [es.enter_context(nc.semaphore("d_pl%d" % i)) for i in range(12)],
        }
        S = Sched(esems, dsems)

        def act(fn, r, w):
            S.op("act", fn, reads=r, writes=w)

        def dve(fn, r, w):
            S.op("dve", fn, reads=r, writes=w)

        def pool(fn, r, w):
            S.op("pool", fn, reads=r, writes=w)

        def sdma(out, in_, r, w):
            S.dma("sp", lambda e: e.dma_start(out=out, in_=in_), reads=r, writes=w)

        def mm(out_ap, outb, pairs, reads, first=True, last=True):
            n = len(pairs)
            for i, (l, r) in enumerate(pairs):
                S.op("pe", lambda e, l=l, r=r, i=i: e.matmul(out_ap, lhsT=l, rhs=r, start=(first and i == 0),
                                                            stop=(last and i == n - 1)),
                     reads=reads, writes=[outb], inc=(i == n - 1))

        def tr(out_ap, outb, in_ap, inb, ident, inc=True):
            S.op("pe", lambda e: e.transpose(out=out_ap, in_=in_ap, identity=ident), reads=[inb, "cb", "cf"],
                 writes=[outb], inc=inc)

        cb = sb("cb", [128, CB_W], BF16)
        cf = sb("cf", [128, CF_W], F32)
        spm = sb("spm", [128, SP_W], F32)
        sdma(cb[:], cst_bf[:, :], [], ["cb"])
        sdma(cf[:], cst_f[:, :], [], ["cf"])
        sdma(spm[:], smallp[:, :], [], ["spm"])
        identb = cb[:, 0:128]
        maskc = cb[:, 128:256]
        masks = cb[:, 256:384]
        qxmask = cb[:, 512:512 + 2048].rearrange("p (j t) -> p j t", j=16)
        vxmask = cb[:, 2560:2576]
        identf = cf[:, 0:128]
        onesf = cf[:, 128:256]
        resetm = cf[:, 256:384]
        negbig = cf[:, 384:512]
        zerocol = cf[:, 512:513]
        P_IB, P_FB, P_AB, P_HNM, P_HNG, P_CW, P_CB = 0, 1, 2, 6, 14, 22, 22 + 3 * KF
        ib64 = spm[0:64, P_IB:P_IB + 1]
        fb64 = spm[0:64, P_FB:P_FB + 1]
        wbc = sb("wbc", [128, D], F32)
        sdma(wbc[:], nmw.partition_broadcast(128), [], ["wbc"])
        flagt = sb("flagt", [1, 1], F32)
        sdma(flagt[:], flag[:, :], [], ["flagt"])

        hT = sb("hT", [128, KD, TM], BF16)
        hTp = sb("hTp", [128, KD, PRE], BF16)
        wsl = [sb("wsl%d" % i, [128, KD * 256], BF16) for i in range(3)]
        stat = sb("stat", [128, 64], F32)
        ARENA = 99 * 512 + 160
        arena = sb("arena", [128, ARENA], BF16)
        apos = [0]
        aphase = [0]

        def carve(name, shape, dt=BF16):
            n = int(np.prod(shape))
            nb = n * (2 if dt == BF16 else 4)
            nb = (nb + 63) // 64 * 64
            off = apos[0]
            apos[0] += nb // 2
            assert apos[0] <= ARENA, ("arena overflow", name, apos[0])
            v = arena[:, off:off + nb // 2]
            if dt != BF16:
                v = v.bitcast(dt)
            v = v[:, 0:n]
            if len(shape) == 2:
                pat, kw = "p (a b) -> p a b", dict(a=shape[0])
            elif len(shape) == 3:
                pat, kw = "p (a b c) -> p a b c", dict(a=shape[0], b=shape[1])
            else:
                pat, kw = None, None
            if pat:
                v = v.rearrange(pat, **kw)
            return v, "ar%d_%s" % (aphase[0], name)

        def arena_reset():
            S.barrier()
            print("arena phase %d used %d / %d" % (aphase[0], apos[0] * 2, ARENA * 2))
            apos[0] = 0
            aphase[0] += 1

        pa = ps("pa", [128, 512])
        pb = ps("pb", [128, 512])
        ptr = [ps("ptr%d" % i, [128, 1024], BF16) for i in range(2)]
        pst = ps("pst", [128, 512])
        pnum = ps("pnum", [128, 512])
        pstate = ps("pstate", [128, 512])
        pmisc = ps("pmisc", [128, 512])
        pbk = [(pa, "pa"), (pb, "pb")]
        pbk6 = [(pa, "pa"), (pb, "pb"), (pst, "pst"), (pnum, "pnum"), (pstate, "pstate"), (pmisc, "pmisc")]
        pbi = [0]
        wide = [False]

        def nextbank():
            pbi[0] += 1
            if wide[0]:
                return pbk6[pbi[0] % 6]
            return pbk[pbi[0] % 2]

        evi = [0]

        def evac(dst, dstb, src, srcb, func=None, scale=None, eng=None):
            if func is not None or eng == "act":
                f = func if func is not None else AF.Copy
                if scale is None:
                    act(lambda e: e.activation(out=dst, in_=src, func=f), [srcb], [dstb])
                else:
                    act(lambda e: e.activation(out=dst, in_=src, func=f, scale=scale), [srcb], [dstb])
                return
            evi[0] += 1
            if eng is None:
                eng = "act" if evi[0] % 2 == 0 else "dve"
            if eng == "act":
                if scale is None:
                    act(lambda e: e.copy(out=dst, in_=src), [srcb], [dstb])
                else:
                    act(lambda e: e.mul(out=dst, in_=src, mul=scale), [srcb], [dstb])
            else:
                if scale is None:
                    dve(lambda e: e.tensor_copy(out=dst, in_=src), [srcb], [dstb])
                else:
                    dve(lambda e: e.tensor_scalar(out=dst, in0=src, scalar1=scale, scalar2=None, op0=ALU.mult),
                        [srcb], [dstb])

        wi = [0]

        def load_w(packed, off, nk, ncols):
            i = wi[0] % 3
            wi[0] += 1
            name = "wsl%d" % i
            tot = nk * ncols
            flat = wsl[i][:, 0:tot]
            dst = flat.rearrange("p (k c) -> p k c", k=nk)
            hf = tot // 2
            S.dma("pool", lambda e: e.dma_start(out=flat[:, 0:hf], in_=packed[:, off:off + hf]), writes=[name])
            S.dma("pool", lambda e: e.dma_start(out=flat[:, hf:tot], in_=packed[:, off + hf:off + tot]),
                  writes=[name])
            return dst, name

        MG = [(0, 512), (512, 512), (1024, 256)]
        PG = [(0, 448), (448, 448)]

        def proj_fm(wt, wname, c0, M, src, srcname, t0, N):
            bank, bname = nextbank()
            mm(bank[0:M, 0:N], bname, [(wt[:, k, c0:c0 + M], src[:, k, t0:t0 + N]) for k in range(KD)],
               [wname, srcname])
            return bank[0:M, 0:N], bname

        def proj_tm(wt, wname, c0, N, src, srcname, ti):
            bank, bname = nextbank()
            mm(bank[:, 0:N], bname, [(src[:, k, ti * 128:(ti + 1) * 128], wt[:, k, c0:c0 + N]) for k in range(KD)],
               [wname, srcname])
            return bank[:, 0:N], bname

        xt = [carve("xt%d" % i, [D], F32) for i in range(4)]
        xn = [carve("xn%d" % i, [D], BF16) for i in range(4)]
        sq1, sq1b = carve("sq", [D], BF16)
        nstat = [0]

        def rstd_from_ss(ss, rs, b, n):
            dve(lambda e: e.tensor_scalar(out=rs, in0=ss, scalar1=1.0 / n, scalar2=EPS, op0=ALU.mult, op1=ALU.add),
                [b], [b + "r"])
            act(lambda e: e.activation(out=rs, in_=rs, func=AF.Sqrt), [b + "r"], [b + "r"])
            dve(lambda e: e.reciprocal(out=rs, in_=rs), [b + "r"], [b + "r"])

        def statslot():
            c = nstat[0] % 16
            nstat[0] += 1
            return stat[:, c:c + 1], stat[:, 16 + c:17 + c], stat[:, 32 + c:33 + c], "stat%d" % c

        def norm_to_T(x_t, xb, x_n, xnb, dstT, dstname, ti, sqj, sqb):
            ss, rs, _, sbn = statslot()
            act(lambda e: e.activation(out=sqj, in_=x_t, func=AF.Square, accum_out=ss), [xb], [sqb, sbn])
            rstd_from_ss(ss, rs, sbn, D)
            dve(lambda e: e.scalar_tensor_tensor(out=x_n, in0=x_t, scalar=rs, in1=wbc[:], op0=ALU.mult,
                                                 op1=ALU.mult), [xb, sbn + "r", "wbc"], [xnb])
            for half in range(2):
                pt = ptr[half]
                pbn = "ptr%d" % half
                for j in range(8):
                    k = half * 8 + j
                    tr(pt[:, j * 128:(j + 1) * 128], pbn, x_n[:, k * 128:(k + 1) * 128], xnb, identb, inc=(j == 7))
                dst = dstT[:, half * 8:(half + 1) * 8, ti * 128:(ti + 1) * 128]
                src = pt[:].rearrange("p (k t) -> p k t", k=8)
                evac(dst, dstname, src, pbn, eng=("act" if half == 0 else "dve"))

        for ti in range(NPT + NMT):
            par = ti % 4
            (x_t, xb), (x_n, xnb) = xt[par], xn[par]
            if ti < NPT:
                sdma(x_t, xpre[ti * 128:(ti + 1) * 128, :], [], [xb])
                norm_to_T(x_t, xb, x_n, xnb, hTp, "hTp", ti, sq1, sq1b)
            else:
                tj = ti - NPT
                sdma(x_t, xmain[tj * 128:(tj + 1) * 128, :], [], [xb])
                norm_to_T(x_t, xb, x_n, xnb, hT, "hT", tj, sq1, sq1b)

        arena_reset()
        if stop_after == "p1":
            return finish(nc, S, dbg_out)
        Cst = carve("Cst", [2, 2, 257], F32)[0]
        Sst = carve("Sst", [2, 256], F32)[0]
        wtok = carve("wtok", [64], F32)[0]
        ftok = carve("ftok", [64], F32)[0]
        decbc = carve("decbc", [64], F32)[0]
        wtoks = carve("wtoks", [4], F32)[0]
        ftoks = carve("ftoks", [4], F32)[0]
        sdecbc = carve("sdecbc", [64], F32)[0]

        galT, galb = carve("galT", [PRE + TM], F32)
        BT, BTb = carve("BT", [PRE + TM], F32)
        gst, gstb = BT[:, 0:512], BTb
        QT, QTb = carve("QT", [2, TM])
        KT, KTb = carve("KT", [2, TM])
        Vt, Vtb = carve("Vt", [NMT, 257])
        OG, OGb = carve("OG", [NMT, 256])
        KTp, KTpb = carve("KTp", [2, PRE])
        Vp, Vpb = carve("Vp", [NPT, 257])
        Cbf, Cbfb = carve("Cbf", [2, 257])
        QTg_, QTgb = carve("QTg", [TM])
        KTg_, KTgb = carve("KTg", [TM])
        Vtg, Vtgb = carve("Vtg", [NMT, 256])
        RG, RGb = carve("RG", [NMT, 256])
        Ktok2 = [carve("Ktok%d" % i, [256]) for i in range(2)]
        Vw2 = [carve("Vw%d" % i, [257]) for i in range(2)]
        PT2 = [carve("PT%d" % i, [128]) for i in range(2)]
        ytok2 = [carve("ytok%d" % i, [256]) for i in range(2)]
        eqt2 = [carve("eqt%d" % i, [128], F32) for i in range(2)]
        QtT2 = [carve("QtT%d" % i, [128]) for i in range(2)]
        KtT2 = [carve("KtT%d" % i, [128]) for i in range(2)]
        KhT2 = [carve("KhT%d" % i, [128]) for i in range(2)]
        Khtok2 = [carve("Khtok%d" % i, [128]) for i in range(2)]
        Sbf, Sbfb = carve("Sbf", [256])
        nT, nTb = carve("nT", [2, 64], F32)
        nTo, nTob = carve("nTo", [2, 64], F32)
        nrow, nrowb = carve("nrow", [256], F32)
        junk, junkb = nrow.bitcast(BF16)[:, 0:256], nrowb
        SG = 2
        aups, aupb = carve("aups", [512], F32)
        negab = carve("negab", [4], F32)[0]
        dSs, dSsb = carve("dSs", [16], F32)
        gmark = apos[0]
        gat = carve("gat", [8, 128], F32)[0][0:64]
        gas = carve("gas", [8, 8], F32)[0][0:64]
        grow = carve("grow", [8, 64], F32)[0][0:1]
        wg, wgn = load_w(w_in, WOFF[C_MI], KD, 8)
        wl, wln = load_w(w_in, WOFF[C_GAL], KD, 16)
        MGG = [(0, 512), (512, 512), (1024, 128), (1152, 128)]
        for (src, sname, groups, base) in ((hTp, "hTp", PG, 0), (hT, "hT", MGG, PRE)):
            for (t0, N) in groups:
                pv, pn_ = proj_fm(wg, wgn, 0, 8, src, sname, t0, N)
                evac(gst[0:8, 0:N], gstb, pv, pn_, eng="act")
                if base + t0 < 2048:
                    sdma(gscr[:, base + t0:base + t0 + N], gst[0:8, 0:N], [gstb], ["gscr"])
                else:
                    sdma(gscs[:, :], gst[0:8, 0:N], [gstb], ["gscs"])
                pv, pn_ = proj_fm(wl, wln, 0, 16, src, sname, t0, N)
                evac(galT[0:16, base + t0:base + t0 + N], galb, pv, pn_, eng="dve")
        GI, GF, GB_, GA, GW, GFL, GT1, GT2 = range(8)

        def gt(i):
            return gat[:, i, :]

        def gs_(i):
            return gas[:, i, :]
        NTOK = PRE + TM - 128
        sdma(gt(GI), gscr[0:4, :].rearrange("h (c l) -> (h c) l", l=128), ["gscr"], ["g_i"])
        sdma(gt(GF), gscr[4:8, :].rearrange("h (c l) -> (h c) l", l=128), ["gscr"], ["g_f"])
        sdma(gs_(GI), gscs[0:4, :].rearrange("h (j l) -> (h j) l", l=8), ["gscs"], ["s_i"])
        sdma(gs_(GF), gscs[4:8, :].rearrange("h (j l) -> (h j) l", l=8), ["gscs"], ["s_f"])
        negfb = stat[0:64, 48:49]
        dve(lambda e: e.tensor_scalar(out=negfb, in0=fb64, scalar1=-1.0, scalar2=None, op0=ALU.mult),
            ["spm"], ["negfb"])

        def gate_math(T, pfx, L):
            i_, f_, b_, a_ = T(GI), T(GF), T(GB_), T(GA)
            act(lambda e: e.activation(out=f_, in_=f_, func=AF.Exp, bias=negfb, scale=-1.0),
                [pfx + "f", "negfb"], [pfx + "f"])
            act(lambda e: e.activation(out=f_, in_=f_, func=AF.Ln, bias=1.0), [pfx + "f"], [pfx + "f"])
            dve(lambda e: e.tensor_scalar(out=f_, in0=f_, scalar1=-1.0, scalar2=None, op0=ALU.mult),
                [pfx + "f"], [pfx + "f"])
            dve(lambda e: e.tensor_tensor_scan(out=b_, data0=onesf[0:64, 0:L], data1=f_, initial=0.0,
                                               op0=ALU.mult, op1=ALU.add), [pfx + "f", "cf"], [pfx + "b"])
            dve(lambda e: e.scalar_tensor_tensor(out=a_, in0=i_, scalar=ib64, in1=b_, op0=ALU.add,
                                                 op1=ALU.subtract), [pfx + "i", pfx + "b", "spm"], [pfx + "a"])
        gate_math(gt, "g_", 128)
        gate_math(gs_, "s_", 8)
        amax = stat[0:64, 49:50]
        bend = gat[:, GB_, 127:128]
        amaxb = stat[0:64, 50:51]
        dve(lambda e: e.reduce_max(out=amax, in_=gt(GA), axis=AX.X), ["g_a"], ["amax"])
        dve(lambda e: e.tensor_tensor(out=amaxb, in0=amax, in1=bend, op=ALU.add), ["amax", "g_b"], ["amaxb"])
        tr(pmisc[0:1, 0:64], "pmisc", bend, "g_b", identf[0:64, 0:64], inc=False)
        tr(pmisc[0:1, 64:128], "pmisc", amaxb, "amaxb", identf[0:64, 0:64])
        R_BE, R_AB, R_MN, R_MP, R_G, R_DEC = range(6)

        def gr(i):
            return grow[0:1, i, :]
        evac(grow[0:1, 0:2, :], "grow", pmisc[0:1, 0:128].rearrange("p (a b) -> p a b", a=2), "pmisc", eng="dve")
        for h in range(4):
            for seg in range(2):
                lo = h * 16 + seg * 8
                if seg == 0:
                    init = 0.0
                    rd = ["grow"]
                else:
                    mf = stat[0:1, 51 + h:52 + h]
                    dve(lambda e, h=h, mf=mf: e.tensor_tensor(out=mf, in0=grow[0:1, R_MN, h * 16 + 7:h * 16 + 8],
                                                             in1=flagt[0:1, 0:1], op=ALU.mult),
                        ["grow", "flagt"], ["mf%d" % h])
                    init = mf
                    rd = ["grow", "mf%d" % h]
                dve(lambda e, lo=lo, init=init: e.tensor_tensor_scan(
                    out=grow[0:1, R_MN, lo:lo + 8], data0=grow[0:1, R_BE, lo:lo + 8],
                    data1=grow[0:1, R_AB, lo:lo + 8], initial=init, op0=ALU.add, op1=ALU.max), rd, ["grow"])
                if seg == 0:
                    dve(lambda e, lo=lo: e.memset(grow[0:1, R_MP, lo:lo + 1], 0.0), [], ["grow"])
                else:
                    dve(lambda e, lo=lo, mf=mf: e.tensor_copy(out=grow[0:1, R_MP, lo:lo + 1], in_=mf),
                        ["mf%d" % h], ["grow"])
                dve(lambda e, lo=lo: e.tensor_copy(out=grow[0:1, R_MP, lo + 1:lo + 8],
                                                   in_=grow[0:1, R_MN, lo:lo + 7]), ["grow"], ["grow"])
        dve(lambda e: e.tensor_tensor(out=gr(R_G), in0=gr(R_MN), in1=gr(R_BE), op=ALU.subtract), ["grow"], ["grow"])
        dve(lambda e: e.tensor_tensor(out=gr(R_DEC), in0=gr(R_MP), in1=gr(R_G), op=ALU.subtract), ["grow"], ["grow"])
        act(lambda e: e.activation(out=gr(R_DEC), in_=gr(R_DEC), func=AF.Exp), ["grow"], ["grow"])
        dve(lambda e: e.tensor_scalar(out=gr(R_G), in0=gr(R_G), scalar1=-1.0, scalar2=None, op0=ALU.mult),
            ["grow"], ["grow"])
        mm(pmisc[:, 128:192], "pmisc", [(onesf[0:1, 0:128], gr(R_DEC))], ["grow", "cf"])
        evac(decbc[:], "decbc", pmisc[:, 128:192], "pmisc", eng="dve")
        mm(pmisc[0:64, 192:193], "pmisc", [(gr(R_G), onesf[0:1, 0:1])], ["grow", "cf"])
        negG = stat[0:64, 56:57]
        evac(negG, "negG", pmisc[0:64, 192:193], "pmisc", eng="dve")
        sdma(o_pm[:, :], gr(R_MN), ["grow"], [])
        act(lambda e: e.activation(out=gt(GW), in_=gt(GA), func=AF.Exp, bias=negG), ["g_a", "negG"], ["g_w"])
        act(lambda e: e.activation(out=gt(GFL), in_=gt(GB_), func=AF.Exp, bias=negG, scale=-1.0),
            ["g_b", "negG"], ["g_fl"])
        tr(pmisc[:, 256:320], "pmisc", gt(GW), "g_w", identf[0:64, 0:64], inc=False)
        tr(pmisc[:, 320:384], "pmisc", gt(GFL), "g_fl", identf[0:64, 0:64])
        evac(wtok[:], "wtok", pmisc[:, 256:320], "pmisc", eng="dve")
        evac(ftok[:], "ftok", pmisc[:, 320:384], "pmisc", eng="dve")
        smc = stat[0:64, 57:58]
        sdma(smc, smcol[:, :], [], ["smc"])
        samax = stat[0:64, 58:59]
        sG = stat[0:64, 59:60]
        snegG = stat[0:64, 60:61]
        sdec = stat[0:64, 61:62]
        smn = stat[0:64, 62:63]
        dve(lambda e: e.reduce_max(out=samax, in_=gs_(GA), axis=AX.X), ["s_a"], ["samax"])
        dve(lambda e: e.tensor_tensor(out=sG, in0=samax, in1=smc, op=ALU.max), ["samax", "smc"], ["sG"])
        dve(lambda e: e.tensor_tensor(out=smn, in0=sG, in1=gas[:, GB_, 7:8], op=ALU.add), ["sG", "s_b"], ["smn"])
        sdma(o_sm[:, :], smn, ["smn"], [])
        dve(lambda e: e.tensor_tensor(out=sdec, in0=smc, in1=sG, op=ALU.subtract), ["smc", "sG"], ["sdec"])
        act(lambda e: e.activation(out=sdec, in_=sdec, func=AF.Exp), ["sdec"], ["sdec"])
        dve(lambda e: e.tensor_scalar(out=snegG, in0=sG, scalar1=-1.0, scalar2=None, op0=ALU.mult), ["sG"], ["snegG"])
        act(lambda e: e.activation(out=gs_(GW), in_=gs_(GA), func=AF.Exp, bias=snegG), ["s_a", "snegG"], ["s_w"])
        act(lambda e: e.activation(out=gs_(GFL), in_=gs_(GB_), func=AF.Exp, bias=snegG, scale=-1.0),
            ["s_b", "snegG"], ["s_fl"])
        dg = gat[:, GT1, 0:64]
        dve(lambda e: e.tensor_scalar(out=dg, in0=identf[0:64, 0:64], scalar1=sdec, scalar2=None, op0=ALU.mult),
            ["sdec", "cf"], ["dg"])
        mm(pmisc[:, 384:448], "pmisc", [(onesf[0:64, 0:128], dg)], ["dg", "cf"])
        evac(sdecbc[:], "sdecbc", pmisc[:, 384:448], "pmisc", eng="dve")
        sdma(gscs[0:4, :].rearrange("h (j l) -> (h j) l", l=8), gs_(GW), ["s_w"], ["gscs"])
        sdma(gscs[4:8, :].rearrange("h (j l) -> (h j) l", l=8), gs_(GFL), ["s_fl"], ["gscs"])
        wfr = gat[0:8, GT2, :]
        sdma(wfr, gscs[0:8, :], ["gscs"], ["wfr"])
        tr(pmisc[:, 448:456], "pmisc", wfr, "wfr", identf[0:8, 0:8])
        evac(wtoks[:], "wtoks", pmisc[:, 448:452], "pmisc", eng="dve")
        evac(ftoks[:], "ftoks", pmisc[:, 452:456], "pmisc", eng="dve")

        if stop_after == "gates":
            if debug.get("gates"):
                for nm, t_, shp in (("wtok", wtok, [128, 64]), ("ftok", ftok, [128, 64]), ("decbc", decbc, [128, 64]),
                                    ("wtoks", wtoks, [128, 4]), ("sdecbc", sdecbc, [128, 64])):
                    dbg_out[nm] = dout("dbg_" + nm, shp)
                    sdma(dbg_out[nm][:, :], t_[:], [nm], [])
            return finish(nc, S, dbg_out)


        dve(lambda e: e.memset(Vt[:, :, 256:257], 1.0), [], [Vtb])
        dve(lambda e: e.memset(Vp[:, :, 256:257], 1.0), [], [Vpb])
        sdma(nrow[0:64, :], sn[:, :], [], [nrowb])
        for dh in range(2):
            tr(pmisc[:, dh * 64:(dh + 1) * 64], "pmisc", nrow[0:64, dh * 128:(dh + 1) * 128], nrowb,
               identf[0:64, 0:64], inc=(dh == 1))
        evac(nT, nTb, pmisc[:, 0:128].rearrange("p (a b) -> p a b", a=2), "pmisc", eng="dve")

        def head_epilogue(num_ap, numb, scale_pre, gate_ap, gateb, tile_i, dstcol0, hnb, ytok, ytokb):
            ss, rs, sc, sbn = statslot()
            if scale_pre is None:
                act(lambda e: e.activation(out=junk, in_=num_ap, func=AF.Square, accum_out=ss), [numb],
                    [junkb, sbn])
            else:
                act(lambda e: e.activation(out=junk, in_=num_ap, func=AF.Square, scale=scale_pre, accum_out=ss),
                    [numb, hnb], [junkb, sbn])
            rstd_from_ss(ss, rs, sbn, 256)
            if scale_pre is not None:
                dve(lambda e: e.tensor_tensor(out=rs, in0=rs, in1=scale_pre, op=ALU.mult), [sbn + "r", hnb],
                    [sbn + "r"])
            dve(lambda e: e.scalar_tensor_tensor(out=ytok, in0=num_ap, scalar=rs, in1=gate_ap, op0=ALU.mult,
                                                 op1=ALU.mult), [numb, sbn + "r", gateb], [ytokb])
            sdma(yscr[tile_i * 128:(tile_i + 1) * 128, dstcol0:dstcol0 + 256], ytok, [ytokb], ["yscr"])

        def mlstm_proj(h):
            wq, wqn = load_w(w_in, WOFF[C_MQ + h * 256], KD, 256)
            wk, wkn = load_w(w_in, WOFF[C_MK + h * 256], KD, 256)
            wv, wvn = load_w(w_in, WOFF[C_MV + h * 256], KD, 256)
            for dh in range(2):
                for (t0, N) in MG:
                    pv, pn_ = proj_fm(wq, wqn, dh * 128, 128, hT, "hT", t0, N)
                    evac(QT[:, dh, t0:t0 + N], QTb, pv, pn_)
            for dh in range(2):
                for (t0, N) in MG:
                    pv, pn_ = proj_fm(wk, wkn, dh * 128, 128, hT, "hT", t0, N)
                    evac(KT[:, dh, t0:t0 + N], KTb, pv, pn_, scale=0.0625)
                for (t0, N) in PG:
                    pv, pn_ = proj_fm(wk, wkn, dh * 128, 128, hTp, "hTp", t0, N)
                    evac(KTp[:, dh, t0:t0 + N], KTpb, pv, pn_, scale=0.0625)
            wo, won = load_w(w_in, WOFF[C_MO + h * 256], KD, 256)
            for ti in range(NPT):
                pv, pn_ = proj_tm(wv, wvn, 0, 256, hTp, "hTp", ti)
                evac(Vp[:, ti, 0:256], Vpb, pv, pn_)
            for ti in range(NMT):
                pv, pn_ = proj_tm(wv, wvn, 0, 256, hT, "hT", ti)
                evac(Vt[:, ti, 0:256], Vtb, pv, pn_)
            for ti in range(NMT):
                pv, pn_ = proj_tm(wo, won, 0, 256, hT, "hT", ti)
                evac(OG[:, ti, :], OGb, pv, pn_, func=AF.Sigmoid)
        csi = [0]

        def mlstm_chunks(h):
            Ch = Cst[:, h % 2, :, :]
            Chb = "Cst%d" % (h % 2)
            dve(lambda e: e.memset(Ch, 0.0), [], [Chb])
            def mchunk(c):
                full = c >= NPT
                samp = c == NCH
                if c < NPT:
                    ktsrc, ktb, vsrc, vb_, ti = KTp, KTpb, Vp, Vpb, c
                else:
                    ktsrc, ktb, vsrc, vb_, ti = KT, KTb, Vt, Vtb, c - NPT
                tk = slice(ti * 128, (ti + 1) * 128)
                (Ktok, Ktokb), (Vw, Vwb), (PT, PTb), (ytok, ytokb) = Ktok2[c % 2], Vw2[c % 2], PT2[c % 2], ytok2[c % 2]
                if samp:
                    wcol, fcol = wtoks[:, h:h + 1], ftoks[:, h:h + 1]
                    wcb, fcb = "wtoks", "ftoks"
                else:
                    wcol, fcol = wtok[:, h * 16 + c:h * 16 + c + 1], ftok[:, h * 16 + c:h * 16 + c + 1]
                    wcb, fcb = "wtok", "ftok"
                for dh in range(2):
                    tr(ptr[0][:, dh * 128:(dh + 1) * 128], "ptr0", ktsrc[:, dh, tk], ktb, identb, inc=(dh == 1))
                evac(Ktok, Ktokb, ptr[0][:, 0:256], "ptr0")
                dve(lambda e, vsrc=vsrc, ti=ti, wcol=wcol: e.tensor_scalar(out=Vw, in0=vsrc[:, ti, :], scalar1=wcol,
                                                                            scalar2=None, op0=ALU.mult),
                     [vb_, wcb], [Vwb])
                if not samp:
                    dcol = decbc[:, h * 16 + c:h * 16 + c + 1]
                    dve(lambda e, dcol=dcol: e.tensor_scalar(out=Ch, in0=Ch, scalar1=dcol, scalar2=None,
                                                             op0=ALU.mult), [Chb, "decbc"], [Chb])
                if full:
                    mm(pst[:, 0:128], "pst", [(ktsrc[:, dh, tk], QT[:, dh, tk]) for dh in range(2)], [ktb, QTb])
                    msk = masks if samp else maskc
                    dve(lambda e, msk=msk: e.tensor_tensor(out=PT, in0=pst[:, 0:128], in1=msk, op=ALU.mult),
                        ["pst", "cb"], [PTb])
                    if not samp:
                        act(lambda e: e.copy(out=Cbf, in_=Ch), [Chb], [Cbfb])
                        mm(pnum[:, 0:257], "pnum",
                           [(PT, Vw)] + [(QT[:, dh, tk], Cbf[:, dh, :]) for dh in range(2)],
                           [PTb, Vwb, QTb, Cbfb])
                    else:
                        mm(pnum[:, 0:257], "pnum", [(PT, Vw)], [PTb, Vwb], first=True, last=False)
                if not samp:
                    for dh in range(2):
                        mm(pstate[:, 0:257], "pstate", [(Ktok[:, dh * 128:(dh + 1) * 128], Vw)], [Ktokb, Vwb])
                        dve(lambda e, dh=dh: e.tensor_tensor(out=Ch[:, dh, :], in0=Ch[:, dh, :],
                                                             in1=pstate[:, 0:257], op=ALU.add),
                            ["pstate", Chb], [Chb])
                else:
                    def sgrp(g, Cs, Csb, Csbf, Csbfb, QX, QXb, VwX, VwXb):
                        js = slice(g * SG, (g + 1) * SG)
                        for dh in range(2):
                            sdma(Cs[:, :, dh, 0:256],
                                 sC[js, h, dh * 128:(dh + 1) * 128, :].rearrange("j p e -> p j e"), [], [Csb])
                        ncols = nT[:, :, g * SG * 4 + h:(g + 1) * SG * 4:4].rearrange("p dh j -> p j dh")
                        dve(lambda e, ncols=ncols: e.tensor_copy(out=Cs[:, :, :, 256], in_=ncols), [nTb], [Csb])
                        dcols = sdecbc[:, h * 16 + g * SG:h * 16 + (g + 1) * SG]
                        Csv = Cs.rearrange("p j dh e -> p j (dh e)")
                        dve(lambda e, dcols=dcols, Csv=Csv: e.tensor_tensor(
                            out=Csv, in0=Csv, in1=dcols.unsqueeze(2).broadcast_to([128, SG, 514]), op=ALU.mult),
                            [Csb, "sdecbc"], [Csb])
                        act(lambda e: e.copy(out=Csbf, in_=Cs), [Csb], [Csbfb])
                        for dh in range(2):
                            dve(lambda e, dh=dh, js=js, tk=tk: e.tensor_tensor(
                                out=QX[:, dh, :, :], in0=QT[:, dh, tk].unsqueeze(1).broadcast_to([128, SG, 128]),
                                in1=qxmask[:, js, :], op=ALU.mult), [QTb, "cb"], [QXb])
                        pairs = [(QX[:, dh, j, :], Csbf[:, j, dh, :]) for j in range(SG) for dh in range(2)]
                        mm(pnum[:, 0:257], "pnum", pairs, [QXb, Csbfb], first=False, last=(g == 16 // SG - 1))
                        dve(lambda e, js=js: e.tensor_tensor(
                            out=VwX, in0=Vw.unsqueeze(1).broadcast_to([128, SG, 257]),
                            in1=vxmask[:, js].unsqueeze(2).broadcast_to([128, SG, 257]), op=ALU.mult),
                            [Vwb, "cb"], [VwXb])
                        for j in range(SG):
                            for dh in range(2):
                                mm(pstate[:, 0:257], "pstate", [(Ktok[:, dh * 128:(dh + 1) * 128], VwX[:, j, :])],
                                   [Ktokb, VwXb])
                                dve(lambda e, j=j, dh=dh: e.tensor_tensor(out=Cs[:, j, dh, :], in0=Cs[:, j, dh, :],
                                                                          in1=pstate[:, 0:257], op=ALU.add),
                                    ["pstate", Csb], [Csb])
                        for dh in range(2):
                            sdma(o_sC[js, h, dh * 128:(dh + 1) * 128, :].rearrange("j p e -> p j e"),
                                 Cs[:, :, dh, 0:256], [Csb], [])
                        ndst = nTo[:, :, g * SG * 4 + h:(g + 1) * SG * 4:4].rearrange("p dh j -> p j dh")
                        dve(lambda e, ndst=ndst: e.tensor_copy(out=ndst, in_=Cs[:, :, :, 256]), [Csb], [nTob])
                    for g in range(16 // SG):
                        k_ = csi[0]
                        csi[0] += 1
                        sgrp(g, *Cs3[k_ % 3], *Csbf2[k_ % 2], *QX2[k_ % 2], *VwX2[k_ % 2])
                if full:
                    ss, rs, sc, sbn = statslot()
                    dve(lambda e, sc=sc: e.tensor_copy(out=sc, in_=pnum[:, 256:257]), ["pnum"], [sbn + "s"])
                    dve(lambda e, sc=sc: e.scalar_tensor_tensor(out=sc, in0=sc, scalar=-1.0, in1=sc, op0=ALU.mult,
                                                                op1=ALU.max), [sbn + "s"], [sbn + "s"])
                    dve(lambda e, fcol=fcol, sc=sc: e.tensor_tensor(out=sc, in0=sc, in1=fcol, op=ALU.max),
                        [sbn + "s", fcb], [sbn + "s"])
                    dve(lambda e, sc=sc: e.reciprocal(out=sc, in_=sc), [sbn + "s"], [sbn + "s"])
                    head_epilogue(pnum[:, 0:256], "pnum", sc, OG[:, ti, :], OGb, ti, h * 256, sbn + "s", ytok, ytokb)
                if c == NCH - 1:
                    sdma(o_pC[h].rearrange("(dh p) e -> p dh e", p=128), Ch[:, :, 0:256], [Chb], [])
            for c in range(NCH + 1):
                mchunk(c)
            tr(pmisc[0:2, 0:128], "pmisc", Ch[:, :, 256], Chb, identf)
            evac(nrow[0:2, 0:128], nrowb, pmisc[0:2, 0:128], "pmisc", eng="dve")
            sdma(o_pn[h * 2:h * 2 + 2, :], nrow[0:2, 0:128], [nrowb], [])

        aupt = stat
        sdma(aups[0:16, :], aup[:, :], [], [aupb])
        dve(lambda e: e.tensor_scalar(out=negab[:, 0:4], in0=spm[:, P_AB:P_AB + 4], scalar1=-1.0, scalar2=None,
                                      op0=ALU.mult), ["spm"], ["negab"])
        BG = [(0, 512), (512, 512), (1024, 512), (1536, 512), (2048, 128)]

        def gla_head(h):
            wq, wqn = load_w(w_in, WOFF[C_GQ + h * 128], KD, 128)
            wk, wkn = load_w(w_in, WOFF[C_GK + h * 128], KD, 128)
            wv, wvn = load_w(w_in, WOFF[C_GV + h * 256], KD, 256)
            QTg, KTg, KTpg = QTg_, KTg_, KTp[:, 0, :]
            QTb, KTb, Vt, Vtb, OG, OGb = QTgb, KTgb, Vtg, Vtgb, RG, RGb
            for (t0, N) in MG:
                pv, pn_ = proj_fm(wq, wqn, 0, 128, hT, "hT", t0, N)
                evac(QTg[:, t0:t0 + N], QTb, pv, pn_, scale=128.0 ** -0.5)
            for (t0, N) in MG:
                pv, pn_ = proj_fm(wk, wkn, 0, 128, hT, "hT", t0, N)
                evac(KTg[:, t0:t0 + N], KTb, pv, pn_)
            for (t0, N) in PG:
                pv, pn_ = proj_fm(wk, wkn, 0, 128, hTp, "hTp", t0, N)
                evac(KTpg[:, t0:t0 + N], KTpb, pv, pn_)
            wr, wrn = load_w(w_in, WOFF[C_GR + h * 256], KD, 256)
            for ti in range(NPT):
                pv, pn_ = proj_tm(wv, wvn, 0, 256, hTp, "hTp", ti)
                evac(Vp[:, ti, 0:256], Vpb, pv, pn_)
            for ti in range(NMT):
                pv, pn_ = proj_tm(wv, wvn, 0, 256, hT, "hT", ti)
                evac(Vt[:, ti, 0:256], Vtb, pv, pn_)
            for ti in range(NMT):
                pv, pn_ = proj_tm(wr, wrn, 0, 256, hT, "hT", ti)
                evac(OG[:, ti, :], OGb, pv, pn_, func=AF.Silu)
            nab = negab[:, h:h + 1]
            for (t0, N) in BG:
                mm(pmisc[:, 0:N], "pmisc", [(aups[0:16, h * 128:(h + 1) * 128], galT[0:16, t0:t0 + N])],
                   [aupb, galb])
                act(lambda e, t0=t0, N=N: e.activation(out=BT[:, t0:t0 + N], in_=pmisc[:, 0:N], func=AF.Exp,
                                                       bias=nab, scale=-1.0), ["pmisc", "negab"], [BTb])
            act(lambda e: e.activation(out=BT, in_=BT, func=AF.Ln, bias=1.0), [BTb], [BTb])
            dve(lambda e: e.tensor_scalar(out=BT, in0=BT, scalar1=-1.0 / 16.0, scalar2=None, op0=ALU.mult),
                [BTb], [BTb])
            dve(lambda e: e.tensor_tensor_scan(out=BT[:, 0:2048], data0=onesf[:, 0:1].broadcast_to([128, 2048]),
                                               data1=BT[:, 0:2048], initial=0.0, op0=ALU.mult, op1=ALU.add),
                [BTb, "cf"], [BTb])
            dve(lambda e: e.tensor_tensor_scan(out=BT[:, 2048:2176], data0=resetm, data1=BT[:, 2048:2176],
                                               initial=0.0, op0=ALU.mult, op1=ALU.add), [BTb, "cf"], [BTb])
            Sh = Sst[:, h % 2, :]
            Shb = "Sst%d" % (h % 2)
            dve(lambda e: e.memset(Sh, 0.0), [], [Shb])

            def gchunk(c):
                full = c >= NPT
                samp = c == NCH
                if c < NPT:
                    ktsrc, ktb, vsrc, vb_, ti = KTpg, KTpb, Vp, Vpb, c
                else:
                    ktsrc, ktb, vsrc, vb_, ti = KTg, KTb, Vt, Vtb, c - NPT
                tk = slice(ti * 128, (ti + 1) * 128)
                tb = slice(2048, 2176) if samp else slice(c * 128, (c + 1) * 128)
                (eqt, eqtb), (QtT, QtTb), (KtT, KtTb), (KhT, KhTb) = eqt2[c % 2], QtT2[c % 2], KtT2[c % 2], KhT2[c % 2]
                (Ktok, Ktokb), (PT, PTb), (ytok, ytokb) = Khtok2[c % 2], PT2[c % 2], ytok2[c % 2]
                ss, rs, sc, sbn = statslot()
                if samp:
                    bend3 = BT[:, 2048 + 7:2176:8].unsqueeze(2).broadcast_to([128, 16, 8])
                    dve(lambda e: e.tensor_tensor(out=eqt.rearrange("p (j l) -> p j l", l=8), in0=bend3,
                                                  in1=BT[:, tb].rearrange("p (j l) -> p j l", l=8),
                                                  op=ALU.subtract), [BTb], [eqtb])
                    act(lambda e: e.activation(out=eqt, in_=eqt, func=AF.Exp), [eqtb], [eqtb])
                    act(lambda e: e.activation(out=dSs, in_=BT[:, 2048 + 7:2176:8], func=AF.Exp), [BTb], [dSsb])
                else:
                    bendc = BT[:, c * 128 + 127:c * 128 + 128]
                    bstc = zerocol if c == 0 else BT[:, c * 128 - 1:c * 128]
                    act(lambda e: e.activation(out=eqt, in_=BT[:, tb], func=AF.Exp, bias=bendc, scale=-1.0),
                        [BTb], [eqtb])
                    dve(lambda e: e.tensor_tensor(out=ss, in0=bendc, in1=bstc, op=ALU.subtract), [BTb, "cf"],
                        [sbn])
                    act(lambda e: e.activation(out=ss, in_=ss, func=AF.Exp), [sbn], [sbn])
                    dve(lambda e: e.tensor_scalar(out=rs, in0=bstc, scalar1=-1.0, scalar2=None, op0=ALU.mult),
                        [BTb, "cf"], [sbn + "r"])
                dve(lambda e: e.tensor_tensor(out=KhT, in0=ktsrc[:, tk], in1=eqt, op=ALU.mult), [ktb, eqtb], [KhTb])
                tr(ptr[0][:, 0:128], "ptr0", KhT, KhTb, identb)
                evac(Ktok[:, 0:128], Ktokb, ptr[0][:, 0:128], "ptr0")
                if full:
                    if samp:
                        act(lambda e: e.activation(out=eqt, in_=BT[:, tb], func=AF.Exp), [BTb, KhTb], [eqtb])
                    else:
                        act(lambda e: e.activation(out=eqt, in_=BT[:, tb], func=AF.Exp, bias=rs), [BTb, sbn + "r", KhTb],
                            [eqtb])
                    dve(lambda e: e.tensor_tensor(out=QtT, in0=QTg[:, tk], in1=eqt, op=ALU.mult), [QTb, eqtb], [QtTb])
                    if samp:
                        act(lambda e: e.activation(out=eqt, in_=BT[:, tb], func=AF.Exp, scale=-1.0), [BTb, QtTb],
                            [eqtb])
                    else:
                        act(lambda e: e.activation(out=eqt, in_=BT[:, tb], func=AF.Exp, bias=bstc, scale=-1.0),
                            [BTb, QtTb, "cf"], [eqtb])
                    dve(lambda e: e.tensor_tensor(out=KtT, in0=ktsrc[:, tk], in1=eqt, op=ALU.mult), [ktb, eqtb],
                        [KtTb])
                    mm(pst[:, 0:128], "pst", [(KtT, QtT)], [KtTb, QtTb])
                    msk = masks if samp else maskc
                    dve(lambda e: e.tensor_tensor(out=PT, in0=pst[:, 0:128], in1=msk, op=ALU.mult), ["pst", "cb"],
                        [PTb])
                    if not samp:
                        mm(pnum[:, 0:256], "pnum", [(PT, vsrc[:, ti, 0:256]), (QtT, Sbf)], [PTb, vb_, QtTb, Sbfb])
                    else:
                        mm(pnum[:, 0:256], "pnum", [(PT, vsrc[:, ti, 0:256])], [PTb, vb_], first=True, last=False)
                if not samp:
                    mm(pstate[:, 0:256], "pstate", [(Ktok[:, 0:128], vsrc[:, ti, 0:256])], [Ktokb, vb_])
                    dve(lambda e: e.scalar_tensor_tensor(out=Sh, in0=Sh, scalar=ss, in1=pstate[:, 0:256],
                                                         op0=ALU.mult, op1=ALU.add), [Shb, sbn, "pstate"], [Shb])
                    act(lambda e: e.copy(out=Sbf, in_=Sh), [Shb], [Sbfb])
                else:
                    def sgrp(g, Cs, Csb, Csbf, Csbfb, QX, QXb, VwX, VwXb):
                        js = slice(g * SG, (g + 1) * SG)
                        Ss = Cs[:, :, 0, 0:256]
                        Ssb = Csbf[:, :, 0, 0:256]
                        sdma(Ss, sS[js, h].rearrange("j p e -> p j e"), [], [Csb])
                        act(lambda e, Ss=Ss, Ssb=Ssb: e.copy(out=Ssb, in_=Ss), [Csb], [Csbfb])
                        dve(lambda e, js=js: e.tensor_tensor(
                            out=QX[:, 0, :, :], in0=QtT.unsqueeze(1).broadcast_to([128, SG, 128]),
                            in1=qxmask[:, js, :], op=ALU.mult), [QtTb, "cb"], [QXb])
                        mm(pnum[:, 0:256], "pnum", [(QX[:, 0, j, :], Ssb[:, j, :]) for j in range(SG)],
                           [QXb, Csbfb], first=False, last=(g == 16 // SG - 1))
                        dve(lambda e, js=js: e.tensor_tensor(
                            out=VwX[:, :, 0:256], in0=vsrc[:, ti, 0:256].unsqueeze(1).broadcast_to([128, SG, 256]),
                            in1=vxmask[:, js].unsqueeze(2).broadcast_to([128, SG, 256]), op=ALU.mult),
                            [vb_, "cb"], [VwXb])
                        for j in range(SG):
                            mm(pstate[:, 0:256], "pstate", [(Ktok[:, 0:128], VwX[:, j, 0:256])], [Ktokb, VwXb])
                            dcol = dSs[:, g * SG + j:g * SG + j + 1]
                            dve(lambda e, j=j, dcol=dcol, Ss=Ss: e.scalar_tensor_tensor(
                                out=Ss[:, j, :], in0=Ss[:, j, :], scalar=dcol, in1=pstate[:, 0:256], op0=ALU.mult,
                                op1=ALU.add), [Csb, dSsb, "pstate"], [Csb])
                        sdma(o_sS[js, h].rearrange("j p e -> p j e"), Ss, [Csb], [])
                    for g in range(16 // SG):
                        k_ = csi[0]
                        csi[0] += 1
                        sgrp(g, *Cs3[k_ % 3], *Csbf2[k_ % 2], *QX2[k_ % 2], *VwX2[k_ % 2])
                if full:
                    head_epilogue(pnum[:, 0:256], "pnum", None, OG[:, ti, :], OGb, ti, 1024 + h * 256, None, ytok, ytokb)
                if c == NCH - 1:
                    sdma(o_pS[h], Sh, [Shb], [])
            dve(lambda e: e.memset(Sbf, 0.0), [], [Sbfb])
            for c in range(NCH + 1):
                gchunk(c)

        mlstm_proj(0)
        S.barrier()
        print("arena phase 1a used %d / %d (mark %d)" % (apos[0] * 2, ARENA * 2, gmark * 2))
        apos[0] = gmark
        Cs3 = [carve("Cs%d" % i, [SG, 2, 257], F32) for i in range(3)]
        Csbf2 = [carve("Csbf%d" % i, [SG, 2, 257]) for i in range(2)]
        QX2 = [carve("QX%d" % i, [2, SG, 128]) for i in range(2)]
        VwX2 = [carve("VwX%d" % i, [SG, 257]) for i in range(2)]
        for h in range(4):
            if h > 0:
                mlstm_proj(h)
            mlstm_chunks(h)
            gla_head(h)
        tr(pmisc[:, 0:128], "pmisc", nTo.rearrange("p a b -> p (a b)"), nTob, identf)
        evac(nrow[:, 0:128], nrowb, pmisc[:, 0:128], "pmisc", eng="dve")
        sdma(o_sn[:, :], nrow[:, 0:128], [nrowb], [])
        if stop_after == "gla":
            return finish(nc, S, dbg_out)

        arena_reset()
        wide[0] = False
        yTa = hTp[:].rearrange("p k t -> p (k t)")[:, 0:8 * TM].rearrange("p (k t) -> p k t", k=8)
        yTg, yTgb = carve("yTg", [8, TM])
        mT, mTb = carve("mT", [KD, TM])
        ystg, ystgb = carve("ystg", [D])
        sg, sgb = carve("sg", [2, TM])
        tmpm, tmpmb = carve("tmpm", [512])
        xs_ = [carve("xs%d" % i, [256], F32)[0] for i in range(8)]
        for ti in range(NMT):
            sdma(ystg, yscr[ti * 128:(ti + 1) * 128, :], ["yscr"], [ystgb])
            for half in range(2):
                pt = ptr[half]
                pbn = "ptr%d" % half
                for j in range(8):
                    k = half * 8 + j
                    tr(pt[:, j * 128:(j + 1) * 128], pbn, ystg[:, k * 128:(k + 1) * 128], ystgb, identb, inc=(j == 7))
                for j in range(8):
                    k = half * 8 + j
                    dstt, dstb = (yTa, "hTp") if half == 0 else (yTg, yTgb)
                    dst = dstt[:, j, ti * 128:(ti + 1) * 128]
                    hcol = spm[:, P_HNM + k:P_HNM + k + 1]
                    if j % 2 == 0:
                        act(lambda e, dst=dst, j=j, pt=pt, hcol=hcol: e.activation(
                            out=dst, in_=pt[:, j * 128:(j + 1) * 128], func=AF.Copy, scale=hcol),
                            [pbn, "spm"], [dstb])
                    else:
                        dve(lambda e, dst=dst, j=j, pt=pt, hcol=hcol: e.tensor_scalar(
                            out=dst, in0=pt[:, j * 128:(j + 1) * 128], scalar1=hcol, scalar2=None, op0=ALU.mult),
                            [pbn, "spm"], [dstb])

        MGB = [(120, 392), (512, 512), (1024, 256)]

        def branch_group(cg):
            MG = MGB
            c0 = cg * 256
            for (gcol, wbr, ysrc, ysb, first) in ((C_GA, w_a, yTa, "hTp", True), (C_GB, w_b, yTg, yTgb, False)):
                wgt, wgtn = load_w(w_in, WOFF[gcol + c0], KD, 256)
                for cb_ in range(2):
                    for (t0, N) in MG:
                        pv, pn_ = proj_fm(wgt, wgtn, cb_ * 128, 128, hT, "hT", t0, N)
                        evac(sg[:, cb_, t0:t0 + N], sgb, pv, pn_, func=AF.Sigmoid)
                wbt, wbtn = load_w(wbr, cg * 8 * 256, 8, 256)
                for cb_ in range(2):
                    kk = cg * 2 + cb_
                    for (t0, N) in MG:
                        bank, bname = nextbank()
                        mm(bank[:, 0:N], bname,
                           [(wbt[:, k, cb_ * 128:(cb_ + 1) * 128], ysrc[:, k, t0:t0 + N]) for k in range(8)],
                           [wbtn, ysb])
                        if first:
                            dve(lambda e, bank=bank, N=N, t0=t0, cb_=cb_, kk=kk: e.tensor_tensor(
                                out=mT[:, kk, t0:t0 + N], in0=bank[:, 0:N], in1=sg[:, cb_, t0:t0 + N], op=ALU.mult),
                                [bname, sgb], [mTb])
                        else:
                            dve(lambda e, bank=bank, N=N, t0=t0, cb_=cb_: e.tensor_tensor(
                                out=tmpm[:, 0:N], in0=bank[:, 0:N], in1=sg[:, cb_, t0:t0 + N], op=ALU.mult),
                                [bname, sgb], [tmpmb])
                            dve(lambda e, N=N, t0=t0, kk=kk: e.tensor_tensor(
                                out=mT[:, kk, t0:t0 + N], in0=mT[:, kk, t0:t0 + N], in1=tmpm[:, 0:N], op=ALU.add),
                                [tmpmb, mTb], [mTb])
        for cg in range(8):
            branch_group(cg)

        def wout_group(cg):
            c0 = cg * 256
            wot, wotn = load_w(w_o, cg * KD * 256, KD, 256)
            for ti in range(NMT):
                xs = xs_[(cg * NMT + ti) % 8]
                xsb = "xsb%d" % ((cg * NMT + ti) % 8)
                S.dma("pool", lambda e, xs=xs, ti=ti: e.dma_start(out=xs[:], in_=xmain[ti * 128:(ti + 1) * 128, c0:c0 + 256]),
                      writes=[xsb])
                bank, bname = nextbank()
                mm(bank[:, 0:256], bname, [(mT[:, k, ti * 128:(ti + 1) * 128], wot[:, k, :]) for k in range(KD)],
                   [wotn, mTb])
                dve(lambda e, xs=xs, bank=bank: e.tensor_tensor(out=xs[:], in0=xs[:], in1=bank[:, 0:256], op=ALU.add),
                    [xsb, bname], [xsb])
                sdma(x1scr[ti * 128:(ti + 1) * 128, c0:c0 + 256], xs[:], [xsb], ["x1scr"])
        for cg in range(8):
            wout_group(cg)

        arena_reset()
        sdma(wbc[:], nfw.partition_broadcast(128), [], ["wbc"])
        xt2 = [carve("xt2_%d" % i, [D], F32) for i in range(4)]
        xn2 = [carve("xn2_%d" % i, [D], BF16) for i in range(4)]
        sq2, sq2b = carve("sq2", [D], BF16)
        for ti in range(NMT):
            (x_t, xb), (x_n, xnb) = xt2[ti % 4], xn2[ti % 4]
            sdma(x_t, x1scr[ti * 128:(ti + 1) * 128, :], ["x1scr"], [xb])
            norm_to_T(x_t, xb, x_n, xnb, hT, "hT", ti, sq2, sq2b)

        arena_reset()
        HT_ = 640
        actT, actTb = carve("actT", [KF, HT_])
        upad2 = [carve("upad%d" % i, [2 + HT_], F32) for i in range(2)]
        tb2 = [carve("tbuf%d" % i, [HT_], F32) for i in range(2)]
        pb2 = [carve("pbuf%d" % i, [HT_], F32) for i in range(2)]
        gb2 = [carve("gbuf%d" % i, [HT_]) for i in range(2)]
        w4f = [wsl[i // 2][:, (i % 2) * KD * 128:((i % 2) + 1) * KD * 128] for i in range(6)]
        w4 = [v.rearrange("p (k c) -> p k c", k=KD) for v in w4f]
        w4i = [0]

        def load_w4(packed, kf):
            i = w4i[0] % 6
            w4i[0] += 1
            name = "w4_%d" % i
            flat = w4f[i]
            off = kf * KD * 128
            S.dma("pool", lambda e: e.dma_start(out=flat, in_=packed[:, off:off + KD * 128]), writes=[name])
            return w4[i], name
        ucar, ucarb = carve("ucar", [KF, 2], F32)
        ucv, ucvb = carve("ucv", [KF, 34], F32)
        scvT, scvTb = carve("scvT", [KF, 32], F32)
        srow, srowb = carve("srow", [512], F32)
        fst, fstb = carve("fst", [5, 128], F32)
        wdsf = [hTp[:].rearrange("p k t -> p (k t)")[:, i * KF * 128:(i + 1) * KF * 128] for i in range(2)]
        wds = [v.rearrange("p (k c) -> p k c", k=KF) for v in wdsf]
        x2scr = x2scr_
        CW = lambda j, k: spm[:, P_CW + j * KF + k:P_CW + j * KF + k + 1]
        CBc = lambda k: spm[:, P_CB + k:P_CB + k + 1]
        sc32 = sconv.rearrange("j r c -> (j r) c")
        for k4 in range(KF // 4):
            sdma(srow[0:32, :], sc32[:, k4 * 512:(k4 + 1) * 512], [], [srowb])
            for q in range(4):
                tr(pmisc[:, q * 32:(q + 1) * 32], "pmisc", srow[0:32, q * 128:(q + 1) * 128], srowb,
                   identf[0:32, 0:32], inc=(q == 3))
            evac(scvT[:, k4 * 4:(k4 + 1) * 4, :], scvTb, pmisc[:, 0:128].rearrange("p (a b) -> p a b", a=4),
                 "pmisc", eng="dve")
        dve(lambda e: e.memset(ucar, 0.0), [], [ucarb])
        wdi = [0]

        FLO = [120, 0]
        FGR = [[(120, 200), (320, 320)], [(0, 320), (320, 320)]]

        def ffn_block(half, kf):
            g0 = half * HT_
            npr = HT_ if half == 0 else 512
            wu, wun = load_w4(w_up, kf)
            wg_, wgn_ = load_w4(w_gt, kf)
            (upad, upadb), (tb_, tbb), (pb_, pbb), (gb_, gbb) = upad2[kf % 2], tb2[kf % 2], pb2[kf % 2], gb2[kf % 2]
            dve(lambda e: e.tensor_copy(out=upad[:, 0:2], in_=ucar[:, kf, :]), [ucarb], [upadb])
            lo = FLO[half]
            for (l0, N) in FGR[half]:
                t0 = g0 + l0
                pv, pn_ = proj_fm(wu, wun, 0, 128, hT, "hT", t0, N)
                evac(upad[:, 2 + l0:2 + l0 + N], upadb, pv, pn_, eng="act")
                act(lambda e, pv=pv, l0=l0, N=N: e.activation(out=tb_[:, l0:l0 + N], in_=pv, func=AF.Identity,
                                                              bias=CBc(kf), scale=CW(2, kf)), [pn_, "spm"], [tbb])
                pv, pn_ = proj_fm(wg_, wgn_, 0, 128, hT, "hT", t0, N)
                evac(gb_[:, l0:l0 + N], gbb, pv, pn_, eng="act")
            if half == 0:
                dve(lambda e: e.tensor_copy(out=ucar[:, kf, :], in_=upad[:, HT_:HT_ + 2]), [upadb], [ucarb])
            else:
                dve(lambda e: e.tensor_copy(out=ucv[:, kf, 0:2], in_=upad[:, 512:514]), [upadb], [ucvb])
                u3 = upad[:, 2 + 512:2 + 640].rearrange("p (j l) -> p j l", l=8)
                dve(lambda e, u3=u3: e.tensor_copy(out=ucv[:, kf, 2:34].rearrange("p (j r) -> p j r", r=2),
                                                   in_=u3[:, :, 6:8]), [upadb], [ucvb])
            dve(lambda e: e.scalar_tensor_tensor(out=tb_[:, lo:npr], in0=upad[:, 1 + lo:1 + npr], scalar=CW(1, kf),
                                                 in1=tb_[:, lo:npr], op0=ALU.mult, op1=ALU.add),
                [upadb, tbb, "spm"], [tbb])
            dve(lambda e: e.scalar_tensor_tensor(out=tb_[:, lo:npr], in0=upad[:, lo:npr], scalar=CW(0, kf),
                                                 in1=tb_[:, lo:npr], op0=ALU.mult, op1=ALU.add),
                [upadb, tbb, "spm"], [tbb])
            if half == 1:
                t3 = tb_[:, 512:640].rearrange("p (j l) -> p j l", l=8)
                u3 = upad[:, 2 + 512:2 + 640].rearrange("p (j l) -> p j l", l=8)
                s3 = scvT[:, kf, :].rearrange("p (j r) -> p j r", r=2)
                dve(lambda e, t3=t3, u3=u3: e.scalar_tensor_tensor(
                    out=t3[:, :, 1:8], in0=u3[:, :, 0:7], scalar=CW(1, kf), in1=t3[:, :, 1:8], op0=ALU.mult,
                    op1=ALU.add), [upadb, tbb, "spm"], [tbb])
                dve(lambda e, t3=t3, s3=s3: e.scalar_tensor_tensor(
                    out=t3[:, :, 0:1], in0=s3[:, :, 1:2], scalar=CW(1, kf), in1=t3[:, :, 0:1], op0=ALU.mult,
                    op1=ALU.add), [scvTb, tbb, "spm"], [tbb])
                dve(lambda e, t3=t3, u3=u3: e.scalar_tensor_tensor(
                    out=t3[:, :, 2:8], in0=u3[:, :, 0:6], scalar=CW(0, kf), in1=t3[:, :, 2:8], op0=ALU.mult,
                    op1=ALU.add), [upadb, tbb, "spm"], [tbb])
                dve(lambda e, t3=t3, s3=s3: e.scalar_tensor_tensor(
                    out=t3[:, :, 0:2], in0=s3[:, :, 0:2], scalar=CW(0, kf), in1=t3[:, :, 0:2], op0=ALU.mult,
                    op1=ALU.add), [scvTb, tbb, "spm"], [tbb])
            fs = slice(lo, HT_)
            act(lambda e: e.activation(out=pb_[:, fs], in_=tb_[:, fs], func=AF.Square, scale=0.044715 ** 0.5), [tbb],
                [pbb])
            dve(lambda e: e.scalar_tensor_tensor(out=pb_[:, fs], in0=pb_[:, fs], scalar=1.0, in1=tb_[:, fs],
                                                 op0=ALU.add, op1=ALU.mult), [pbb, tbb], [pbb])
            act(lambda e: e.activation(out=pb_[:, fs], in_=pb_[:, fs], func=AF.Sigmoid, scale=1.5957691216057308),
                [pbb], [pbb])
            dve(lambda e: e.tensor_tensor(out=pb_[:, fs], in0=pb_[:, fs], in1=tb_[:, fs], op=ALU.mult), [pbb, tbb],
                [pbb])
            dve(lambda e: e.tensor_tensor(out=actT[:, kf, fs], in0=pb_[:, fs], in1=gb_[:, fs], op=ALU.mult),
                [pbb, gbb], [actTb])

        def down_block(half, cbk):
            g0 = half * HT_
            i = wdi[0] % 2
            wdi[0] += 1
            (tb_, tbb) = tb2[i]
            wd = wds[i]
            wdn = "wds%d" % i
            off = cbk * KF * 128
            wdf = wdsf[i]
            S.dma("pool", lambda e: e.dma_start(out=wdf[:, 0:22 * 128], in_=w_dn[:, off:off + 22 * 128]),
                  writes=[wdn])
            S.dma("pool", lambda e: e.dma_start(out=wdf[:, 22 * 128:44 * 128],
                                                in_=w_dn[:, off + 22 * 128:off + 44 * 128]), writes=[wdn])
            for (l0, N) in FGR[half]:
                bank, bname = nextbank()
                mm(bank[:, 0:N], bname, [(wd[:, k, :], actT[:, k, l0:l0 + N]) for k in range(KF)],
                   [wdn, actTb])
                evac(tb_[:, l0:l0 + N], tbb, bank[:, 0:N], bname, eng="act")
            tts = range(1, 5) if half == 0 else range(5)
            for tt in tts:
                dstp = pst[:, tt * 128:(tt + 1) * 128] if tt < 4 else pnum[:, 0:128]
                dstn = "pst" if tt < 4 else "pnum"
                tr(dstp, dstn, tb_[:, tt * 128:(tt + 1) * 128], tbb, identf, inc=(tt >= 3))
            evac(fst[:, 0:4, :], fstb, pst[:, 0:512].rearrange("p (a b) -> p a b", a=4), "pst", eng="dve")
            evac(fst[:, 4, :], fstb, pnum[:, 0:128], "pnum", eng="dve")
            for tt in tts:
                r0 = g0 + tt * 128
                sdma(x2scr[r0:r0 + 128, cbk * 128:(cbk + 1) * 128], fst[:, tt, :], [fstb], ["x2scr"])

        for half in range(2):
            for kf in range(KF):
                ffn_block(half, kf)
            for cbk in range(KD):
                down_block(half, cbk)
        oconv_s = o_sconv.rearrange("j r c -> (j r) c")
        for k4 in range(KF // 4):
            for q in range(4):
                tr(pmisc[0:34, q * 128:(q + 1) * 128], "pmisc", ucv[:, k4 * 4 + q, :], ucvb, identf, inc=(q == 3))
            evac(srow[0:34, :], srowb, pmisc[0:34, 0:512], "pmisc", eng="dve")
            sdma(o_pconv[:, k4 * 512:(k4 + 1) * 512], srow[0:2, :], [srowb], [])
            sdma(oconv_s[:, k4 * 512:(k4 + 1) * 512], srow[2:34, :], [srowb], [])

        arena_reset()
        sdma(wbc[:], fnw.partition_broadcast(128), [], ["wbc"])
        xa = [carve("xa%d" % i, [D], F32) for i in range(4)]
        xf = [carve("xf%d" % i, [D], F32) for i in range(4)]
        sq3, sq3b = carve("sq3", [D], BF16)
        for ti in range(1, NMT):
            (x_a, xab), (x_f, xfb) = xa[ti % 4], xf[ti % 4]
            sdma(x_a, x1scr[ti * 128:(ti + 1) * 128, :], ["x1scr"], [xab])
            S.dma("pool", lambda e, x_f=x_f, ti=ti: e.dma_start(out=x_f, in_=x2scr[ti * 128:(ti + 1) * 128, :]),
                  reads=["x2scr"], writes=[xfb])
            dve(lambda e, x_a=x_a, x_f=x_f: e.tensor_tensor(out=x_a, in0=x_a, in1=x_f, op=ALU.add), [xab, xfb], [xab])
            ss, rs, _, sbn = statslot()
            act(lambda e, x_a=x_a, ss=ss: e.activation(out=sq3, in_=x_a, func=AF.Square, accum_out=ss), [xab],
                [sq3b, sbn])
            rstd_from_ss(ss, rs, sbn, D)
            dve(lambda e, x_a=x_a, x_f=x_f, rs=rs: e.scalar_tensor_tensor(out=x_f, in0=x_a, scalar=rs, in1=wbc[:],
                                                                          op0=ALU.mult, op1=ALU.mult),
                [xab, sbn + "r", "wbc"], [xfb])
            sdma(yout[(ti - 1) * 128:ti * 128, :], x_f, [xfb], [])
        return finish(nc, S, dbg_out)


def finish(nc, S, dbg_out):
    S.finish()
    print("sim phase us:", [int(x) for x in S.sim_phase_us], "units", len(S.units))
    with nc.Block() as block:
        @block.sync
        def _(eng):
            S.emit("sp", eng)

        @block.tensor
        def _(eng):
            S.emit("pe", eng)

        @block.scalar
        def _(eng):
            S.emit("act", eng)

        @block.vector
        def _(eng):
            S.emit("dve", eng)

        @block.gpsimd
        def _(eng):
            S.emit("pool", eng)
    return nc


def make_consts():
    import ml_dtypes
    cb = np.zeros((128, CB_W), np.float32)
    cb[:, 0:128] = np.eye(128)
    s = np.arange(128)[:, None]
    t = np.arange(128)[None, :]
    cb[:, 128:256] = (s <= t)
    cb[:, 256:384] = (s <= t) & ((s // 8) == (t // 8))
    j = np.arange(16)[:, None]
    cb[:, 512:512 + 2048] = ((np.arange(128)[None, :] // 8) == j).astype(np.float32).reshape(1, 2048)
    cb[:, 2560:2576] = ((np.arange(128)[:, None] // 8) == np.arange(16)[None, :])
    cf = np.zeros((128, CF_W), np.float32)
    cf[:, 0:128] = np.eye(128)
    cf[:, 128:256] = 1.0
    cf[:, 256:384] = (np.arange(128)[None, :] % 8 != 0)
    cf[:, 384:512] = np.where(np.arange(128)[None, :] % 8 == 0, -1e30, 0.0)
    return cb.astype(ml_dtypes.bfloat16), cf


_NC_CACHE = {}


def _prep_inputs(inp):
    f32 = np.float32
    cbc, cfc = make_consts()
    xp = np.asarray(inp["x_prompt"], f32)
    xs = np.asarray(inp["x_sample"], f32)
    sp = np.zeros((128, SP_W), f32)
    ib = np.asarray(inp["mlstm_i_bias"], f32)[0]
    fb = np.asarray(inp["mlstm_f_bias"], f32)[0]
    sp[0:64, 0] = np.repeat(ib, 16)
    sp[0:64, 1] = np.repeat(fb, 16)
    sp[:, 2:6] = np.asarray(inp["gla_alpha_bias"], f32)[0].reshape(4, 128).T
    sp[:, 6:14] = np.asarray(inp["mlstm_head_norm_w"], f32)[0].reshape(8, 128).T
    sp[:, 14:22] = np.asarray(inp["gla_head_norm_w"], f32)[0].reshape(8, 128).T
    cw = np.asarray(inp["ffn_conv_w"], f32)[0]
    for j in range(3):
        sp[:, 22 + j * KF:22 + (j + 1) * KF] = cw[j].reshape(KF, 128).T
    sp[:, 22 + 3 * KF:22 + 4 * KF] = np.asarray(inp["ffn_conv_b"], f32)[0].reshape(KF, 128).T
    shared = {
        "w_in": _pack(np.asarray(inp["w_in"], f32)[0], KD, _win_blocks()),
        "nmw": np.asarray(inp["norm_mix_w"], f32).reshape(1, D),
        "nfw": np.asarray(inp["norm_ffn_w"], f32).reshape(1, D),
        "fnw": np.asarray(inp["final_norm_w"], f32).reshape(1, D),
        "cst_bf": cbc, "cst_f": cfc, "smallp": sp,
        "aup": np.asarray(inp["gla_alpha_up"], f32)[0],
        "w_a": _pack(np.asarray(inp["w_branch_a"], f32)[0], 8, [(c * 256, 256) for c in range(8)]),
        "w_b": _pack(np.asarray(inp["w_branch_b"], f32)[0], 8, [(c * 256, 256) for c in range(8)]),
        "w_o": _pack(np.asarray(inp["w_out"], f32)[0], KD, [(c * 256, 256) for c in range(8)]),
        "w_up": _pack(np.asarray(inp["ffn_w_up"], f32)[0], KD, [(c * 128, 128) for c in range(KF)]),
        "w_gt": _pack(np.asarray(inp["ffn_w_gate"], f32)[0], KD, [(c * 128, 128) for c in range(KF)]),
        "w_dn": _pack(np.asarray(inp["ffn_w_down"], f32)[0], KF, [(c * 128, 128) for c in range(KD)]),
    }
    sC = np.asarray(inp["state_mlstm_C"], f32)[0]
    sn = np.asarray(inp["state_mlstm_n"], f32)[0]
    sm = np.asarray(inp["state_mlstm_m"], f32)[0]
    sS = np.asarray(inp["state_gla_S"], f32)[0]
    scv = np.asarray(inp["state_ffn_conv"], f32)[0]
    maps = []
    for c in range(8):
        s, half = c // 2, c % 2
        xmain = np.zeros((TM, D), f32)
        if half == 1:
            xpre = np.ascontiguousarray(xp[s, 0:PRE])
            xmain[0:1152] = xp[s, PRE:2048]
        else:
            xpre = np.zeros((PRE, D), f32)
            xmain[128:1152] = xp[s, 0:1024]
        xmain[1152:1280] = xs[16 * c:16 * c + 16].reshape(128, D)
        m = dict(shared)
        m.update({
            "xpre": xpre, "xmain": xmain, "flag": np.full((1, 1), float(half), f32),
            "sC": np.ascontiguousarray(sC[16 * c:16 * c + 16]),
            "sn": np.ascontiguousarray(sn[16 * c:16 * c + 16].reshape(64, 256)),
            "smcol": np.ascontiguousarray(sm[16 * c:16 * c + 16].T.reshape(64, 1)),
            "sS": np.ascontiguousarray(sS[16 * c:16 * c + 16]),
            "sconv": np.ascontiguousarray(scv[16 * c:16 * c + 16]),
        })
        maps.append(m)
    return maps


def _assemble(results):
    f32 = np.float32
    y_p = np.zeros((4, 2048, D), f32)
    y_s = np.zeros((128, 8, D), f32)
    pC = np.zeros((1, 4, 4, 256, 256), f32)
    pn = np.zeros((1, 4, 4, 256), f32)
    pm = np.zeros((1, 4, 4), f32)
    pS = np.zeros((1, 4, 4, 128, 256), f32)
    pcv = np.zeros((1, 4, 2, DFF), f32)
    sCo = np.zeros((1, 128, 4, 256, 256), f32)
    sno = np.zeros((1, 128, 4, 256), f32)
    smo = np.zeros((1, 128, 4), f32)
    sSo = np.zeros((1, 128, 4, 128, 256), f32)
    scvo = np.zeros((1, 128, 2, DFF), f32)
    for c in range(8):
        r = results[c]
        s, half = c // 2, c % 2
        yo = np.asarray(r["yout"], f32)
        y_p[s, half * 1024:(half + 1) * 1024] = yo[0:1024]
        y_s[16 * c:16 * c + 16] = yo[1024:1152].reshape(16, 8, D)
        if half == 1:
            pC[0, s] = np.asarray(r["o_pC"], f32)
            pn[0, s] = np.asarray(r["o_pn"], f32).reshape(4, 256)
            pm[0, s] = np.asarray(r["o_pm"], f32).reshape(4, 16)[:, 15]
            pS[0, s] = np.asarray(r["o_pS"], f32)
            pcv[0, s] = np.asarray(r["o_pconv"], f32)
        sl = slice(16 * c, 16 * c + 16)
        sCo[0, sl] = np.asarray(r["o_sC"], f32)
        sno[0, sl] = np.asarray(r["o_sn"], f32).reshape(2, 16, 4, 128).transpose(1, 2, 0, 3).reshape(16, 4, 256)
        smo[0, sl] = np.asarray(r["o_sm"], f32).reshape(4, 16).T
        sSo[0, sl] = np.asarray(r["o_sS"], f32)
        scvo[0, sl] = np.asarray(r["o_sconv"], f32)
    return (y_p, y_s, pC, pn, pm, pS, pcv, sCo, sno, smo, sSo, scvo)


def kernel(**inputs):
    if "nc" not in _NC_CACHE:
        _NC_CACHE["nc"] = build()
    nc = _NC_CACHE["nc"]
    maps = _prep_inputs(inputs)
    res = run_bass_kernel_spmd(nc, maps, core_ids=list(range(8)))
    return _assemble(res.results)
```

```python
import numpy as np
from contextlib import ExitStack
import concourse.bass as bass
import concourse.mybir as mybir
from concourse.bass_utils import run_bass_kernel_spmd

F32 = mybir.dt.float32
BF16 = mybir.dt.bfloat16
AF = mybir.ActivationFunctionType
ALU = mybir.AluOpType
AX = mybir.AxisListType

D = 2048
NIN = 11288
DFF = 5632
KD = D // 128
KF = DFF // 128
PRE = 896
NPT = PRE // 128
TM = 1280
CB_W = 2576
CF_W = 520
SP_W = 22 + 4 * KF
NMT = TM // 128
NCH = 16
EPS = 1e-6

C_MQ, C_MK, C_MV, C_MO = 0, 1024, 2048, 3072
C_MI, C_MF = 4096, 4100
C_GQ, C_GK, C_GV, C_GR = 4104, 4616, 5128, 6152
C_GAL = 7176
C_GA, C_GB = 7192, 9240


def _win_blocks():
    blks = [(C_MI, 8), (C_GAL, 16)]
    for hh in range(4):
        blks += [(C_MQ + hh * 256, 256), (C_MK + hh * 256, 256), (C_MV + hh * 256, 256), (C_MO + hh * 256, 256)]
    for hh in range(4):
        blks += [(C_GQ + hh * 128, 128), (C_GK + hh * 128, 128), (C_GV + hh * 256, 256), (C_GR + hh * 256, 256)]
    for cg in range(8):
        blks += [(C_GA + cg * 256, 256), (C_GB + cg * 256, 256)]
    return blks


def _win_offsets():
    off = {}
    o = 0
    for c0, n in _win_blocks():
        off[c0] = o
        o += KD * n
    return off


def _pack(w, nk, blocks):
    outs = []
    for c0, n in blocks:
        outs.append(np.ascontiguousarray(w[:, c0:c0 + n].reshape(nk, 128, n).transpose(1, 0, 2)).reshape(128, nk * n))
    return np.ascontiguousarray(np.concatenate(outs, axis=1))


class _Probe:
    def __init__(self):
        self.calls = []

    def __getattr__(self, name):
        def f(*a, **k):
            self.calls.append((name, a, k))
            return self
        return f

    def then_inc(self, *a, **k):
        return self


def _free_elems(ap):
    n = 1
    for s in ap.shape[1:]:
        n *= int(s)
    return n


def _est(eng, fn):
    p = _Probe()
    fn(p)
    name, a, k = p.calls[0]
    out = k.get("out", a[0] if a else None)
    if eng == "pe":
        if name == "transpose":
            return 0.09
        rhs = k.get("rhs")
        n = _free_elems(rhs)
        f = 4.0 if rhs.dtype == F32 else 1.0
        return max(n, 64) * f / 2400.0 + 0.012
    n = _free_elems(out) if out is not None else 64
    if eng == "act":
        return (n + 400) / 1400.0 + (0.1 if k.get("accum_out") is not None else 0.0)
    if eng in ("dve", "pool"):
        f = 2.0 if name in ("tensor_tensor_scan",) else 1.0
        return max(n, 60) * f / 960.0 + 0.2
    return 0.1


def _dma_est(fn):
    p = _Probe()
    fn(p)
    name, a, k = p.calls[0]
    out = k.get("out")
    nbytes = 1
    for s in out.shape:
        nbytes *= int(s)
    nbytes *= 4 if out.dtype == F32 else 2
    return 2.0 + nbytes / 180e3, nbytes


class Sched:
    ENGS = ("pe", "act", "dve", "pool", "sp")
    XLAT = 0.4

    def __init__(self, esems, dsems):
        self.esem = esems
        self.dsems = dsems
        self.units = []
        self.bufs = {}
        self.phase = 0
        self.trace_phase = None
        self.open = {e: None for e in esems}

    def _st(self, b):
        st = self.bufs.get(b)
        if st is None:
            st = {"w": None, "r": {}}
            self.bufs[b] = st
        return st

    def _deps(self, uid, reads, writes):
        deps = set()
        for b in reads:
            st = self._st(b)
            if st["w"] is not None:
                deps.add(st["w"])
        for b in writes:
            st = self._st(b)
            if st["w"] is not None:
                deps.add(st["w"])
            deps.update(st["r"].keys())
        deps.discard(uid)
        return deps

    def _mark(self, uid, reads, writes):
        for b in reads:
            self._st(b)["r"][uid] = True
        for b in writes:
            st = self._st(b)
            st["w"] = uid
            st["r"] = {}

    def op(self, eng, fn, reads=(), writes=(), inc=True):
        u = self.open[eng]
        if u is None:
            u = {"eng": eng, "fns": [], "deps": set(), "dur": 0.0, "dma": False, "phase": self.phase,
                 "id": len(self.units), "lab": ",".join(writes)}
            self.units.append(u)
            self.open[eng] = u
        u["deps"] |= self._deps(u["id"], reads, writes)
        self._mark(u["id"], reads, writes)
        u["fns"].append(fn)
        u["dur"] += _est(eng, fn)
        if inc:
            self.open[eng] = None

    def dma(self, q, fn, reads=(), writes=()):
        uid = len(self.units)
        lat, nbytes = _dma_est(fn)
        u = {"eng": q, "fns": [fn], "deps": self._deps(uid, reads, writes), "dur": 0.06 if q == "sp" else 0.35,
             "dma": True, "lat": lat, "phase": self.phase, "id": uid, "lab": "dma:" + ",".join(writes) + "<" + ",".join(reads)}
        self.units.append(u)
        self._mark(uid, reads, writes)

    def barrier(self):
        for e, v in self.open.items():
            assert v is None, e
        self.phase += 1
        self.bufs = {}

    def barrier_all_dma(self, q="sp"):
        pass

    def finish(self):
        import heapq
        order = {e: [] for e in self.ENGS}
        nph = self.phase + 1
        by_phase = [[] for _ in range(nph)]
        for u in self.units:
            by_phase[u["phase"]].append(u)
        for ph in range(nph):
            us = by_phase[ph]
            ids = {u["id"] for u in us}
            ndep = {}
            users = {}
            for u in us:
                d = [x for x in u["deps"] if x in ids]
                u["deps"] = set(d)
                ndep[u["id"]] = len(d)
                for x in d:
                    users.setdefault(x, []).append(u)
            byid = {u["id"]: u for u in us}
            ready = {e: [] for e in self.ENGS}
            avail = {}
            for u in us:
                if ndep[u["id"]] == 0:
                    heapq.heappush(ready[u["eng"]], u["id"])
                    avail[u["id"]] = 0.0
            tfree = {e: 0.0 for e in self.ENGS}
            lastu = {}
            done = 0
            fin = {}
            while done < len(us):
                best = None
                for e in self.ENGS:
                    h = ready[e]
                    if not h:
                        continue
                    cand = None
                    tnow = tfree[e]
                    low = [i for i in h if avail[i] <= tnow]
                    if low:
                        cid = min(low)
                        st = tnow
                    else:
                        cid = min(h, key=lambda i: (avail[i], i))
                        st = avail[cid]
                    if best is None or st < best[0] or (st == best[0] and cid < best[2]):
                        best = (st, e, cid)
                st, e, cid = best
                ready[e].remove(cid)
                heapq.heapify(ready[e])
                u = byid[cid]
                u["st"] = st
                if avail[cid] >= tfree[e] - 1e-9 and u["deps"]:
                    u["why"] = max(u["deps"], key=lambda x: fin[x])
                else:
                    u["why"] = lastu.get(e)
                lastu[e] = cid
                end = st + u["dur"]
                tfree[e] = end
                f = end + (u["lat"] if u["dma"] else 0.0)
                fin[cid] = f
                order[e].append(u)
                done += 1
                for v in users.get(cid, ()):
                    ndep[v["id"]] -= 1
                    if ndep[v["id"]] == 0:
                        t = 0.0
                        for x in v["deps"]:
                            lat = 0.0 if (byid[x]["eng"] == v["eng"] and v["eng"] == "pe") else self.XLAT
                            t = max(t, fin[x] + lat)
                        avail[v["id"]] = t
                        heapq.heappush(ready[v["eng"]], v["id"])
            self.sim_phase_us = getattr(self, "sim_phase_us", []) + [max(tfree.values())]
            if getattr(self, "trace_phase", None) == ph:
                cur = max(us, key=lambda u: fin[u["id"]])["id"]
                chain = []
                while cur is not None and len(chain) < 400:
                    u = byid[cur]
                    chain.append((round(u["st"], 2), u["eng"], round(u["dur"], 2), u["lab"], len(u["fns"])))
                    cur = u.get("why")
                for c in chain[:400]:
                    print("   CP", c)
            busy = {e: 0.0 for e in self.ENGS}
            for u in us:
                busy[u["eng"]] += u["dur"]
            print("phase", ph, "sim %.0f us" % max(tfree.values()), {e: int(v) for e, v in busy.items()}, "units", len(us))
        cnt = {e: 0 for e in self.esem}
        dcnt = {q: [0] * len(v) for q, v in self.dsems.items()}
        dnext = {q: 0 for q in self.dsems}
        for e in self.ENGS:
            for u in order[e]:
                if u["dma"]:
                    i = dnext[e]
                    dnext[e] = (i + 1) % len(self.dsems[e])
                    u["prev"] = (self.dsems[e][i], dcnt[e][i]) if dcnt[e][i] > 0 else None
                    dcnt[e][i] += 16
                    u["ev"] = (self.dsems[e][i], dcnt[e][i])
                else:
                    cnt[e] += 1
                    u["ev"] = (self.esem[e], cnt[e])
        self.order = order
        byid = {u["id"]: u for u in self.units}
        self.prog = {e: [] for e in self.ENGS}
        all_dma = [u for u in self.units if u["dma"]]
        for e in self.ENGS:
            wd = {}
            cur_phase = 0
            for u in order[e]:
                waits = []
                if u["phase"] != cur_phase:
                    for e2 in self.esem:
                        if e2 == e and e == "pe":
                            continue
                        c = max([x["ev"][1] for x in order[e2] if x["phase"] < u["phase"] and not x["dma"]] or [0])
                        sem = self.esem[e2]
                        if c > 0 and wd.get(id(sem), 0) < c:
                            wd[id(sem)] = c
                            waits.append((sem, c))
                    for x in all_dma:
                        if x["phase"] < u["phase"] and wd.get(id(x["ev"][0]), 0) < x["ev"][1]:
                            wd[id(x["ev"][0])] = x["ev"][1]
                            waits.append(x["ev"])
                    cur_phase = u["phase"]
                for d in sorted(u["deps"]):
                    x = byid[d]
                    if x["eng"] == e and e == "pe":
                        continue
                    sem, c = x["ev"]
                    if wd.get(id(sem), 0) >= c:
                        continue
                    wd[id(sem)] = c
                    waits.append((sem, c))
                if u["dma"] and u["prev"] is not None:
                    sem, c = u["prev"]
                    if wd.get(id(sem), 0) < c:
                        wd[id(sem)] = c
                        waits.append((sem, c))
                self.prog[e].append((waits, u["fns"], u["ev"][0], 16 if u["dma"] else 1))
        wd = {}
        waits = []
        for x in all_dma:
            if wd.get(id(x["ev"][0]), 0) < x["ev"][1]:
                wd[id(x["ev"][0])] = x["ev"][1]
        semobj = {}
        for x in all_dma:
            semobj[id(x["ev"][0])] = x["ev"][0]
        self.final_waits = [(semobj[k], v) for k, v in wd.items()]

    def emit(self, eng_name, eng):
        for waits, fns, sem, inc in self.prog[eng_name]:
            for s, v in waits:
                eng.wait_ge(s, v)
            ins = None
            for fn in fns:
                ins = fn(eng)
            ins.then_inc(sem, inc)
        if eng_name == "sp":
            for s, v in self.final_waits:
                eng.wait_ge(s, v)


def build(debug=None, stop_after=None):
    debug = debug or {}
    nc = bass.Bass("TRN2", target_bir_lowering=False)
    es = ExitStack()

    def din(name, shape, dt=F32):
        return nc.dram_tensor(name, list(shape), dt, kind="ExternalInput").ap()

    def dout(name, shape, dt=F32):
        return nc.dram_tensor(name, list(shape), dt, kind="ExternalOutput").ap()

    def dint(name, shape, dt=F32):
        return nc.dram_tensor(name, list(shape), dt, kind="Internal").ap()

    xpre = din("xpre", [PRE, D])
    xmain = din("xmain", [TM, D])
    flag = din("flag", [1, 1])
    w_in = din("w_in", [128, KD * NIN])
    WOFF = _win_offsets()
    nmw = din("nmw", [1, D])
    nfw = din("nfw", [1, D])
    fnw = din("fnw", [1, D])
    cst_bf = din("cst_bf", [128, CB_W], BF16)
    cst_f = din("cst_f", [128, CF_W])
    smallp = din("smallp", [128, SP_W])
    aup = din("aup", [16, 512])
    w_a = din("w_a", [128, 8 * D])
    w_b = din("w_b", [128, 8 * D])
    w_o = din("w_o", [128, KD * D])
    w_up = din("w_up", [128, KD * DFF])
    w_gt = din("w_gt", [128, KD * DFF])
    w_dn = din("w_dn", [128, KF * D])
    sC = din("sC", [16, 4, 256, 256])
    sn = din("sn", [64, 256])
    smcol = din("smcol", [64, 1])
    sS = din("sS", [16, 4, 128, 256])
    sconv = din("sconv", [16, 2, DFF])

    yout = dout("yout", [TM - 128, D])
    o_pC = dout("o_pC", [4, 256, 256])
    o_pn = dout("o_pn", [8, 128])
    o_pm = dout("o_pm", [1, 64])
    o_pS = dout("o_pS", [4, 128, 256])
    o_pconv = dout("o_pconv", [2, DFF])
    o_sC = dout("o_sC", [16, 4, 256, 256])
    o_sn = dout("o_sn", [128, 128])
    o_sm = dout("o_sm", [64, 1])
    o_sS = dout("o_sS", [16, 4, 128, 256])
    o_sconv = dout("o_sconv", [16, 2, DFF])

    gscr = dint("gscr", [8, 2048])
    gscs = dint("gscs", [8, 128])
    yscr = dint("yscr", [TM, D], BF16)
    x1scr = dint("x1scr", [TM, D])
    x2scr_ = dint("x2scr", [TM, D])
    dbg_out = {}

    with es:
        def sb(name, shape, dt=F32):
            return es.enter_context(nc.sbuf_tensor(name, list(shape), dt))

        def ps(name, shape, dt=F32):
            return es.enter_context(nc.psum_tensor(name, list(shape), dt))

        esems = {e: es.enter_context(nc.semaphore("s_" + e)) for e in ("pe", "act", "dve", "pool")}
        dsems = {
            "sp": [es.enter_context(nc.semaphore("d_sp%d" % i)) for i in range(24)],
            "pool": [es.enter_context(nc.semaphore("d_pl%d" % i)) for i in range(12)],
        }
        S = Sched(esems, dsems)

        def act(fn, r, w):
            S.op("act", fn, reads=r, writes=w)

        def dve(fn, r, w):
            S.op("dve", fn, reads=r, writes=w)

        def pool(fn, r, w):
            S.op("pool", fn, reads=r, writes=w)

        def sdma(out, in_, r, w):
            S.dma("sp", lambda e: e.dma_start(out=out, in_=in_), reads=r, writes=w)

        def mm(out_ap, outb, pairs, reads, first=True, last=True):
            n = len(pairs)
            for i, (l, r) in enumerate(pairs):
                S.op("pe", lambda e, l=l, r=r, i=i: e.matmul(out_ap, lhsT=l, rhs=r, start=(first and i == 0),
                                                            stop=(last and i == n - 1)),
                     reads=reads, writes=[outb], inc=(i == n - 1))

        def tr(out_ap, outb, in_ap, inb, ident, inc=True):
            S.op("pe", lambda e: e.transpose(out=out_ap, in_=in_ap, identity=ident), reads=[inb, "cb", "cf"],
                 writes=[outb], inc=inc)

        cb = sb("cb", [128, CB_W], BF16)
        cf = sb("cf", [128, CF_W], F32)
        spm = sb("spm", [128, SP_W], F32)
        sdma(cb[:], cst_bf[:, :], [], ["cb"])
        sdma(cf[:], cst_f[:, :], [], ["cf"])
        sdma(spm[:], smallp[:, :], [], ["spm"])
        identb = cb[:, 0:128]
        maskc = cb[:, 128:256]
        masks = cb[:, 256:384]
        qxmask = cb[:, 512:512 + 2048].rearrange("p (j t) -> p j t", j=16)
        vxmask = cb[:, 2560:2576]
        identf = cf[:, 0:128]
        onesf = cf[:, 128:256]
        resetm = cf[:, 256:384]
        negbig = cf[:, 384:512]
        zerocol = cf[:, 512:513]
        P_IB, P_FB, P_AB, P_HNM, P_HNG, P_CW, P_CB = 0, 1, 2, 6, 14, 22, 22 + 3 * KF
        ib64 = spm[0:64, P_IB:P_IB + 1]
        fb64 = spm[0:64, P_FB:P_FB + 1]
        wbc = sb("wbc", [128, D], F32)
        sdma(wbc[:], nmw.partition_broadcast(128), [], ["wbc"])
        flagt = sb("flagt", [1, 1], F32)
        sdma(flagt[:], flag[:, :], [], ["flagt"])

        hT = sb("hT", [128, KD, TM], BF16)
        hTp = sb("hTp", [128, KD, PRE], BF16)
        wsl = [sb("wsl%d" % i, [128, KD * 256], BF16) for i in range(3)]
        stat = sb("stat", [128, 64], F32)
        ARENA = 99 * 512 + 160
        arena = sb("arena", [128, ARENA], BF16)
        apos = [0]
        aphase = [0]

        def carve(name, shape, dt=BF16):
            n = int(np.prod(shape))
            nb = n * (2 if dt == BF16 else 4)
            nb = (nb + 63) // 64 * 64
            off = apos[0]
            apos[0] += nb // 2
            assert apos[0] <= ARENA, ("arena overflow", name, apos[0])
            v = arena[:, off:off + nb // 2]
            if dt != BF16:
                v = v.bitcast(dt)
            v = v[:, 0:n]
            if len(shape) == 2:
                pat, kw = "p (a b) -> p a b", dict(a=shape[0])
            elif len(shape) == 3:
                pat, kw = "p (a b c) -> p a b c", dict(a=shape[0], b=shape[1])
            else:
                pat, kw = None, None
            if pat:
                v = v.rearrange(pat, **kw)
            return v, "ar%d_%s" % (aphase[0], name)

        def arena_reset():
            S.barrier()
            print("arena phase %d used %d / %d" % (aphase[0], apos[0] * 2, ARENA * 2))
            apos[0] = 0
            aphase[0] += 1

        pa = ps("pa", [128, 512])
        pb = ps("pb", [128, 512])
        ptr = [ps("ptr%d" % i, [128, 1024], BF16) for i in range(2)]
        pst = ps("pst", [128, 512])
        pnum = ps("pnum", [128, 512])
        pstate = ps("pstate", [128, 512])
        pmisc = ps("pmisc", [128, 512])
        pbk = [(pa, "pa"), (pb, "pb")]
        pbk6 = [(pa, "pa"), (pb, "pb"), (pst, "pst"), (pnum, "pnum"), (pstate, "pstate"), (pmisc, "pmisc")]
        pbi = [0]
        wide = [False]

        def nextbank():
            pbi[0] += 1
            if wide[0]:
                return pbk6[pbi[0] % 6]
            return pbk[pbi[0] % 2]

        evi = [0]

        def evac(dst, dstb, src, srcb, func=None, scale=None, eng=None):
            if func is not None or eng == "act":
                f = func if func is not None else AF.Copy
                if scale is None:
                    act(lambda e: e.activation(out=dst, in_=src, func=f), [srcb], [dstb])
                else:
                    act(lambda e: e.activation(out=dst, in_=src, func=f, scale=scale), [srcb], [dstb])
                return
            evi[0] += 1
            if eng is None:
                eng = "act" if evi[0] % 2 == 0 else "dve"
            if eng == "act":
                if scale is None:
                    act(lambda e: e.copy(out=dst, in_=src), [srcb], [dstb])
                else:
                    act(lambda e: e.mul(out=dst, in_=src, mul=scale), [srcb], [dstb])
            else:
                if scale is None:
                    dve(lambda e: e.tensor_copy(out=dst, in_=src), [srcb], [dstb])
                else:
                    dve(lambda e: e.tensor_scalar(out=dst, in0=src, scalar1=scale, scalar2=None, op0=ALU.mult),
                        [srcb], [dstb])

        wi = [0]

        def load_w(packed, off, nk, ncols):
            i = wi[0] % 3
            wi[0] += 1
            name = "wsl%d" % i
            tot = nk * ncols
            flat = wsl[i][:, 0:tot]
            dst = flat.rearrange("p (k c) -> p k c", k=nk)
            hf = tot // 2
            S.dma("pool", lambda e: e.dma_start(out=flat[:, 0:hf], in_=packed[:, off:off + hf]), writes=[name])
            S.dma("pool", lambda e: e.dma_start(out=flat[:, hf:tot], in_=packed[:, off + hf:off + tot]),
                  writes=[name])
            return dst, name

        MG = [(0, 512), (512, 512), (1024, 256)]
        PG = [(0, 448), (448, 448)]

        def proj_fm(wt, wname, c0, M, src, srcname, t0, N):
            bank, bname = nextbank()
            mm(bank[0:M, 0:N], bname, [(wt[:, k, c0:c0 + M], src[:, k, t0:t0 + N]) for k in range(KD)],
               [wname, srcname])
            return bank[0:M, 0:N], bname

        def proj_tm(wt, wname, c0, N, src, srcname, ti):
            bank, bname = nextbank()
            mm(bank[:, 0:N], bname, [(src[:, k, ti * 128:(ti + 1) * 128], wt[:, k, c0:c0 + N]) for k in range(KD)],
               [wname, srcname])
            return bank[:, 0:N], bname

        xt = [carve("xt%d" % i, [D], F32) for i in range(4)]
        xn = [carve("xn%d" % i, [D], BF16) for i in range(4)]
        sq1, sq1b = carve("sq", [D], BF16)
        nstat = [0]

        def rstd_from_ss(ss, rs, b, n):
            dve(lambda e: e.tensor_scalar(out=rs, in0=ss, scalar1=1.0 / n, scalar2=EPS, op0=ALU.mult, op1=ALU.add),
                [b], [b + "r"])
            act(lambda e: e.activation(out=rs, in_=rs, func=AF.Sqrt), [b + "r"], [b + "r"])
            dve(lambda e: e.reciprocal(out=rs, in_=rs), [b + "r"], [b + "r"])

        def statslot():
            c = nstat[0] % 16
            nstat[0] += 1
            return stat[:, c:c + 1], stat[:, 16 + c:17 + c], stat[:, 32 + c:33 + c], "stat%d" % c

        def norm_to_T(x_t, xb, x_n, xnb, dstT, dstname, ti, sqj, sqb):
            ss, rs, _, sbn = statslot()
            act(lambda e: e.activation(out=sqj, in_=x_t, func=AF.Square, accum_out=ss), [xb], [sqb, sbn])
            rstd_from_ss(ss, rs, sbn, D)
            dve(lambda e: e.scalar_tensor_tensor(out=x_n, in0=x_t, scalar=rs, in1=wbc[:], op0=ALU.mult,
                                                 op1=ALU.mult), [xb, sbn + "r", "wbc"], [xnb])
            for half in range(2):
                pt = ptr[half]
                pbn = "ptr%d" % half
                for j in range(8):
                    k = half * 8 + j
                    tr(pt[:, j * 128:(j + 1) * 128], pbn, x_n[:, k * 128:(k + 1) * 128], xnb, identb, inc=(j == 7))
                dst = dstT[:, half * 8:(half + 1) * 8, ti * 128:(ti + 1) * 128]
                src = pt[:].rearrange("p (k t) -> p k t", k=8)
                evac(dst, dstname, src, pbn, eng=("act" if half == 0 else "dve"))

        for ti in range(NPT + NMT):
            par = ti % 4
            (x_t, xb), (x_n, xnb) = xt[par], xn[par]
            if ti < NPT:
                sdma(x_t, xpre[ti * 128:(ti + 1) * 128, :], [], [xb])
                norm_to_T(x_t, xb, x_n, xnb, hTp, "hTp", ti, sq1, sq1b)
            else:
                tj = ti - NPT
                sdma(x_t, xmain[tj * 128:(tj + 1) * 128, :], [], [xb])
                norm_to_T(x_t, xb, x_n, xnb, hT, "hT", tj, sq1, sq1b)

        arena_reset()
        if stop_after == "p1":
            return finish(nc, S, dbg_out)
        Cst = carve("Cst", [2, 2, 257], F32)[0]
        Sst = carve("Sst", [2, 256], F32)[0]
        wtok = carve("wtok", [64], F32)[0]
        ftok = carve("ftok", [64], F32)[0]
        decbc = carve("decbc", [64], F32)[0]
        wtoks = carve("wtoks", [4], F32)[0]
        ftoks = carve("ftoks", [4], F32)[0]
        sdecbc = carve("sdecbc", [64], F32)[0]

        galT, galb = carve("galT", [PRE + TM], F32)
        BT, BTb = carve("BT", [PRE + TM], F32)
        gst, gstb = BT[:, 0:512], BTb
        QT, QTb = carve("QT", [2, TM])
        KT, KTb = carve("KT", [2, TM])
        Vt, Vtb = carve("Vt", [NMT, 257])
        OG, OGb = carve("OG", [NMT, 256])
        KTp, KTpb = carve("KTp", [2, PRE])
        Vp, Vpb = carve("Vp", [NPT, 257])
        Cbf, Cbfb = carve("Cbf", [2, 257])
        QTg_, QTgb = carve("QTg", [TM])
        KTg_, KTgb = carve("KTg", [TM])
        Vtg, Vtgb = carve("Vtg", [NMT, 256])
        RG, RGb = carve("RG", [NMT, 256])
        Ktok2 = [carve("Ktok%d" % i, [256]) for i in range(2)]
        Vw2 = [carve("Vw%d" % i, [257]) for i in range(2)]
        PT2 = [carve("PT%d" % i, [128]) for i in range(2)]
        ytok2 = [carve("ytok%d" % i, [256]) for i in range(2)]
        eqt2 = [carve("eqt%d" % i, [128], F32) for i in range(2)]
        QtT2 = [carve("QtT%d" % i, [128]) for i in range(2)]
        KtT2 = [carve("KtT%d" % i, [128]) for i in range(2)]
        KhT2 = [carve("KhT%d" % i, [128]) for i in range(2)]
        Khtok2 = [carve("Khtok%d" % i, [128]) for i in range(2)]
        Sbf, Sbfb = carve("Sbf", [256])
        nT, nTb = carve("nT", [2, 64], F32)
        nTo, nTob = carve("nTo", [2, 64], F32)
        nrow, nrowb = carve("nrow", [256], F32)
        junk, junkb = nrow.bitcast(BF16)[:, 0:256], nrowb
        SG = 2
        aups, aupb = carve("aups", [512], F32)
        negab = carve("negab", [4], F32)[0]
        dSs, dSsb = carve("dSs", [16], F32)
        gmark = apos[0]
        gat = carve("gat", [8, 128], F32)[0][0:64]
        gas = carve("gas", [8, 8], F32)[0][0:64]
        grow = carve("grow", [8, 64], F32)[0][0:1]
        wg, wgn = load_w(w_in, WOFF[C_MI], KD, 8)
        wl, wln = load_w(w_in, WOFF[C_GAL], KD, 16)
        MGG = [(0, 512), (512, 512), (1024, 128), (1152, 128)]
        for (src, sname, groups, base) in ((hTp, "hTp", PG, 0), (hT, "hT", MGG, PRE)):
            for (t0, N) in groups:
                pv, pn_ = proj_fm(wg, wgn, 0, 8, src, sname, t0, N)
                evac(gst[0:8, 0:N], gstb, pv, pn_, eng="act")
                if base + t0 < 2048:
                    sdma(gscr[:, base + t0:base + t0 + N], gst[0:8, 0:N], [gstb], ["gscr"])
                else:
                    sdma(gscs[:, :], gst[0:8, 0:N], [gstb], ["gscs"])
                pv, pn_ = proj_fm(wl, wln, 0, 16, src, sname, t0, N)
                evac(galT[0:16, base + t0:base + t0 + N], galb, pv, pn_, eng="dve")
        GI, GF, GB_, GA, GW, GFL, GT1, GT2 = range(8)

        def gt(i):
            return gat[:, i, :]

        def gs_(i):
            return gas[:, i, :]
        NTOK = PRE + TM - 128
        sdma(gt(GI), gscr[0:4, :].rearrange("h (c l) -> (h c) l", l=128), ["gscr"], ["g_i"])
        sdma(gt(GF), gscr[4:8, :].rearrange("h (c l) -> (h c) l", l=128), ["gscr"], ["g_f"])
        sdma(gs_(GI), gscs[0:4, :].rearrange("h (j l) -> (h j) l", l=8), ["gscs"], ["s_i"])
        sdma(gs_(GF), gscs[4:8, :].rearrange("h (j l) -> (h j) l", l=8), ["gscs"], ["s_f"])
        negfb = stat[0:64, 48:49]
        dve(lambda e: e.tensor_scalar(out=negfb, in0=fb64, scalar1=-1.0, scalar2=None, op0=ALU.mult),
            ["spm"], ["negfb"])

        def gate_math(T, pfx, L):
            i_, f_, b_, a_ = T(GI), T(GF), T(GB_), T(GA)
            act(lambda e: e.activation(out=f_, in_=f_, func=AF.Exp, bias=negfb, scale=-1.0),
                [pfx + "f", "negfb"], [pfx + "f"])
            act(lambda e: e.activation(out=f_, in_=f_, func=AF.Ln, bias=1.0), [pfx + "f"], [pfx + "f"])
            dve(lambda e: e.tensor_scalar(out=f_, in0=f_, scalar1=-1.0, scalar2=None, op0=ALU.mult),
                [pfx + "f"], [pfx + "f"])
            dve(lambda e: e.tensor_tensor_scan(out=b_, data0=onesf[0:64, 0:L], data1=f_, initial=0.0,
                                               op0=ALU.mult, op1=ALU.add), [pfx + "f", "cf"], [pfx + "b"])
            dve(lambda e: e.scalar_tensor_tensor(out=a_, in0=i_, scalar=ib64, in1=b_, op0=ALU.add,
                                                 op1=ALU.subtract), [pfx + "i", pfx + "b", "spm"], [pfx + "a"])
        gate_math(gt, "g_", 128)
        gate_math(gs_, "s_", 8)
        amax = stat[0:64, 49:50]
        bend = gat[:, GB_, 127:128]
        amaxb = stat[0:64, 50:51]
        dve(lambda e: e.reduce_max(out=amax, in_=gt(GA), axis=AX.X), ["g_a"], ["amax"])
        dve(lambda e: e.tensor_tensor(out=amaxb, in0=amax, in1=bend, op=ALU.add), ["amax", "g_b"], ["amaxb"])
        tr(pmisc[0:1, 0:64], "pmisc", bend, "g_b", identf[0:64, 0:64], inc=False)
        tr(pmisc[0:1, 64:128], "pmisc", amaxb, "amaxb", identf[0:64, 0:64])
        R_BE, R_AB, R_MN, R_MP, R_G, R_DEC = range(6)

        def gr(i):
            return grow[0:1, i, :]
        evac(grow[0:1, 0:2, :], "grow", pmisc[0:1, 0:128].rearrange("p (a b) -> p a b", a=2), "pmisc", eng="dve")
        for h in range(4):
            for seg in range(2):
                lo = h * 16 + seg * 8
                if seg == 0:
                    init = 0.0
                    rd = ["grow"]
                else:
                    mf = stat[0:1, 51 + h:52 + h]
                    dve(lambda e, h=h, mf=mf: e.tensor_tensor(out=mf, in0=grow[0:1, R_MN, h * 16 + 7:h * 16 + 8],
                                                             in1=flagt[0:1, 0:1], op=ALU.mult),
                        ["grow", "flagt"], ["mf%d" % h])
                    init = mf
                    rd = ["grow", "mf%d" % h]
                dve(lambda e, lo=lo, init=init: e.tensor_tensor_scan(
                    out=grow[0:1, R_MN, lo:lo + 8], data0=grow[0:1, R_BE, lo:lo + 8],
                    data1=grow[0:1, R_AB, lo:lo + 8], initial=init, op0=ALU.add, op1=ALU.max), rd, ["grow"])
                if seg == 0:
                    dve(lambda e, lo=lo: e.memset(grow[0:1, R_MP, lo:lo + 1], 0.0), [], ["grow"])
                else:
                    dve(lambda e, lo=lo, mf=mf: e.tensor_copy(out=grow[0:1, R_MP, lo:lo + 1], in_=mf),
                        ["mf%d" % h], ["grow"])
                dve(lambda e, lo=lo: e.tensor_copy(out=grow[0:1, R_MP, lo + 1:lo + 8],
                                                   in_=grow[0:1, R_MN, lo:lo + 7]), ["grow"], ["grow"])
        dve(lambda e: e.tensor_tensor(out=gr(R_G), in0=gr(R_MN), in1=gr(R_BE), op=ALU.subtract), ["grow"], ["grow"])
        dve(lambda e: e.tensor_tensor(out=gr(R_DEC), in0=gr(R_MP), in1=gr(R_G), op=ALU.subtract), ["grow"], ["grow"])
        act(lambda e: e.activation(out=gr(R_DEC), in_=gr(R_DEC), func=AF.Exp), ["grow"], ["grow"])
        dve(lambda e: e.tensor_scalar(out=gr(R_G), in0=gr(R_G), scalar1=-1.0, scalar2=None, op0=ALU.mult),
            ["grow"], ["grow"])
        mm(pmisc[:, 128:192], "pmisc", [(onesf[0:1, 0:128], gr(R_DEC))], ["grow", "cf"])
        evac(decbc[:], "decbc", pmisc[:, 128:192], "pmisc", eng="dve")
        mm(pmisc[0:64, 192:193], "pmisc", [(gr(R_G), onesf[0:1, 0:1])], ["grow", "cf"])
        negG = stat[0:64, 56:57]
        evac(negG, "negG", pmisc[0:64, 192:193], "pmisc", eng="dve")
        sdma(o_pm[:, :], gr(R_MN), ["grow"], [])
        act(lambda e: e.activation(out=gt(GW), in_=gt(GA), func=AF.Exp, bias=negG), ["g_a", "negG"], ["g_w"])
        act(lambda e: e.activation(out=gt(GFL), in_=gt(GB_), func=AF.Exp, bias=negG, scale=-1.0),
            ["g_b", "negG"], ["g_fl"])
        tr(pmisc[:, 256:320], "pmisc", gt(GW), "g_w", identf[0:64, 0:64], inc=False)
        tr(pmisc[:, 320:384], "pmisc", gt(GFL), "g_fl", identf[0:64, 0:64])
        evac(wtok[:], "wtok", pmisc[:, 256:320], "pmisc", eng="dve")
        evac(ftok[:], "ftok", pmisc[:, 320:384], "pmisc", eng="dve")
        smc = stat[0:64, 57:58]
        sdma(smc, smcol[:, :], [], ["smc"])
        samax = stat[0:64, 58:59]
        sG = stat[0:64, 59:60]
        snegG = stat[0:64, 60:61]
        sdec = stat[0:64, 61:62]
        smn = stat[0:64, 62:63]
        dve(lambda e: e.reduce_max(out=samax, in_=gs_(GA), axis=AX.X), ["s_a"], ["samax"])
        dve(lambda e: e.tensor_tensor(out=sG, in0=samax, in1=smc, op=ALU.max), ["samax", "smc"], ["sG"])
        dve(lambda e: e.tensor_tensor(out=smn, in0=sG, in1=gas[:, GB_, 7:8], op=ALU.add), ["sG", "s_b"], ["smn"])
        sdma(o_sm[:, :], smn, ["smn"], [])
        dve(lambda e: e.tensor_tensor(out=sdec, in0=smc, in1=sG, op=ALU.subtract), ["smc", "sG"], ["sdec"])
        act(lambda e: e.activation(out=sdec, in_=sdec, func=AF.Exp), ["sdec"], ["sdec"])
        dve(lambda e: e.tensor_scalar(out=snegG, in0=sG, scalar1=-1.0, scalar2=None, op0=ALU.mult), ["sG"], ["snegG"])
        act(lambda e: e.activation(out=gs_(GW), in_=gs_(GA), func=AF.Exp, bias=snegG), ["s_a", "snegG"], ["s_w"])
        act(lambda e: e.activation(out=gs_(GFL), in_=gs_(GB_), func=AF.Exp, bias=snegG, scale=-1.0),
            ["s_b", "snegG"], ["s_fl"])
        dg = gat[:, GT1, 0:64]
        dve(lambda e: e.tensor_scalar(out=dg, in0=identf[0:64, 0:64], scalar1=sdec, scalar2=None, op0=ALU.mult),
            ["sdec", "cf"], ["dg"])
        mm(pmisc[:, 384:448], "pmisc", [(onesf[0:64, 0:128], dg)], ["dg", "cf"])
        evac(sdecbc[:], "sdecbc", pmisc[:, 384:448], "pmisc", eng="dve")
        sdma(gscs[0:4, :].rearrange("h (j l) -> (h j) l", l=8), gs_(GW), ["s_w"], ["gscs"])
        sdma(gscs[4:8, :].rearrange("h (j l) -> (h j) l", l=8), gs_(GFL), ["s_fl"], ["gscs"])
        wfr = gat[0:8, GT2, :]
        sdma(wfr, gscs[0:8, :], ["gscs"], ["wfr"])
        tr(pmisc[:, 448:456], "pmisc", wfr, "wfr", identf[0:8, 0:8])
        evac(wtoks[:], "wtoks", pmisc[:, 448:452], "pmisc", eng="dve")
        evac(ftoks[:], "ftoks", pmisc[:, 452:456], "pmisc", eng="dve")

        if stop_after == "gates":
            if debug.get("gates"):
                for nm, t_, shp in (("wtok", wtok, [128, 64]), ("ftok", ftok, [128, 64]), ("decbc", decbc, [128, 64]),
                                    ("wtoks", wtoks, [128, 4]), ("sdecbc", sdecbc, [128, 64])):
                    dbg_out[nm] = dout("dbg_" + nm, shp)
                    sdma(dbg_out[nm][:, :], t_[:], [nm], [])
            return finish(nc, S, dbg_out)


        dve(lambda e: e.memset(Vt[:, :, 256:257], 1.0), [], [Vtb])
        dve(lambda e: e.memset(Vp[:, :, 256:257], 1.0), [], [Vpb])
        sdma(nrow[0:64, :], sn[:, :], [], [nrowb])
        for dh in range(2):
            tr(pmisc[:, dh * 64:(dh + 1) * 64], "pmisc", nrow[0:64, dh * 128:(dh + 1) * 128], nrowb,
               identf[0:64, 0:64], inc=(dh == 1))
        evac(nT, nTb, pmisc[:, 0:128].rearrange("p (a b) -> p a b", a=2), "pmisc", eng="dve")

        def head_epilogue(num_ap, numb, scale_pre, gate_ap, gateb, tile_i, dstcol0, hnb, ytok, ytokb):
            ss, rs, sc, sbn = statslot()
            if scale_pre is None:
                act(lambda e: e.activation(out=junk, in_=num_ap, func=AF.Square, accum_out=ss), [numb],
                    [junkb, sbn])
            else:
                act(lambda e: e.activation(out=junk, in_=num_ap, func=AF.Square, scale=scale_pre, accum_out=ss),
                    [numb, hnb], [junkb, sbn])
            rstd_from_ss(ss, rs, sbn, 256)
            if scale_pre is not None:
                dve(lambda e: e.tensor_tensor(out=rs, in0=rs, in1=scale_pre, op=ALU.mult), [sbn + "r", hnb],
                    [sbn + "r"])
            dve(lambda e: e.scalar_tensor_tensor(out=ytok, in0=num_ap, scalar=rs, in1=gate_ap, op0=ALU.mult,
                                                 op1=ALU.mult), [numb, sbn + "r", gateb], [ytokb])
            sdma(yscr[tile_i * 128:(tile_i + 1) * 128, dstcol0:dstcol0 + 256], ytok, [ytokb], ["yscr"])

        def mlstm_proj(h):
            wq, wqn = load_w(w_in, WOFF[C_MQ + h * 256], KD, 256)
            wk, wkn = load_w(w_in, WOFF[C_MK + h * 256], KD, 256)
            wv, wvn = load_w(w_in, WOFF[C_MV + h * 256], KD, 256)
            for dh in range(2):
                for (t0, N) in MG:
                    pv, pn_ = proj_fm(wq, wqn, dh * 128, 128, hT, "hT", t0, N)
                    evac(QT[:, dh, t0:t0 + N], QTb, pv, pn_)
            for dh in range(2):
                for (t0, N) in MG:
                    pv, pn_ = proj_fm(wk, wkn, dh * 128, 128, hT, "hT", t0, N)
                    evac(KT[:, dh, t0:t0 + N], KTb, pv, pn_, scale=0.0625)
                for (t0, N) in PG:
                    pv, pn_ = proj_fm(wk, wkn, dh * 128, 128, hTp, "hTp", t0, N)
                    evac(KTp[:, dh, t0:t0 + N], KTpb, pv, pn_, scale=0.0625)
            wo, won = load_w(w_in, WOFF[C_MO + h * 256], KD, 256)
            for ti in range(NPT):
                pv, pn_ = proj_tm(wv, wvn, 0, 256, hTp, "hTp", ti)
                evac(Vp[:, ti, 0:256], Vpb, pv, pn_)
            for ti in range(NMT):
                pv, pn_ = proj_tm(wv, wvn, 0, 256, hT, "hT", ti)
                evac(Vt[:, ti, 0:256], Vtb, pv, pn_)
            for ti in range(NMT):
                pv, pn_ = proj_tm(wo, won, 0, 256, hT, "hT", ti)
                evac(OG[:, ti, :], OGb, pv, pn_, func=AF.Sigmoid)
        csi = [0]

        def mlstm_chunks(h):
            Ch = Cst[:, h % 2, :, :]
            Chb = "Cst%d" % (h % 2)
            dve(lambda e: e.memset(Ch, 0.0), [], [Chb])
            def mchunk(c):
                full = c >= NPT
                samp = c == NCH
                if c < NPT:
                    ktsrc, ktb, vsrc, vb_, ti = KTp, KTpb, Vp, Vpb, c
                else:
                    ktsrc, ktb, vsrc, vb_, ti = KT, KTb, Vt, Vtb, c - NPT
                tk = slice(ti * 128, (ti + 1) * 128)
                (Ktok, Ktokb), (Vw, Vwb), (PT, PTb), (ytok, ytokb) = Ktok2[c % 2], Vw2[c % 2], PT2[c % 2], ytok2[c % 2]
                if samp:
                    wcol, fcol = wtoks[:, h:h + 1], ftoks[:, h:h + 1]
                    wcb, fcb = "wtoks", "ftoks"
                else:
                    wcol, fcol = wtok[:, h * 16 + c:h * 16 + c + 1], ftok[:, h * 16 + c:h * 16 + c + 1]
                    wcb, fcb = "wtok", "ftok"
                for dh in range(2):
                    tr(ptr[0][:, dh * 128:(dh + 1) * 128], "ptr0", ktsrc[:, dh, tk], ktb, identb, inc=(dh == 1))
                evac(Ktok, Ktokb, ptr[0][:, 0:256], "ptr0")
                dve(lambda e, vsrc=vsrc, ti=ti, wcol=wcol: e.tensor_scalar(out=Vw, in0=vsrc[:, ti, :], scalar1=wcol,
                                                                            scalar2=None, op0=ALU.mult),
                     [vb_, wcb], [Vwb])
                if not samp:
                    dcol = decbc[:, h * 16 + c:h * 16 + c + 1]
                    dve(lambda e, dcol=dcol: e.tensor_scalar(out=Ch, in0=Ch, scalar1=dcol, scalar2=None,
                                                             op0=ALU.mult), [Chb, "decbc"], [Chb])
                if full:
                    mm(pst[:, 0:128], "pst", [(ktsrc[:, dh, tk], QT[:, dh, tk]) for dh in range(2)], [ktb, QTb])
                    msk = masks if samp else maskc
                    dve(lambda e, msk=msk: e.tensor_tensor(out=PT, in0=pst[:, 0:128], in1=msk, op=ALU.mult),
                        ["pst", "cb"], [PTb])
                    if not samp:
                        act(lambda e: e.copy(out=Cbf, in_=Ch), [Chb], [Cbfb])
                        mm(pnum[:, 0:257], "pnum",
                           [(PT, Vw)] + [(QT[:, dh, tk], Cbf[:, dh, :]) for dh in range(2)],
                           [PTb, Vwb, QTb, Cbfb])
                    else:
                        mm(pnum[:, 0:257], "pnum", [(PT, Vw)], [PTb, Vwb], first=True, last=False)
                if not samp:
                    for dh in range(2):
                        mm(pstate[:, 0:257], "pstate", [(Ktok[:, dh * 128:(dh + 1) * 128], Vw)], [Ktokb, Vwb])
                        dve(lambda e, dh=dh: e.tensor_tensor(out=Ch[:, dh, :], in0=Ch[:, dh, :],
                                                             in1=pstate[:, 0:257], op=ALU.add),
                            ["pstate", Chb], [Chb])
                else:
                    def sgrp(g, Cs, Csb, Csbf, Csbfb, QX, QXb, VwX, VwXb):
                        js = slice(g * SG, (g + 1) * SG)
                        for dh in range(2):
                            sdma(Cs[:, :, dh, 0:256],
                                 sC[js, h, dh * 128:(dh + 1) * 128, :].rearrange("j p e -> p j e"), [], [Csb])
                        ncols = nT[:, :, g * SG * 4 + h:(g + 1) * SG * 4:4].rearrange("p dh j -> p j dh")
                        dve(lambda e, ncols=ncols: e.tensor_copy(out=Cs[:, :, :, 256], in_=ncols), [nTb], [Csb])
                        dcols = sdecbc[:, h * 16 + g * SG:h * 16 + (g + 1) * SG]
                        Csv = Cs.rearrange("p j dh e -> p j (dh e)")
                        dve(lambda e, dcols=dcols, Csv=Csv: e.tensor_tensor(
                            out=Csv, in0=Csv, in1=dcols.unsqueeze(2).broadcast_to([128, SG, 514]), op=ALU.mult),
                            [Csb, "sdecbc"], [Csb])
                        act(lambda e: e.copy(out=Csbf, in_=Cs), [Csb], [Csbfb])
                        for dh in range(2):
                            dve(lambda e, dh=dh, js=js, tk=tk: e.tensor_tensor(
                                out=QX[:, dh, :, :], in0=QT[:, dh, tk].unsqueeze(1).broadcast_to([128, SG, 128]),
                                in1=qxmask[:, js, :], op=ALU.mult), [QTb, "cb"], [QXb])
                        pairs = [(QX[:, dh, j, :], Csbf[:, j, dh, :]) for j in range(SG) for dh in range(2)]
                        mm(pnum[:, 0:257], "pnum", pairs, [QXb, Csbfb], first=False, last=(g == 16 // SG - 1))
                        dve(lambda e, js=js: e.tensor_tensor(
                            out=VwX, in0=Vw.unsqueeze(1).broadcast_to([128, SG, 257]),
                            in1=vxmask[:, js].unsqueeze(2).broadcast_to([128, SG, 257]), op=ALU.mult),
                            [Vwb, "cb"], [VwXb])
                        for j in range(SG):
                            for dh in range(2):
                                mm(pstate[:, 0:257], "pstate", [(Ktok[:, dh * 128:(dh + 1) * 128], VwX[:, j, :])],
                                   [Ktokb, VwXb])
                                dve(lambda e, j=j, dh=dh: e.tensor_tensor(out=Cs[:, j, dh, :], in0=Cs[:, j, dh, :],
                                                                          in1=pstate[:, 0:257], op=ALU.add),
                                    ["pstate", Csb], [Csb])
                        for dh in range(2):
                            sdma(o_sC[js, h, dh * 128:(dh + 1) * 128, :].rearrange("j p e -> p j e"),
                                 Cs[:, :, dh, 0:256], [Csb], [])
                        ndst = nTo[:, :, g * SG * 4 + h:(g + 1) * SG * 4:4].rearrange("p dh j -> p j dh")
                        dve(lambda e, ndst=ndst: e.tensor_copy(out=ndst, in_=Cs[:, :, :, 256]), [Csb], [nTob])
                    for g in range(16 // SG):
                        k_ = csi[0]
                        csi[0] += 1
                        sgrp(g, *Cs3[k_ % 3], *Csbf2[k_ % 2], *QX2[k_ % 2], *VwX2[k_ % 2])
                if full:
                    ss, rs, sc, sbn = statslot()
                    dve(lambda e, sc=sc: e.tensor_copy(out=sc, in_=pnum[:, 256:257]), ["pnum"], [sbn + "s"])
                    dve(lambda e, sc=sc: e.scalar_tensor_tensor(out=sc, in0=sc, scalar=-1.0, in1=sc, op0=ALU.mult,
                                                                op1=ALU.max), [sbn + "s"], [sbn + "s"])
                    dve(lambda e, fcol=fcol, sc=sc: e.tensor_tensor(out=sc, in0=sc, in1=fcol, op=ALU.max),
                        [sbn + "s", fcb], [sbn + "s"])
                    dve(lambda e, sc=sc: e.reciprocal(out=sc, in_=sc), [sbn + "s"], [sbn + "s"])
                    head_epilogue(pnum[:, 0:256], "pnum", sc, OG[:, ti, :], OGb, ti, h * 256, sbn + "s", ytok, ytokb)
                if c == NCH - 1:
                    sdma(o_pC[h].rearrange("(dh p) e -> p dh e", p=128), Ch[:, :, 0:256], [Chb], [])
            for c in range(NCH + 1):
                mchunk(c)
            tr(pmisc[0:2, 0:128], "pmisc", Ch[:, :, 256], Chb, identf)
            evac(nrow[0:2, 0:128], nrowb, pmisc[0:2, 0:128], "pmisc", eng="dve")
            sdma(o_pn[h * 2:h * 2 + 2, :], nrow[0:2, 0:128], [nrowb], [])

        aupt = stat
        sdma(aups[0:16, :], aup[:, :], [], [aupb])
        dve(lambda e: e.tensor_scalar(out=negab[:, 0:4], in0=spm[:, P_AB:P_AB + 4], scalar1=-1.0, scalar2=None,
                                      op0=ALU.mult), ["spm"], ["negab"])
        BG = [(0, 512), (512, 512), (1024, 512), (1536, 512), (2048, 128)]

        def gla_head(h):
            wq, wqn = load_w(w_in, WOFF[C_GQ + h * 128], KD, 128)
            wk, wkn = load_w(w_in, WOFF[C_GK + h * 128], KD, 128)
            wv, wvn = load_w(w_in, WOFF[C_GV + h * 256], KD, 256)
            QTg, KTg, KTpg = QTg_, KTg_, KTp[:, 0, :]
            QTb, KTb, Vt, Vtb, OG, OGb = QTgb, KTgb, Vtg, Vtgb, RG, RGb
            for (t0, N) in MG:
                pv, pn_ = proj_fm(wq, wqn, 0, 128, hT, "hT", t0, N)
                evac(QTg[:, t0:t0 + N], QTb, pv, pn_, scale=128.0 ** -0.5)
            for (t0, N) in MG:
                pv, pn_ = proj_fm(wk, wkn, 0, 128, hT, "hT", t0, N)
                evac(KTg[:, t0:t0 + N], KTb, pv, pn_)
            for (t0, N) in PG:
                pv, pn_ = proj_fm(wk, wkn, 0, 128, hTp, "hTp", t0, N)
                evac(KTpg[:, t0:t0 + N], KTpb, pv, pn_)
            wr, wrn = load_w(w_in, WOFF[C_GR + h * 256], KD, 256)
            for ti in range(NPT):
                pv, pn_ = proj_tm(wv, wvn, 0, 256, hTp, "hTp", ti)
                evac(Vp[:, ti, 0:256], Vpb, pv, pn_)
            for ti in range(NMT):
                pv, pn_ = proj_tm(wv, wvn, 0, 256, hT, "hT", ti)
                evac(Vt[:, ti, 0:256], Vtb, pv, pn_)
            for ti in range(NMT):
                pv, pn_ = proj_tm(wr, wrn, 0, 256, hT, "hT", ti)
                evac(OG[:, ti, :], OGb, pv, pn_, func=AF.Silu)
            nab = negab[:, h:h + 1]
            for (t0, N) in BG:
                mm(pmisc[:, 0:N], "pmisc", [(aups[0:16, h * 128:(h + 1) * 128], galT[0:16, t0:t0 + N])],
                   [aupb, galb])
                act(lambda e, t0=t0, N=N: e.activation(out=BT[:, t0:t0 + N], in_=pmisc[:, 0:N], func=AF.Exp,
                                                       bias=nab, scale=-1.0), ["pmisc", "negab"], [BTb])
            act(lambda e: e.activation(out=BT, in_=BT, func=AF.Ln, bias=1.0), [BTb], [BTb])
            dve(lambda e: e.tensor_scalar(out=BT, in0=BT, scalar1=-1.0 / 16.0, scalar2=None, op0=ALU.mult),
                [BTb], [BTb])
            dve(lambda e: e.tensor_tensor_scan(out=BT[:, 0:2048], data0=onesf[:, 0:1].broadcast_to([128, 2048]),
                                               data1=BT[:, 0:2048], initial=0.0, op0=ALU.mult, op1=ALU.add),
                [BTb, "cf"], [BTb])
            dve(lambda e: e.tensor_tensor_scan(out=BT[:, 2048:2176], data0=resetm, data1=BT[:, 2048:2176],
                                               initial=0.0, op0=ALU.mult, op1=ALU.add), [BTb, "cf"], [BTb])
            Sh = Sst[:, h % 2, :]
            Shb = "Sst%d" % (h % 2)
            dve(lambda e: e.memset(Sh, 0.0), [], [Shb])

            def gchunk(c):
                full = c >= NPT
                samp = c == NCH
                if c < NPT:
                    ktsrc, ktb, vsrc, vb_, ti = KTpg, KTpb, Vp, Vpb, c
                else:
                    ktsrc, ktb, vsrc, vb_, ti = KTg, KTb, Vt, Vtb, c - NPT
                tk = slice(ti * 128, (ti + 1) * 128)
                tb = slice(2048, 2176) if samp else slice(c * 128, (c + 1) * 128)
                (eqt, eqtb), (QtT, QtTb), (KtT, KtTb), (KhT, KhTb) = eqt2[c % 2], QtT2[c % 2], KtT2[c % 2], KhT2[c % 2]
                (Ktok, Ktokb), (PT, PTb), (ytok, ytokb) = Khtok2[c % 2], PT2[c % 2], ytok2[c % 2]
                ss, rs, sc, sbn = statslot()
                if samp:
                    bend3 = BT[:, 2048 + 7:2176:8].unsqueeze(2).broadcast_to([128, 16, 8])
                    dve(lambda e: e.tensor_tensor(out=eqt.rearrange("p (j l) -> p j l", l=8), in0=bend3,
                                                  in1=BT[:, tb].rearrange("p (j l) -> p j l", l=8),
                                                  op=ALU.subtract), [BTb], [eqtb])
                    act(lambda e: e.activation(out=eqt, in_=eqt, func=AF.Exp), [eqtb], [eqtb])
                    act(lambda e: e.activation(out=dSs, in_=BT[:, 2048 + 7:2176:8], func=AF.Exp), [BTb], [dSsb])
                else:
                    bendc = BT[:, c * 128 + 127:c * 128 + 128]
                    bstc = zerocol if c == 0 else BT[:, c * 128 - 1:c * 128]
                    act(lambda e: e.activation(out=eqt, in_=BT[:, tb], func=AF.Exp, bias=bendc, scale=-1.0),
                        [BTb], [eqtb])
                    dve(lambda e: e.tensor_tensor(out=ss, in0=bendc, in1=bstc, op=ALU.subtract), [BTb, "cf"],
                        [sbn])
                    act(lambda e: e.activation(out=ss, in_=ss, func=AF.Exp), [sbn], [sbn])
                    dve(lambda e: e.tensor_scalar(out=rs, in0=bstc, scalar1=-1.0, scalar2=None, op0=ALU.mult),
                        [BTb, "cf"], [sbn + "r"])
                dve(lambda e: e.tensor_tensor(out=KhT, in0=ktsrc[:, tk], in1=eqt, op=ALU.mult), [ktb, eqtb], [KhTb])
                tr(ptr[0][:, 0:128], "ptr0", KhT, KhTb, identb)
                evac(Ktok[:, 0:128], Ktokb, ptr[0][:, 0:128], "ptr0")
                if full:
                    if samp:
                        act(lambda e: e.activation(out=eqt, in_=BT[:, tb], func=AF.Exp), [BTb, KhTb], [eqtb])
                    else:
                        act(lambda e: e.activation(out=eqt, in_=BT[:, tb], func=AF.Exp, bias=rs), [BTb, sbn + "r", KhTb],
                            [eqtb])
                    dve(lambda e: e.tensor_tensor(out=QtT, in0=QTg[:, tk], in1=eqt, op=ALU.mult), [QTb, eqtb], [QtTb])
                    if samp:
                        act(lambda e: e.activation(out=eqt, in_=BT[:, tb], func=AF.Exp, scale=-1.0), [BTb, QtTb],
                            [eqtb])
                    else:
                        act(lambda e: e.activation(out=eqt, in_=BT[:, tb], func=AF.Exp, bias=bstc, scale=-1.0),
                            [BTb, QtTb, "cf"], [eqtb])
                    dve(lambda e: e.tensor_tensor(out=KtT, in0=ktsrc[:, tk], in1=eqt, op=ALU.mult), [ktb, eqtb],
                        [KtTb])
                    mm(pst[:, 0:128], "pst", [(KtT, QtT)], [KtTb, QtTb])
                    msk = masks if samp else maskc
                    dve(lambda e: e.tensor_tensor(out=PT, in0=pst[:, 0:128], in1=msk, op=ALU.mult), ["pst", "cb"],
                        [PTb])
                    if not samp:
                        mm(pnum[:, 0:256], "pnum", [(PT, vsrc[:, ti, 0:256]), (QtT, Sbf)], [PTb, vb_, QtTb, Sbfb])
                    else:
                        mm(pnum[:, 0:256], "pnum", [(PT, vsrc[:, ti, 0:256])], [PTb, vb_], first=True, last=False)
                if not samp:
                    mm(pstate[:, 0:256], "pstate", [(Ktok[:, 0:128], vsrc[:, ti, 0:256])], [Ktokb, vb_])
                    dve(lambda e: e.scalar_tensor_tensor(out=Sh, in0=Sh, scalar=ss, in1=pstate[:, 0:256],
                                                         op0=ALU.mult, op1=ALU.add), [Shb, sbn, "pstate"], [Shb])
                    act(lambda e: e.copy(out=Sbf, in_=Sh), [Shb], [Sbfb])
                else:
                    def sgrp(g, Cs, Csb, Csbf, Csbfb, QX, QXb, VwX, VwXb):
                        js = slice(g * SG, (g + 1) * SG)
                        Ss = Cs[:, :, 0, 0:256]
                        Ssb = Csbf[:, :, 0, 0:256]
                        sdma(Ss, sS[js, h].rearrange("j p e -> p j e"), [], [Csb])
                        act(lambda e, Ss=Ss, Ssb=Ssb: e.copy(out=Ssb, in_=Ss), [Csb], [Csbfb])
                        dve(lambda e, js=js: e.tensor_tensor(
                            out=QX[:, 0, :, :], in0=QtT.unsqueeze(1).broadcast_to([128, SG, 128]),
                            in1=qxmask[:, js, :], op=ALU.mult), [QtTb, "cb"], [QXb])
                        mm(pnum[:, 0:256], "pnum", [(QX[:, 0, j, :], Ssb[:, j, :]) for j in range(SG)],
                           [QXb, Csbfb], first=False, last=(g == 16 // SG - 1))
                        dve(lambda e, js=js: e.tensor_tensor(
                            out=VwX[:, :, 0:256], in0=vsrc[:, ti, 0:256].unsqueeze(1).broadcast_to([128, SG, 256]),
                            in1=vxmask[:, js].unsqueeze(2).broadcast_to([128, SG, 256]), op=ALU.mult),
                            [vb_, "cb"], [VwXb])
                        for j in range(SG):
                            mm(pstate[:, 0:256], "pstate", [(Ktok[:, 0:128], VwX[:, j, 0:256])], [Ktokb, VwXb])
                            dcol = dSs[:, g * SG + j:g * SG + j + 1]
                            dve(lambda e, j=j, dcol=dcol, Ss=Ss: e.scalar_tensor_tensor(
                                out=Ss[:, j, :], in0=Ss[:, j, :], scalar=dcol, in1=pstate[:, 0:256], op0=ALU.mult,
                                op1=ALU.add), [Csb, dSsb, "pstate"], [Csb])
                        sdma(o_sS[js, h].rearrange("j p e -> p j e"), Ss, [Csb], [])
                    for g in range(16 // SG):
                        k_ = csi[0]
                        csi[0] += 1
                        sgrp(g, *Cs3[k_ % 3], *Csbf2[k_ % 2], *QX2[k_ % 2], *VwX2[k_ % 2])
                if full:
                    head_epilogue(pnum[:, 0:256], "pnum", None, OG[:, ti, :], OGb, ti, 1024 + h * 256, None, ytok, ytokb)
                if c == NCH - 1:
                    sdma(o_pS[h], Sh, [Shb], [])
            dve(lambda e: e.memset(Sbf, 0.0), [], [Sbfb])
            for c in range(NCH + 1):
                gchunk(c)

        mlstm_proj(0)
        S.barrier()
        print("arena phase 1a used %d / %d (mark %d)" % (apos[0] * 2, ARENA * 2, gmark * 2))
        apos[0] = gmark
        Cs3 = [carve("Cs%d" % i, [SG, 2, 257], F32) for i in range(3)]
        Csbf2 = [carve("Csbf%d" % i, [SG, 2, 257]) for i in range(2)]
        QX2 = [carve("QX%d" % i, [2, SG, 128]) for i in range(2)]
        VwX2 = [carve("VwX%d" % i, [SG, 257]) for i in range(2)]
        for h in range(4):
            if h > 0:
                mlstm_proj(h)
            mlstm_chunks(h)
            gla_head(h)
        tr(pmisc[:, 0:128], "pmisc", nTo.rearrange("p a b -> p (a b)"), nTob, identf)
        evac(nrow[:, 0:128], nrowb, pmisc[:, 0:128], "pmisc", eng="dve")
        sdma(o_sn[:, :], nrow[:, 0:128], [nrowb], [])
        if stop_after == "gla":
            return finish(nc, S, dbg_out)

        arena_reset()
        wide[0] = False
        yTa = hTp[:].rearrange("p k t -> p (k t)")[:, 0:8 * TM].rearrange("p (k t) -> p k t", k=8)
        yTg, yTgb = carve("yTg", [8, TM])
        mT, mTb = carve("mT", [KD, TM])
        ystg, ystgb = carve("ystg", [D])
        sg, sgb = carve("sg", [2, TM])
        tmpm, tmpmb = carve("tmpm", [512])
        xs_ = [carve("xs%d" % i, [256], F32)[0] for i in range(8)]
        for ti in range(NMT):
            sdma(ystg, yscr[ti * 128:(ti + 1) * 128, :], ["yscr"], [ystgb])
            for half in range(2):
                pt = ptr[half]
                pbn = "ptr%d" % half
                for j in range(8):
                    k = half * 8 + j
                    tr(pt[:, j * 128:(j + 1) * 128], pbn, ystg[:, k * 128:(k + 1) * 128], ystgb, identb, inc=(j == 7))
                for j in range(8):
                    k = half * 8 + j
                    dstt, dstb = (yTa, "hTp") if half == 0 else (yTg, yTgb)
                    dst = dstt[:, j, ti * 128:(ti + 1) * 128]
                    hcol = spm[:, P_HNM + k:P_HNM + k + 1]
                    if j % 2 == 0:
                        act(lambda e, dst=dst, j=j, pt=pt, hcol=hcol: e.activation(
                            out=dst, in_=pt[:, j * 128:(j + 1) * 128], func=AF.Copy, scale=hcol),
                            [pbn, "spm"], [dstb])
                    else:
                        dve(lambda e, dst=dst, j=j, pt=pt, hcol=hcol: e.tensor_scalar(
                            out=dst, in0=pt[:, j * 128:(j + 1) * 128], scalar1=hcol, scalar2=None, op0=ALU.mult),
                            [pbn, "spm"], [dstb])

        MGB = [(120, 392), (512, 512), (1024, 256)]
        dve(lambda e: e.memset(mT[:, :, 0:120], 0.0), [], [mTb])

        def branch_group(cg):
            MG = MGB
            c0 = cg * 256
            for (gcol, wbr, ysrc, ysb, first) in ((C_GA, w_a, yTa, "hTp", True), (C_GB, w_b, yTg, yTgb, False)):
                wgt, wgtn = load_w(w_in, WOFF[gcol + c0], KD, 256)
                for cb_ in range(2):
                    for (t0, N) in MG:
                        pv, pn_ = proj_fm(wgt, wgtn, cb_ * 128, 128, hT, "hT", t0, N)
                        evac(sg[:, cb_, t0:t0 + N], sgb, pv, pn_, func=AF.Sigmoid)
                wbt, wbtn = load_w(wbr, cg * 8 * 256, 8, 256)
                for cb_ in range(2):
                    kk = cg * 2 + cb_
                    for (t0, N) in MG:
                        bank, bname = nextbank()
                        mm(bank[:, 0:N], bname,
                           [(wbt[:, k, cb_ * 128:(cb_ + 1) * 128], ysrc[:, k, t0:t0 + N]) for k in range(8)],
                           [wbtn, ysb])
                        if first:
                            dve(lambda e, bank=bank, N=N, t0=t0, cb_=cb_, kk=kk: e.tensor_tensor(
                                out=mT[:, kk, t0:t0 + N], in0=bank[:, 0:N], in1=sg[:, cb_, t0:t0 + N], op=ALU.mult),
                                [bname, sgb], [mTb])
                        else:
                            dve(lambda e, bank=bank, N=N, t0=t0, cb_=cb_: e.tensor_tensor(
                                out=tmpm[:, 0:N], in0=bank[:, 0:N], in1=sg[:, cb_, t0:t0 + N], op=ALU.mult),
                                [bname, sgb], [tmpmb])
                            dve(lambda e, N=N, t0=t0, kk=kk: e.tensor_tensor(
                                out=mT[:, kk, t0:t0 + N], in0=mT[:, kk, t0:t0 + N], in1=tmpm[:, 0:N], op=ALU.add),
                                [tmpmb, mTb], [mTb])
        for cg in range(8):
            branch_group(cg)

        def wout_group(cg):
            c0 = cg * 256
            wot, wotn = load_w(w_o, cg * KD * 256, KD, 256)
            for ti in range(NMT):
                xs = xs_[(cg * NMT + ti) % 8]
                xsb = "xsb%d" % ((cg * NMT + ti) % 8)
                S.dma("pool", lambda e, xs=xs, ti=ti: e.dma_start(out=xs[:], in_=xmain[ti * 128:(ti + 1) * 128, c0:c0 + 256]),
                      writes=[xsb])
                bank, bname = nextbank()
                mm(bank[:, 0:256], bname, [(mT[:, k, ti * 128:(ti + 1) * 128], wot[:, k, :]) for k in range(KD)],
                   [wotn, mTb])
                dve(lambda e, xs=xs, bank=bank: e.tensor_tensor(out=xs[:], in0=xs[:], in1=bank[:, 0:256], op=ALU.add),
                    [xsb, bname], [xsb])
                sdma(x1scr[ti * 128:(ti + 1) * 128, c0:c0 + 256], xs[:], [xsb], ["x1scr"])
        for cg in range(8):
            wout_group(cg)

        arena_reset()
        sdma(wbc[:], nfw.partition_broadcast(128), [], ["wbc"])
        xt2 = [carve("xt2_%d" % i, [D], F32) for i in range(4)]
        xn2 = [carve("xn2_%d" % i, [D], BF16) for i in range(4)]
        sq2, sq2b = carve("sq2", [D], BF16)
        for ti in range(NMT):
            (x_t, xb), (x_n, xnb) = xt2[ti % 4], xn2[ti % 4]
            sdma(x_t, x1scr[ti * 128:(ti + 1) * 128, :], ["x1scr"], [xb])
            norm_to_T(x_t, xb, x_n, xnb, hT, "hT", ti, sq2, sq2b)

        arena_reset()
        HT_ = 640
        actT, actTb = carve("actT", [KF, HT_])
        upad2 = [carve("upad%d" % i, [2 + HT_], F32) for i in range(2)]
        tb2 = [carve("tbuf%d" % i, [HT_], F32) for i in range(2)]
        pb2 = [carve("pbuf%d" % i, [HT_], F32) for i in range(2)]
        gb2 = [carve("gbuf%d" % i, [HT_]) for i in range(2)]
        w4f = [wsl[i // 2][:, (i % 2) * KD * 128:((i % 2) + 1) * KD * 128] for i in range(6)]
        w4 = [v.rearrange("p (k c) -> p k c", k=KD) for v in w4f]
        w4i = [0]

        def load_w4(packed, kf):
            i = w4i[0] % 6
            w4i[0] += 1
            name = "w4_%d" % i
            flat = w4f[i]
            off = kf * KD * 128
            S.dma("pool", lambda e: e.dma_start(out=flat, in_=packed[:, off:off + KD * 128]), writes=[name])
            return w4[i], name
        ucar, ucarb = carve("ucar", [KF, 2], F32)
        ucv, ucvb = carve("ucv", [KF, 34], F32)
        scvT, scvTb = carve("scvT", [KF, 32], F32)
        srow, srowb = carve("srow", [512], F32)
        fst, fstb = carve("fst", [5, 128], F32)
        wdsf = [hTp[:].rearrange("p k t -> p (k t)")[:, i * KF * 128:(i + 1) * KF * 128] for i in range(2)]
        wds = [v.rearrange("p (k c) -> p k c", k=KF) for v in wdsf]
        x2scr = x2scr_
        CW = lambda j, k: spm[:, P_CW + j * KF + k:P_CW + j * KF + k + 1]
        CBc = lambda k: spm[:, P_CB + k:P_CB + k + 1]
        sc32 = sconv.rearrange("j r c -> (j r) c")
        for k4 in range(KF // 4):
            sdma(srow[0:32, :], sc32[:, k4 * 512:(k4 + 1) * 512], [], [srowb])
            for q in range(4):
                tr(pmisc[:, q * 32:(q + 1) * 32], "pmisc", srow[0:32, q * 128:(q + 1) * 128], srowb,
                   identf[0:32, 0:32], inc=(q == 3))
            evac(scvT[:, k4 * 4:(k4 + 1) * 4, :], scvTb, pmisc[:, 0:128].rearrange("p (a b) -> p a b", a=4),
                 "pmisc", eng="dve")
        dve(lambda e: e.memset(ucar, 0.0), [], [ucarb])
        for (up_, upb_) in upad2:
            dve(lambda e, up_=up_: e.memset(up_[:, 0:122], 0.0), [], [upb_])
        wdi = [0]

        FLO = [120, 0]
        FGR = [[(120, 200), (320, 320)], [(0, 320), (320, 320)]]

        def ffn_block(half, kf):
            g0 = half * HT_
            npr = HT_ if half == 0 else 512
            wu, wun = load_w4(w_up, kf)
            wg_, wgn_ = load_w4(w_gt, kf)
            (upad, upadb), (tb_, tbb), (pb_, pbb), (gb_, gbb) = upad2[kf % 2], tb2[kf % 2], pb2[kf % 2], gb2[kf % 2]
            dve(lambda e: e.tensor_copy(out=upad[:, 0:2], in_=ucar[:, kf, :]), [ucarb], [upadb])
            lo = FLO[half]
            for (l0, N) in FGR[half]:
                t0 = g0 + l0
                pv, pn_ = proj_fm(wu, wun, 0, 128, hT, "hT", t0, N)
                evac(upad[:, 2 + l0:2 + l0 + N], upadb, pv, pn_, eng="act")
                act(lambda e, pv=pv, l0=l0, N=N: e.activation(out=tb_[:, l0:l0 + N], in_=pv, func=AF.Identity,
                                                              bias=CBc(kf), scale=CW(2, kf)), [pn_, "spm"], [tbb])
                pv, pn_ = proj_fm(wg_, wgn_, 0, 128, hT, "hT", t0, N)
                evac(gb_[:, l0:l0 + N], gbb, pv, pn_, eng="act")
            if half == 0:
                dve(lambda e: e.tensor_copy(out=ucar[:, kf, :], in_=upad[:, HT_:HT_ + 2]), [upadb], [ucarb])
            else:
                dve(lambda e: e.tensor_copy(out=ucv[:, kf, 0:2], in_=upad[:, 512:514]), [upadb], [ucvb])
                u3 = upad[:, 2 + 512:2 + 640].rearrange("p (j l) -> p j l", l=8)
                dve(lambda e, u3=u3: e.tensor_copy(out=ucv[:, kf, 2:34].rearrange("p (j r) -> p j r", r=2),
                                                   in_=u3[:, :, 6:8]), [upadb], [ucvb])
            dve(lambda e: e.scalar_tensor_tensor(out=tb_[:, lo:npr], in0=upad[:, 1 + lo:1 + npr], scalar=CW(1, kf),
                                                 in1=tb_[:, lo:npr], op0=ALU.mult, op1=ALU.add),
                [upadb, tbb, "spm"], [tbb])
            dve(lambda e: e.scalar_tensor_tensor(out=tb_[:, lo:npr], in0=upad[:, lo:npr], scalar=CW(0, kf),
                                                 in1=tb_[:, lo:npr], op0=ALU.mult, op1=ALU.add),
                [upadb, tbb, "spm"], [tbb])
            if half == 1:
                t3 = tb_[:, 512:640].rearrange("p (j l) -> p j l", l=8)
                u3 = upad[:, 2 + 512:2 + 640].rearrange("p (j l) -> p j l", l=8)
                s3 = scvT[:, kf, :].rearrange("p (j r) -> p j r", r=2)
                dve(lambda e, t3=t3, u3=u3: e.scalar_tensor_tensor(
                    out=t3[:, :, 1:8], in0=u3[:, :, 0:7], scalar=CW(1, kf), in1=t3[:, :, 1:8], op0=ALU.mult,
                    op1=ALU.add), [upadb, tbb, "spm"], [tbb])
                dve(lambda e, t3=t3, s3=s3: e.scalar_tensor_tensor(
                    out=t3[:, :, 0:1], in0=s3[:, :, 1:2], scalar=CW(1, kf), in1=t3[:, :, 0:1], op0=ALU.mult,
                    op1=ALU.add), [scvTb, tbb, "spm"], [tbb])
                dve(lambda e, t3=t3, u3=u3: e.scalar_tensor_tensor(
                    out=t3[:, :, 2:8], in0=u3[:, :, 0:6], scalar=CW(0, kf), in1=t3[:, :, 2:8], op0=ALU.mult,
                    op1=ALU.add), [upadb, tbb, "spm"], [tbb])
                dve(lambda e, t3=t3, s3=s3: e.scalar_tensor_tensor(
                    out=t3[:, :, 0:2], in0=s3[:, :, 0:2], scalar=CW(0, kf), in1=t3[:, :, 0:2], op0=ALU.mult,
                    op1=ALU.add), [scvTb, tbb, "spm"], [tbb])
            fs = slice(lo, HT_)
            act(lambda e: e.activation(out=pb_[:, fs], in_=tb_[:, fs], func=AF.Square, scale=0.044715 ** 0.5), [tbb],
                [pbb])
            dve(lambda e: e.scalar_tensor_tensor(out=pb_[:, fs], in0=pb_[:, fs], scalar=1.0, in1=tb_[:, fs],
                                                 op0=ALU.add, op1=ALU.mult), [pbb, tbb], [pbb])
            act(lambda e: e.activation(out=pb_[:, fs], in_=pb_[:, fs], func=AF.Sigmoid, scale=1.5957691216057308),
                [pbb], [pbb])
            dve(lambda e: e.tensor_tensor(out=pb_[:, fs], in0=pb_[:, fs], in1=tb_[:, fs], op=ALU.mult), [pbb, tbb],
                [pbb])
            dve(lambda e: e.tensor_tensor(out=actT[:, kf, fs], in0=pb_[:, fs], in1=gb_[:, fs], op=ALU.mult),
                [pbb, gbb], [actTb])

        def down_block(half, cbk):
            g0 = half * HT_
            i = wdi[0] % 2
            wdi[0] += 1
            (tb_, tbb) = tb2[i]
            wd = wds[i]
            wdn = "wds%d" % i
            off = cbk * KF * 128
            wdf = wdsf[i]
            S.dma("pool", lambda e: e.dma_start(out=wdf[:, 0:22 * 128], in_=w_dn[:, off:off + 22 * 128]),
                  writes=[wdn])
            S.dma("pool", lambda e: e.dma_start(out=wdf[:, 22 * 128:44 * 128],
                                                in_=w_dn[:, off + 22 * 128:off + 44 * 128]), writes=[wdn])
            for (l0, N) in FGR[half]:
                bank, bname = nextbank()
                mm(bank[:, 0:N], bname, [(wd[:, k, :], actT[:, k, l0:l0 + N]) for k in range(KF)],
                   [wdn, actTb])
                evac(tb_[:, l0:l0 + N], tbb, bank[:, 0:N], bname, eng="act")
            tts = range(1, 5) if half == 0 else range(5)
            for tt in tts:
                dstp = pst[:, tt * 128:(tt + 1) * 128] if tt < 4 else pnum[:, 0:128]
                dstn = "pst" if tt < 4 else "pnum"
                tr(dstp, dstn, tb_[:, tt * 128:(tt + 1) * 128], tbb, identf, inc=(tt >= 3))
            evac(fst[:, 0:4, :], fstb, pst[:, 0:512].rearrange("p (a b) -> p a b", a=4), "pst", eng="dve")
            evac(fst[:, 4, :], fstb, pnum[:, 0:128], "pnum", eng="dve")
            for tt in tts:
                r0 = g0 + tt * 128
                sdma(x2scr[r0:r0 + 128, cbk * 128:(cbk + 1) * 128], fst[:, tt, :], [fstb], ["x2scr"])

        for half in range(2):
            for kf in range(KF):
                ffn_block(half, kf)
            for cbk in range(KD):
                down_block(half, cbk)
        oconv_s = o_sconv.rearrange("j r c -> (j r) c")
        for k4 in range(KF // 4):
            for q in range(4):
                tr(pmisc[0:34, q * 128:(q + 1) * 128], "pmisc", ucv[:, k4 * 4 + q, :], ucvb, identf, inc=(q == 3))
            evac(srow[0:34, :], srowb, pmisc[0:34, 0:512], "pmisc", eng="dve")
            sdma(o_pconv[:, k4 * 512:(k4 + 1) * 512], srow[0:2, :], [srowb], [])
            sdma(oconv_s[:, k4 * 512:(k4 + 1) * 512], srow[2:34, :], [srowb], [])

        arena_reset()
        sdma(wbc[:], fnw.partition_broadcast(128), [], ["wbc"])
        xa = [carve("xa%d" % i, [D], F32) for i in range(4)]
        xf = [carve("xf%d" % i, [D], F32) for i in range(4)]
        sq3, sq3b = carve("sq3", [D], BF16)
        for ti in range(1, NMT):
            (x_a, xab), (x_f, xfb) = xa[ti % 4], xf[ti % 4]
            sdma(x_a, x1scr[ti * 128:(ti + 1) * 128, :], ["x1scr"], [xab])
            S.dma("pool", lambda e, x_f=x_f, ti=ti: e.dma_start(out=x_f, in_=x2scr[ti * 128:(ti + 1) * 128, :]),
                  reads=["x2scr"], writes=[xfb])
            dve(lambda e, x_a=x_a, x_f=x_f: e.tensor_tensor(out=x_a, in0=x_a, in1=x_f, op=ALU.add), [xab, xfb], [xab])
            ss, rs, _, sbn = statslot()
            act(lambda e, x_a=x_a, ss=ss: e.activation(out=sq3, in_=x_a, func=AF.Square, accum_out=ss), [xab],
                [sq3b, sbn])
            rstd_from_ss(ss, rs, sbn, D)
            dve(lambda e, x_a=x_a, x_f=x_f, rs=rs: e.scalar_tensor_tensor(out=x_f, in0=x_a, scalar=rs, in1=wbc[:],
                                                                          op0=ALU.mult, op1=ALU.mult),
                [xab, sbn + "r", "wbc"], [xfb])
            sdma(yout[(ti - 1) * 128:ti * 128, :], x_f, [xfb], [])
        return finish(nc, S, dbg_out)


def finish(nc, S, dbg_out):
    S.finish()
    print("sim phase us:", [int(x) for x in S.sim_phase_us], "units", len(S.units))
    with nc.Block() as block:
        @block.sync
        def _(eng):
            S.emit("sp", eng)

        @block.tensor
        def _(eng):
            S.emit("pe", eng)

        @block.scalar
        def _(eng):
            S.emit("act", eng)

        @block.vector
        def _(eng):
            S.emit("dve", eng)

        @block.gpsimd
        def _(eng):
            S.emit("pool", eng)
    return nc


def make_consts():
    import ml_dtypes
    cb = np.zeros((128, CB_W), np.float32)
    cb[:, 0:128] = np.eye(128)
    s = np.arange(128)[:, None]
    t = np.arange(128)[None, :]
    cb[:, 128:256] = (s <= t)
    cb[:, 256:384] = (s <= t) & ((s // 8) == (t // 8))
    j = np.arange(16)[:, None]
    cb[:, 512:512 + 2048] = ((np.arange(128)[None, :] // 8) == j).astype(np.float32).reshape(1, 2048)
    cb[:, 2560:2576] = ((np.arange(128)[:, None] // 8) == np.arange(16)[None, :])
    cf = np.zeros((128, CF_W), np.float32)
    cf[:, 0:128] = np.eye(128)
    cf[:, 128:256] = 1.0
    cf[:, 256:384] = (np.arange(128)[None, :] % 8 != 0)
    cf[:, 384:512] = np.where(np.arange(128)[None, :] % 8 == 0, -1e30, 0.0)
    return cb.astype(ml_dtypes.bfloat16), cf


_NC_CACHE = {}


def _prep_inputs(inp):
    f32 = np.float32
    cbc, cfc = make_consts()
    xp = np.asarray(inp["x_prompt"], f32)
    xs = np.asarray(inp["x_sample"], f32)
    sp = np.zeros((128, SP_W), f32)
    ib = np.asarray(inp["mlstm_i_bias"], f32)[0]
    fb = np.asarray(inp["mlstm_f_bias"], f32)[0]
    sp[0:64, 0] = np.repeat(ib, 16)
    sp[0:64, 1] = np.repeat(fb, 16)
    sp[:, 2:6] = np.asarray(inp["gla_alpha_bias"], f32)[0].reshape(4, 128).T
    sp[:, 6:14] = np.asarray(inp["mlstm_head_norm_w"], f32)[0].reshape(8, 128).T
    sp[:, 14:22] = np.asarray(inp["gla_head_norm_w"], f32)[0].reshape(8, 128).T
    cw = np.asarray(inp["ffn_conv_w"], f32)[0]
    for j in range(3):
        sp[:, 22 + j * KF:22 + (j + 1) * KF] = cw[j].reshape(KF, 128).T
    sp[:, 22 + 3 * KF:22 + 4 * KF] = np.asarray(inp["ffn_conv_b"], f32)[0].reshape(KF, 128).T
    shared = {
        "w_in": _pack(np.asarray(inp["w_in"], f32)[0], KD, _win_blocks()),
        "nmw": np.asarray(inp["norm_mix_w"], f32).reshape(1, D),
        "nfw": np.asarray(inp["norm_ffn_w"], f32).reshape(1, D),
        "fnw": np.asarray(inp["final_norm_w"], f32).reshape(1, D),
        "cst_bf": cbc, "cst_f": cfc, "smallp": sp,
        "aup": np.asarray(inp["gla_alpha_up"], f32)[0],
        "w_a": _pack(np.asarray(inp["w_branch_a"], f32)[0], 8, [(c * 256, 256) for c in range(8)]),
        "w_b": _pack(np.asarray(inp["w_branch_b"], f32)[0], 8, [(c * 256, 256) for c in range(8)]),
        "w_o": _pack(np.asarray(inp["w_out"], f32)[0], KD, [(c * 256, 256) for c in range(8)]),
        "w_up": _pack(np.asarray(inp["ffn_w_up"], f32)[0], KD, [(c * 128, 128) for c in range(KF)]),
        "w_gt": _pack(np.asarray(inp["ffn_w_gate"], f32)[0], KD, [(c * 128, 128) for c in range(KF)]),
        "w_dn": _pack(np.asarray(inp["ffn_w_down"], f32)[0], KF, [(c * 128, 128) for c in range(KD)]),
    }
    sC = np.asarray(inp["state_mlstm_C"], f32)[0]
    sn = np.asarray(inp["state_mlstm_n"], f32)[0]
    sm = np.asarray(inp["state_mlstm_m"], f32)[0]
    sS = np.asarray(inp["state_gla_S"], f32)[0]
    scv = np.asarray(inp["state_ffn_conv"], f32)[0]
    maps = []
    for c in range(8):
        s, half = c // 2, c % 2
        xmain = np.zeros((TM, D), f32)
        if half == 1:
            xpre = np.ascontiguousarray(xp[s, 0:PRE])
            xmain[0:1152] = xp[s, PRE:2048]
        else:
            xpre = np.zeros((PRE, D), f32)
            xmain[128:1152] = xp[s, 0:1024]
        xmain[1152:1280] = xs[16 * c:16 * c + 16].reshape(128, D)
        m = dict(shared)
        m.update({
            "xpre": xpre, "xmain": xmain, "flag": np.full((1, 1), float(half), f32),
            "sC": np.ascontiguousarray(sC[16 * c:16 * c + 16]),
            "sn": np.ascontiguousarray(sn[16 * c:16 * c + 16].reshape(64, 256)),
            "smcol": np.ascontiguousarray(sm[16 * c:16 * c + 16].T.reshape(64, 1)),
            "sS": np.ascontiguousarray(sS[16 * c:16 * c + 16]),
            "sconv": np.ascontiguousarray(scv[16 * c:16 * c + 16]),
        })
        maps.append(m)
    return maps


def _assemble(results):
    f32 = np.float32
    y_p = np.zeros((4, 2048, D), f32)
    y_s = np.zeros((128, 8, D), f32)
    pC = np.zeros((1, 4, 4, 256, 256), f32)
    pn = np.zeros((1, 4, 4, 256), f32)
    pm = np.zeros((1, 4, 4), f32)
    pS = np.zeros((1, 4, 4, 128, 256), f32)
    pcv = np.zeros((1, 4, 2, DFF), f32)
    sCo = np.zeros((1, 128, 4, 256, 256), f32)
    sno = np.zeros((1, 128, 4, 256), f32)
    smo = np.zeros((1, 128, 4), f32)
    sSo = np.zeros((1, 128, 4, 128, 256), f32)
    scvo = np.zeros((1, 128, 2, DFF), f32)
    for c in range(8):
        r = results[c]
        s, half = c // 2, c % 2
        yo = np.asarray(r["yout"], f32)
        y_p[s, half * 1024:(half + 1) * 1024] = yo[0:1024]
        y_s[16 * c:16 * c + 16] = yo[1024:1152].reshape(16, 8, D)
        if half == 1:
            pC[0, s] = np.asarray(r["o_pC"], f32)
            pn[0, s] = np.asarray(r["o_pn"], f32).reshape(4, 256)
            pm[0, s] = np.asarray(r["o_pm"], f32).reshape(4, 16)[:, 15]
            pS[0, s] = np.asarray(r["o_pS"], f32)
            pcv[0, s] = np.asarray(r["o_pconv"], f32)
        sl = slice(16 * c, 16 * c + 16)
        sCo[0, sl] = np.asarray(r["o_sC"], f32)
        sno[0, sl] = np.asarray(r["o_sn"], f32).reshape(2, 16, 4, 128).transpose(1, 2, 0, 3).reshape(16, 4, 256)
        smo[0, sl] = np.asarray(r["o_sm"], f32).reshape(4, 16).T
        sSo[0, sl] = np.asarray(r["o_sS"], f32)
        scvo[0, sl] = np.asarray(r["o_sconv"], f32)
    return (y_p, y_s, pC, pn, pm, pS, pcv, sCo, sno, smo, sSo, scvo)


def kernel(**inputs):
    if "nc" not in _NC_CACHE:
        _NC_CACHE["nc"] = build()
    nc = _NC_CACHE["nc"]
    maps = _prep_inputs(inputs)
    res = run_bass_kernel_spmd(nc, maps, core_ids=list(range(8)))
    return _assemble(res.results)
```

```python
import numpy as np
from contextlib import ExitStack
import concourse.bass as bass
import concourse.mybir as mybir
from concourse.bass_utils import run_bass_kernel_spmd

F32 = mybir.dt.float32
BF16 = mybir.dt.bfloat16
AF = mybir.ActivationFunctionType
ALU = mybir.AluOpType
AX = mybir.AxisListType

D = 2048
NIN = 11288
DFF = 5632
KD = D // 128
KF = DFF // 128
PRE = 896
NPT = PRE // 128
TM = 1280
CB_W = 2576
CF_W = 520
SP_W = 22 + 4 * KF
NMT = TM // 128
NCH = 16
EPS = 1e-6

C_MQ, C_MK, C_MV, C_MO = 0, 1024, 2048, 3072
C_MI, C_MF = 4096, 4100
C_GQ, C_GK, C_GV, C_GR = 4104, 4616, 5128, 6152
C_GAL = 7176
C_GA, C_GB = 7192, 9240


def _win_blocks():
    blks = [(C_MI, 8), (C_GAL, 16)]
    for hh in range(4):
        blks += [(C_MQ + hh * 256, 256), (C_MK + hh * 256, 256), (C_MV + hh * 256, 256), (C_MO + hh * 256, 256)]
    for hh in range(4):
        blks += [(C_GQ + hh * 128, 128), (C_GK + hh * 128, 128), (C_GV + hh * 256, 256), (C_GR + hh * 256, 256)]
    for cg in range(8):
        blks += [(C_GA + cg * 256, 256), (C_GB + cg * 256, 256)]
    return blks


def _win_offsets():
    off = {}
    o = 0
    for c0, n in _win_blocks():
        off[c0] = o
        o += KD * n
    return off


def _pack(w, nk, blocks):
    outs = []
    for c0, n in blocks:
        outs.append(np.ascontiguousarray(w[:, c0:c0 + n].reshape(nk, 128, n).transpose(1, 0, 2)).reshape(128, nk * n))
    return np.ascontiguousarray(np.concatenate(outs, axis=1))


class _Probe:
    def __init__(self):
        self.calls = []

    def __getattr__(self, name):
        def f(*a, **k):
            self.calls.append((name, a, k))
            return self
        return f

    def then_inc(self, *a, **k):
        return self


def _free_elems(ap):
    n = 1
    for s in ap.shape[1:]:
        n *= int(s)
    return n


def _est(eng, fn):
    p = _Probe()
    fn(p)
    name, a, k = p.calls[0]
    out = k.get("out", a[0] if a else None)
    if eng == "pe":
        if name == "transpose":
            return 0.09
        rhs = k.get("rhs")
        n = _free_elems(rhs)
        f = 4.0 if rhs.dtype == F32 else 1.0
        return max(n, 64) * f / 2400.0 + 0.012
    n = _free_elems(out) if out is not None else 64
    if eng == "act":
        return (n + 400) / 1400.0 + (0.1 if k.get("accum_out") is not None else 0.0)
    if eng in ("dve", "pool"):
        f = 2.0 if name in ("tensor_tensor_scan",) else 1.0
        return max(n, 60) * f / 960.0 + 0.2
    return 0.1


def _dma_est(fn):
    p = _Probe()
    fn(p)
    name, a, k = p.calls[0]
    out = k.get("out")
    nbytes = 1
    for s in out.shape:
        nbytes *= int(s)
    nbytes *= 4 if out.dtype == F32 else 2
    return 2.0 + nbytes / 180e3, nbytes


class Sched:
    ENGS = ("pe", "act", "dve", "pool", "sp")
    XLAT = 0.4

    def __init__(self, esems, dsems):
        self.esem = esems
        self.dsems = dsems
        self.units = []
        self.bufs = {}
        self.phase = 0
        self.trace_phase = None
        self.open = {e: None for e in esems}

    def _st(self, b):
        st = self.bufs.get(b)
        if st is None:
            st = {"w": None, "r": {}}
            self.bufs[b] = st
        return st

    def _deps(self, uid, reads, writes):
        deps = set()
        for b in reads:
            st = self._st(b)
            if st["w"] is not None:
                deps.add(st["w"])
        for b in writes:
            st = self._st(b)
            if st["w"] is not None:
                deps.add(st["w"])
            deps.update(st["r"].keys())
        deps.discard(uid)
        return deps

    def _mark(self, uid, reads, writes):
        for b in reads:
            self._st(b)["r"][uid] = True
        for b in writes:
            st = self._st(b)
            st["w"] = uid
            st["r"] = {}

    def op(self, eng, fn, reads=(), writes=(), inc=True):
        u = self.open[eng]
        if u is None:
            u = {"eng": eng, "fns": [], "deps": set(), "dur": 0.0, "dma": False, "phase": self.phase,
                 "id": len(self.units), "lab": ",".join(writes)}
            self.units.append(u)
            self.open[eng] = u
        u["deps"] |= self._deps(u["id"], reads, writes)
        self._mark(u["id"], reads, writes)
        u["fns"].append(fn)
        u["dur"] += _est(eng, fn)
        if inc:
            self.open[eng] = None

    def dma(self, q, fn, reads=(), writes=()):
        uid = len(self.units)
        lat, nbytes = _dma_est(fn)
        u = {"eng": q, "fns": [fn], "deps": self._deps(uid, reads, writes), "dur": 0.06 if q == "sp" else 0.35,
             "dma": True, "lat": lat, "phase": self.phase, "id": uid, "lab": "dma:" + ",".join(writes) + "<" + ",".join(reads)}
        self.units.append(u)
        self._mark(uid, reads, writes)

    def barrier(self):
        for e, v in self.open.items():
            assert v is None, e
        self.phase += 1
        self.bufs = {}

    def barrier_all_dma(self, q="sp"):
        pass

    def finish(self):
        import heapq
        order = {e: [] for e in self.ENGS}
        nph = self.phase + 1
        by_phase = [[] for _ in range(nph)]
        for u in self.units:
            by_phase[u["phase"]].append(u)
        for ph in range(nph):
            us = by_phase[ph]
            ids = {u["id"] for u in us}
            ndep = {}
            users = {}
            for u in us:
                d = [x for x in u["deps"] if x in ids]
                u["deps"] = set(d)
                ndep[u["id"]] = len(d)
                for x in d:
                    users.setdefault(x, []).append(u)
            byid = {u["id"]: u for u in us}
            ready = {e: [] for e in self.ENGS}
            avail = {}
            for u in us:
                if ndep[u["id"]] == 0:
                    heapq.heappush(ready[u["eng"]], u["id"])
                    avail[u["id"]] = 0.0
            tfree = {e: 0.0 for e in self.ENGS}
            lastu = {}
            done = 0
            fin = {}
            while done < len(us):
                best = None
                for e in self.ENGS:
                    h = ready[e]
                    if not h:
                        continue
                    cand = None
                    tnow = tfree[e]
                    low = [i for i in h if avail[i] <= tnow]
                    if low:
                        cid = min(low)
                        st = tnow
                    else:
                        cid = min(h, key=lambda i: (avail[i], i))
                        st = avail[cid]
                    if best is None or st < best[0] or (st == best[0] and cid < best[2]):
                        best = (st, e, cid)
                st, e, cid = best
                ready[e].remove(cid)
                heapq.heapify(ready[e])
                u = byid[cid]
                u["st"] = st
                if avail[cid] >= tfree[e] - 1e-9 and u["deps"]:
                    u["why"] = max(u["deps"], key=lambda x: fin[x])
                else:
                    u["why"] = lastu.get(e)
                lastu[e] = cid
                end = st + u["dur"]
                tfree[e] = end
                f = end + (u["lat"] if u["dma"] else 0.0)
                fin[cid] = f
                order[e].append(u)
                done += 1
                for v in users.get(cid, ()):
                    ndep[v["id"]] -= 1
                    if ndep[v["id"]] == 0:
                        t = 0.0
                        for x in v["deps"]:
                            lat = 0.0 if (byid[x]["eng"] == v["eng"] and v["eng"] == "pe") else self.XLAT
                            t = max(t, fin[x] + lat)
                        avail[v["id"]] = t
                        heapq.heappush(ready[v["eng"]], v["id"])
            self.sim_phase_us = getattr(self, "sim_phase_us", []) + [max(tfree.values())]
            if getattr(self, "trace_phase", None) == ph:
                cur = max(us, key=lambda u: fin[u["id"]])["id"]
                chain = []
                while cur is not None and len(chain) < 400:
                    u = byid[cur]
                    chain.append((round(u["st"], 2), u["eng"], round(u["dur"], 2), u["lab"], len(u["fns"])))
                    cur = u.get("why")
                for c in chain[:400]:
                    print("   CP", c)
            busy = {e: 0.0 for e in self.ENGS}
            for u in us:
                busy[u["eng"]] += u["dur"]
            print("phase", ph, "sim %.0f us" % max(tfree.values()), {e: int(v) for e, v in busy.items()}, "units", len(us))
        cnt = {e: 0 for e in self.esem}
        dcnt = {q: [0] * len(v) for q, v in self.dsems.items()}
        dnext = {q: 0 for q in self.dsems}
        for e in self.ENGS:
            for u in order[e]:
                if u["dma"]:
                    i = dnext[e]
                    dnext[e] = (i + 1) % len(self.dsems[e])
                    u["prev"] = (self.dsems[e][i], dcnt[e][i]) if dcnt[e][i] > 0 else None
                    dcnt[e][i] += 16
                    u["ev"] = (self.dsems[e][i], dcnt[e][i])
                else:
                    cnt[e] += 1
                    u["ev"] = (self.esem[e], cnt[e])
        self.order = order
        byid = {u["id"]: u for u in self.units}
        self.prog = {e: [] for e in self.ENGS}
        all_dma = [u for u in self.units if u["dma"]]
        for e in self.ENGS:
            wd = {}
            cur_phase = 0
            for u in order[e]:
                waits = []
                if u["phase"] != cur_phase:
                    for e2 in self.esem:
                        if e2 == e and e == "pe":
                            continue
                        c = max([x["ev"][1] for x in order[e2] if x["phase"] < u["phase"] and not x["dma"]] or [0])
                        sem = self.esem[e2]
                        if c > 0 and wd.get(id(sem), 0) < c:
                            wd[id(sem)] = c
                            waits.append((sem, c))
                    for x in all_dma:
                        if x["phase"] < u["phase"] and wd.get(id(x["ev"][0]), 0) < x["ev"][1]:
                            wd[id(x["ev"][0])] = x["ev"][1]
                            waits.append(x["ev"])
                    cur_phase = u["phase"]
                for d in sorted(u["deps"]):
                    x = byid[d]
                    if x["eng"] == e and e == "pe":
                        continue
                    sem, c = x["ev"]
                    if wd.get(id(sem), 0) >= c:
                        continue
                    wd[id(sem)] = c
                    waits.append((sem, c))
                if u["dma"] and u["prev"] is not None:
                    sem, c = u["prev"]
                    if wd.get(id(sem), 0) < c:
                        wd[id(sem)] = c
                        waits.append((sem, c))
                self.prog[e].append((waits, u["fns"], u["ev"][0], 16 if u["dma"] else 1))
        wd = {}
        waits = []
        for x in all_dma:
            if wd.get(id(x["ev"][0]), 0) < x["ev"][1]:
                wd[id(x["ev"][0])] = x["ev"][1]
        semobj = {}
        for x in all_dma:
            semobj[id(x["ev"][0])] = x["ev"][0]
        self.final_waits = [(semobj[k], v) for k, v in wd.items()]

    def emit(self, eng_name, eng):
        for waits, fns, sem, inc in self.prog[eng_name]:
            for s, v in waits:
                eng.wait_ge(s, v)
            ins = None
            for fn in fns:
                ins = fn(eng)
            ins.then_inc(sem, inc)
        if eng_name == "sp":
            for s, v in self.final_waits:
                eng.wait_ge(s, v)


def build(debug=None, stop_after=None):
    debug = debug or {}
    nc = bass.Bass("TRN2", target_bir_lowering=False)
    es = ExitStack()

    def din(name, shape, dt=F32):
        return nc.dram_tensor(name, list(shape), dt, kind="ExternalInput").ap()

    def dout(name, shape, dt=F32):
        return nc.dram_tensor(name, list(shape), dt, kind="ExternalOutput").ap()

    def dint(name, shape, dt=F32):
        return nc.dram_tensor(name, list(shape), dt, kind="Internal").ap()

    xpre = din("xpre", [PRE, D])
    xmain = din("xmain", [TM, D])
    flag = din("flag", [1, 1])
    w_in = din("w_in", [128, KD * NIN])
    WOFF = _win_offsets()
    nmw = din("nmw", [1, D])
    nfw = din("nfw", [1, D])
    fnw = din("fnw", [1, D])
    cst_bf = din("cst_bf", [128, CB_W], BF16)
    cst_f = din("cst_f", [128, CF_W])
    smallp = din("smallp", [128, SP_W])
    aup = din("aup", [16, 512])
    w_a = din("w_a", [128, 8 * D])
    w_b = din("w_b", [128, 8 * D])
    w_o = din("w_o", [128, KD * D])
    w_up = din("w_up", [128, KD * DFF])
    w_gt = din("w_gt", [128, KD * DFF])
    w_dn = din("w_dn", [128, KF * D])
    sC = din("sC", [16, 4, 256, 256])
    sn = din("sn", [64, 256])
    smcol = din("smcol", [64, 1])
    sS = din("sS", [16, 4, 128, 256])
    sconv = din("sconv", [16, 2, DFF])

    yout = dout("yout", [TM - 128, D])
    o_pC = dout("o_pC", [4, 256, 256])
    o_pn = dout("o_pn", [8, 128])
    o_pm = dout("o_pm", [1, 64])
    o_pS = dout("o_pS", [4, 128, 256])
    o_pconv = dout("o_pconv", [2, DFF])
    o_sC = dout("o_sC", [16, 4, 256, 256])
    o_sn = dout("o_sn", [128, 128])
    o_sm = dout("o_sm", [64, 1])
    o_sS = dout("o_sS", [16, 4, 128, 256])
    o_sconv = dout("o_sconv", [16, 2, DFF])

    gscr = dint("gscr", [8, 2048])
    gscs = dint("gscs", [8, 128])
    yscr = dint("yscr", [TM, D], BF16)
    x1scr = dint("x1scr", [TM, D])
    x2scr_ = dint("x2scr", [TM, D])
    dbg_out = {}

    with es:
        def sb(name, shape, dt=F32):
            return es.enter_context(nc.sbuf_tensor(name, list(shape), dt))

        def ps(name, shape, dt=F32):
            return es.enter_context(nc.psum_tensor(name, list(shape), dt))

        esems = {e: es.enter_context(nc.semaphore("s_" + e)) for e in ("pe", "act", "dve", "pool")}
        dsems = {
            "sp": [es.enter_context(nc.semaphore("d_sp%d" % i)) for i in range(24)],
            "pool": [es.enter_context(nc.semaphore("d_pl%d" % i)) for i in range(12)],
        }
        S = Sched(esems, dsems)

        def act(fn, r, w):
            S.op("act", fn, reads=r, writes=w)

        def dve(fn, r, w):
            S.op("dve", fn, reads=r, writes=w)

        def pool(fn, r, w):
            S.op("pool", fn, reads=r, writes=w)

        def sdma(out, in_, r, w):
            S.dma("sp", lambda e: e.dma_start(out=out, in_=in_), reads=r, writes=w)

        def mm(out_ap, outb, pairs, reads, first=True, last=True):
            n = len(pairs)
            for i, (l, r) in enumerate(pairs):
                S.op("pe", lambda e, l=l, r=r, i=i: e.matmul(out_ap, lhsT=l, rhs=r, start=(first and i == 0),
                                                            stop=(last and i == n - 1)),
                     reads=reads, writes=[outb], inc=(i == n - 1))

        def tr(out_ap, outb, in_ap, inb, ident, inc=True):
            S.op("pe", lambda e: e.transpose(out=out_ap, in_=in_ap, identity=ident), reads=[inb, "cb", "cf"],
                 writes=[outb], inc=inc)

        cb = sb("cb", [128, CB_W], BF16)
        cf = sb("cf", [128, CF_W], F32)
        spm = sb("spm", [128, SP_W], F32)
        sdma(cb[:], cst_bf[:, :], [], ["cb"])
        sdma(cf[:], cst_f[:, :], [], ["cf"])
        sdma(spm[:], smallp[:, :], [], ["spm"])
        identb = cb[:, 0:128]
        maskc = cb[:, 128:256]
        masks = cb[:, 256:384]
        qxmask = cb[:, 512:512 + 2048].rearrange("p (j t) -> p j t", j=16)
        vxmask = cb[:, 2560:2576]
        identf = cf[:, 0:128]
        onesf = cf[:, 128:256]
        resetm = cf[:, 256:384]
        negbig = cf[:, 384:512]
        zerocol = cf[:, 512:513]
        P_IB, P_FB, P_AB, P_HNM, P_HNG, P_CW, P_CB = 0, 1, 2, 6, 14, 22, 22 + 3 * KF
        ib64 = spm[0:64, P_IB:P_IB + 1]
        fb64 = spm[0:64, P_FB:P_FB + 1]
        wbc = sb("wbc", [128, D], F32)
        sdma(wbc[:], nmw.partition_broadcast(128), [], ["wbc"])
        flagt = sb("flagt", [1, 1], F32)
        sdma(flagt[:], flag[:, :], [], ["flagt"])

        hT = sb("hT", [128, KD, TM], BF16)
        hTp = sb("hTp", [128, KD, PRE], BF16)
        wsl = [sb("wsl%d" % i, [128, KD * 256], BF16) for i in range(3)]
        stat = sb("stat", [128, 64], F32)
        ARENA = 99 * 512 + 160
        arena = sb("arena", [128, ARENA], BF16)
        apos = [0]
        aphase = [0]

        def carve(name, shape, dt=BF16):
            n = int(np.prod(shape))
            nb = n * (2 if dt == BF16 else 4)
            nb = (nb + 63) // 64 * 64
            off = apos[0]
            apos[0] += nb // 2
            assert apos[0] <= ARENA, ("arena overflow", name, apos[0])
            v = arena[:, off:off + nb // 2]
            if dt != BF16:
                v = v.bitcast(dt)
            v = v[:, 0:n]
            if len(shape) == 2:
                pat, kw = "p (a b) -> p a b", dict(a=shape[0])
            elif len(shape) == 3:
                pat, kw = "p (a b c) -> p a b c", dict(a=shape[0], b=shape[1])
            else:
                pat, kw = None, None
            if pat:
                v = v.rearrange(pat, **kw)
            return v, "ar%d_%s" % (aphase[0], name)

        def arena_reset():
            S.barrier()
            print("arena phase %d used %d / %d" % (aphase[0], apos[0] * 2, ARENA * 2))
            apos[0] = 0
            aphase[0] += 1

        pa = ps("pa", [128, 512])
        pb = ps("pb", [128, 512])
        ptr = [ps("ptr%d" % i, [128, 1024], BF16) for i in range(2)]
        pst = ps("pst", [128, 512])
        pnum = ps("pnum", [128, 512])
        pstate = ps("pstate", [128, 512])
        pmisc = ps("pmisc", [128, 512])
        pbk = [(pa, "pa"), (pb, "pb")]
        pbk6 = [(pa, "pa"), (pb, "pb"), (pst, "pst"), (pnum, "pnum"), (pstate, "pstate"), (pmisc, "pmisc")]
        pbi = [0]
        wide = [False]

        def nextbank():
            pbi[0] += 1
            if wide[0]:
                return pbk6[pbi[0] % 6]
            return pbk[pbi[0] % 2]

        evi = [0]

        def evac(dst, dstb, src, srcb, func=None, scale=None, eng=None):
            if func is not None or eng == "act":
                f = func if func is not None else AF.Copy
                if scale is None:
                    act(lambda e: e.activation(out=dst, in_=src, func=f), [srcb], [dstb])
                else:
                    act(lambda e: e.activation(out=dst, in_=src, func=f, scale=scale), [srcb], [dstb])
                return
            evi[0] += 1
            if eng is None:
                eng = "dve" if evi[0] % 3 == 0 else "act"
            if eng == "act":
                if scale is None:
                    act(lambda e: e.copy(out=dst, in_=src), [srcb], [dstb])
                else:
                    act(lambda e: e.mul(out=dst, in_=src, mul=scale), [srcb], [dstb])
            else:
                if scale is None:
                    dve(lambda e: e.tensor_copy(out=dst, in_=src), [srcb], [dstb])
                else:
                    dve(lambda e: e.tensor_scalar(out=dst, in0=src, scalar1=scale, scalar2=None, op0=ALU.mult),
                        [srcb], [dstb])

        wi = [0]

        def load_w(packed, off, nk, ncols):
            i = wi[0] % 3
            wi[0] += 1
            name = "wsl%d" % i
            tot = nk * ncols
            flat = wsl[i][:, 0:tot]
            dst = flat.rearrange("p (k c) -> p k c", k=nk)
            hf = tot // 2
            S.dma("pool", lambda e: e.dma_start(out=flat[:, 0:hf], in_=packed[:, off:off + hf]), writes=[name])
            S.dma("pool", lambda e: e.dma_start(out=flat[:, hf:tot], in_=packed[:, off + hf:off + tot]),
                  writes=[name])
            return dst, name

        MG = [(0, 512), (512, 512), (1024, 256)]
        PG = [(0, 448), (448, 448)]

        def proj_fm(wt, wname, c0, M, src, srcname, t0, N):
            bank, bname = nextbank()
            mm(bank[0:M, 0:N], bname, [(wt[:, k, c0:c0 + M], src[:, k, t0:t0 + N]) for k in range(KD)],
               [wname, srcname])
            return bank[0:M, 0:N], bname

        def proj_tm(wt, wname, c0, N, src, srcname, ti):
            bank, bname = nextbank()
            mm(bank[:, 0:N], bname, [(src[:, k, ti * 128:(ti + 1) * 128], wt[:, k, c0:c0 + N]) for k in range(KD)],
               [wname, srcname])
            return bank[:, 0:N], bname

        xt = [carve("xt%d" % i, [D], F32) for i in range(4)]
        xn = [carve("xn%d" % i, [D], BF16) for i in range(4)]
        sq1, sq1b = carve("sq", [D], BF16)
        nstat = [0]

        def rstd_from_ss(ss, rs, b, n):
            dve(lambda e: e.tensor_scalar(out=rs, in0=ss, scalar1=1.0 / n, scalar2=EPS, op0=ALU.mult, op1=ALU.add),
                [b], [b + "r"])
            act(lambda e: e.activation(out=rs, in_=rs, func=AF.Sqrt), [b + "r"], [b + "r"])
            dve(lambda e: e.reciprocal(out=rs, in_=rs), [b + "r"], [b + "r"])

        def statslot():
            c = nstat[0] % 16
            nstat[0] += 1
            return stat[:, c:c + 1], stat[:, 16 + c:17 + c], stat[:, 32 + c:33 + c], "stat%d" % c

        def norm_to_T(x_t, xb, x_n, xnb, dstT, dstname, ti, sqj, sqb):
            ss, rs, _, sbn = statslot()
            act(lambda e: e.activation(out=sqj, in_=x_t, func=AF.Square, accum_out=ss), [xb], [sqb, sbn])
            rstd_from_ss(ss, rs, sbn, D)
            dve(lambda e: e.scalar_tensor_tensor(out=x_n, in0=x_t, scalar=rs, in1=wbc[:], op0=ALU.mult,
                                                 op1=ALU.mult), [xb, sbn + "r", "wbc"], [xnb])
            for half in range(2):
                pt = ptr[half]
                pbn = "ptr%d" % half
                for j in range(8):
                    k = half * 8 + j
                    tr(pt[:, j * 128:(j + 1) * 128], pbn, x_n[:, k * 128:(k + 1) * 128], xnb, identb, inc=(j == 7))
                dst = dstT[:, half * 8:(half + 1) * 8, ti * 128:(ti + 1) * 128]
                src = pt[:].rearrange("p (k t) -> p k t", k=8)
                evac(dst, dstname, src, pbn, eng=("act" if half == 0 else "dve"))

        for ti in range(NPT + NMT):
            par = ti % 4
            (x_t, xb), (x_n, xnb) = xt[par], xn[par]
            if ti < NPT:
                sdma(x_t, xpre[ti * 128:(ti + 1) * 128, :], [], [xb])
                norm_to_T(x_t, xb, x_n, xnb, hTp, "hTp", ti, sq1, sq1b)
            else:
                tj = ti - NPT
                sdma(x_t, xmain[tj * 128:(tj + 1) * 128, :], [], [xb])
                norm_to_T(x_t, xb, x_n, xnb, hT, "hT", tj, sq1, sq1b)

        arena_reset()
        if stop_after == "p1":
            return finish(nc, S, dbg_out)
        Cst = carve("Cst", [2, 2, 257], F32)[0]
        Sst = carve("Sst", [2, 256], F32)[0]
        wtok = carve("wtok", [64], F32)[0]
        ftok = carve("ftok", [64], F32)[0]
        decbc = carve("decbc", [64], F32)[0]
        wtoks = carve("wtoks", [4], F32)[0]
        ftoks = carve("ftoks", [4], F32)[0]
        sdecbc = carve("sdecbc", [64], F32)[0]

        galT, galb = carve("galT", [PRE + TM], F32)
        BT, BTb = carve("BT", [PRE + TM], F32)
        gst, gstb = BT[:, 0:512], BTb
        QT, QTb = carve("QT", [2, TM])
        KT, KTb = carve("KT", [2, TM])
        Vt, Vtb = carve("Vt", [NMT, 257])
        OG, OGb = carve("OG", [NMT, 256])
        KTp, KTpb = carve("KTp", [2, PRE])
        Vp, Vpb = carve("Vp", [NPT, 257])
        Cbf, Cbfb = carve("Cbf", [2, 257])
        QTg_, QTgb = carve("QTg", [TM])
        KTg_, KTgb = carve("KTg", [TM])
        Vtg, Vtgb = carve("Vtg", [NMT, 256])
        RG, RGb = carve("RG", [NMT, 256])
        Ktok2 = [carve("Ktok%d" % i, [256]) for i in range(2)]
        Vw2 = [carve("Vw%d" % i, [257]) for i in range(2)]
        PT2 = [carve("PT%d" % i, [128]) for i in range(2)]
        ytok2 = [carve("ytok%d" % i, [256]) for i in range(2)]
        eqt2 = [carve("eqt%d" % i, [128], F32) for i in range(2)]
        QtT2 = [carve("QtT%d" % i, [128]) for i in range(2)]
        KtT2 = [carve("KtT%d" % i, [128]) for i in range(2)]
        KhT2 = [carve("KhT%d" % i, [128]) for i in range(2)]
        Khtok2 = [carve("Khtok%d" % i, [128]) for i in range(2)]
        Sbf, Sbfb = carve("Sbf", [256])
        nT, nTb = carve("nT", [2, 64], F32)
        nTo, nTob = carve("nTo", [2, 64], F32)
        nrow, nrowb = carve("nrow", [256], F32)
        junk, junkb = nrow.bitcast(BF16)[:, 0:256], nrowb
        SG = 2
        aups, aupb = carve("aups", [512], F32)
        negab = carve("negab", [4], F32)[0]
        dSs, dSsb = carve("dSs", [16], F32)
        gmark = apos[0]
        gat = carve("gat", [8, 128], F32)[0][0:64]
        gas = carve("gas", [8, 8], F32)[0][0:64]
        grow = carve("grow", [8, 64], F32)[0][0:1]
        wg, wgn = load_w(w_in, WOFF[C_MI], KD, 8)
        wl, wln = load_w(w_in, WOFF[C_GAL], KD, 16)
        MGG = [(0, 512), (512, 512), (1024, 128), (1152, 128)]
        for (src, sname, groups, base) in ((hTp, "hTp", PG, 0), (hT, "hT", MGG, PRE)):
            for (t0, N) in groups:
                pv, pn_ = proj_fm(wg, wgn, 0, 8, src, sname, t0, N)
                evac(gst[0:8, 0:N], gstb, pv, pn_, eng="act")
                if base + t0 < 2048:
                    sdma(gscr[:, base + t0:base + t0 + N], gst[0:8, 0:N], [gstb], ["gscr"])
                else:
                    sdma(gscs[:, :], gst[0:8, 0:N], [gstb], ["gscs"])
                pv, pn_ = proj_fm(wl, wln, 0, 16, src, sname, t0, N)
                evac(galT[0:16, base + t0:base + t0 + N], galb, pv, pn_, eng="dve")
        GI, GF, GB_, GA, GW, GFL, GT1, GT2 = range(8)

        def gt(i):
            return gat[:, i, :]

        def gs_(i):
            return gas[:, i, :]
        NTOK = PRE + TM - 128
        sdma(gt(GI), gscr[0:4, :].rearrange("h (c l) -> (h c) l", l=128), ["gscr"], ["g_i"])
        sdma(gt(GF), gscr[4:8, :].rearrange("h (c l) -> (h c) l", l=128), ["gscr"], ["g_f"])
        sdma(gs_(GI), gscs[0:4, :].rearrange("h (j l) -> (h j) l", l=8), ["gscs"], ["s_i"])
        sdma(gs_(GF), gscs[4:8, :].rearrange("h (j l) -> (h j) l", l=8), ["gscs"], ["s_f"])
        negfb = stat[0:64, 48:49]
        dve(lambda e: e.tensor_scalar(out=negfb, in0=fb64, scalar1=-1.0, scalar2=None, op0=ALU.mult),
            ["spm"], ["negfb"])

        def gate_math(T, pfx, L):
            i_, f_, b_, a_ = T(GI), T(GF), T(GB_), T(GA)
            act(lambda e: e.activation(out=f_, in_=f_, func=AF.Exp, bias=negfb, scale=-1.0),
                [pfx + "f", "negfb"], [pfx + "f"])
            act(lambda e: e.activation(out=f_, in_=f_, func=AF.Ln, bias=1.0), [pfx + "f"], [pfx + "f"])
            dve(lambda e: e.tensor_scalar(out=f_, in0=f_, scalar1=-1.0, scalar2=None, op0=ALU.mult),
                [pfx + "f"], [pfx + "f"])
            dve(lambda e: e.tensor_tensor_scan(out=b_, data0=onesf[0:64, 0:L], data1=f_, initial=0.0,
                                               op0=ALU.mult, op1=ALU.add), [pfx + "f", "cf"], [pfx + "b"])
            dve(lambda e: e.scalar_tensor_tensor(out=a_, in0=i_, scalar=ib64, in1=b_, op0=ALU.add,
                                                 op1=ALU.subtract), [pfx + "i", pfx + "b", "spm"], [pfx + "a"])
        gate_math(gt, "g_", 128)
        gate_math(gs_, "s_", 8)
        amax = stat[0:64, 49:50]
        bend = gat[:, GB_, 127:128]
        amaxb = stat[0:64, 50:51]
        dve(lambda e: e.reduce_max(out=amax, in_=gt(GA), axis=AX.X), ["g_a"], ["amax"])
        dve(lambda e: e.tensor_tensor(out=amaxb, in0=amax, in1=bend, op=ALU.add), ["amax", "g_b"], ["amaxb"])
        tr(pmisc[0:1, 0:64], "pmisc", bend, "g_b", identf[0:64, 0:64], inc=False)
        tr(pmisc[0:1, 64:128], "pmisc", amaxb, "amaxb", identf[0:64, 0:64])
        R_BE, R_AB, R_MN, R_MP, R_G, R_DEC = range(6)

        def gr(i):
            return grow[0:1, i, :]
        evac(grow[0:1, 0:2, :], "grow", pmisc[0:1, 0:128].rearrange("p (a b) -> p a b", a=2), "pmisc", eng="dve")
        for h in range(4):
            for seg in range(2):
                lo = h * 16 + seg * 8
                if seg == 0:
                    init = 0.0
                    rd = ["grow"]
                else:
                    mf = stat[0:1, 51 + h:52 + h]
                    dve(lambda e, h=h, mf=mf: e.tensor_tensor(out=mf, in0=grow[0:1, R_MN, h * 16 + 7:h * 16 + 8],
                                                             in1=flagt[0:1, 0:1], op=ALU.mult),
                        ["grow", "flagt"], ["mf%d" % h])
                    init = mf
                    rd = ["grow", "mf%d" % h]
                dve(lambda e, lo=lo, init=init: e.tensor_tensor_scan(
                    out=grow[0:1, R_MN, lo:lo + 8], data0=grow[0:1, R_BE, lo:lo + 8],
                    data1=grow[0:1, R_AB, lo:lo + 8], initial=init, op0=ALU.add, op1=ALU.max), rd, ["grow"])
                if seg == 0:
                    dve(lambda e, lo=lo: e.memset(grow[0:1, R_MP, lo:lo + 1], 0.0), [], ["grow"])
                else:
                    dve(lambda e, lo=lo, mf=mf: e.tensor_copy(out=grow[0:1, R_MP, lo:lo + 1], in_=mf),
                        ["mf%d" % h], ["grow"])
                dve(lambda e, lo=lo: e.tensor_copy(out=grow[0:1, R_MP, lo + 1:lo + 8],
                                                   in_=grow[0:1, R_MN, lo:lo + 7]), ["grow"], ["grow"])
        dve(lambda e: e.tensor_tensor(out=gr(R_G), in0=gr(R_MN), in1=gr(R_BE), op=ALU.subtract), ["grow"], ["grow"])
        dve(lambda e: e.tensor_tensor(out=gr(R_DEC), in0=gr(R_MP), in1=gr(R_G), op=ALU.subtract), ["grow"], ["grow"])
        act(lambda e: e.activation(out=gr(R_DEC), in_=gr(R_DEC), func=AF.Exp), ["grow"], ["grow"])
        dve(lambda e: e.tensor_scalar(out=gr(R_G), in0=gr(R_G), scalar1=-1.0, scalar2=None, op0=ALU.mult),
            ["grow"], ["grow"])
        mm(pmisc[:, 128:192], "pmisc", [(onesf[0:1, 0:128], gr(R_DEC))], ["grow", "cf"])
        evac(decbc[:], "decbc", pmisc[:, 128:192], "pmisc", eng="dve")
        mm(pmisc[0:64, 192:193], "pmisc", [(gr(R_G), onesf[0:1, 0:1])], ["grow", "cf"])
        negG = stat[0:64, 56:57]
        evac(negG, "negG", pmisc[0:64, 192:193], "pmisc", eng="dve")
        sdma(o_pm[:, :], gr(R_MN), ["grow"], [])
        act(lambda e: e.activation(out=gt(GW), in_=gt(GA), func=AF.Exp, bias=negG), ["g_a", "negG"], ["g_w"])
        act(lambda e: e.activation(out=gt(GFL), in_=gt(GB_), func=AF.Exp, bias=negG, scale=-1.0),
            ["g_b", "negG"], ["g_fl"])
        tr(pmisc[:, 256:320], "pmisc", gt(GW), "g_w", identf[0:64, 0:64], inc=False)
        tr(pmisc[:, 320:384], "pmisc", gt(GFL), "g_fl", identf[0:64, 0:64])
        evac(wtok[:], "wtok", pmisc[:, 256:320], "pmisc", eng="dve")
        evac(ftok[:], "ftok", pmisc[:, 320:384], "pmisc", eng="dve")
        smc = stat[0:64, 57:58]
        sdma(smc, smcol[:, :], [], ["smc"])
        samax = stat[0:64, 58:59]
        sG = stat[0:64, 59:60]
        snegG = stat[0:64, 60:61]
        sdec = stat[0:64, 61:62]
        smn = stat[0:64, 62:63]
        dve(lambda e: e.reduce_max(out=samax, in_=gs_(GA), axis=AX.X), ["s_a"], ["samax"])
        dve(lambda e: e.tensor_tensor(out=sG, in0=samax, in1=smc, op=ALU.max), ["samax", "smc"], ["sG"])
        dve(lambda e: e.tensor_tensor(out=smn, in0=sG, in1=gas[:, GB_, 7:8], op=ALU.add), ["sG", "s_b"], ["smn"])
        sdma(o_sm[:, :], smn, ["smn"], [])
        dve(lambda e: e.tensor_tensor(out=sdec, in0=smc, in1=sG, op=ALU.subtract), ["smc", "sG"], ["sdec"])
        act(lambda e: e.activation(out=sdec, in_=sdec, func=AF.Exp), ["sdec"], ["sdec"])
        dve(lambda e: e.tensor_scalar(out=snegG, in0=sG, scalar1=-1.0, scalar2=None, op0=ALU.mult), ["sG"], ["snegG"])
        act(lambda e: e.activation(out=gs_(GW), in_=gs_(GA), func=AF.Exp, bias=snegG), ["s_a", "snegG"], ["s_w"])
        act(lambda e: e.activation(out=gs_(GFL), in_=gs_(GB_), func=AF.Exp, bias=snegG, scale=-1.0),
            ["s_b", "snegG"], ["s_fl"])
        dg = gat[:, GT1, 0:64]
        dve(lambda e: e.tensor_scalar(out=dg, in0=identf[0:64, 0:64], scalar1=sdec, scalar2=None, op0=ALU.mult),
            ["sdec", "cf"], ["dg"])
        mm(pmisc[:, 384:448], "pmisc", [(onesf[0:64, 0:128], dg)], ["dg", "cf"])
        evac(sdecbc[:], "sdecbc", pmisc[:, 384:448], "pmisc", eng="dve")
        sdma(gscs[0:4, :].rearrange("h (j l) -> (h j) l", l=8), gs_(GW), ["s_w"], ["gscs"])
        sdma(gscs[4:8, :].rearrange("h (j l) -> (h j) l", l=8), gs_(GFL), ["s_fl"], ["gscs"])
        wfr = gat[0:8, GT2, :]
        sdma(wfr, gscs[0:8, :], ["gscs"], ["wfr"])
        tr(pmisc[:, 448:456], "pmisc", wfr, "wfr", identf[0:8, 0:8])
        evac(wtoks[:], "wtoks", pmisc[:, 448:452], "pmisc", eng="dve")
        evac(ftoks[:], "ftoks", pmisc[:, 452:456], "pmisc", eng="dve")

        if stop_after == "gates":
            if debug.get("gates"):
                for nm, t_, shp in (("wtok", wtok, [128, 64]), ("ftok", ftok, [128, 64]), ("decbc", decbc, [128, 64]),
                                    ("wtoks", wtoks, [128, 4]), ("sdecbc", sdecbc, [128, 64])):
                    dbg_out[nm] = dout("dbg_" + nm, shp)
                    sdma(dbg_out[nm][:, :], t_[:], [nm], [])
            return finish(nc, S, dbg_out)


        dve(lambda e: e.memset(Vt[:, :, 256:257], 1.0), [], [Vtb])
        dve(lambda e: e.memset(Vp[:, :, 256:257], 1.0), [], [Vpb])
        sdma(nrow[0:64, :], sn[:, :], [], [nrowb])
        for dh in range(2):
            tr(pmisc[:, dh * 64:(dh + 1) * 64], "pmisc", nrow[0:64, dh * 128:(dh + 1) * 128], nrowb,
               identf[0:64, 0:64], inc=(dh == 1))
        evac(nT, nTb, pmisc[:, 0:128].rearrange("p (a b) -> p a b", a=2), "pmisc", eng="dve")

        def head_epilogue(num_ap, numb, scale_pre, gate_ap, gateb, tile_i, dstcol0, hnb, ytok, ytokb):
            ss, rs, sc, sbn = statslot()
            if scale_pre is None:
                act(lambda e: e.activation(out=junk, in_=num_ap, func=AF.Square, accum_out=ss), [numb],
                    [junkb, sbn])
            else:
                act(lambda e: e.activation(out=junk, in_=num_ap, func=AF.Square, scale=scale_pre, accum_out=ss),
                    [numb, hnb], [junkb, sbn])
            rstd_from_ss(ss, rs, sbn, 256)
            if scale_pre is not None:
                dve(lambda e: e.tensor_tensor(out=rs, in0=rs, in1=scale_pre, op=ALU.mult), [sbn + "r", hnb],
                    [sbn + "r"])
            dve(lambda e: e.scalar_tensor_tensor(out=ytok, in0=num_ap, scalar=rs, in1=gate_ap, op0=ALU.mult,
                                                 op1=ALU.mult), [numb, sbn + "r", gateb], [ytokb])
            sdma(yscr[tile_i * 128:(tile_i + 1) * 128, dstcol0:dstcol0 + 256], ytok, [ytokb], ["yscr"])

        def mlstm_proj(h):
            wq, wqn = load_w(w_in, WOFF[C_MQ + h * 256], KD, 256)
            wk, wkn = load_w(w_in, WOFF[C_MK + h * 256], KD, 256)
            wv, wvn = load_w(w_in, WOFF[C_MV + h * 256], KD, 256)
            for dh in range(2):
                for (t0, N) in MG:
                    pv, pn_ = proj_fm(wq, wqn, dh * 128, 128, hT, "hT", t0, N)
                    evac(QT[:, dh, t0:t0 + N], QTb, pv, pn_)
            for dh in range(2):
                for (t0, N) in MG:
                    pv, pn_ = proj_fm(wk, wkn, dh * 128, 128, hT, "hT", t0, N)
                    evac(KT[:, dh, t0:t0 + N], KTb, pv, pn_, scale=0.0625)
                for (t0, N) in PG:
                    pv, pn_ = proj_fm(wk, wkn, dh * 128, 128, hTp, "hTp", t0, N)
                    evac(KTp[:, dh, t0:t0 + N], KTpb, pv, pn_, scale=0.0625)
            wo, won = load_w(w_in, WOFF[C_MO + h * 256], KD, 256)
            for ti in range(NPT):
                pv, pn_ = proj_tm(wv, wvn, 0, 256, hTp, "hTp", ti)
                evac(Vp[:, ti, 0:256], Vpb, pv, pn_)
            for ti in range(NMT):
                pv, pn_ = proj_tm(wv, wvn, 0, 256, hT, "hT", ti)
                evac(Vt[:, ti, 0:256], Vtb, pv, pn_)
            for ti in range(NMT):
                pv, pn_ = proj_tm(wo, won, 0, 256, hT, "hT", ti)
                evac(OG[:, ti, :], OGb, pv, pn_, func=AF.Sigmoid)
        csi = [0]

        def mlstm_chunks(h):
            Ch = Cst[:, h % 2, :, :]
            Chb = "Cst%d" % (h % 2)
            dve(lambda e: e.memset(Ch, 0.0), [], [Chb])
            def mchunk(c):
                full = c >= NPT
                samp = c == NCH
                if c < NPT:
                    ktsrc, ktb, vsrc, vb_, ti = KTp, KTpb, Vp, Vpb, c
                else:
                    ktsrc, ktb, vsrc, vb_, ti = KT, KTb, Vt, Vtb, c - NPT
                tk = slice(ti * 128, (ti + 1) * 128)
                (Ktok, Ktokb), (Vw, Vwb), (PT, PTb), (ytok, ytokb) = Ktok2[c % 2], Vw2[c % 2], PT2[c % 2], ytok2[c % 2]
                if samp:
                    wcol, fcol = wtoks[:, h:h + 1], ftoks[:, h:h + 1]
                    wcb, fcb = "wtoks", "ftoks"
                else:
                    wcol, fcol = wtok[:, h * 16 + c:h * 16 + c + 1], ftok[:, h * 16 + c:h * 16 + c + 1]
                    wcb, fcb = "wtok", "ftok"
                for dh in range(2):
                    tr(ptr[0][:, dh * 128:(dh + 1) * 128], "ptr0", ktsrc[:, dh, tk], ktb, identb, inc=(dh == 1))
                evac(Ktok, Ktokb, ptr[0][:, 0:256], "ptr0")
                dve(lambda e, vsrc=vsrc, ti=ti, wcol=wcol: e.tensor_scalar(out=Vw, in0=vsrc[:, ti, :], scalar1=wcol,
                                                                            scalar2=None, op0=ALU.mult),
                     [vb_, wcb], [Vwb])
                if not samp:
                    dcol = decbc[:, h * 16 + c:h * 16 + c + 1]
                    dve(lambda e, dcol=dcol: e.tensor_scalar(out=Ch, in0=Ch, scalar1=dcol, scalar2=None,
                                                             op0=ALU.mult), [Chb, "decbc"], [Chb])
                if full:
                    mm(pst[:, 0:128], "pst", [(ktsrc[:, dh, tk], QT[:, dh, tk]) for dh in range(2)], [ktb, QTb])
                    msk = masks if samp else maskc
                    dve(lambda e, msk=msk: e.tensor_tensor(out=PT, in0=pst[:, 0:128], in1=msk, op=ALU.mult),
                        ["pst", "cb"], [PTb])
                    if not samp:
                        act(lambda e: e.copy(out=Cbf, in_=Ch), [Chb], [Cbfb])
                        mm(pnum[:, 0:257], "pnum",
                           [(PT, Vw)] + [(QT[:, dh, tk], Cbf[:, dh, :]) for dh in range(2)],
                           [PTb, Vwb, QTb, Cbfb])
                    else:
                        mm(pnum[:, 0:257], "pnum", [(PT, Vw)], [PTb, Vwb], first=True, last=False)
                if not samp:
                    for dh in range(2):
                        mm(pstate[:, 0:257], "pstate", [(Ktok[:, dh * 128:(dh + 1) * 128], Vw)], [Ktokb, Vwb])
                        dve(lambda e, dh=dh: e.tensor_tensor(out=Ch[:, dh, :], in0=Ch[:, dh, :],
                                                             in1=pstate[:, 0:257], op=ALU.add),
                            ["pstate", Chb], [Chb])
                else:
                    def sgrp(g, Cs, Csb, Csbf, Csbfb, QX, QXb, VwX, VwXb):
                        js = slice(g * SG, (g + 1) * SG)
                        for dh in range(2):
                            sdma(Cs[:, :, dh, 0:256],
                                 sC[js, h, dh * 128:(dh + 1) * 128, :].rearrange("j p e -> p j e"), [], [Csb])
                        ncols = nT[:, :, g * SG * 4 + h:(g + 1) * SG * 4:4].rearrange("p dh j -> p j dh")
                        dve(lambda e, ncols=ncols: e.tensor_copy(out=Cs[:, :, :, 256], in_=ncols), [nTb], [Csb])
                        dcols = sdecbc[:, h * 16 + g * SG:h * 16 + (g + 1) * SG]
                        Csv = Cs.rearrange("p j dh e -> p j (dh e)")
                        dve(lambda e, dcols=dcols, Csv=Csv: e.tensor_tensor(
                            out=Csv, in0=Csv, in1=dcols.unsqueeze(2).broadcast_to([128, SG, 514]), op=ALU.mult),
                            [Csb, "sdecbc"], [Csb])
                        act(lambda e: e.copy(out=Csbf, in_=Cs), [Csb], [Csbfb])
                        for dh in range(2):
                            dve(lambda e, dh=dh, js=js, tk=tk: e.tensor_tensor(
                                out=QX[:, dh, :, :], in0=QT[:, dh, tk].unsqueeze(1).broadcast_to([128, SG, 128]),
                                in1=qxmask[:, js, :], op=ALU.mult), [QTb, "cb"], [QXb])
                        pairs = [(QX[:, dh, j, :], Csbf[:, j, dh, :]) for j in range(SG) for dh in range(2)]
                        mm(pnum[:, 0:257], "pnum", pairs, [QXb, Csbfb], first=False, last=(g == 16 // SG - 1))
                        dve(lambda e, js=js: e.tensor_tensor(
                            out=VwX, in0=Vw.unsqueeze(1).broadcast_to([128, SG, 257]),
                            in1=vxmask[:, js].unsqueeze(2).broadcast_to([128, SG, 257]), op=ALU.mult),
                            [Vwb, "cb"], [VwXb])
                        for j in range(SG):
                            for dh in range(2):
                                mm(pstate[:, 0:257], "pstate", [(Ktok[:, dh * 128:(dh + 1) * 128], VwX[:, j, :])],
                                   [Ktokb, VwXb])
                                dve(lambda e, j=j, dh=dh: e.tensor_tensor(out=Cs[:, j, dh, :], in0=Cs[:, j, dh, :],
                                                                          in1=pstate[:, 0:257], op=ALU.add),
                                    ["pstate", Csb], [Csb])
                        for dh in range(2):
                            sdma(o_sC[js, h, dh * 128:(dh + 1) * 128, :].rearrange("j p e -> p j e"),
                                 Cs[:, :, dh, 0:256], [Csb], [])
                        ndst = nTo[:, :, g * SG * 4 + h:(g + 1) * SG * 4:4].rearrange("p dh j -> p j dh")
                        dve(lambda e, ndst=ndst: e.tensor_copy(out=ndst, in_=Cs[:, :, :, 256]), [Csb], [nTob])
                    for g in range(16 // SG):
                        k_ = csi[0]
                        csi[0] += 1
                        sgrp(g, *Cs3[k_ % 3], *Csbf2[k_ % 2], *QX2[k_ % 2], *VwX2[k_ % 2])
                if full:
                    ss, rs, sc, sbn = statslot()
                    dve(lambda e, sc=sc: e.tensor_copy(out=sc, in_=pnum[:, 256:257]), ["pnum"], [sbn + "s"])
                    dve(lambda e, sc=sc: e.scalar_tensor_tensor(out=sc, in0=sc, scalar=-1.0, in1=sc, op0=ALU.mult,
                                                                op1=ALU.max), [sbn + "s"], [sbn + "s"])
                    dve(lambda e, fcol=fcol, sc=sc: e.tensor_tensor(out=sc, in0=sc, in1=fcol, op=ALU.max),
                        [sbn + "s", fcb], [sbn + "s"])
                    dve(lambda e, sc=sc: e.reciprocal(out=sc, in_=sc), [sbn + "s"], [sbn + "s"])
                    head_epilogue(pnum[:, 0:256], "pnum", sc, OG[:, ti, :], OGb, ti, h * 256, sbn + "s", ytok, ytokb)
                if c == NCH - 1:
                    sdma(o_pC[h].rearrange("(dh p) e -> p dh e", p=128), Ch[:, :, 0:256], [Chb], [])
            for c in range(NCH + 1):
                mchunk(c)
            tr(pmisc[0:2, 0:128], "pmisc", Ch[:, :, 256], Chb, identf)
            evac(nrow[0:2, 0:128], nrowb, pmisc[0:2, 0:128], "pmisc", eng="dve")
            sdma(o_pn[h * 2:h * 2 + 2, :], nrow[0:2, 0:128], [nrowb], [])

        aupt = stat
        sdma(aups[0:16, :], aup[:, :], [], [aupb])
        dve(lambda e: e.tensor_scalar(out=negab[:, 0:4], in0=spm[:, P_AB:P_AB + 4], scalar1=-1.0, scalar2=None,
                                      op0=ALU.mult), ["spm"], ["negab"])
        BG = [(0, 512), (512, 512), (1024, 512), (1536, 512), (2048, 128)]

        def gla_head(h):
            wq, wqn = load_w(w_in, WOFF[C_GQ + h * 128], KD, 128)
            wk, wkn = load_w(w_in, WOFF[C_GK + h * 128], KD, 128)
            wv, wvn = load_w(w_in, WOFF[C_GV + h * 256], KD, 256)
            QTg, KTg, KTpg = QTg_, KTg_, KTp[:, 0, :]
            QTb, KTb, Vt, Vtb, OG, OGb = QTgb, KTgb, Vtg, Vtgb, RG, RGb
            for (t0, N) in MG:
                pv, pn_ = proj_fm(wq, wqn, 0, 128, hT, "hT", t0, N)
                evac(QTg[:, t0:t0 + N], QTb, pv, pn_, scale=128.0 ** -0.5)
            for (t0, N) in MG:
                pv, pn_ = proj_fm(wk, wkn, 0, 128, hT, "hT", t0, N)
                evac(KTg[:, t0:t0 + N], KTb, pv, pn_)
            for (t0, N) in PG:
                pv, pn_ = proj_fm(wk, wkn, 0, 128, hTp, "hTp", t0, N)
                evac(KTpg[:, t0:t0 + N], KTpb, pv, pn_)
            wr, wrn = load_w(w_in, WOFF[C_GR + h * 256], KD, 256)
            for ti in range(NPT):
                pv, pn_ = proj_tm(wv, wvn, 0, 256, hTp, "hTp", ti)
                evac(Vp[:, ti, 0:256], Vpb, pv, pn_)
            for ti in range(NMT):
                pv, pn_ = proj_tm(wv, wvn, 0, 256, hT, "hT", ti)
                evac(Vt[:, ti, 0:256], Vtb, pv, pn_)
            for ti in range(NMT):
                pv, pn_ = proj_tm(wr, wrn, 0, 256, hT, "hT", ti)
                evac(OG[:, ti, :], OGb, pv, pn_, func=AF.Silu)
            nab = negab[:, h:h + 1]
            for (t0, N) in BG:
                mm(pmisc[:, 0:N], "pmisc", [(aups[0:16, h * 128:(h + 1) * 128], galT[0:16, t0:t0 + N])],
                   [aupb, galb])
                act(lambda e, t0=t0, N=N: e.activation(out=BT[:, t0:t0 + N], in_=pmisc[:, 0:N], func=AF.Exp,
                                                       bias=nab, scale=-1.0), ["pmisc", "negab"], [BTb])
            act(lambda e: e.activation(out=BT, in_=BT, func=AF.Ln, bias=1.0), [BTb], [BTb])
            dve(lambda e: e.tensor_scalar(out=BT, in0=BT, scalar1=-1.0 / 16.0, scalar2=None, op0=ALU.mult),
                [BTb], [BTb])
            dve(lambda e: e.tensor_tensor_scan(out=BT[:, 0:2048], data0=onesf[:, 0:1].broadcast_to([128, 2048]),
                                               data1=BT[:, 0:2048], initial=0.0, op0=ALU.mult, op1=ALU.add),
                [BTb, "cf"], [BTb])
            dve(lambda e: e.tensor_tensor_scan(out=BT[:, 2048:2176], data0=resetm, data1=BT[:, 2048:2176],
                                               initial=0.0, op0=ALU.mult, op1=ALU.add), [BTb, "cf"], [BTb])
            Sh = Sst[:, h % 2, :]
            Shb = "Sst%d" % (h % 2)
            dve(lambda e: e.memset(Sh, 0.0), [], [Shb])

            def gchunk(c):
                full = c >= NPT
                samp = c == NCH
                if c < NPT:
                    ktsrc, ktb, vsrc, vb_, ti = KTpg, KTpb, Vp, Vpb, c
                else:
                    ktsrc, ktb, vsrc, vb_, ti = KTg, KTb, Vt, Vtb, c - NPT
                tk = slice(ti * 128, (ti + 1) * 128)
                tb = slice(2048, 2176) if samp else slice(c * 128, (c + 1) * 128)
                (eqt, eqtb), (QtT, QtTb), (KtT, KtTb), (KhT, KhTb) = eqt2[c % 2], QtT2[c % 2], KtT2[c % 2], KhT2[c % 2]
                (Ktok, Ktokb), (PT, PTb), (ytok, ytokb) = Khtok2[c % 2], PT2[c % 2], ytok2[c % 2]
                ss, rs, sc, sbn = statslot()
                if samp:
                    bend3 = BT[:, 2048 + 7:2176:8].unsqueeze(2).broadcast_to([128, 16, 8])
                    dve(lambda e: e.tensor_tensor(out=eqt.rearrange("p (j l) -> p j l", l=8), in0=bend3,
                                                  in1=BT[:, tb].rearrange("p (j l) -> p j l", l=8),
                                                  op=ALU.subtract), [BTb], [eqtb])
                    act(lambda e: e.activation(out=eqt, in_=eqt, func=AF.Exp), [eqtb], [eqtb])
                    act(lambda e: e.activation(out=dSs, in_=BT[:, 2048 + 7:2176:8], func=AF.Exp), [BTb], [dSsb])
                else:
                    bendc = BT[:, c * 128 + 127:c * 128 + 128]
                    bstc = zerocol if c == 0 else BT[:, c * 128 - 1:c * 128]
                    act(lambda e: e.activation(out=eqt, in_=BT[:, tb], func=AF.Exp, bias=bendc, scale=-1.0),
                        [BTb], [eqtb])
                    dve(lambda e: e.tensor_tensor(out=ss, in0=bendc, in1=bstc, op=ALU.subtract), [BTb, "cf"],
                        [sbn])
                    act(lambda e: e.activation(out=ss, in_=ss, func=AF.Exp), [sbn], [sbn])
                    dve(lambda e: e.tensor_scalar(out=rs, in0=bstc, scalar1=-1.0, scalar2=None, op0=ALU.mult),
                        [BTb, "cf"], [sbn + "r"])
                dve(lambda e: e.tensor_tensor(out=KhT, in0=ktsrc[:, tk], in1=eqt, op=ALU.mult), [ktb, eqtb], [KhTb])
                tr(ptr[0][:, 0:128], "ptr0", KhT, KhTb, identb)
                evac(Ktok[:, 0:128], Ktokb, ptr[0][:, 0:128], "ptr0")
                if full:
                    if samp:
                        act(lambda e: e.activation(out=eqt, in_=BT[:, tb], func=AF.Exp), [BTb, KhTb], [eqtb])
                    else:
                        act(lambda e: e.activation(out=eqt, in_=BT[:, tb], func=AF.Exp, bias=rs), [BTb, sbn + "r", KhTb],
                            [eqtb])
                    dve(lambda e: e.tensor_tensor(out=QtT, in0=QTg[:, tk], in1=eqt, op=ALU.mult), [QTb, eqtb], [QtTb])
                    if samp:
                        act(lambda e: e.activation(out=eqt, in_=BT[:, tb], func=AF.Exp, scale=-1.0), [BTb, QtTb],
                            [eqtb])
                    else:
                        act(lambda e: e.activation(out=eqt, in_=BT[:, tb], func=AF.Exp, bias=bstc, scale=-1.0),
                            [BTb, QtTb, "cf"], [eqtb])
                    dve(lambda e: e.tensor_tensor(out=KtT, in0=ktsrc[:, tk], in1=eqt, op=ALU.mult), [ktb, eqtb],
                        [KtTb])
                    mm(pst[:, 0:128], "pst", [(KtT, QtT)], [KtTb, QtTb])
                    msk = masks if samp else maskc
                    dve(lambda e: e.tensor_tensor(out=PT, in0=pst[:, 0:128], in1=msk, op=ALU.mult), ["pst", "cb"],
                        [PTb])
                    if not samp:
                        mm(pnum[:, 0:256], "pnum", [(PT, vsrc[:, ti, 0:256]), (QtT, Sbf)], [PTb, vb_, QtTb, Sbfb])
                    else:
                        mm(pnum[:, 0:256], "pnum", [(PT, vsrc[:, ti, 0:256])], [PTb, vb_], first=True, last=False)
                if not samp:
                    mm(pstate[:, 0:256], "pstate", [(Ktok[:, 0:128], vsrc[:, ti, 0:256])], [Ktokb, vb_])
                    dve(lambda e: e.scalar_tensor_tensor(out=Sh, in0=Sh, scalar=ss, in1=pstate[:, 0:256],
                                                         op0=ALU.mult, op1=ALU.add), [Shb, sbn, "pstate"], [Shb])
                    act(lambda e: e.copy(out=Sbf, in_=Sh), [Shb], [Sbfb])
                else:
                    def sgrp(g, Cs, Csb, Csbf, Csbfb, QX, QXb, VwX, VwXb):
                        js = slice(g * SG, (g + 1) * SG)
                        Ss = Cs[:, :, 0, 0:256]
                        Ssb = Csbf[:, :, 0, 0:256]
                        sdma(Ss, sS[js, h].rearrange("j p e -> p j e"), [], [Csb])
                        act(lambda e, Ss=Ss, Ssb=Ssb: e.copy(out=Ssb, in_=Ss), [Csb], [Csbfb])
                        dve(lambda e, js=js: e.tensor_tensor(
                            out=QX[:, 0, :, :], in0=QtT.unsqueeze(1).broadcast_to([128, SG, 128]),
                            in1=qxmask[:, js, :], op=ALU.mult), [QtTb, "cb"], [QXb])
                        mm(pnum[:, 0:256], "pnum", [(QX[:, 0, j, :], Ssb[:, j, :]) for j in range(SG)],
                           [QXb, Csbfb], first=False, last=(g == 16 // SG - 1))
                        dve(lambda e, js=js: e.tensor_tensor(
                            out=VwX[:, :, 0:256], in0=vsrc[:, ti, 0:256].unsqueeze(1).broadcast_to([128, SG, 256]),
                            in1=vxmask[:, js].unsqueeze(2).broadcast_to([128, SG, 256]), op=ALU.mult),
                            [vb_, "cb"], [VwXb])
                        for j in range(SG):
                            mm(pstate[:, 0:256], "pstate", [(Ktok[:, 0:128], VwX[:, j, 0:256])], [Ktokb, VwXb])
                            dcol = dSs[:, g * SG + j:g * SG + j + 1]
                            dve(lambda e, j=j, dcol=dcol, Ss=Ss: e.scalar_tensor_tensor(
                                out=Ss[:, j, :], in0=Ss[:, j, :], scalar=dcol, in1=pstate[:, 0:256], op0=ALU.mult,
                                op1=ALU.add), [Csb, dSsb, "pstate"], [Csb])
                        sdma(o_sS[js, h].rearrange("j p e -> p j e"), Ss, [Csb], [])
                    for g in range(16 // SG):
                        k_ = csi[0]
                        csi[0] += 1
                        sgrp(g, *Cs3[k_ % 3], *Csbf2[k_ % 2], *QX2[k_ % 2], *VwX2[k_ % 2])
                if full:
                    head_epilogue(pnum[:, 0:256], "pnum", None, OG[:, ti, :], OGb, ti, 1024 + h * 256, None, ytok, ytokb)
                if c == NCH - 1:
                    sdma(o_pS[h], Sh, [Shb], [])
            dve(lambda e: e.memset(Sbf, 0.0), [], [Sbfb])
            for c in range(NCH + 1):
                gchunk(c)

        mlstm_proj(0)
        S.barrier()
        print("arena phase 1a used %d / %d (mark %d)" % (apos[0] * 2, ARENA * 2, gmark * 2))
        apos[0] = gmark
        Cs3 = [carve("Cs%d" % i, [SG, 2, 257], F32) for i in range(3)]
        Csbf2 = [carve("Csbf%d" % i, [SG, 2, 257]) for i in range(2)]
        QX2 = [carve("QX%d" % i, [2, SG, 128]) for i in range(2)]
        VwX2 = [carve("VwX%d" % i, [SG, 257]) for i in range(2)]
        for h in range(4):
            if h > 0:
                mlstm_proj(h)
            mlstm_chunks(h)
            gla_head(h)
        tr(pmisc[:, 0:128], "pmisc", nTo.rearrange("p a b -> p (a b)"), nTob, identf)
        evac(nrow[:, 0:128], nrowb, pmisc[:, 0:128], "pmisc", eng="dve")
        sdma(o_sn[:, :], nrow[:, 0:128], [nrowb], [])
        if stop_after == "gla":
            return finish(nc, S, dbg_out)

        arena_reset()
        wide[0] = False
        yTa = hTp[:].rearrange("p k t -> p (k t)")[:, 0:8 * TM].rearrange("p (k t) -> p k t", k=8)
        yTg, yTgb = carve("yTg", [8, TM])
        mT, mTb = carve("mT", [KD, TM])
        ystg, ystgb = carve("ystg", [D])
        sg, sgb = carve("sg", [2, TM])
        tmpm, tmpmb = carve("tmpm", [512])
        xs_ = [carve("xs%d" % i, [256], F32)[0] for i in range(8)]
        for ti in range(NMT):
            sdma(ystg, yscr[ti * 128:(ti + 1) * 128, :], ["yscr"], [ystgb])
            for half in range(2):
                pt = ptr[half]
                pbn = "ptr%d" % half
                for j in range(8):
                    k = half * 8 + j
                    tr(pt[:, j * 128:(j + 1) * 128], pbn, ystg[:, k * 128:(k + 1) * 128], ystgb, identb, inc=(j == 7))
                for j in range(8):
                    k = half * 8 + j
                    dstt, dstb = (yTa, "hTp") if half == 0 else (yTg, yTgb)
                    dst = dstt[:, j, ti * 128:(ti + 1) * 128]
                    hcol = spm[:, P_HNM + k:P_HNM + k + 1]
                    if j % 2 == 0:
                        act(lambda e, dst=dst, j=j, pt=pt, hcol=hcol: e.activation(
                            out=dst, in_=pt[:, j * 128:(j + 1) * 128], func=AF.Copy, scale=hcol),
                            [pbn, "spm"], [dstb])
                    else:
                        dve(lambda e, dst=dst, j=j, pt=pt, hcol=hcol: e.tensor_scalar(
                            out=dst, in0=pt[:, j * 128:(j + 1) * 128], scalar1=hcol, scalar2=None, op0=ALU.mult),
                            [pbn, "spm"], [dstb])

        MGB = [(120, 392), (512, 512), (1024, 256)]
        dve(lambda e: e.memset(mT[:, :, 0:120], 0.0), [], [mTb])

        def branch_group(cg):
            MG = MGB
            c0 = cg * 256
            for (gcol, wbr, ysrc, ysb, first) in ((C_GA, w_a, yTa, "hTp", True), (C_GB, w_b, yTg, yTgb, False)):
                wgt, wgtn = load_w(w_in, WOFF[gcol + c0], KD, 256)
                for cb_ in range(2):
                    for (t0, N) in MG:
                        pv, pn_ = proj_fm(wgt, wgtn, cb_ * 128, 128, hT, "hT", t0, N)
                        evac(sg[:, cb_, t0:t0 + N], sgb, pv, pn_, func=AF.Sigmoid)
                wbt, wbtn = load_w(wbr, cg * 8 * 256, 8, 256)
                for cb_ in range(2):
                    kk = cg * 2 + cb_
                    for (t0, N) in MG:
                        bank, bname = nextbank()
                        mm(bank[:, 0:N], bname,
                           [(wbt[:, k, cb_ * 128:(cb_ + 1) * 128], ysrc[:, k, t0:t0 + N]) for k in range(8)],
                           [wbtn, ysb])
                        if first:
                            dve(lambda e, bank=bank, N=N, t0=t0, cb_=cb_, kk=kk: e.tensor_tensor(
                                out=mT[:, kk, t0:t0 + N], in0=bank[:, 0:N], in1=sg[:, cb_, t0:t0 + N], op=ALU.mult),
                                [bname, sgb], [mTb])
                        else:
                            dve(lambda e, bank=bank, N=N, t0=t0, cb_=cb_: e.tensor_tensor(
                                out=tmpm[:, 0:N], in0=bank[:, 0:N], in1=sg[:, cb_, t0:t0 + N], op=ALU.mult),
                                [bname, sgb], [tmpmb])
                            dve(lambda e, N=N, t0=t0, kk=kk: e.tensor_tensor(
                                out=mT[:, kk, t0:t0 + N], in0=mT[:, kk, t0:t0 + N], in1=tmpm[:, 0:N], op=ALU.add),
                                [tmpmb, mTb], [mTb])
        for cg in range(8):
            branch_group(cg)

        def wout_group(cg):
            c0 = cg * 256
            wot, wotn = load_w(w_o, cg * KD * 256, KD, 256)
            for ti in range(NMT):
                xs = xs_[(cg * NMT + ti) % 8]
                xsb = "xsb%d" % ((cg * NMT + ti) % 8)
                S.dma("pool", lambda e, xs=xs, ti=ti: e.dma_start(out=xs[:], in_=xmain[ti * 128:(ti + 1) * 128, c0:c0 + 256]),
                      writes=[xsb])
                bank, bname = nextbank()
                mm(bank[:, 0:256], bname, [(mT[:, k, ti * 128:(ti + 1) * 128], wot[:, k, :]) for k in range(KD)],
                   [wotn, mTb])
                dve(lambda e, xs=xs, bank=bank: e.tensor_tensor(out=xs[:], in0=xs[:], in1=bank[:, 0:256], op=ALU.add),
                    [xsb, bname], [xsb])
                sdma(x1scr[ti * 128:(ti + 1) * 128, c0:c0 + 256], xs[:], [xsb], ["x1scr"])
        for cg in range(8):
            wout_group(cg)

        arena_reset()
        sdma(wbc[:], nfw.partition_broadcast(128), [], ["wbc"])
        xt2 = [carve("xt2_%d" % i, [D], F32) for i in range(4)]
        xn2 = [carve("xn2_%d" % i, [D], BF16) for i in range(4)]
        sq2, sq2b = carve("sq2", [D], BF16)
        for ti in range(NMT):
            (x_t, xb), (x_n, xnb) = xt2[ti % 4], xn2[ti % 4]
            sdma(x_t, x1scr[ti * 128:(ti + 1) * 128, :], ["x1scr"], [xb])
            norm_to_T(x_t, xb, x_n, xnb, hT, "hT", ti, sq2, sq2b)

        arena_reset()
        HT_ = 640
        actT, actTb = carve("actT", [KF, HT_])
        upad2 = [carve("upad%d" % i, [2 + HT_], F32) for i in range(2)]
        tb2 = [carve("tbuf%d" % i, [HT_], F32) for i in range(2)]
        pb2 = [carve("pbuf%d" % i, [HT_], F32) for i in range(2)]
        gb2 = [carve("gbuf%d" % i, [HT_]) for i in range(2)]
        w4f = [wsl[i // 2][:, (i % 2) * KD * 128:((i % 2) + 1) * KD * 128] for i in range(6)]
        w4 = [v.rearrange("p (k c) -> p k c", k=KD) for v in w4f]
        w4i = [0]

        def load_w4(packed, kf):
            i = w4i[0] % 6
            w4i[0] += 1
            name = "w4_%d" % i
            flat = w4f[i]
            off = kf * KD * 128
            S.dma("pool", lambda e: e.dma_start(out=flat, in_=packed[:, off:off + KD * 128]), writes=[name])
            return w4[i], name
        ucar, ucarb = carve("ucar", [KF, 2], F32)
        ucv, ucvb = carve("ucv", [KF, 34], F32)
        scvT, scvTb = carve("scvT", [KF, 32], F32)
        srow, srowb = carve("srow", [512], F32)
        fst, fstb = carve("fst", [5, 128], F32)
        wdsf = [hTp[:].rearrange("p k t -> p (k t)")[:, i * KF * 128:(i + 1) * KF * 128] for i in range(2)]
        wds = [v.rearrange("p (k c) -> p k c", k=KF) for v in wdsf]
        x2scr = x2scr_
        CW = lambda j, k: spm[:, P_CW + j * KF + k:P_CW + j * KF + k + 1]
        CBc = lambda k: spm[:, P_CB + k:P_CB + k + 1]
        sc32 = sconv.rearrange("j r c -> (j r) c")
        for k4 in range(KF // 4):
            sdma(srow[0:32, :], sc32[:, k4 * 512:(k4 + 1) * 512], [], [srowb])
            for q in range(4):
                tr(pmisc[:, q * 32:(q + 1) * 32], "pmisc", srow[0:32, q * 128:(q + 1) * 128], srowb,
                   identf[0:32, 0:32], inc=(q == 3))
            evac(scvT[:, k4 * 4:(k4 + 1) * 4, :], scvTb, pmisc[:, 0:128].rearrange("p (a b) -> p a b", a=4),
                 "pmisc", eng="dve")
        dve(lambda e: e.memset(ucar, 0.0), [], [ucarb])
        for (up_, upb_) in upad2:
            dve(lambda e, up_=up_: e.memset(up_[:, 0:122], 0.0), [], [upb_])
        wdi = [0]

        FLO = [120, 0]
        FGR = [[(120, 200), (320, 320)], [(0, 320), (320, 320)]]

        def ffn_block(half, kf):
            g0 = half * HT_
            npr = HT_ if half == 0 else 512
            wu, wun = load_w4(w_up, kf)
            wg_, wgn_ = load_w4(w_gt, kf)
            (upad, upadb), (tb_, tbb), (pb_, pbb), (gb_, gbb) = upad2[kf % 2], tb2[kf % 2], pb2[kf % 2], gb2[kf % 2]
            dve(lambda e: e.tensor_copy(out=upad[:, 0:2], in_=ucar[:, kf, :]), [ucarb], [upadb])
            lo = FLO[half]
            for (l0, N) in FGR[half]:
                t0 = g0 + l0
                pv, pn_ = proj_fm(wu, wun, 0, 128, hT, "hT", t0, N)
                evac(upad[:, 2 + l0:2 + l0 + N], upadb, pv, pn_, eng="act")
                act(lambda e, pv=pv, l0=l0, N=N: e.activation(out=tb_[:, l0:l0 + N], in_=pv, func=AF.Identity,
                                                              bias=CBc(kf), scale=CW(2, kf)), [pn_, "spm"], [tbb])
                pv, pn_ = proj_fm(wg_, wgn_, 0, 128, hT, "hT", t0, N)
                evac(gb_[:, l0:l0 + N], gbb, pv, pn_, eng="act")
            if half == 0:
                dve(lambda e: e.tensor_copy(out=ucar[:, kf, :], in_=upad[:, HT_:HT_ + 2]), [upadb], [ucarb])
            else:
                dve(lambda e: e.tensor_copy(out=ucv[:, kf, 0:2], in_=upad[:, 512:514]), [upadb], [ucvb])
                u3 = upad[:, 2 + 512:2 + 640].rearrange("p (j l) -> p j l", l=8)
                dve(lambda e, u3=u3: e.tensor_copy(out=ucv[:, kf, 2:34].rearrange("p (j r) -> p j r", r=2),
                                                   in_=u3[:, :, 6:8]), [upadb], [ucvb])
            dve(lambda e: e.scalar_tensor_tensor(out=tb_[:, lo:npr], in0=upad[:, 1 + lo:1 + npr], scalar=CW(1, kf),
                                                 in1=tb_[:, lo:npr], op0=ALU.mult, op1=ALU.add),
                [upadb, tbb, "spm"], [tbb])
            dve(lambda e: e.scalar_tensor_tensor(out=tb_[:, lo:npr], in0=upad[:, lo:npr], scalar=CW(0, kf),
                                                 in1=tb_[:, lo:npr], op0=ALU.mult, op1=ALU.add),
                [upadb, tbb, "spm"], [tbb])
            if half == 1:
                t3 = tb_[:, 512:640].rearrange("p (j l) -> p j l", l=8)
                u3 = upad[:, 2 + 512:2 + 640].rearrange("p (j l) -> p j l", l=8)
                s3 = scvT[:, kf, :].rearrange("p (j r) -> p j r", r=2)
                dve(lambda e, t3=t3, u3=u3: e.scalar_tensor_tensor(
                    out=t3[:, :, 1:8], in0=u3[:, :, 0:7], scalar=CW(1, kf), in1=t3[:, :, 1:8], op0=ALU.mult,
                    op1=ALU.add), [upadb, tbb, "spm"], [tbb])
                dve(lambda e, t3=t3, s3=s3: e.scalar_tensor_tensor(
                    out=t3[:, :, 0:1], in0=s3[:, :, 1:2], scalar=CW(1, kf), in1=t3[:, :, 0:1], op0=ALU.mult,
                    op1=ALU.add), [scvTb, tbb, "spm"], [tbb])
                dve(lambda e, t3=t3, u3=u3: e.scalar_tensor_tensor(
                    out=t3[:, :, 2:8], in0=u3[:, :, 0:6], scalar=CW(0, kf), in1=t3[:, :, 2:8], op0=ALU.mult,
                    op1=ALU.add), [upadb, tbb, "spm"], [tbb])
                dve(lambda e, t3=t3, s3=s3: e.scalar_tensor_tensor(
                    out=t3[:, :, 0:2], in0=s3[:, :, 0:2], scalar=CW(0, kf), in1=t3[:, :, 0:2], op0=ALU.mult,
                    op1=ALU.add), [scvTb, tbb, "spm"], [tbb])
            fs = slice(lo, HT_)
            act(lambda e: e.activation(out=pb_[:, fs], in_=tb_[:, fs], func=AF.Square, scale=0.044715 ** 0.5), [tbb],
                [pbb])
            dve(lambda e: e.scalar_tensor_tensor(out=pb_[:, fs], in0=pb_[:, fs], scalar=1.0, in1=tb_[:, fs],
                                                 op0=ALU.add, op1=ALU.mult), [pbb, tbb], [pbb])
            act(lambda e: e.activation(out=pb_[:, fs], in_=pb_[:, fs], func=AF.Sigmoid, scale=1.5957691216057308),
                [pbb], [pbb])
            dve(lambda e: e.tensor_tensor(out=pb_[:, fs], in0=pb_[:, fs], in1=tb_[:, fs], op=ALU.mult), [pbb, tbb],
                [pbb])
            dve(lambda e: e.tensor_tensor(out=actT[:, kf, fs], in0=pb_[:, fs], in1=gb_[:, fs], op=ALU.mult),
                [pbb, gbb], [actTb])

        def down_block(half, cbk):
            g0 = half * HT_
            i = wdi[0] % 2
            wdi[0] += 1
            (tb_, tbb) = tb2[i]
            wd = wds[i]
            wdn = "wds%d" % i
            off = cbk * KF * 128
            wdf = wdsf[i]
            S.dma("pool", lambda e: e.dma_start(out=wdf[:, 0:22 * 128], in_=w_dn[:, off:off + 22 * 128]),
                  writes=[wdn])
            S.dma("pool", lambda e: e.dma_start(out=wdf[:, 22 * 128:44 * 128],
                                                in_=w_dn[:, off + 22 * 128:off + 44 * 128]), writes=[wdn])
            for (l0, N) in FGR[half]:
                bank, bname = nextbank()
                mm(bank[:, 0:N], bname, [(wd[:, k, :], actT[:, k, l0:l0 + N]) for k in range(KF)],
                   [wdn, actTb])
                evac(tb_[:, l0:l0 + N], tbb, bank[:, 0:N], bname, eng="act")
            tts = range(1, 5) if half == 0 else range(5)
            for tt in tts:
                dstp = pst[:, tt * 128:(tt + 1) * 128] if tt < 4 else pnum[:, 0:128]
                dstn = "pst" if tt < 4 else "pnum"
                tr(dstp, dstn, tb_[:, tt * 128:(tt + 1) * 128], tbb, identf, inc=(tt >= 3))
            evac(fst[:, 0:4, :], fstb, pst[:, 0:512].rearrange("p (a b) -> p a b", a=4), "pst", eng="dve")
            evac(fst[:, 4, :], fstb, pnum[:, 0:128], "pnum", eng="dve")
            for tt in tts:
                r0 = g0 + tt * 128
                sdma(x2scr[r0:r0 + 128, cbk * 128:(cbk + 1) * 128], fst[:, tt, :], [fstb], ["x2scr"])

        for half in range(2):
            for kf in range(KF):
                ffn_block(half, kf)
            for cbk in range(KD):
                down_block(half, cbk)
        oconv_s = o_sconv.rearrange("j r c -> (j r) c")
        for k4 in range(KF // 4):
            for q in range(4):
                tr(pmisc[0:34, q * 128:(q + 1) * 128], "pmisc", ucv[:, k4 * 4 + q, :], ucvb, identf, inc=(q == 3))
            evac(srow[0:34, :], srowb, pmisc[0:34, 0:512], "pmisc", eng="dve")
            sdma(o_pconv[:, k4 * 512:(k4 + 1) * 512], srow[0:2, :], [srowb], [])
            sdma(oconv_s[:, k4 * 512:(k4 + 1) * 512], srow[2:34, :], [srowb], [])

        arena_reset()
        sdma(wbc[:], fnw.partition_broadcast(128), [], ["wbc"])
        xa = [carve("xa%d" % i, [D], F32) for i in range(4)]
        xf = [carve("xf%d" % i, [D], F32) for i in range(4)]
        sq3, sq3b = carve("sq3", [D], BF16)
        for ti in range(1, NMT):
            (x_a, xab), (x_f, xfb) = xa[ti % 4], xf[ti % 4]
            sdma(x_a, x1scr[ti * 128:(ti + 1) * 128, :], ["x1scr"], [xab])
            S.dma("pool", lambda e, x_f=x_f, ti=ti: e.dma_start(out=x_f, in_=x2scr[ti * 128:(ti + 1) * 128, :]),
                  reads=["x2scr"], writes=[xfb])
            dve(lambda e, x_a=x_a, x_f=x_f: e.tensor_tensor(out=x_a, in0=x_a, in1=x_f, op=ALU.add), [xab, xfb], [xab])
            ss, rs, _, sbn = statslot()
            act(lambda e, x_a=x_a, ss=ss: e.activation(out=sq3, in_=x_a, func=AF.Square, accum_out=ss), [xab],
                [sq3b, sbn])
            rstd_from_ss(ss, rs, sbn, D)
            dve(lambda e, x_a=x_a, x_f=x_f, rs=rs: e.scalar_tensor_tensor(out=x_f, in0=x_a, scalar=rs, in1=wbc[:],
                                                                          op0=ALU.mult, op1=ALU.mult),
                [xab, sbn + "r", "wbc"], [xfb])
            sdma(yout[(ti - 1) * 128:ti * 128, :], x_f, [xfb], [])
        return finish(nc, S, dbg_out)


def finish(nc, S, dbg_out):
    S.finish()
    print("sim phase us:", [int(x) for x in S.sim_phase_us], "units", len(S.units))
    with nc.Block() as block:
        @block.sync
        def _(eng):
            S.emit("sp", eng)

        @block.tensor
        def _(eng):
            S.emit("pe", eng)

        @block.scalar
        def _(eng):
            S.emit("act", eng)

        @block.vector
        def _(eng):
            S.emit("dve", eng)

        @block.gpsimd
        def _(eng):
            S.emit("pool", eng)
    return nc


def make_consts():
    import ml_dtypes
    cb = np.zeros((128, CB_W), np.float32)
    cb[:, 0:128] = np.eye(128)
    s = np.arange(128)[:, None]
    t = np.arange(128)[None, :]
    cb[:, 128:256] = (s <= t)
    cb[:, 256:384] = (s <= t) & ((s // 8) == (t // 8))
    j = np.arange(16)[:, None]
    cb[:, 512:512 + 2048] = ((np.arange(128)[None, :] // 8) == j).astype(np.float32).reshape(1, 2048)
    cb[:, 2560:2576] = ((np.arange(128)[:, None] // 8) == np.arange(16)[None, :])
    cf = np.zeros((128, CF_W), np.float32)
    cf[:, 0:128] = np.eye(128)
    cf[:, 128:256] = 1.0
    cf[:, 256:384] = (np.arange(128)[None, :] % 8 != 0)
    cf[:, 384:512] = np.where(np.arange(128)[None, :] % 8 == 0, -1e30, 0.0)
    return cb.astype(ml_dtypes.bfloat16), cf


_NC_CACHE = {}


def _prep_inputs(inp):
    f32 = np.float32
    cbc, cfc = make_consts()
    xp = np.asarray(inp["x_prompt"], f32)
    xs = np.asarray(inp["x_sample"], f32)
    sp = np.zeros((128, SP_W), f32)
    ib = np.asarray(inp["mlstm_i_bias"], f32)[0]
    fb = np.asarray(inp["mlstm_f_bias"], f32)[0]
    sp[0:64, 0] = np.repeat(ib, 16)
    sp[0:64, 1] = np.repeat(fb, 16)
    sp[:, 2:6] = np.asarray(inp["gla_alpha_bias"], f32)[0].reshape(4, 128).T
    sp[:, 6:14] = np.asarray(inp["mlstm_head_norm_w"], f32)[0].reshape(8, 128).T
    sp[:, 14:22] = np.asarray(inp["gla_head_norm_w"], f32)[0].reshape(8, 128).T
    cw = np.asarray(inp["ffn_conv_w"], f32)[0]
    for j in range(3):
        sp[:, 22 + j * KF:22 + (j + 1) * KF] = cw[j].reshape(KF, 128).T
    sp[:, 22 + 3 * KF:22 + 4 * KF] = np.asarray(inp["ffn_conv_b"], f32)[0].reshape(KF, 128).T
    shared = {
        "w_in": _pack(np.asarray(inp["w_in"], f32)[0], KD, _win_blocks()),
        "nmw": np.asarray(inp["norm_mix_w"], f32).reshape(1, D),
        "nfw": np.asarray(inp["norm_ffn_w"], f32).reshape(1, D),
        "fnw": np.asarray(inp["final_norm_w"], f32).reshape(1, D),
        "cst_bf": cbc, "cst_f": cfc, "smallp": sp,
        "aup": np.asarray(inp["gla_alpha_up"], f32)[0],
        "w_a": _pack(np.asarray(inp["w_branch_a"], f32)[0], 8, [(c * 256, 256) for c in range(8)]),
        "w_b": _pack(np.asarray(inp["w_branch_b"], f32)[0], 8, [(c * 256, 256) for c in range(8)]),
        "w_o": _pack(np.asarray(inp["w_out"], f32)[0], KD, [(c * 256, 256) for c in range(8)]),
        "w_up": _pack(np.asarray(inp["ffn_w_up"], f32)[0], KD, [(c * 128, 128) for c in range(KF)]),
        "w_gt": _pack(np.asarray(inp["ffn_w_gate"], f32)[0], KD, [(c * 128, 128) for c in range(KF)]),
        "w_dn": _pack(np.asarray(inp["ffn_w_down"], f32)[0], KF, [(c * 128, 128) for c in range(KD)]),
    }
    sC = np.asarray(inp["state_mlstm_C"], f32)[0]
    sn = np.asarray(inp["state_mlstm_n"], f32)[0]
    sm = np.asarray(inp["state_mlstm_m"], f32)[0]
    sS = np.asarray(inp["state_gla_S"], f32)[0]
    scv = np.asarray(inp["state_ffn_conv"], f32)[0]
    maps = []
    for c in range(8):
        s, half = c // 2, c % 2
        xmain = np.zeros((TM, D), f32)
        if half == 1:
            xpre = np.ascontiguousarray(xp[s, 0:PRE])
            xmain[0:1152] = xp[s, PRE:2048]
        else:
            xpre = np.zeros((PRE, D), f32)
            xmain[128:1152] = xp[s, 0:1024]
        xmain[1152:1280] = xs[16 * c:16 * c + 16].reshape(128, D)
        m = dict(shared)
        m.update({
            "xpre": xpre, "xmain": xmain, "flag": np.full((1, 1), float(half), f32),
            "sC": np.ascontiguousarray(sC[16 * c:16 * c + 16]),
            "sn": np.ascontiguousarray(sn[16 * c:16 * c + 16].reshape(64, 256)),
            "smcol": np.ascontiguousarray(sm[16 * c:16 * c + 16].T.reshape(64, 1)),
            "sS": np.ascontiguousarray(sS[16 * c:16 * c + 16]),
            "sconv": np.ascontiguousarray(scv[16 * c:16 * c + 16]),
        })
        maps.append(m)
    return maps


def _assemble(results):
    f32 = np.float32
    y_p = np.zeros((4, 2048, D), f32)
    y_s = np.zeros((128, 8, D), f32)
    pC = np.zeros((1, 4, 4, 256, 256), f32)
    pn = np.zeros((1, 4, 4, 256), f32)
    pm = np.zeros((1, 4, 4), f32)
    pS = np.zeros((1, 4, 4, 128, 256), f32)
    pcv = np.zeros((1, 4, 2, DFF), f32)
    sCo = np.zeros((1, 128, 4, 256, 256), f32)
    sno = np.zeros((1, 128, 4, 256), f32)
    smo = np.zeros((1, 128, 4), f32)
    sSo = np.zeros((1, 128, 4, 128, 256), f32)
    scvo = np.zeros((1, 128, 2, DFF), f32)
    for c in range(8):
        r = results[c]
        s, half = c // 2, c % 2
        yo = np.asarray(r["yout"], f32)
        y_p[s, half * 1024:(half + 1) * 1024] = yo[0:1024]
        y_s[16 * c:16 * c + 16] = yo[1024:1152].reshape(16, 8, D)
        if half == 1:
            pC[0, s] = np.asarray(r["o_pC"], f32)
            pn[0, s] = np.asarray(r["o_pn"], f32).reshape(4, 256)
            pm[0, s] = np.asarray(r["o_pm"], f32).reshape(4, 16)[:, 15]
            pS[0, s] = np.asarray(r["o_pS"], f32)
            pcv[0, s] = np.asarray(r["o_pconv"], f32)
        sl = slice(16 * c, 16 * c + 16)
        sCo[0, sl] = np.asarray(r["o_sC"], f32)
        sno[0, sl] = np.asarray(r["o_sn"], f32).reshape(2, 16, 4, 128).transpose(1, 2, 0, 3).reshape(16, 4, 256)
        smo[0, sl] = np.asarray(r["o_sm"], f32).reshape(4, 16).T
        sSo[0, sl] = np.asarray(r["o_sS"], f32)
        scvo[0, sl] = np.asarray(r["o_sconv"], f32)
    return (y_p, y_s, pC, pn, pm, pS, pcv, sCo, sno, smo, sSo, scvo)


def kernel(**inputs):
    if "nc" not in _NC_CACHE:
        _NC_CACHE["nc"] = build()
    nc = _NC_CACHE["nc"]
    maps = _prep_inputs(inputs)
    res = run_bass_kernel_spmd(nc, maps, core_ids=list(range(8)))
    return _assemble(res.results)
```
